# Optimizing a Trainium2 kernel written in Bass

```python
import jax
import jax.numpy as jnp
from jax import lax
import numpy as np

D_MODEL = 1024
BATCH = 16
SEQ = 2048
DEPTH = 2

GRID_W = 64
ROPE_THETA = 10000.0
NORM_EPS = 1e-6
Q_BLOCK = 128
N_BRANCH = 4

SSD_HEADS = 16
SSD_HEAD_DIM = 64
SSD_INNER = SSD_HEADS * SSD_HEAD_DIM
SSD_GROUPS = 2
SSD_STATE = 128
SSD_CONV = 5
SSD_CHUNK = 128
SSD_CONV_DIM = SSD_INNER + 2 * SSD_GROUPS * SSD_STATE

MLA_HEADS = 8
MLA_Q_RANK = 384
MLA_KV_RANK = 256
MLA_NOPE = 64
MLA_ROPE = 32
MLA_V = 64
MLA_WIDTH = MLA_HEADS * MLA_V

GLA_HEADS = 4
GLA_DK = 64
GLA_DV = 128
GLA_GATE_RANK = 16
GLA_TAU = 16.0
GLA_CHUNK = 64
GLA_WIDTH = GLA_HEADS * GLA_DV

GQA_HEADS = 8
GQA_KV_HEADS = 2
GQA_HEAD_DIM = 64
GQA_WIDTH = GQA_HEADS * GQA_HEAD_DIM

IN_WIDTHS = (
    N_BRANCH * D_MODEL,
    SSD_INNER,
    SSD_CONV_DIM,
    2 * SSD_HEADS,
    MLA_WIDTH,
    MLA_Q_RANK,
    MLA_KV_RANK,
    MLA_ROPE,
    GLA_WIDTH,
    GLA_HEADS * GLA_DK,
    GLA_HEADS * GLA_DK,
    GLA_WIDTH,
    2 * GLA_GATE_RANK,
    GQA_WIDTH,
    GQA_WIDTH,
    GQA_KV_HEADS * GQA_HEAD_DIM,
    GQA_KV_HEADS * GQA_HEAD_DIM,
)
N_IN = sum(IN_WIDTHS)

kernel_name = 'hybrid_gated_branch_encoder'


def rmsnorm(x, g):
    xf = x.astype(jnp.float32)
    y = xf * lax.rsqrt(jnp.mean(xf * xf, axis=-1, keepdims=True) + NORM_EPS)
    return (y * g.astype(jnp.float32)).astype(x.dtype)


def rev(t):
    return jnp.flip(t, axis=1)


def axial_rope_tables(rows, d_rot):
    row = jnp.repeat(jnp.arange(rows), GRID_W).astype(jnp.float32)
    col = jnp.tile(jnp.arange(GRID_W), rows).astype(jnp.float32)
    m = d_rot // 2
    inv = ROPE_THETA ** (-jnp.arange(0, m, 2, dtype=jnp.float32) / m)
    ang_r = row[:, None] * inv
    ang_c = col[:, None] * inv
    ang = jnp.concatenate([ang_r, ang_r, ang_c, ang_c], axis=-1)
    return jnp.cos(ang), jnp.sin(ang)


def apply_axial_rope(x, cos, sin):
    d = x.shape[-1]
    m = d // 2
    hm = m // 2
    xf = x.astype(jnp.float32)
    x1 = xf[..., :m]
    x2 = xf[..., m:]
    rot = jnp.concatenate([-x1[..., hm:], x1[..., :hm], -x2[..., hm:], x2[..., :hm]], axis=-1)
    return (xf * cos[None, :, None, :] + rot * sin[None, :, None, :]).astype(x.dtype)


def block_attention(q, k, v, scale):
    b, L, hq, d = q.shape
    hk = k.shape[2]
    r = hq // hk
    dv = v.shape[-1]
    nb = L // Q_BLOCK
    qb = q.reshape(b, nb, Q_BLOCK, hk, r, d).transpose(1, 0, 2, 3, 4, 5)

    def one_block(qblk):
        s = jnp.einsum('bqgrd,bkgd->bgrqk', qblk, k).astype(jnp.float32) * scale
        p = jax.nn.softmax(s, axis=-1).astype(v.dtype)
        return jnp.einsum('bgrqk,bkgv->bqgrv', p, v)

    o = lax.map(one_block, qb)
    return o.transpose(1, 0, 2, 3, 4, 5).reshape(b, L, hq * dv)


def centred_depthwise_conv(u, w, bias):
    pad = (SSD_CONV - 1) // 2
    y = lax.conv_general_dilated(u, w[:, None, :].astype(u.dtype), window_strides=(1,),
                                 padding=[(pad, pad)], dimension_numbers=('NWC', 'WIO', 'NWC'),
                                 feature_group_count=u.shape[-1])
    return y + bias.astype(u.dtype)


def ssd_chunked(x, dt, a_neg, bm, cm):
    b, L, H, P = x.shape
    G, N = bm.shape[2], bm.shape[3]
    E = H // G
    Q = SSD_CHUNK
    nc = L // Q
    xd = (x.astype(jnp.float32) * dt[..., None]).astype(x.dtype).reshape(b, nc, Q, G, E, P)
    a_cum = jnp.cumsum((dt * a_neg).reshape(b, nc, Q, G, E), axis=2)
    bc = bm.reshape(b, nc, Q, G, N)
    cc = cm.reshape(b, nc, Q, G, N)
    tri = jnp.tril(jnp.ones((Q, Q), dtype=bool))
    seg = a_cum[:, :, :, None] - a_cum[:, :, None, :]
    decay = jnp.exp(jnp.where(tri[None, None, :, :, None, None], seg, -jnp.inf)).astype(x.dtype)
    cb = jnp.einsum('bclgn,bcsgn->bclsg', cc, bc)
    y_diag = jnp.einsum('bclsge,bcsgep->bclgep', cb[..., None] * decay, xd)
    decay_end = jnp.exp(a_cum[:, :, -1:] - a_cum).astype(x.dtype)
    states = jnp.einsum('bcsgn,bcsgep->bcgepn', bc, xd * decay_end[..., None])
    chunk_decay = jnp.exp(a_cum[:, :, -1]).astype(x.dtype)

    def step(s, inp):
        dec, st = inp
        return dec[..., None, None] * s + st, s

    s0 = jnp.zeros_like(states[:, 0])
    _, s_prev = lax.scan(step, s0, (chunk_decay.transpose(1, 0, 2, 3),
                                    states.transpose(1, 0, 2, 3, 4, 5)))
    s_prev = s_prev.transpose(1, 0, 2, 3, 4, 5)
    y_off = jnp.einsum('bclgn,bcgepn->bclgep', cc, s_prev) * jnp.exp(a_cum).astype(x.dtype)[..., None]
    return (y_diag + y_off).reshape(b, L, H, P)


def gla_chunked(q, k, v, g_log):
    b, L, H, K = q.shape
    V = v.shape[-1]
    Q = GLA_CHUNK
    nc = L // Q
    g_cum = jnp.cumsum(g_log.reshape(b, nc, Q, H, K), axis=2)
    qc = q.reshape(b, nc, Q, H, K).astype(jnp.float32)
    kc = k.reshape(b, nc, Q, H, K).astype(jnp.float32)
    vc = v.reshape(b, nc, Q, H, V)
    qg = (qc * jnp.exp(g_cum)).astype(q.dtype)
    kg = (kc * jnp.exp(-g_cum)).astype(q.dtype)
    k_end = (kc * jnp.exp(g_cum[:, :, -1:] - g_cum)).astype(q.dtype)
    tri = jnp.tril(jnp.ones((Q, Q), dtype=bool))
    att = jnp.einsum('bclhk,bcshk->bchls', qg, kg)
    att = jnp.where(tri, att, jnp.zeros((), att.dtype))
    o_intra = jnp.einsum('bchls,bcshv->bclhv', att, vc)
    u = jnp.einsum('bcshk,bcshv->bchkv', k_end, vc)
    chunk_decay = jnp.exp(g_cum[:, :, -1]).astype(q.dtype)

    def step(s, inp):
        dec, st = inp
        return dec[..., None] * s + st, s

    s0 = jnp.zeros_like(u[:, 0])
    _, s_prev = lax.scan(step, s0, (chunk_decay.transpose(1, 0, 2, 3),
                                    u.transpose(1, 0, 2, 3, 4)))
    s_prev = s_prev.transpose(1, 0, 2, 3, 4)
    o_inter = jnp.einsum('bclhk,bchkv->bclhv', qg, s_prev)
    return (o_intra + o_inter).reshape(b, L, H, V)


def ssd_branch(z, xbc, dt_raw, conv_w, conv_b, a_log, dt_bias, d_skip, norm_g, w_br):
    b, L, _ = xbc.shape
    xbc = jax.nn.silu(centred_depthwise_conv(xbc, conv_w, conv_b))
    xs, bm, cm = jnp.split(xbc, [SSD_INNER, SSD_INNER + SSD_GROUPS * SSD_STATE], axis=-1)
    xs = xs.reshape(b, L, SSD_HEADS, SSD_HEAD_DIM)
    bm = bm.reshape(b, L, SSD_GROUPS, SSD_STATE)
    cm = cm.reshape(b, L, SSD_GROUPS, SSD_STATE)
    dt = jax.nn.softplus(dt_raw.astype(jnp.float32).reshape(b, L, 2, SSD_HEADS)
                         + dt_bias.astype(jnp.float32))
    a_neg = -jnp.exp(a_log.astype(jnp.float32))
    y_f = ssd_chunked(xs, dt[:, :, 0], a_neg[0], bm, cm)
    y_b = rev(ssd_chunked(rev(xs), rev(dt[:, :, 1]), a_neg[1], rev(bm), rev(cm)))
    y = (y_f + y_b + xs * d_skip[:, None].astype(xs.dtype)).reshape(b, L, SSD_INNER)
    return rmsnorm(y * jax.nn.silu(z), norm_g) @ w_br


def mla_branch(z, q_lat, kv_lat, k_rope, q_lat_norm_g, kv_lat_norm_g, w_q_b, w_kv_b, w_br, cos, sin):
    b, L, _ = q_lat.shape
    q = (rmsnorm(q_lat, q_lat_norm_g) @ w_q_b).reshape(b, L, MLA_HEADS, MLA_NOPE + MLA_ROPE)
    q = jnp.concatenate([q[..., :MLA_NOPE], apply_axial_rope(q[..., MLA_NOPE:], cos, sin)], axis=-1)
    kv = (rmsnorm(kv_lat, kv_lat_norm_g) @ w_kv_b).reshape(b, L, MLA_HEADS, MLA_NOPE + MLA_V)
    k_nope = kv[..., :MLA_NOPE]
    v = kv[..., MLA_NOPE:]
    k_r = apply_axial_rope(k_rope.reshape(b, L, 1, MLA_ROPE), cos, sin)
    k = jnp.concatenate([k_nope, jnp.broadcast_to(k_r, (b, L, MLA_HEADS, MLA_ROPE))], axis=-1)
    o = block_attention(q, k, v, (MLA_NOPE + MLA_ROPE) ** -0.5)
    return (o * jax.nn.silu(z)) @ w_br


def gla_branch(z, q_c, k_c, v_c, g_lr, w_gate_up, b_gate, norm_g, w_br):
    b, L, _ = q_c.shape
    q = q_c.reshape(b, L, GLA_HEADS, GLA_DK) * (GLA_DK ** -0.5)
    k = k_c.reshape(b, L, GLA_HEADS, GLA_DK)
    v = v_c.reshape(b, L, GLA_HEADS, GLA_DV)
    g_pre = jnp.einsum('blnr,nrk->blnk', g_lr.astype(jnp.float32).reshape(b, L, 2, GLA_GATE_RANK),
                       w_gate_up.astype(jnp.float32)) + b_gate.astype(jnp.float32)
    g_log = jax.nn.log_sigmoid(g_pre) / GLA_TAU
    g_f = g_log[:, :, 0].reshape(b, L, GLA_HEADS, GLA_DK)
    g_b = g_log[:, :, 1].reshape(b, L, GLA_HEADS, GLA_DK)
    o = gla_chunked(q, k, v, g_f) + rev(gla_chunked(rev(q), rev(k), rev(v), rev(g_b)))
    o = rmsnorm(o, norm_g.reshape(GLA_HEADS, GLA_DV)).reshape(b, L, GLA_WIDTH)
    return (o * jax.nn.silu(z)) @ w_br


def gqa_branch(z, q_d, k_d, v_d, q_norm_g, k_norm_g, w_br, cos, sin):
    b, L, _ = q_d.shape
    q = rmsnorm(q_d.reshape(b, L, GQA_HEADS, GQA_HEAD_DIM), q_norm_g)
    k = rmsnorm(k_d.reshape(b, L, GQA_KV_HEADS, GQA_HEAD_DIM), k_norm_g)
    q = apply_axial_rope(q, cos, sin)
    k = apply_axial_rope(k, cos, sin)
    v = v_d.reshape(b, L, GQA_KV_HEADS, GQA_HEAD_DIM)
    o = block_attention(q, k, v, GQA_HEAD_DIM ** -0.5)
    return (o * jax.nn.silu(z)) @ w_br


def setup_inputs(seed: int = 0) -> dict:
    key = jax.random.key(seed)
    ks = jax.random.split(key, 32)
    f32 = jnp.float32

    def nrm(k, shape, scale):
        return jax.random.normal(k, shape, f32) * scale

    def gain(k, shape):
        return 1.0 + 0.02 * jax.random.normal(k, shape, f32)

    dt0 = jnp.exp(jax.random.uniform(ks[5], (DEPTH, 2, SSD_HEADS), f32, np.log(1e-3), np.log(1e-1)))
    return {
        'x': jax.random.normal(ks[0], (BATCH, SEQ, D_MODEL), f32),
        'norm_g': gain(ks[1], (DEPTH, D_MODEL)),
        'w_in': nrm(ks[2], (DEPTH, D_MODEL, N_IN), D_MODEL ** -0.5),
        'conv_w': nrm(ks[3], (DEPTH, SSD_CONV, SSD_CONV_DIM), SSD_CONV ** -0.5),
        'conv_b': nrm(ks[4], (DEPTH, SSD_CONV_DIM), 0.01),
        'a_log': jnp.log(jax.random.uniform(ks[6], (DEPTH, 2, SSD_HEADS), f32, 1.0, 16.0)),
        'dt_bias': dt0 + jnp.log(-jnp.expm1(-dt0)),
        'd_skip': gain(ks[7], (DEPTH, SSD_HEADS)),
        'ssd_norm_g': gain(ks[8], (DEPTH, SSD_INNER)),
        'q_lat_norm_g': gain(ks[9], (DEPTH, MLA_Q_RANK)),
        'kv_lat_norm_g': gain(ks[10], (DEPTH, MLA_KV_RANK)),
        'w_q_b': nrm(ks[11], (DEPTH, MLA_Q_RANK, MLA_HEADS * (MLA_NOPE + MLA_ROPE)), MLA_Q_RANK ** -0.5),
        'w_kv_b': nrm(ks[12], (DEPTH, MLA_KV_RANK, MLA_HEADS * (MLA_NOPE + MLA_V)), MLA_KV_RANK ** -0.5),
        'w_gate_up': nrm(ks[13], (DEPTH, 2, GLA_GATE_RANK, GLA_HEADS * GLA_DK), GLA_GATE_RANK ** -0.5),
        'b_gate': nrm(ks[14], (DEPTH, 2, GLA_HEADS * GLA_DK), 0.01),
        'gla_norm_g': gain(ks[15], (DEPTH, GLA_WIDTH)),
        'q_norm_g': gain(ks[16], (DEPTH, GQA_HEAD_DIM)),
        'k_norm_g': gain(ks[17], (DEPTH, GQA_HEAD_DIM)),
        'w_br_a': nrm(ks[18], (DEPTH, SSD_INNER, D_MODEL), SSD_INNER ** -0.5),
        'w_br_b': nrm(ks[19], (DEPTH, MLA_WIDTH, D_MODEL), MLA_WIDTH ** -0.5),
        'w_br_c': nrm(ks[20], (DEPTH, GLA_WIDTH, D_MODEL), GLA_WIDTH ** -0.5),
        'w_br_d': nrm(ks[21], (DEPTH, GQA_WIDTH, D_MODEL), GQA_WIDTH ** -0.5),
        'w_out': nrm(ks[22], (DEPTH, D_MODEL, D_MODEL), 0.5 * D_MODEL ** -0.5),
        'final_g': gain(ks[23], (D_MODEL,)),
    }


def reference(x, norm_g, w_in, conv_w, conv_b, a_log, dt_bias, d_skip, ssd_norm_g,
              q_lat_norm_g, kv_lat_norm_g, w_q_b, w_kv_b, w_gate_up, b_gate, gla_norm_g,
              q_norm_g, k_norm_g, w_br_a, w_br_b, w_br_c, w_br_d, w_out, final_g):
    b, L, _ = x.shape
    rows = L // GRID_W
    cos_m, sin_m = axial_rope_tables(rows, MLA_ROPE)
    cos_g, sin_g = axial_rope_tables(rows, GQA_HEAD_DIM)
    split_idx = [int(v) for v in np.cumsum(IN_WIDTHS)[:-1]]
    for i in range(DEPTH):
        h = rmsnorm(x, norm_g[i])
        u = h @ w_in[i]
        (g_merge, z_a, xbc, dt_raw, z_b, q_lat, kv_lat, k_rope, z_c, q_c, k_c, v_c, g_lr,
         z_d, q_d, k_d, v_d) = jnp.split(u, split_idx, axis=-1)
        y_a = ssd_branch(z_a, xbc, dt_raw, conv_w[i], conv_b[i], a_log[i], dt_bias[i], d_skip[i],
                         ssd_norm_g[i], w_br_a[i])
        y_b = mla_branch(z_b, q_lat, kv_lat, k_rope, q_lat_norm_g[i], kv_lat_norm_g[i], w_q_b[i],
                         w_kv_b[i], w_br_b[i], cos_m, sin_m)
        y_c = gla_branch(z_c, q_c, k_c, v_c, g_lr, w_gate_up[i], b_gate[i], gla_norm_g[i], w_br_c[i])
        y_d = gqa_branch(z_d, q_d, k_d, v_d, q_norm_g[i], k_norm_g[i], w_br_d[i], cos_g, sin_g)
        gates = jax.nn.sigmoid(g_merge.astype(jnp.float32)).astype(x.dtype).reshape(b, L, N_BRANCH, D_MODEL)
        mixed = gates[:, :, 0] * y_a + gates[:, :, 1] * y_b + gates[:, :, 2] * y_c + gates[:, :, 3] * y_d
        x = x + mixed @ w_out[i]
    return rmsnorm(x, final_g)
```

```python
import sys
import numpy as np
from contextlib import ExitStack
import concourse.bass as bass
import concourse.mybir as mybir
from concourse.bass_utils import run_bass_kernel_spmd

F32 = mybir.dt.float32
BF16 = mybir.dt.bfloat16
ALU = mybir.AluOpType
AF = mybir.ActivationFunctionType
AX = mybir.AxisListType

_ESZ = {F32: 4, BF16: 2}


def _esz(dt):
    return _ESZ.get(dt, 4)


def region(ap):
    a = ap.ap
    off = int(ap.offset)
    es = _esz(ap.dtype)
    name = ap.tensor.name
    sp = str(ap.space)
    if sp == "DRAM":
        ext = sum((c - 1) * abs(s) for s, c in a) + 1
        return (name, 0, 1, off * es, (off + ext) * es)
    pstep, pcnt = a[0]
    if pstep == 0:
        pstep = 1 << 40
    p0 = off // pstep
    f0 = off % pstep
    ext = sum((c - 1) * abs(s) for s, c in a[1:]) + 1
    if sp == "PSUM":
        b0 = (f0 * es) // 2048 * 2048
        b1 = ((f0 + ext) * es + 2047) // 2048 * 2048
        return (name, p0 // 32 * 32, (p0 + pcnt + 31) // 32 * 32, b0, b1)
    return (name, p0, p0 + pcnt, f0 * es, (f0 + ext) * es)


def _ovl(r, s):
    return r[1] < s[2] and s[1] < r[2] and r[3] < s[4] and s[3] < r[4]


def _covers(r, s):
    return r[1] <= s[1] and r[2] >= s[2] and r[3] <= s[3] and r[4] >= s[4]


class Op:
    __slots__ = ("eng", "fn", "deps", "signal", "dma", "eidx", "rank", "waits", "line", "phase")

    def __init__(self, eng, fn):
        self.eng = eng
        self.fn = fn
        self.deps = set()
        self.signal = False
        self.dma = None
        self.waits = []


ENGS = ("pe", "act", "dve", "pool", "sp")
_WRAPPERS = ("mm", "tr", "actv", "tt", "ts", "stt", "copy", "memset", "recip", "dma")
NDMASEM = 8
EPOCH = 20000


class Prog:
    def __init__(self, nc):
        self.nc = nc
        self.ops = []
        self.acc = {}
        self.ndma = 0
        self.phase = ""

    def add(self, eng, fn, reads, writes, dma=False):
        op = Op(eng, fn)
        op.phase = self.phase
        try:
            fr = sys._getframe(1)
            if fr.f_code.co_filename == __file__ and fr.f_code.co_name in _WRAPPERS:
                fr = fr.f_back
            op.line = fr.f_lineno
        except Exception:
            op.line = 0
        rec_eng = "dma" if dma else eng
        idx = len(self.ops)
        ops = self.ops
        for ap in reads:
            r = region(ap)
            lst = self.acc.setdefault(r[0], [])
            done = False
            is_psum = (r[0] == "ps")
            for rec in lst:
                if rec[2]:
                    if _ovl(rec[0], r):
                        op.deps.add(rec[1])
                elif (not done) and (not dma) and rec[3] == rec_eng and rec[0] == r:
                    rec[1] = idx
                    done = True
                elif is_psum and rec[3] != rec_eng and _ovl(rec[0], r):
                    op.deps.add(rec[1])
            if not done:
                lst.append([r, idx, False, rec_eng])
        for ap in writes:
            r = region(ap)
            lst = self.acc.setdefault(r[0], [])
            keep = []
            for rec in lst:
                if _ovl(rec[0], r):
                    if rec[1] != idx:
                        op.deps.add(rec[1])
                    if _covers(r, rec[0]) and rec[1] != idx:
                        continue
                keep.append(rec)
            keep.append([r, idx, True, rec_eng])
            self.acc[r[0]] = keep
        if eng == "pe":
            op.deps = {d for d in op.deps if ops[d].eng != "pe"}
        if dma:
            op.dma = self.ndma
            self.ndma += 1
        ops.append(op)
        return op

    def mm(self, out, lhsT, rhs, start=True, stop=True):
        self.add("pe", lambda e: e.matmul(out, lhsT, rhs, start=start, stop=stop),
                 [lhsT, rhs], [out])

    def tr(self, out, in_, ident):
        self.add("pe", lambda e: e.transpose(out, in_, ident), [in_, ident], [out])

    def actv(self, out, in_, func, bias=None, scale=None, accum_out=None):
        kw = {}
        rd = [in_]
        wr = [out]
        if bias is not None:
            kw["bias"] = bias
            if not isinstance(bias, (int, float)):
                rd.append(bias)
        if scale is not None:
            kw["scale"] = scale
            if not isinstance(scale, (int, float)):
                rd.append(scale)
        if accum_out is not None:
            kw["accum_out"] = accum_out
            wr.append(accum_out)
        self.add("act", lambda e: e.activation(out, in_, func, **kw), rd, wr)

    def _veng(self, eng):
        return eng

    def tt(self, eng, out, in0, in1, op):
        self.add(eng, lambda e: e.tensor_tensor(out, in0, in1, op), [in0, in1], [out])

    def ts(self, eng, out, in0, s1, s2, op0, op1=None, accum_out=None):
        rd = [in0]
        if not isinstance(s1, (int, float)):
            rd.append(s1)
        if s2 is not None and not isinstance(s2, (int, float)):
            rd.append(s2)
        wr = [out]
        kw = {}
        if accum_out is not None:
            kw["accum_out"] = accum_out
            wr.append(accum_out)
        if op1 is None:
            self.add(eng, lambda e: e.tensor_scalar(out, in0, s1, None, op0, **kw), rd, wr)
        else:
            self.add(eng, lambda e: e.tensor_scalar(out, in0, s1, s2, op0, op1, **kw), rd, wr)

    def stt(self, eng, out, in0, scalar, in1, op0, op1):
        rd = [in0, in1]
        if not isinstance(scalar, (int, float)):
            rd.append(scalar)
        self.add(eng, lambda e: e.scalar_tensor_tensor(out, in0, scalar, in1, op0, op1), rd, [out])

    def copy(self, eng, out, in_):
        if eng == "act":
            self.add("act", lambda e: e.activation(out, in_, AF.Copy), [in_], [out])
        else:
            self.add(eng, lambda e: e.tensor_copy(out, in_), [in_], [out])

    def memset(self, eng, out, val):
        self.add(eng, lambda e: e.memset(out, val), [], [out])

    def recip(self, out, in_):
        self.add("dve", lambda e: e.reciprocal(out, in_), [in_], [out])

    def dma(self, out, in_, eng="sp", **kw):
        self.add(eng, lambda e: e.dma_start(out, in_, **kw), [in_], [out], dma=True)

    def emit(self, es, final_wait_all=True):
        nc = self.nc
        ops = self.ops
        cnt = {e: 0 for e in ENGS}
        for op in ops:
            op.eidx = cnt[op.eng]
            cnt[op.eng] += 1
        for i, op in enumerate(ops):
            for d in op.deps:
                dop = ops[d]
                if dop.dma is None:
                    if dop.eng == op.eng and op.eng != "sp":
                        pass
                    dop.signal = True
        last = {}
        for i, op in enumerate(ops):
            if op.dma is None:
                last[op.eng] = i
        for e, i in last.items():
            ops[i].signal = True
        rk = {e: 0 for e in ENGS}
        for op in ops:
            if op.dma is None and op.signal:
                rk[op.eng] += 1
                op.rank = rk[op.eng]
        nep = {e: (rk[e] + EPOCH - 1) // EPOCH + 1 for e in ENGS}
        sems = {}
        for e in ENGS:
            if e == "sp":
                continue
            sems[e] = [es.enter_context(nc.semaphore(f"s_{e}_{k}")) for k in range(nep[e])]
        dsem = [es.enter_context(nc.semaphore(f"s_dma_{k}")) for k in range(NDMASEM)]

        import os as _os
        simmode = bool(_os.environ.get("SIMMODE"))
        dinfo = {}
        m = 0
        for op in ops:
            if op.dma is None:
                continue
            if simmode and op.eng == "pool":
                sem = es.enter_context(nc.semaphore(f"s_u_{op.dma}"))
                dinfo[op.dma] = (("udma", op.dma), sem, 16)
            else:
                k = m % NDMASEM
                dinfo[op.dma] = (("dma", k), dsem[k], 16 * (m // NDMASEM + 1))
                m += 1

        def target(dop):
            if dop.dma is not None:
                return dinfo[dop.dma]
            r = dop.rank - 1
            return ((dop.eng, r // EPOCH), sems[dop.eng][r // EPOCH], r % EPOCH + 1)

        waited = {e: {} for e in ENGS}
        for i, op in enumerate(ops):
            w = waited[op.eng]
            need = {}
            for d in op.deps:
                key, sem, val = target(ops[d])
                if w.get(key, 0) >= val:
                    continue
                if key not in need or need[key][1] < val:
                    need[key] = (sem, val)
            if op.dma is not None:
                key, sem, val = dinfo[op.dma]
                val -= 16
                if val > 0 and w.get(key, 0) < val and (key not in need or need[key][1] < val):
                    need[key] = (sem, val)
            for key, (sem, val) in need.items():
                w[key] = val
                op.waits.append((sem, val))
        final_waits = []
        for e, i in last.items():
            key, sem, val = target(ops[i])
            final_waits.append((sem, val))
        fin = {}
        for key, sem, val in dinfo.values():
            if key not in fin or fin[key][1] < val:
                fin[key] = (sem, val)
        final_waits.extend(fin.values())

        byeng = {e: [op for op in ops if op.eng == e] for e in ENGS}

        annotate = bool(_os.environ.get("SIMMODE"))

        def run(e, name):
            for op in byeng[name]:
                for sem, val in op.waits:
                    e.wait_ge(sem, val)
                ins = op.fn(e)
                if annotate:
                    ins.annotate(f"L{op.line}")
                if op.dma is not None:
                    ins.then_inc(dinfo[op.dma][1], 16)
                elif op.signal:
                    r = op.rank - 1
                    ins.then_inc(sems[name][r // EPOCH], 1)
            if name == "sp":
                for sem, val in final_waits:
                    e.wait_ge(sem, val)

        with nc.Block() as block:
            @block.tensor
            def _(e):
                run(e, "pe")

            @block.scalar
            def _(e):
                run(e, "act")

            @block.vector
            def _(e):
                run(e, "dve")

            @block.gpsimd
            def _(e):
                run(e, "pool")

            @block.sync
            def _(e):
                run(e, "sp")
        return cnt


L = 2048
D = 1024
NT = 16
NB = 4
EPS = 1e-6
OFF = dict(g=0, za=4096, xbc=5120, dt=6656, zb=6688, qlat=7200, kvlat=7584, krope=7840,
           zc=7872, qc=8384, kc=8640, vc=8896, glr=9408, zd=9440, qd=9952, kd=10464, vd=10592)
N_IN = 10720
KB = 1024
BULK = ("dve", "dve", "pool")


def _rope_perm_sign(d):
    m = d // 2
    hm = m // 2
    perm = np.zeros(d, np.int64)
    sign = np.zeros(d, np.float32)
    for j in range(d):
        jj = j % m
        if jj < hm:
            perm[j] = j + hm
            sign[j] = -1.0
        else:
            perm[j] = j - hm
            sign[j] = 1.0
    return perm, sign


def _rope_tables(d):
    rows = L // 64
    row = np.repeat(np.arange(rows), 64).astype(np.float32)
    col = np.tile(np.arange(64), rows).astype(np.float32)
    m = d // 2
    inv = (np.float32(10000.0) ** (-np.arange(0, m, 2, dtype=np.float32) / np.float32(m))).astype(np.float32)
    ang_r = row[:, None] * inv
    ang_c = col[:, None] * inv
    ang = np.concatenate([ang_r, ang_r, ang_c, ang_c], axis=-1).astype(np.float32)
    _, sign = _rope_perm_sign(d)
    cos = np.cos(ang).astype(np.float32).T
    sin = (np.sin(ang).astype(np.float32) * sign[None, :]).T
    return np.ascontiguousarray(cos), np.ascontiguousarray(sin)


class MK:
    def __init__(self, nseq=2, layers=(0, 1), branches="abcd", debug=False):
        self.nseq = nseq
        self.layers = layers
        self.branches = branches
        self.debug = debug
        nc = self.nc = bass.Bass("TRN2", target_bir_lowering=False)
        es = self.es = ExitStack()
        self.P = Prog(nc)
        self.dbg_outs = {}
        di = lambda n, s: nc.dram_tensor(n, s, F32, kind="ExternalInput").ap()
        self.x = di("x", [nseq, L, D])
        self.out = nc.dram_tensor("out", [nseq, L, D], F32, kind="ExternalOutput").ap()
        self.xres = nc.dram_tensor("xres", [nseq, L, D], F32).ap()
        self.w_in = di("w_in", [2, D, N_IN])
        self.w_perm = di("w_perm", [2, D, 672])
        self.norm_g = di("norm_g", [3, D])
        self.qk_g = di("qk_g", [2, 64, 4])
        self.w_q_b = di("w_q_b", [2, 384, 768])
        self.w_qb_perm = di("w_qb_perm", [2, 384, 256])
        self.w_kv_b = di("w_kv_b", [2, 256, 1024])
        self.lat_g = di("lat_g", [2, 128, 5])
        self.w_gate_up = di("w_gate_up", [2, 2, 16, 256])
        self.b_gate = di("b_gate", [2, 2, 256])
        self.gla_g = di("gla_g", [2, 128, 4])
        self.conv_p = di("conv_p", [2, 128, 12, 6])
        self.a_log = di("a_log", [2, 32])
        self.dt_bias = di("dt_bias", [2, 32])
        self.d_skip = di("d_skip", [2, 16])
        self.ssd_norm_g = di("ssd_norm_g", [2, D])
        self.yb_d = nc.dram_tensor("yb_d", [L, D], F32).ap()
        self.w_br_a = di("w_br_a", [2, 1024, D])
        self.w_br_b = di("w_br_b", [2, 512, D])
        self.w_br_c = di("w_br_c", [2, 512, D])
        self.w_br_d = di("w_br_d", [2, 512, D])
        self.w_out = di("w_out", [2, D, D])
        self.rope_cos = di("rope_cos", [128, L])
        self.rope_sin = di("rope_sin", [128, L])
        sb = lambda n, s, d: es.enter_context(nc.sbuf_tensor(n, s, d))
        self.hT = sb("hT", [128, 8, L], BF16)
        self.COS = sb("COS", [128, L], F32)
        self.SIN = sb("SIN", [128, L], F32)
        self.ident = sb("ident", [128, 128], BF16)
        self.identf = sb("identf", [128, 128], F32)
        self.ones = sb("ones", [128, 128], BF16)
        self.onesf = sb("onesf", [128, 128], F32)
        self.triF = sb("triF", [128, 128], F32)
        self.triB = sb("triB", [128, 128], F32)
        self.LF = sb("LF", [128, 128], BF16)
        self.LB = sb("LB", [128, 128], BF16)
        self.TF = sb("TF", [128, 128], BF16)
        self.TB = sb("TB", [128, 128], BF16)
        self.TmFb = sb("TmFb", [128, 128], BF16)
        self.TmBb = sb("TmBb", [128, 128], BF16)
        self.ARW = 155 * 256
        self.AR = sb("AR", [128, self.ARW], F32)
        self.ps = es.enter_context(nc.psum_tensor("ps", [128, 8, 512], F32))
        self.mixT = self.view(0, [8, L], BF16)
        self.onT = self.view(32 * KB, [8, L], BF16)
        self.acur = 64 * KB
        self.stage = self.view(147 * KB, [4, 512], F32)
        self.nst = 0
        self.pre = {}
        self.alim = 147 * KB

    def view(self, off, shape, dt, p0=0, parts=128):
        es_ = _esz(dt)
        n = int(np.prod(shape))
        assert off % 4 == 0
        w0 = off // 4
        w1 = (off + n * es_ + 3) // 4
        assert w1 <= self.ARW, (off, shape)
        ap = self.AR[p0:p0 + parts, w0:w1]
        if dt != F32:
            ap = ap.bitcast(dt)
        if len(shape) == 2:
            ap = ap.rearrange("p (a b) -> p a b", a=shape[0])
        elif len(shape) == 3:
            ap = ap.rearrange("p (a b c) -> p a b c", a=shape[0], b=shape[1])
        return ap

    def alloc(self, shape, dt, p0=0, parts=128):
        n = int(np.prod(shape)) * _esz(dt)
        n = (n + 63) // 64 * 64
        v = self.view(self.acur, shape, dt, p0, parts)
        self.acur += n
        assert self.acur <= self.alim or getattr(self, "allow_over", False), (self.acur, shape)
        return v

    def dbg(self, name, ap_sb, shape):
        if not self.debug:
            return
        d = self.nc.dram_tensor("dbg_" + name, shape, ap_sb.dtype, kind="ExternalOutput").ap()
        self.dbg_outs[name] = d
        self.P.dma(d, ap_sb)

    def wcols(self, src, l, c0, c1):
        return src[l].rearrange("(k p) c -> p k c", p=128)[:, :, c0:c1]

    def wload(self, dst, src, l, c0, c1, engs=("pool",)):
        P = self.P
        nk = dst.shape[1]
        C = c1 - c0
        for k in range(nk):
            for cc in range(0, C, 512):
                w = min(512, C - cc)
                st = self.stage[:, self.nst % 4, 0:w]
                self.nst += 1
                P.dma(st, src[l, k * 128:(k + 1) * 128, c0 + cc:c0 + cc + w])
                P.copy(engs[self.nst % len(engs)], dst[:, k, cc:cc + w], st)

    def setup(self):
        P = self.P
        P.dma(self.COS[:], self.rope_cos)
        P.dma(self.SIN[:], self.rope_sin)
        P.memset("pool", self.onesf[:], 1.0)
        P.memset("pool", self.ones[:], 1.0)
        P.memset("pool", self.identf[:], 0.0)
        idf = self.identf
        onf = self.onesf
        P.add("pool", lambda e: e.affine_select(idf[:], onf[:], [[-1, 128]], ALU.is_equal, 0.0,
                                                base=0, channel_multiplier=1), [onf[:]], [idf[:]])
        P.copy("dve", self.ident[:], self.identf[:])
        tf, tb = self.triF, self.triB
        P.add("pool", lambda e: e.affine_select(tf[:], onf[:], [[1, 128]], ALU.is_ge, 0.0,
                                                base=0, channel_multiplier=-1), [onf[:]], [tf[:]])
        P.add("pool", lambda e: e.affine_select(tb[:], onf[:], [[-1, 128]], ALU.is_ge, 0.0,
                                                base=0, channel_multiplier=1), [onf[:]], [tb[:]])
        P.ts("dve", self.TmFb[:], tf[:], -1.0 / 16, None, ALU.mult)
        P.copy("dve", self.TF[:], tf[:])
        P.copy("dve", self.TB[:], tb[:])
        P.tt("dve", self.LF[:], tb[:], self.identf[:], ALU.subtract)
        P.tt("dve", self.LB[:], tf[:], self.identf[:], ALU.subtract)
        P.ts("dve", self.TmBb[:], tb[:], -1.0 / 16, None, ALU.mult)

    def norm_to_hT(self, xt, t, gB, scr):
        P = self.P
        junk, ss, hb = scr
        i = t % 2
        P.actv(junk[:, :], xt, AF.Square, accum_out=ss[:, t:t + 1])
        P.actv(ss[:, 16 + t:17 + t], ss[:, t:t + 1], AF.Ln, bias=EPS, scale=1.0 / D)
        P.actv(ss[:, 32 + t:33 + t], ss[:, 16 + t:17 + t], AF.Exp, scale=-0.5)
        P.stt("dve", hb[:, i, :], xt, ss[:, 32 + t:33 + t], gB, ALU.mult, ALU.mult)
        pb = self.ps[:, 7, :].bitcast(BF16)
        for k in range(8):
            P.tr(pb[:, k * 128:(k + 1) * 128], hb[:, i, k * 128:(k + 1) * 128], self.ident[:])
        P.copy("act", self.hT[:, :, t * 128:(t + 1) * 128],
               pb.rearrange("p (k c) -> p k c", k=8))

    def phase_a(self, s, l):
        P = self.P
        self.P.phase = "A"
        self.pre["ssd"] = self.ssd_load(l)
        self.acur = 100 * KB
        xt = self.alloc([2, D], F32)
        gB = self.alloc([D], F32)
        junk = self.alloc([D], BF16)
        ss = self.alloc([48], F32)
        hb = self.alloc([2, D], BF16)
        P.dma(gB, bass.AP(self.norm_g.tensor, l * D, [[0, 128], [1, D]]))
        for t in range(NT):
            P.dma(xt[:, t % 2, :], self.x[s, t * 128:(t + 1) * 128, :])
            self.norm_to_hT(xt[:, t % 2, :], t, gB, (junk, ss, hb))

    def attn_units(self, kT_fn, qT, V_fn, ob, pT, scale):
        P = self.P
        ps = self.ps
        P.mm(ps[:, 0, :], kT_fn(0), qT)
        for kc in range(16):
            if kc + 1 < 16:
                P.mm(ps[:, (kc + 1) % 2, :], kT_fn(kc + 1), qT)
            P.actv(pT[:, kc % 2, :], ps[:, kc % 2, :], AF.Exp, scale=scale)
            P.mm(ps[:, ob, :], V_fn(kc), pT[:, kc % 2, :], start=(kc == 0), stop=(kc == 15))

    def qk_norm_rope(self, psA, psB, gtile, gc, out, b, tmp):
        P = self.P
        sq, rt, t1, t2 = tmp
        bl = slice(b * 512, (b + 1) * 512)
        P.actv(sq[0:64, :], psA, AF.Square)
        P.mm(self.ps[0:64, 6, :], self.ones[0:64, 0:64], sq[0:64, :])
        P.actv(rt[0:64, :], self.ps[0:64, 6, :], AF.Ln, bias=EPS, scale=1.0 / 64)
        P.actv(rt[0:64, :], rt[0:64, :], AF.Exp, scale=-0.5)
        P.stt("dve", t1[0:64, :], psA, gtile[0:64, gc:gc + 1], self.COS[0:64, bl], ALU.mult, ALU.mult)
        P.stt("dve", t2[0:64, :], psB, gtile[0:64, gc + 1:gc + 2], self.SIN[0:64, bl], ALU.mult, ALU.mult)
        P.tt("pool", t1[0:64, :], t1[0:64, :], t2[0:64, :], ALU.add)
        P.tt("pool", out, t1[0:64, :], rt[0:64, :], ALU.mult)

    def gqa_load(self, l):
        P = self.P
        self.acur = 108 * KB
        COSG = self.alloc([L], F32)
        SING = self.alloc([L], F32)
        wk2 = self.alloc([2, 8, 128], BF16)
        wkp2 = self.alloc([2, 8, 128], BF16)
        wv = self.alloc([8, 128], BF16)
        wq = self.alloc([2, 8, 128], BF16)
        wqp = self.alloc([2, 8, 128], BF16)
        wz = self.alloc([2, 8, 128], BF16)
        gt = self.alloc([4], F32)
        for g in range(2):
            for hlf in range(2):
                self.wload(wk2[:, g, :, 64 * hlf:64 * hlf + 64], self.w_in, l, OFF["kd"] + g * 64, OFF["kd"] + (g + 1) * 64, engs=BULK)
                self.wload(wkp2[:, g, :, 64 * hlf:64 * hlf + 64], self.w_perm, l, 544 + g * 64, 544 + (g + 1) * 64, engs=BULK)
        for hlf in range(2):
            rows = slice(64 * hlf, 64 * hlf + 64)
            P.dma(COSG[rows, :], self.rope_cos[0:64, :])
            P.dma(SING[rows, :], self.rope_sin[0:64, :])
            P.dma(gt[rows, :], self.qk_g[l])
        self.wload(wv, self.w_in, l, OFF["vd"], OFF["vd"] + 128, engs=BULK)
        self.wload(wq[:, 0], self.w_in, l, OFF["qd"], OFF["qd"] + 128, engs=BULK)
        self.wload(wqp[:, 0], self.w_perm, l, 32, 32 + 128, engs=BULK)
        self.wload(wz[:, 0], self.w_in, l, OFF["zd"], OFF["zd"] + 128, engs=BULK)
        return (COSG, SING, wk2, wkp2, wv, wq, wqp, wz, gt)

    def phase_gqa(self, l):
        P = self.P
        self.P.phase = "gqa_prep"
        ps = self.ps
        hT = self.hT
        onT = self.onT
        self.acur = 64 * KB
        BD = self.alloc([128], BF16)
        kT2 = self.alloc([2, L], BF16)
        Va = self.alloc([2, 16, 192], BF16)
        qT2 = self.alloc([2, 512], BF16)
        pT = self.alloc([2, 2, 512], BF16)
        sq = self.alloc([512], BF16)
        rt = self.alloc([512], F32)
        t1 = self.alloc([512], F32)
        t2 = self.alloc([512], F32)
        sz = self.alloc([2, 512], F32)
        ez = self.alloc([512], F32)
        rs = t1
        nt = t2
        osb = self.alloc([2, 512], F32)
        assert self.acur <= 108 * KB
        w = self.pre.pop("gqa", None) or self.gqa_load(l)
        COSG, SING, wk2, wkp2, wv, wq, wqp, wz, gt = w
        P.memset("pool", BD, 0.0)
        P.memset("pool", BD[0:64, 0:64], 1.0)
        P.memset("pool", BD[64:128, 64:128], 1.0)
        P.memset("pool", Va[:, :, :, 64:128], 1.0)

        def proj(wa, wb, bl):
            for k in range(8):
                P.mm(ps[:, 6, :], wa(k), hT[:, k, bl], k == 0, k == 7)
            for k in range(8):
                P.mm(ps[:, 7, :], wb(k), hT[:, k, bl], k == 0, k == 7)
            P.actv(sq, ps[:, 6, :], AF.Square)

        def rope1(gc, bl):
            P.stt("dve", t1, ps[:, 6, :], gt[:, gc:gc + 1], COSG[:, bl], ALU.mult, ALU.mult)
            P.stt("dve", t2, ps[:, 7, :], gt[:, gc + 1:gc + 2], SING[:, bl], ALU.mult, ALU.mult)

        def rope2(out):
            P.mm(ps[:, 7, :], BD, sq, True, True)
            P.actv(rt, ps[:, 7, :], AF.Ln, bias=EPS, scale=1.0 / 64)
            P.actv(rt, rt, AF.Exp, scale=-0.5)
            P.tt("pool", t1, t1, t2, ALU.add)
            P.tt("pool", out, t1, rt, ALU.mult)

        for g in range(2):
            for b in range(NB):
                bl = slice(b * 512, (b + 1) * 512)
                proj(lambda k: wk2[:, g, k, :], lambda k: wkp2[:, g, k, :], bl)
                rope1(2, bl)
                rope2(kT2[:, g, bl])
        for t in range(NT):
            for k in range(8):
                P.mm(ps[:, 4 + t % 2, 0:128], hT[:, k, t * 128:(t + 1) * 128], wv[:, k, :], k == 0, k == 7)
            for g in range(2):
                P.copy("act", Va[:, g, t, 0:64], ps[:, 4 + t % 2, g * 64:(g + 1) * 64])
                P.copy("dve", Va[:, g, t, 128:192], ps[:, 4 + t % 2, g * 64:(g + 1) * 64])
        P.phase = "gqa_heads"
        items = [(pr, b) for pr in range(4) for b in range(NB)]

        def prep_parts(it):
            pr, b = items[it]
            i = pr % 2
            bl = slice(b * 512, (b + 1) * 512)
            hooks = {}

            def add(kc, f):
                prev = hooks.get(kc)

                def both(prev=prev, f=f):
                    if prev is not None:
                        prev()
                    f()
                hooks[kc] = both

            def loads():
                if b == 0 and pr > 0:
                    self.wload(wq[:, i], self.w_in, l, OFF["qd"] + pr * 128, OFF["qd"] + (pr + 1) * 128)
                    self.wload(wqp[:, i], self.w_perm, l, 32 + pr * 128, 32 + (pr + 1) * 128)
                    self.wload(wz[:, i], self.w_in, l, OFF["zd"] + pr * 128, OFF["zd"] + (pr + 1) * 128)
            add(0, loads)

            def pk(k):
                P.mm(ps[:, 6, :], wq[:, i, k, :], hT[:, k, bl], k == 0, k == 7)
                P.mm(ps[:, 7, :], wqp[:, i, k, :], hT[:, k, bl], k == 0, k == 7)
            for k in range(8):
                add(k, lambda k=k: pk(k))

            def sq_rope1():
                P.actv(sq, ps[:, 6, :], AF.Square)
                rope1(0, bl)
            add(8, sq_rope1)
            add(9, lambda: rope2(qT2[:, it % 2, :]))
            for k in range(8):
                add(10 + min(k, 5), lambda k=k: P.mm(ps[:, 6, :], wz[:, i, k, :], hT[:, k, bl], k == 0, k == 7))

            def silu():
                P.actv(ez, ps[:, 6, :], AF.Exp, scale=-1.0)
                P.actv(ez, ez, AF.Ln, bias=1.0)
                P.actv(ez, ez, AF.Exp, scale=-1.0)
                P.tt("dve", sz[:, it % 2, :], ps[:, 6, :], ez, ALU.mult)
            add(15, silu)
            return hooks

        def units(it, hooks):
            pr, b = items[it]
            g = pr // 2
            bl = slice(b * 512, (b + 1) * 512)
            qb = qT2[:, it % 2, :]
            sbank = [[0, 1], [4, 5]]
            rows = [slice(0, 64), slice(64, 128)]
            vsel = [slice(0, 128), slice(64, 192)]
            for par in range(2):
                P.mm(ps[:, sbank[par][0], :], kT2[rows[par], g, 0:128], qb[rows[par], :])
            for kc in range(16):
                if kc + 1 < 16:
                    for par in range(2):
                        P.mm(ps[:, sbank[par][(kc + 1) % 2], :], kT2[rows[par], g, (kc + 1) * 128:(kc + 2) * 128], qb[rows[par], :])
                for par in range(2):
                    P.actv(pT[:, par, kc % 2, :], ps[:, sbank[par][kc % 2], :], AF.Exp, scale=0.125)
                for par in range(2):
                    P.mm(ps[:, 2 + par, :], Va[:, g, kc, vsel[par]], pT[:, par, kc % 2, :], kc == 0, kc == 15)
                if kc in hooks:
                    hooks[kc]()
            for par in range(2):
                P.copy("dve", osb[:, par, :], ps[:, 2 + par, :])
            for par in range(2):
                orow = rows[par]
                srow = rows[1 - par]
                P.recip(rs[orow, :], osb[srow, par, :])
                P.tt("dve", nt[orow, :], osb[orow, par, :], rs[orow, :], ALU.mult)
                P.tt("pool", onT[orow, pr, bl], nt[orow, :], sz[orow, it % 2, :], ALU.mult)

        h0 = prep_parts(0)
        for kc in sorted(h0):
            h0[kc]()
        for it in range(len(items)):
            hooks = prep_parts(it + 1) if it + 1 < len(items) else {}
            units(it, hooks)

    def mla_load(self, l):
        self.acur = 124 * KB
        wq = self.alloc([3, 768], BF16)
        wqp = self.alloc([3, 256], BF16)
        wkv = self.alloc([2, 1024], BF16)
        wql = self.alloc([8, 384], BF16)
        wkvl = self.alloc([8, 256], BF16)
        wkr = self.alloc([2, 8, 32], BF16)
        lg = self.alloc([5], F32)
        self.wload(wql, self.w_in, l, OFF["qlat"], OFF["qlat"] + 384, engs=BULK)
        self.wload(wkvl, self.w_in, l, OFF["kvlat"], OFF["kvlat"] + 256, engs=BULK)
        self.wload(wkr[:, 0], self.w_in, l, OFF["krope"], OFF["krope"] + 32, engs=BULK)
        self.wload(wkr[:, 1], self.w_perm, l, 0, 32, engs=BULK)
        self.wload(wq, self.w_q_b, l, 0, 768, engs=BULK)
        self.wload(wqp, self.w_qb_perm, l, 0, 256, engs=BULK)
        self.wload(wkv, self.w_kv_b, l, 0, 1024, engs=BULK)
        self.P.dma(lg, self.lat_g[l])
        return (wq, wqp, wkv, wql, wkvl, wkr, lg)

    def phase_mla(self, l):
        P = self.P
        self.P.phase = "mla_prep"
        ps = self.ps
        hT = self.hT
        onT = self.onT
        COS, SIN = self.COS, self.SIN
        self.acur = 64 * KB
        qln = self.alloc([3, L], BF16)
        kvn = self.alloc([2, L], BF16)
        krT = self.alloc([L], BF16)
        Va = self.alloc([16, 4, 192], BF16)
        qT = self.alloc([2, 512], BF16)
        pT = self.alloc([4, 512], BF16)
        sz = self.alloc([2, 512], F32)
        wz = self.alloc([2, 8, 64], BF16)
        assert self.acur <= 124 * KB
        self.acur = 48 * KB
        kT = self.alloc([2, L], BF16)
        t1 = self.alloc([512], F32)
        t2 = self.alloc([512], F32)
        sqc = self.alloc([2, 512], BF16)
        rt = self.alloc([512], F32)
        rs = t1
        assert self.acur <= 64 * KB
        w = self.pre.pop("mla", None) or self.mla_load(l)
        wq, wqp, wkv, wql, wkvl, wkr, lg = w
        P.memset("pool", Va[:, :, :, 64:128], 1.0)
        nsq = 0
        for b in range(NB):
            bl = slice(b * 512, (b + 1) * 512)
            for (wsrc, nch, bank0, sbank, dst, gofs, dim) in ((wql, 3, 0, 3, qln, 0, 384), (wkvl, 2, 4, 6, kvn, 3, 256)):
                for c in range(nch):
                    for k in range(8):
                        P.mm(ps[:, bank0 + c, :], wsrc[:, k, c * 128:(c + 1) * 128], hT[:, k, bl], k == 0, k == 7)
                for c in range(nch):
                    P.actv(sqc[:, nsq % 2, :], ps[:, bank0 + c, :], AF.Square)
                    P.mm(ps[:, sbank, :], self.ones[:, :], sqc[:, nsq % 2, :], c == 0, c == nch - 1)
                    nsq += 1
                P.actv(rt, ps[:, sbank, :], AF.Ln, bias=EPS, scale=1.0 / dim)
                P.actv(rt, rt, AF.Exp, scale=-0.5)
                for c in range(nch):
                    P.stt("dve", dst[:, c, bl], ps[:, bank0 + c, :], lg[:, gofs + c:gofs + c + 1], rt, ALU.mult, ALU.mult)
            for k in range(8):
                P.mm(ps[64:96, 7, :], wkr[:, 0, k, :], hT[:, k, bl], k == 0, k == 7)
            P.tt("dve", t1[64:96, :], ps[64:96, 7, :], COS[64:96, bl], ALU.mult)
            for k in range(8):
                P.mm(ps[64:96, 7, :], wkr[:, 1, k, :], hT[:, k, bl], k == 0, k == 7)
            P.tt("dve", t2[64:96, :], ps[64:96, 7, :], SIN[64:96, bl], ALU.mult)
            P.tt("pool", krT[64:96, bl], t1[64:96, :], t2[64:96, :], ALU.add)
        wv_view = wkv.rearrange("p c (h two d) -> p c h two d", h=8, two=2)
        for t in range(NT):
            tl = slice(t * 128, (t + 1) * 128)
            for c in range(2):
                P.mm(ps[:, 4 + t % 2, :].rearrange("p (h d) -> p h d", h=8), kvn[:, c, tl], wv_view[:, c, :, 1, :], c == 0, c == 1)
            pv = ps[:, 4 + t % 2, :].rearrange("p (j two d) -> p j two d", j=4, two=2)
            P.copy("act", Va[:, t, :, 0:64], pv[:, :, 0, :])
            P.copy("dve", Va[:, t, :, 128:192], pv[:, :, 1, :])
        scale = 96.0 ** -0.5
        P.phase = "mla_heads"
        items = [(h, b) for h in range(8) for b in range(NB)]
        ez = t2

        def head_prep(h):
            i = h % 2
            self.wload(wz[:, i], self.w_in, l, OFF["zb"] + h * 64, OFF["zb"] + (h + 1) * 64)
            for b in range(NB):
                bl = slice(b * 512, (b + 1) * 512)
                for c in range(2):
                    P.mm(ps[0:64, 6, :], wkv[:, c, h * 128:h * 128 + 64], kvn[:, c, bl], c == 0, c == 1)
                P.copy("dve", kT[0:64, i, bl], ps[0:64, 6, :])
            P.copy("pool", kT[64:96, i, :], krT[64:96, :])

        def prep_parts(it):
            h, b = items[it]
            i = h % 2
            par = h % 2
            orow = slice(64 * par, 64 * par + 64)
            bl = slice(b * 512, (b + 1) * 512)
            qb = qT[:, it % 2, :]

            def p0():
                if b == 0:
                    head_prep(h)

            def p1():
                for c in range(3):
                    P.mm(ps[0:96, 6, :], wq[:, c, h * 96:(h + 1) * 96], qln[:, c, bl], c == 0, c == 2)
                for c in range(3):
                    P.mm(ps[64:96, 7, :], wqp[:, c, h * 32:(h + 1) * 32], qln[:, c, bl], c == 0, c == 2)
                P.copy("dve", qb[0:64, :], ps[0:64, 6, :])
                P.tt("dve", t1[64:96, :], ps[64:96, 6, :], COS[64:96, bl], ALU.mult)
                P.tt("dve", t2[64:96, :], ps[64:96, 7, :], SIN[64:96, bl], ALU.mult)
                P.tt("pool", qb[64:96, :], t1[64:96, :], t2[64:96, :], ALU.add)

            def p3():
                for k in range(8):
                    P.mm(ps[orow, 7, :], wz[:, i, k, :], hT[:, k, bl], k == 0, k == 7)
                P.actv(ez[orow, :], ps[orow, 7, :], AF.Exp, scale=-1.0)
                P.actv(ez[orow, :], ez[orow, :], AF.Ln, bias=1.0)
                P.actv(ez[orow, :], ez[orow, :], AF.Exp, scale=-1.0)
                P.tt("dve", sz[orow, it % 2, :], ps[orow, 7, :], ez[orow, :], ALU.mult)

            return [p0, p1, p3]

        def units(it, hooks):
            h, b = items[it]
            i = h % 2
            par = h % 2
            pr = h // 2
            orow = slice(64 * par, 64 * par + 64)
            srow = slice(64 * (1 - par), 64 * (1 - par) + 64)
            vsel = slice(0, 128) if par == 0 else slice(64, 192)
            bl = slice(b * 512, (b + 1) * 512)
            qb = qT[:, it % 2, :]
            ob = 2 + it % 2
            sbanks = [[0, 1], [4, 5]]

            def S(kc):
                P.mm(ps[:, sbanks[(kc // 2) % 2][kc % 2], :], kT[0:96, i, kc * 128:(kc + 1) * 128], qb[0:96, :])
            S(0)
            S(1)
            for k2 in range(8):
                if k2 + 1 < 8:
                    S(2 * k2 + 2)
                    S(2 * k2 + 3)
                for kc in (2 * k2, 2 * k2 + 1):
                    P.actv(pT[:, kc % 4, :], ps[:, sbanks[(kc // 2) % 2][kc % 2], :], AF.Exp, scale=scale)
                for kc in (2 * k2, 2 * k2 + 1):
                    P.mm(ps[:, ob, :], Va[:, kc, pr, vsel], pT[:, kc % 4, :], kc == 0, kc == 15)
                if k2 in hooks:
                    hooks[k2]()
            P.recip(rs[orow, :], ps[srow, ob, :])
            P.tt("dve", rs[orow, :], ps[orow, ob, :], rs[orow, :], ALU.mult)
            P.tt("pool", onT[orow, pr, bl], rs[orow, :], sz[orow, it % 2, :], ALU.mult)

        for f in prep_parts(0):
            f()
        for it in range(len(items)):
            hooks = {}
            if it + 1 < len(items):
                pp = prep_parts(it + 1)
                hooks = {0: pp[0], 2: pp[1], 5: pp[2]}
            units(it, hooks)

    def gla_load(self, l):
        P = self.P
        self.acur = 48 * KB
        wvc = self.alloc([8, 512], BF16)
        wzc = self.view(48 * KB, [8, 512], BF16)
        wqc = self.alloc([8, 256], BF16)
        wkc = self.alloc([8, 256], BF16)
        assert self.acur <= 64 * KB
        self.acur = 139 * KB
        wgu = self.alloc([2, 256], BF16)
        bg = self.alloc([2, 256], BF16)
        wglr = self.alloc([2, 8, 32], BF16)
        gg = self.alloc([4], F32)
        self.wload(wqc, self.w_in, l, OFF["qc"], OFF["qc"] + 256, engs=BULK)
        self.wload(wkc, self.w_in, l, OFF["kc"], OFF["kc"] + 256, engs=BULK)
        self.wload(wvc, self.w_in, l, OFF["vc"], OFF["vc"] + 512, engs=BULK)
        self.wload(wglr[:, 0], self.w_in, l, OFF["glr"], OFF["glr"] + 32, engs=BULK)
        self.wload(wglr[:, 1, :, 0:16], self.w_in, l, OFF["glr"] + 16, OFF["glr"] + 32, engs=BULK)
        self.wload(wglr[:, 1, :, 16:32], self.w_in, l, OFF["glr"], OFF["glr"] + 16, engs=BULK)
        P.memset("pool", wgu[0:32], 0.0)
        st = self.stage[0:16, self.nst % 4, :]
        self.nst += 1
        P.dma(st.rearrange("p (d c) -> p d c", d=2), self.w_gate_up[l].rearrange("d r c -> r d c"))
        P.copy("pool", wgu[0:16], st.rearrange("p (d c) -> p d c", d=2))
        st = self.stage[0:1, self.nst % 4, :]
        self.nst += 1
        P.dma(st.rearrange("p (d c) -> p d c", d=2), self.b_gate[l:l + 1])
        P.copy("pool", bg[0:1], st.rearrange("p (d c) -> p d c", d=2))
        P.dma(gg, self.gla_g[l])
        return (wvc, wzc, wqc, wkc, wglr, wgu, bg, gg)

    def phase_gla(self, l):
        P = self.P
        self.P.phase = "gla_prep"
        ps = self.ps
        hT = self.hT
        onT = self.onT
        self.acur = 64 * KB
        qTc = self.alloc([2, L], BF16)
        kTc = self.alloc([2, L], BF16)
        Vc = self.alloc([16, 512], BF16)
        ob = self.alloc([4, L], BF16)
        glrT = self.alloc([2, L], BF16)
        oblk = self.alloc([4, 512], F32)
        S = self.alloc([2, 128], F32)
        Sbf = self.alloc([2, 128], BF16)
        gsp = self.alloc([256], F32)
        gh = self.alloc([256], BF16)
        gl = self.alloc([256], BF16)
        eg = self.alloc([2, 128], F32)
        eng = self.alloc([2, 128], F32)
        ek = self.alloc([2, 128], F32)
        qg = self.alloc([2, 128], BF16)
        kg = self.alloc([2, 128], BF16)
        kend = self.alloc([2, 128], BF16)
        attm = self.alloc([4, 128], BF16)
        kendT = self.alloc([2, 128], BF16)
        glast = self.alloc([2], F32)
        cd = self.alloc([2], F32)
        assert self.acur <= 139 * KB
        self.acur = 56 * KB
        sq = self.alloc([512], BF16)
        rt = self.alloc([512], F32)
        sz = self.alloc([512], F32)
        tmp = self.alloc([512], F32)
        assert self.acur <= 64 * KB
        w = self.pre.pop("gla", None) or self.gla_load(l)
        wvc, wzc, wqc, wkc, wglr, wgu, bg, gg = w
        n = 0
        for j in range(2):
            for (wsrc, dst) in ((wqc, qTc), (wkc, kTc)):
                for b in range(NB):
                    bl = slice(b * 512, (b + 1) * 512)
                    bank = 4 + n % 4
                    for k in range(8):
                        P.mm(ps[:, bank, :], wsrc[:, k, j * 128:(j + 1) * 128], hT[:, k, bl], k == 0, k == 7)
                    P.copy("act" if n % 2 else "dve", dst[:, j, bl], ps[:, bank, :])
                    n += 1
        for t in range(NT):
            tl = slice(t * 128, (t + 1) * 128)
            bank = 4 + n % 4
            for k in range(8):
                P.mm(ps[:, bank, :], hT[:, k, tl], wvc[:, k, :], k == 0, k == 7)
            P.copy("act" if n % 2 else "dve", Vc[:, t, :], ps[:, bank, :])
            n += 1
        for d in range(2):
            for b in range(NB):
                bl = slice(b * 512, (b + 1) * 512)
                bank = 4 + n % 4
                for k in range(8):
                    P.mm(ps[0:32, bank, :], wglr[:, d, k, :], hT[:, k, bl], k == 0, k == 7)
                P.copy("act" if n % 2 else "dve", glrT[0:32, d, bl], ps[0:32, bank, :])
                n += 1

        self.wload(wzc, self.w_in, l, OFF["zc"], OFF["zc"] + 512)

        def bank3(b, a):
            return ps[:, b, 0:a * 128].rearrange("p (a c) -> p a c", a=a)


        P.phase = "gla_scan"
        psC = bank3(1, 2)
        psA2 = [bank3(2, 2), bank3(3, 2)]
        psO2 = [bank3(4, 2), bank3(5, 2)]
        psT = ps[:, 6, :].bitcast(BF16)[:, 0:256].rearrange("p (a c) -> p a c", a=2)
        psU = bank3(7, 2)
        for d in (1, 0):
            Tm = self.TmFb if d == 0 else self.TmBb
            tri = self.triF if d == 0 else self.triB
            last = 127 if d == 0 else 0
            P.memset("pool", S, 0.0)
            P.memset("pool", Sbf, 0.0)
            order = range(NT) if d == 0 else range(NT - 1, -1, -1)
            for ci, t in enumerate(order):
                tl = slice(t * 128, (t + 1) * 128)
                P.mm(ps[:, 0, 0:256], glrT[0:32, d, tl], wgu[0:32, d, :], True, False)
                P.mm(ps[:, 0, 0:256], self.ones[0:1, 0:128], bg[0:1, d, :], False, True)
                P.actv(gsp, ps[:, 0, 0:256], AF.Exp, scale=-1.0)
                P.actv(gsp, gsp, AF.Ln, bias=1.0)
                P.copy("dve", gh, gsp)
                P.tt("dve", gl, gsp, gh, ALU.subtract)
                for j in range(2):
                    P.mm(psC[:, j, :], gh[:, j * 128:(j + 1) * 128], Tm[:, :], True, False)
                    P.mm(psC[:, j, :], gl[:, j * 128:(j + 1) * 128], Tm[:, :], False, True)
                P.copy("dve", glast, psC[:, :, last])
                P.actv(eg.rearrange("p a c -> p (a c)"), ps[:, 1, 0:256], AF.Exp)
                P.actv(eng.rearrange("p a c -> p (a c)"), ps[:, 1, 0:256], AF.Exp, scale=-1.0)
                for j in range(2):
                    P.actv(ek[:, j, :], psC[:, j, :], AF.Exp, scale=-1.0, bias=glast[:, j:j + 1])
                P.actv(cd, glast, AF.Exp)
                P.stt("dve", qg, qTc[:, :, tl], 0.125, eg, ALU.mult, ALU.mult)
                P.tt("dve", kg, kTc[:, :, tl], eng, ALU.mult)
                P.tt("dve", kend, kTc[:, :, tl], ek, ALU.mult)
                tri_b = bass.AP(tri, 0, [[128, 128], [0, 2], [1, 128]])
                obv = ob.rearrange("p (j two) t -> p j two t", two=2)
                oblv = oblk.rearrange("p (j two) t -> p j two t", two=2)
                attv = attm.rearrange("p (j two) c -> p j two c", two=2)
                for par in range(2):
                    r = slice(64 * par, 64 * par + 64)
                    for j in range(2):
                        P.mm(psA2[par][:, j, :], kg[r, j, :], qg[r, j, :], True, True)
                    P.tt("dve", attv[:, :, par, :], psA2[par], tri_b, ALU.mult)
                for par in range(2):
                    r = slice(64 * par, 64 * par + 64)
                    for j in range(2):
                        h = 2 * j + par
                        P.mm(psO2[par][:, j, :], Vc[:, t, h * 128:(h + 1) * 128], attm[:, h, :], True, ci == 0)
                        if ci > 0:
                            P.mm(psO2[par][:, j, :], Sbf[r, j, :], qg[r, j, :], False, True)
                    if d == 1:
                        P.copy("act", obv[:, :, par, tl], psO2[par])
                    else:
                        c4 = t % 4
                        P.tt("dve", oblv[:, :, par, c4 * 128:(c4 + 1) * 128], psO2[par], obv[:, :, par, tl], ALU.add)
                if ci < NT - 1:
                    for j in range(2):
                        P.tr(psT[:, j, :], kend[:, j, :], self.ident[:])
                    P.copy("act", kendT, psT)
                    for h in range(4):
                        j = h // 2
                        r = slice(64 * (h % 2), 64 * (h % 2) + 64)
                        P.mm(psU[r, j, :], kendT[:, j, 64 * (h % 2):64 * (h % 2) + 64], Vc[:, t, h * 128:(h + 1) * 128], True, True)
                    for j in range(2):
                        P.stt("dve", S[:, j, :], S[:, j, :], cd[:, j:j + 1], psU[:, j, :], ALU.mult, ALU.add)
                    P.copy("pool", Sbf, S)
                if d == 0 and t % 4 == 3:
                    b = t // 4
                    bl = slice(b * 512, (b + 1) * 512)
                    for h in range(4):
                        P.actv(sq, oblk[:, h, :], AF.Square)
                        P.mm(ps[:, 0, :], self.ones[:, :], sq, True, True)
                        P.actv(rt, ps[:, 0, :], AF.Ln, bias=EPS, scale=1.0 / 128)
                        P.actv(rt, rt, AF.Exp, scale=-0.5)
                        for k in range(8):
                            P.mm(ps[:, 1, :], wzc[:, k, h * 128:(h + 1) * 128], hT[:, k, bl], k == 0, k == 7)
                        P.actv(sz, ps[:, 1, :], AF.Silu)
                        P.stt("dve", tmp, oblk[:, h, :], gg[:, h:h + 1], rt, ALU.mult, ALU.mult)
                        P.tt("pool", onT[:, h, bl], tmp, sz, ALU.mult)


    def ssd_load(self, l):
        P = self.P
        self.acur = 64 * KB
        wz = self.alloc([8, D], BF16)
        wdt = self.alloc([8, 32], BF16)
        anb = self.alloc([32], F32)
        dtb = self.alloc([32], F32)
        dsk = self.alloc([16], F32)
        gnb = self.alloc([D], F32)
        cp = self.alloc([12, 6], F32)
        self.ssd_end = self.acur
        P.dma(cp, self.conv_p[l])
        P.dma(anb, bass.AP(self.a_log.tensor, l * 32, [[0, 128], [1, 32]]))
        P.dma(dtb, bass.AP(self.dt_bias.tensor, l * 32, [[0, 128], [1, 32]]))
        P.dma(dsk, bass.AP(self.d_skip.tensor, l * 16, [[0, 128], [1, 16]]))
        P.dma(gnb, bass.AP(self.ssd_norm_g.tensor, l * D, [[0, 128], [1, D]]))
        self.wload(wdt, self.w_in, l, OFF["dt"], OFF["dt"] + 32, engs=BULK)
        self.wload(wz, self.w_in, l, OFF["za"], OFF["za"] + D, engs=BULK)
        return (wz, wdt, cp, anb, dtb, dsk, gnb)

    def phase_ssd(self, l):
        P = self.P
        self.P.phase = "ssd_conv"
        ps = self.ps
        hT = self.hT
        onT = self.onT
        xs_tok = self.view(0, [16, D], BF16)
        w = self.pre.pop("ssd", None) or self.ssd_load(l)
        wz, wdt, cp, anb, dtb, dsk, gnb = w
        self.acur = self.ssd_end
        BT = self.alloc([2, L], BF16)
        CT = self.alloc([2, L], BF16)
        Btok = self.alloc([16, 256], BF16)
        dt_all = self.alloc([16, 32], F32)
        dta_hi = self.alloc([16, 32], BF16)
        dta_lo = self.alloc([16, 32], BF16)
        base = self.acur
        xb = self.alloc([L + 4], F32)
        acc = self.alloc([L], F32)
        cv = self.alloc([L], BF16)
        wx = self.alloc([2, 8, 128], BF16)
        P.actv(anb, anb, AF.Exp)
        P.ts("dve", anb, anb, -1.0, None, ALU.mult)
        P.memset("pool", xb[:, 0:2], 0.0)
        P.memset("pool", xb[:, L + 2:L + 4], 0.0)
        for c in range(12):
            i = c % 2
            self.wload(wx[:, i], self.w_in, l, OFF["xbc"] + c * 128, OFF["xbc"] + (c + 1) * 128)
            for b in range(NB):
                bank = 4 + (c * 4 + b) % 4
                for k in range(8):
                    P.mm(ps[:, bank, :], wx[:, i, k, :], hT[:, k, b * 512:(b + 1) * 512], k == 0, k == 7)
                P.copy("act", xb[:, 2 + b * 512:2 + (b + 1) * 512], ps[:, bank, :])
            eng = "dve"
            P.ts(eng, acc, xb[:, 0:L], cp[:, c, 0:1], cp[:, c, 5:6], ALU.mult, ALU.add)
            for j in range(1, 5):
                P.stt(eng, acc, xb[:, j:j + L], cp[:, c, j:j + 1], acc, ALU.mult, ALU.add)
            if c < 8:
                P.actv(cv, acc, AF.Silu)
                dst_tok = lambda t, c=c: xs_tok[:, t, c * 128:(c + 1) * 128]
                src = cv
            elif c < 10:
                P.actv(BT[:, c - 8, :], acc, AF.Silu)
                dst_tok = lambda t, c=c: Btok[:, t, (c - 8) * 128:(c - 7) * 128]
                src = BT[:, c - 8, :]
            else:
                P.actv(CT[:, c - 10, :], acc, AF.Silu)
                src = None
            if src is not None:
                for half in range(2):
                    pb = ps[:, 2 + half, :].bitcast(BF16)
                    for tt_ in range(8):
                        t = half * 8 + tt_
                        P.tr(pb[:, tt_ * 128:(tt_ + 1) * 128], src[:, t * 128:(t + 1) * 128], self.ident[:])
                    if c < 8:
                        P.copy("act" if half else "dve", xs_tok[:, half * 8:(half + 1) * 8, c * 128:(c + 1) * 128],
                               pb.rearrange("p (t c) -> p t c", t=8))
                    else:
                        P.copy("act" if half else "dve", Btok[:, half * 8:(half + 1) * 8, (c - 8) * 128:(c - 7) * 128],
                               pb.rearrange("p (t c) -> p t c", t=8))
        P.phase = "ssd_scan"
        for t in range(NT):
            for k in range(8):
                P.mm(ps[:, 7, t * 32:(t + 1) * 32], hT[:, k, t * 128:(t + 1) * 128], wdt[:, k, :], k == 0, k == 7)
        dtb_b = bass.AP(dtb.tensor, dtb.offset, [list(dtb.ap[0]), [0, 16], [1, 32]])
        anb_b = bass.AP(anb.tensor, anb.offset, [list(anb.ap[0]), [0, 16], [1, 32]])
        P.tt("dve", dt_all, ps[:, 7, :].rearrange("p (t c) -> p t c", t=16), dtb_b, ALU.add)
        P.actv(dt_all, dt_all, AF.Exp)
        P.actv(dt_all, dt_all, AF.Ln, bias=1.0)
        self.acur = base
        self.allow_over = True
        Ahi = self.alloc([1, 8, 128], BF16)
        Alo = self.alloc([1, 8, 128], BF16)
        E = self.alloc([1, 8, 128], BF16)
        W = self.alloc([1, 8, 128], BF16)
        Gm = self.alloc([2, 128], BF16)
        dlf = self.alloc([16, 32], F32)
        xd = self.alloc([D], BF16)
        xdd = self.alloc([D], BF16)
        S = self.alloc([D], F32)
        Sbf = self.alloc([D], BF16)
        ytmp = self.alloc([D], F32)
        y2 = self.alloc([2, D], F32)
        ybt = self.alloc([D], F32)
        eac = self.alloc([16], F32)
        cdb = self.alloc([16], F32)
        sz = self.alloc([D], F32)
        dta = sz[:, 0:512].rearrange("p (t c) -> p t c", t=16)
        junk = ytmp
        hb = xdd
        ss = self.alloc([4], F32)
        P.tt("dve", dta, dt_all, anb_b, ALU.mult)
        P.copy("dve", dta_hi, dta)
        P.tt("dve", dta_lo, dta, dta_hi, ALU.subtract)
        P.tt("dve", dlf, dta, dta_hi, ALU.subtract)

        def hv(ap2):
            return ap2.rearrange("p (h q) -> p h q", h=16)

        def bc_h(ap_col16, n):
            return bass.AP(ap_col16.tensor, ap_col16.offset, [list(ap_col16.ap[0]), [ap_col16.ap[1][0], 16], [0, n]])

        def bc_h2(ap16, g):
            return bass.AP(ap16.tensor, ap16.offset + g * 8, [list(ap16.ap[0]), [ap16.ap[1][0], 8], [0, 128]])

        def fin_stages(ci, t):
            y = y2[:, ci % 2, :]
            tl = slice(t * 128, (t + 1) * 128)
            pz = ps[:, 0:2, :].rearrange("p b c -> p (b c)")

            def f1a():
                P.tt("dve", hv(ytmp), hv(xs_tok[:, t, :]), bc_h(dsk, 64), ALU.mult)
                P.tt("dve", y, y, ytmp, ALU.add)

            def f1b():
                for hf in range(2):
                    for k in range(8):
                        P.mm(ps[:, hf, :], hT[:, k, tl], wz[:, k, hf * 512:(hf + 1) * 512], k == 0, k == 7)

            def f2():
                P.actv(sz, pz, AF.Silu)

            def f3():
                P.tt("dve", y, y, sz, ALU.mult)

            def f4():
                P.actv(junk, y, AF.Square, accum_out=ss[:, 0:1])
                P.actv(ss[:, 1:2], ss[:, 0:1], AF.Ln, bias=EPS, scale=1.0 / D)
                P.actv(ss[:, 2:3], ss[:, 1:2], AF.Exp, scale=-0.5)

            def f5():
                P.stt("dve", hb, y, ss[:, 2:3], gnb, ALU.mult, ALU.mult)
                pb = ps[:, 7, :].bitcast(BF16)
                for k in range(8):
                    P.tr(pb[:, k * 128:(k + 1) * 128], hb[:, k * 128:(k + 1) * 128], self.ident[:])
                P.copy("act", onT[:, :, tl], pb.rearrange("p (k c) -> p k c", k=8))

            return [f1a, f1b, f2, f3, f4, f5]

        def main_stages(d, ci, t):
            Lm = self.LF if d == 0 else self.LB
            Tm = self.TF if d == 0 else self.TB
            tri = self.triF if d == 0 else self.triB
            last = 127 if d == 0 else 0
            y = y2[:, ci % 2, :]
            tl = slice(t * 128, (t + 1) * 128)
            dh = dta_hi[:, t, d * 16:(d + 1) * 16]
            dl = dta_lo[:, t, d * 16:(d + 1) * 16]

            def m1():
                if d == 0:
                    P.dma(ybt, self.yb_d[tl, :])
                for g in range(2):
                    P.mm(ps[:, 4, g * 128:(g + 1) * 128], BT[:, g, tl], CT[:, g, tl], True, True)
                P.mm(ps[:, 4, 256:272], Tm[:, :], dh, True, False)
                P.mm(ps[:, 4, 256:272], Tm[:, :], dl, False, True)
                P.mm(ps[:, 4, 272:288], self.ones[:, :], dh, True, False)
                P.mm(ps[:, 4, 272:288], self.ones[:, :], dl, False, True)
                tri_b = bass.AP(tri, 0, [[128, 128], [0, 2], [1, 128]])
                P.tt("dve", Gm, ps[:, 4, 0:256].rearrange("p (g c) -> p g c", g=2), tri_b, ALU.mult)
                P.actv(eac, ps[:, 4, 256:272], AF.Exp)
                P.actv(cdb, ps[:, 4, 272:288], AF.Exp)
                P.tt("dve", hv(xd), hv(xs_tok[:, t, :]), bc_h(dt_all[:, t, d * 16:(d + 1) * 16], 64), ALU.mult)

            def mg(g):
                def f():
                    Lb = bass.AP(Lm, 0, [[128, 128], [0, 8], [1, 128]])
                    P.tt("pool", Ahi[:, 0], Lb, bc_h2(dh, g), ALU.mult)
                    for e in range(8):
                        P.actv(Alo[:, 0, e, :], Lm[:, :], AF.Copy, scale=dlf[:, t, d * 16 + g * 8 + e:d * 16 + g * 8 + e + 1])
                    for e in range(8):
                        out = ps[:, 2 * g + e // 4, (e % 4) * 128:(e % 4 + 1) * 128]
                        P.mm(out, Ahi[:, 0, e, :], Tm[:, :], True, False)
                        P.mm(out, Alo[:, 0, e, :], Tm[:, :], False, True)
                    P.actv(E[:, 0], ps[:, 2 * g:2 * g + 2, :].rearrange("p b (e c) -> p (b e) c", e=4), AF.Exp)
                    Gb = bass.AP(Gm.tensor, Gm.offset + g * 128, [list(Gm.ap[0]), [0, 8], [1, 128]])
                    P.tt("dve", W[:, 0], E[:, 0], Gb, ALU.mult)
                    Elast = bass.AP(E.tensor, E.offset + last, [list(E.ap[0]), [128, 8], [0, 64]])
                    P.tt("dve", hv(xdd)[:, g * 8:(g + 1) * 8, :], hv(xd)[:, g * 8:(g + 1) * 8, :], Elast, ALU.mult)
                    for e in range(8):
                        h = g * 8 + e
                        P.mm(ps[:, 5 + g, e * 64:(e + 1) * 64], W[:, 0, e, :], xd[:, h * 64:(h + 1) * 64], True, True)
                return f

            def m4():
                if ci > 0:
                    for g in range(2):
                        P.mm(ps[:, g, :], CT[:, g, tl], Sbf[:, g * 512:(g + 1) * 512], True, True)
                    P.tt("dve", hv(ytmp), ps[:, 0:2, :].rearrange("p b (e q) -> p (b e) q", e=8), bc_h(eac, 64), ALU.mult)
                    P.tt("dve", y, ytmp, ps[:, 5:7, :].rearrange("p b c -> p (b c)"), ALU.add)
                else:
                    P.copy("act", y, ps[:, 5:7, :].rearrange("p b c -> p (b c)"))

            def m5():
                if ci < NT - 1:
                    for g in range(2):
                        P.mm(ps[:, 2 + g, :], Btok[:, t, g * 128:(g + 1) * 128], xdd[:, g * 512:(g + 1) * 512], True, True)
                    if ci > 0:
                        P.tt("pool", hv(S), hv(S), bc_h(cdb, 64), ALU.mult)
                        P.tt("dve", S, S, ps[:, 2:4, :].rearrange("p b c -> p (b c)"), ALU.add)
                    else:
                        P.copy("act", S, ps[:, 2:4, :].rearrange("p b c -> p (b c)"))
                    P.copy("act", Sbf, S)

            def m6():
                if d == 1:
                    P.dma(self.yb_d[tl, :], y)
                else:
                    P.tt("dve", y, y, ybt, ALU.add)

            return [m1, mg(0), mg(1), m4, m5, m6]

        for d in (1, 0):
            order = range(NT) if d == 0 else range(NT - 1, -1, -1)
            pending = None
            for ci, t in enumerate(order):
                ms = main_stages(d, ci, t)
                fs = fin_stages(*pending) if pending is not None else []
                for i in range(6):
                    ms[i]()
                    if i < len(fs):
                        fs[i]()
                if d == 0:
                    pending = (ci, t)
            if pending is not None:
                for f in fin_stages(*pending):
                    f()
        self.allow_over = False

    def merge(self, l, bi, nk, wbr_src, first, pre=None):
        P = self.P
        self.P.phase = "merge"
        ps = self.ps
        hT = self.hT
        onT = self.onT
        self.acur = 64 * KB
        wbr = self.alloc([nk, D], BF16)
        wg = self.alloc([2, 8, 128], BF16)
        sg = self.alloc([2, 512], F32)
        tm = self.alloc([2, 512], F32)
        self.wload(wbr, wbr_src, l, 0, D, engs=BULK)
        self.wload(wg[:, 0], self.w_in, l, bi * D, bi * D + 128)
        if pre is not None:
            save = self.acur
            pre()
            self.acur = save
        n = 0
        for m in range(8):
            if m > 0:
                self.wload(wg[:, m % 2], self.w_in, l, bi * D + m * 128, bi * D + (m + 1) * 128)
            for b in range(NB):
                bl = slice(b * 512, (b + 1) * 512)
                yb = 4 + n % 2
                gb = 6 + n % 2
                for k in range(nk):
                    P.mm(ps[:, yb, :], wbr[:, k, m * 128:(m + 1) * 128], onT[:, k, bl], k == 0, k == nk - 1)
                for k in range(8):
                    P.mm(ps[:, gb, :], wg[:, m % 2, k, :], hT[:, k, bl], k == 0, k == 7)
                P.actv(sg[:, n % 2, :], ps[:, gb, :], AF.Sigmoid)
                if first:
                    P.tt("dve", self.mixT[:, m, bl], ps[:, yb, :], sg[:, n % 2, :], ALU.mult)
                else:
                    P.tt("dve", tm[:, n % 2, :], ps[:, yb, :], sg[:, n % 2, :], ALU.mult)
                    P.tt("pool", self.mixT[:, m, bl], self.mixT[:, m, bl], tm[:, n % 2, :], ALU.add)
                n += 1

    def outproj_load(self, l, last):
        self.acur = 96 * KB
        wo = self.alloc([8, D], BF16)
        gB = self.alloc([D], F32)
        self.wload(wo, self.w_out, l, 0, D, engs=BULK)
        grow = 2 if last else l + 1
        self.P.dma(gB, bass.AP(self.norm_g.tensor, grow * D, [[0, 128], [1, D]]))
        return (wo, gB)

    def outproj(self, s, l, last):
        P = self.P
        self.P.phase = "outproj"
        ps = self.ps
        w = self.pre.pop("outproj", None) or self.outproj_load(l, last)
        wo, gB = w
        self.acur = 116 * KB
        xt = self.alloc([2, D], F32)
        xn = self.alloc([2, D], F32)
        junk = self.alloc([D], BF16)
        ss = self.alloc([48], F32)
        hb = self.alloc([2, D], BF16)
        yo = self.alloc([2, D], F32)
        if not last:
            self.pre["ssd"] = self.ssd_load(self.layers[self.layers.index(l) + 1])
        xsrc = self.x if l == 0 else self.xres
        for t in range(NT):
            i = t % 2
            tl = slice(t * 128, (t + 1) * 128)
            P.dma(xt[:, i, :], xsrc[s, tl, :])
            for hf in range(2):
                bank = 4 + hf
                for k in range(8):
                    P.mm(ps[:, bank, :], self.mixT[:, k, tl], wo[:, k, hf * 512:(hf + 1) * 512], k == 0, k == 7)
                P.tt("dve", xn[:, i, hf * 512:(hf + 1) * 512], ps[:, bank, :], xt[:, i, hf * 512:(hf + 1) * 512], ALU.add)
            if not last:
                P.dma(self.xres[s, tl, :], xn[:, i, :])
                self.norm_to_hT(xn[:, i, :], t, gB, (junk, ss, hb))
            else:
                P.actv(junk[:, :], xn[:, i, :], AF.Square, accum_out=ss[:, t:t + 1])
                P.actv(ss[:, 16 + t:17 + t], ss[:, t:t + 1], AF.Ln, bias=EPS, scale=1.0 / D)
                P.actv(ss[:, 32 + t:33 + t], ss[:, 16 + t:17 + t], AF.Exp, scale=-0.5)
                P.stt("dve", yo[:, i, :], xn[:, i, :], ss[:, 32 + t:33 + t], gB, ALU.mult, ALU.mult)
                P.dma(self.out[s, tl, :], yo[:, i, :])

    def build(self):
        self.setup()
        for s in range(self.nseq):
            nl = len(self.layers)
            for li, l in enumerate(self.layers):
                if li == 0:
                    self.phase_a(s, l)
                last = (li == nl - 1)
                order = [b for b in "abcd" if b in self.branches]
                phase = {"a": self.phase_ssd, "b": self.phase_mla, "c": self.phase_gla, "d": self.phase_gqa}
                loader = {"b": ("mla", self.mla_load), "c": ("gla", self.gla_load), "d": ("gqa", self.gqa_load)}
                mrg = {"a": (0, 8, self.w_br_a), "b": (1, 4, self.w_br_b), "c": (2, 4, self.w_br_c), "d": (3, 4, self.w_br_d)}
                for bi_, br in enumerate(order):
                    phase[br](l)
                    if bi_ + 1 < len(order):
                        key, fn = loader[order[bi_ + 1]]
                        pre = (lambda key=key, fn=fn: self.pre.__setitem__(key, fn(l)))
                    else:
                        pre = (lambda: self.pre.__setitem__("outproj", self.outproj_load(l, last)))
                    i_, nk_, wsrc_ = mrg[br]
                    self.merge(l, i_, nk_, wsrc_, bi_ == 0, pre=pre)
                self.outproj(s, l, last=(li == nl - 1))
        cnt = self.P.emit(self.es)
        return cnt


def _prep_inputs(inp):
    f = lambda a: np.ascontiguousarray(np.asarray(a, dtype=np.float32))
    w_in = f(inp["w_in"])
    p32, _ = _rope_perm_sign(32)
    p64, _ = _rope_perm_sign(64)
    kr = w_in[:, :, OFF["krope"]:OFF["krope"] + 32][:, :, p32]
    qd = w_in[:, :, OFF["qd"]:OFF["qd"] + 512].reshape(2, D, 8, 64)[:, :, :, p64].reshape(2, D, 512)
    kd = w_in[:, :, OFF["kd"]:OFF["kd"] + 128].reshape(2, D, 2, 64)[:, :, :, p64].reshape(2, D, 128)
    w_perm = np.ascontiguousarray(np.concatenate([kr, qd, kd], axis=2))
    qg = f(inp["q_norm_g"])
    kg = f(inp["k_norm_g"])
    qk_g = np.ascontiguousarray(np.stack([qg, qg[:, p64], kg, kg[:, p64]], axis=2))
    cos64, sin64 = _rope_tables(64)
    cos32, sin32 = _rope_tables(32)
    rope_cos = np.ones((128, L), np.float32)
    rope_sin = np.zeros((128, L), np.float32)
    rope_cos[0:64] = cos64
    rope_sin[0:64] = sin64
    rope_cos[64:96] = cos32
    rope_sin[64:96] = sin32
    wqb = f(inp["w_q_b"])
    w_qb_perm = np.ascontiguousarray(wqb.reshape(2, 384, 8, 96)[:, :, :, 64:96][:, :, :, p32].reshape(2, 384, 256))
    qlg = f(inp["q_lat_norm_g"]).reshape(2, 3, 128).transpose(0, 2, 1)
    kvg = f(inp["kv_lat_norm_g"]).reshape(2, 2, 128).transpose(0, 2, 1)
    lat_g = np.ascontiguousarray(np.concatenate([qlg, kvg], axis=2))
    gla_g = np.ascontiguousarray(f(inp["gla_norm_g"]).reshape(2, 4, 128).transpose(0, 2, 1))
    cw = f(inp["conv_w"]).reshape(2, 5, 12, 128).transpose(0, 3, 2, 1)
    cb = f(inp["conv_b"]).reshape(2, 12, 128).transpose(0, 2, 1)[..., None]
    conv_p = np.ascontiguousarray(np.concatenate([cw, cb], axis=3))
    shared = dict(
        conv_p=conv_p, a_log=f(inp["a_log"]).reshape(2, 32), dt_bias=f(inp["dt_bias"]).reshape(2, 32),
        d_skip=f(inp["d_skip"]), ssd_norm_g=f(inp["ssd_norm_g"]),
        w_gate_up=f(inp["w_gate_up"]), b_gate=f(inp["b_gate"]), gla_g=gla_g,
        w_in=w_in, w_perm=w_perm, w_q_b=wqb, w_qb_perm=w_qb_perm, w_kv_b=f(inp["w_kv_b"]), lat_g=lat_g,
        norm_g=np.ascontiguousarray(np.concatenate([f(inp["norm_g"]), f(inp["final_g"])[None, :]], axis=0)),
        qk_g=qk_g,
        w_br_a=f(inp["w_br_a"]), w_br_b=f(inp["w_br_b"]), w_br_c=f(inp["w_br_c"]), w_br_d=f(inp["w_br_d"]),
        w_out=f(inp["w_out"]), rope_cos=rope_cos, rope_sin=rope_sin,
    )
    return shared


_CACHE = {}


def kernel(**inputs):
    x = np.ascontiguousarray(np.asarray(inputs["x"], dtype=np.float32))
    n_cores = 8
    nseq = x.shape[0] // n_cores
    shared = _prep_inputs(inputs)
    mk = MK(nseq=nseq, layers=(0, 1), branches="abcd")
    mk.build()
    in_maps = []
    for c in range(n_cores):
        m = dict(shared)
        m["x"] = np.ascontiguousarray(x[c * nseq:(c + 1) * nseq])
        in_maps.append(m)
    res = run_bass_kernel_spmd(mk.nc, in_maps, core_ids=list(range(n_cores)))
    out = np.concatenate([np.asarray(r["out"]) for r in res.results], axis=0)
    return out.astype(np.float32)
```

```python
import sys
import numpy as np
from contextlib import ExitStack
import concourse.bass as bass
import concourse.mybir as mybir
from concourse.bass_utils import run_bass_kernel_spmd

F32 = mybir.dt.float32
BF16 = mybir.dt.bfloat16
ALU = mybir.AluOpType
AF = mybir.ActivationFunctionType
AX = mybir.AxisListType

_ESZ = {F32: 4, BF16: 2}


def _esz(dt):
    return _ESZ.get(dt, 4)


def region(ap):
    a = ap.ap
    off = int(ap.offset)
    es = _esz(ap.dtype)
    name = ap.tensor.name
    sp = str(ap.space)
    if sp == "DRAM":
        ext = sum((c - 1) * abs(s) for s, c in a) + 1
        return (name, 0, 1, off * es, (off + ext) * es)
    pstep, pcnt = a[0]
    if pstep == 0:
        pstep = 1 << 40
    p0 = off // pstep
    f0 = off % pstep
    ext = sum((c - 1) * abs(s) for s, c in a[1:]) + 1
    if sp == "PSUM":
        b0 = (f0 * es) // 2048 * 2048
        b1 = ((f0 + ext) * es + 2047) // 2048 * 2048
        return (name, p0 // 32 * 32, (p0 + pcnt + 31) // 32 * 32, b0, b1)
    return (name, p0, p0 + pcnt, f0 * es, (f0 + ext) * es)


def _ovl(r, s):
    return r[1] < s[2] and s[1] < r[2] and r[3] < s[4] and s[3] < r[4]


def _covers(r, s):
    return r[1] <= s[1] and r[2] >= s[2] and r[3] <= s[3] and r[4] >= s[4]


class Op:
    __slots__ = ("eng", "fn", "deps", "signal", "dma", "eidx", "rank", "waits", "line", "phase")

    def __init__(self, eng, fn):
        self.eng = eng
        self.fn = fn
        self.deps = set()
        self.signal = False
        self.dma = None
        self.waits = []


ENGS = ("pe", "act", "dve", "pool", "sp")
_WRAPPERS = ("mm", "tr", "actv", "tt", "ts", "stt", "copy", "memset", "recip", "dma")
NDMASEM = 8
EPOCH = 20000


class Prog:
    def __init__(self, nc):
        self.nc = nc
        self.ops = []
        self.acc = {}
        self.ndma = 0
        self.phase = ""

    def add(self, eng, fn, reads, writes, dma=False):
        op = Op(eng, fn)
        op.phase = self.phase
        try:
            fr = sys._getframe(1)
            if fr.f_code.co_filename == __file__ and fr.f_code.co_name in _WRAPPERS:
                fr = fr.f_back
            op.line = fr.f_lineno
        except Exception:
            op.line = 0
        rec_eng = "dma" if dma else eng
        idx = len(self.ops)
        ops = self.ops
        for ap in reads:
            r = region(ap)
            lst = self.acc.setdefault(r[0], [])
            done = False
            is_psum = (r[0] == "ps")
            for rec in lst:
                if rec[2]:
                    if _ovl(rec[0], r):
                        op.deps.add(rec[1])
                elif (not done) and (not dma) and rec[3] == rec_eng and rec[0] == r:
                    rec[1] = idx
                    done = True
                elif is_psum and rec[3] != rec_eng and _ovl(rec[0], r):
                    op.deps.add(rec[1])
            if not done:
                lst.append([r, idx, False, rec_eng])
        for ap in writes:
            r = region(ap)
            lst = self.acc.setdefault(r[0], [])
            keep = []
            for rec in lst:
                if _ovl(rec[0], r):
                    if rec[1] != idx:
                        op.deps.add(rec[1])
                    if _covers(r, rec[0]) and rec[1] != idx:
                        continue
                keep.append(rec)
            keep.append([r, idx, True, rec_eng])
            self.acc[r[0]] = keep
        if eng == "pe":
            op.deps = {d for d in op.deps if ops[d].eng != "pe"}
        if dma:
            op.dma = self.ndma
            self.ndma += 1
        ops.append(op)
        return op

    def mm(self, out, lhsT, rhs, start=True, stop=True):
        self.add("pe", lambda e: e.matmul(out, lhsT, rhs, start=start, stop=stop),
                 [lhsT, rhs], [out])

    def tr(self, out, in_, ident):
        self.add("pe", lambda e: e.transpose(out, in_, ident), [in_, ident], [out])

    def actv(self, out, in_, func, bias=None, scale=None, accum_out=None):
        kw = {}
        rd = [in_]
        wr = [out]
        if bias is not None:
            kw["bias"] = bias
            if not isinstance(bias, (int, float)):
                rd.append(bias)
        if scale is not None:
            kw["scale"] = scale
            if not isinstance(scale, (int, float)):
                rd.append(scale)
        if accum_out is not None:
            kw["accum_out"] = accum_out
            wr.append(accum_out)
        self.add("act", lambda e: e.activation(out, in_, func, **kw), rd, wr)

    def _veng(self, eng):
        return eng

    def tt(self, eng, out, in0, in1, op):
        self.add(eng, lambda e: e.tensor_tensor(out, in0, in1, op), [in0, in1], [out])

    def ts(self, eng, out, in0, s1, s2, op0, op1=None, accum_out=None):
        rd = [in0]
        if not isinstance(s1, (int, float)):
            rd.append(s1)
        if s2 is not None and not isinstance(s2, (int, float)):
            rd.append(s2)
        wr = [out]
        kw = {}
        if accum_out is not None:
            kw["accum_out"] = accum_out
            wr.append(accum_out)
        if op1 is None:
            self.add(eng, lambda e: e.tensor_scalar(out, in0, s1, None, op0, **kw), rd, wr)
        else:
            self.add(eng, lambda e: e.tensor_scalar(out, in0, s1, s2, op0, op1, **kw), rd, wr)

    def stt(self, eng, out, in0, scalar, in1, op0, op1):
        rd = [in0, in1]
        if not isinstance(scalar, (int, float)):
            rd.append(scalar)
        self.add(eng, lambda e: e.scalar_tensor_tensor(out, in0, scalar, in1, op0, op1), rd, [out])

    def copy(self, eng, out, in_):
        if eng == "act":
            self.add("act", lambda e: e.activation(out, in_, AF.Copy), [in_], [out])
        else:
            self.add(eng, lambda e: e.tensor_copy(out, in_), [in_], [out])

    def memset(self, eng, out, val):
        self.add(eng, lambda e: e.memset(out, val), [], [out])

    def recip(self, out, in_):
        self.add("dve", lambda e: e.reciprocal(out, in_), [in_], [out])

    def dma(self, out, in_, eng="sp", **kw):
        self.add(eng, lambda e: e.dma_start(out, in_, **kw), [in_], [out], dma=True)

    def emit(self, es, final_wait_all=True):
        nc = self.nc
        ops = self.ops
        cnt = {e: 0 for e in ENGS}
        for op in ops:
            op.eidx = cnt[op.eng]
            cnt[op.eng] += 1
        for i, op in enumerate(ops):
            for d in op.deps:
                dop = ops[d]
                if dop.dma is None:
                    if dop.eng == op.eng and op.eng != "sp":
                        pass
                    dop.signal = True
        last = {}
        for i, op in enumerate(ops):
            if op.dma is None:
                last[op.eng] = i
        for e, i in last.items():
            ops[i].signal = True
        rk = {e: 0 for e in ENGS}
        for op in ops:
            if op.dma is None and op.signal:
                rk[op.eng] += 1
                op.rank = rk[op.eng]
        nep = {e: (rk[e] + EPOCH - 1) // EPOCH + 1 for e in ENGS}
        sems = {}
        for e in ENGS:
            if e == "sp":
                continue
            sems[e] = [es.enter_context(nc.semaphore(f"s_{e}_{k}")) for k in range(nep[e])]
        dsem = [es.enter_context(nc.semaphore(f"s_dma_{k}")) for k in range(NDMASEM)]

        import os as _os
        simmode = bool(_os.environ.get("SIMMODE"))
        dinfo = {}
        m = 0
        for op in ops:
            if op.dma is None:
                continue
            if simmode and op.eng == "pool":
                sem = es.enter_context(nc.semaphore(f"s_u_{op.dma}"))
                dinfo[op.dma] = (("udma", op.dma), sem, 16)
            else:
                k = m % NDMASEM
                dinfo[op.dma] = (("dma", k), dsem[k], 16 * (m // NDMASEM + 1))
                m += 1

        def target(dop):
            if dop.dma is not None:
                return dinfo[dop.dma]
            r = dop.rank - 1
            return ((dop.eng, r // EPOCH), sems[dop.eng][r // EPOCH], r % EPOCH + 1)

        waited = {e: {} for e in ENGS}
        for i, op in enumerate(ops):
            w = waited[op.eng]
            need = {}
            for d in op.deps:
                key, sem, val = target(ops[d])
                if w.get(key, 0) >= val:
                    continue
                if key not in need or need[key][1] < val:
                    need[key] = (sem, val)
            if op.dma is not None:
                key, sem, val = dinfo[op.dma]
                val -= 16
                if val > 0 and w.get(key, 0) < val and (key not in need or need[key][1] < val):
                    need[key] = (sem, val)
            for key, (sem, val) in need.items():
                w[key] = val
                op.waits.append((sem, val))
        final_waits = []
        for e, i in last.items():
            key, sem, val = target(ops[i])
            final_waits.append((sem, val))
        fin = {}
        for key, sem, val in dinfo.values():
            if key not in fin or fin[key][1] < val:
                fin[key] = (sem, val)
        final_waits.extend(fin.values())

        byeng = {e: [op for op in ops if op.eng == e] for e in ENGS}

        annotate = bool(_os.environ.get("SIMMODE"))

        def run(e, name):
            for op in byeng[name]:
                for sem, val in op.waits:
                    e.wait_ge(sem, val)
                ins = op.fn(e)
                if annotate:
                    ins.annotate(f"L{op.line}")
                if op.dma is not None:
                    ins.then_inc(dinfo[op.dma][1], 16)
                elif op.signal:
                    r = op.rank - 1
                    ins.then_inc(sems[name][r // EPOCH], 1)
            if name == "sp":
                for sem, val in final_waits:
                    e.wait_ge(sem, val)

        with nc.Block() as block:
            @block.tensor
            def _(e):
                run(e, "pe")

            @block.scalar
            def _(e):
                run(e, "act")

            @block.vector
            def _(e):
                run(e, "dve")

            @block.gpsimd
            def _(e):
                run(e, "pool")

            @block.sync
            def _(e):
                run(e, "sp")
        return cnt


L = 2048
D = 1024
NT = 16
NB = 4
EPS = 1e-6
OFF = dict(g=0, za=4096, xbc=5120, dt=6656, zb=6688, qlat=7200, kvlat=7584, krope=7840,
           zc=7872, qc=8384, kc=8640, vc=8896, glr=9408, zd=9440, qd=9952, kd=10464, vd=10592)
N_IN = 10720
KB = 1024
BULK = ("dve", "dve", "pool")


def _rope_perm_sign(d):
    m = d // 2
    hm = m // 2
    perm = np.zeros(d, np.int64)
    sign = np.zeros(d, np.float32)
    for j in range(d):
        jj = j % m
        if jj < hm:
            perm[j] = j + hm
            sign[j] = -1.0
        else:
            perm[j] = j - hm
            sign[j] = 1.0
    return perm, sign


def _rope_tables(d):
    rows = L // 64
    row = np.repeat(np.arange(rows), 64).astype(np.float32)
    col = np.tile(np.arange(64), rows).astype(np.float32)
    m = d // 2
    inv = (np.float32(10000.0) ** (-np.arange(0, m, 2, dtype=np.float32) / np.float32(m))).astype(np.float32)
    ang_r = row[:, None] * inv
    ang_c = col[:, None] * inv
    ang = np.concatenate([ang_r, ang_r, ang_c, ang_c], axis=-1).astype(np.float32)
    _, sign = _rope_perm_sign(d)
    cos = np.cos(ang).astype(np.float32).T
    sin = (np.sin(ang).astype(np.float32) * sign[None, :]).T
    return np.ascontiguousarray(cos), np.ascontiguousarray(sin)


class MK:
    def __init__(self, nseq=2, layers=(0, 1), branches="abcd", debug=False):
        self.nseq = nseq
        self.layers = layers
        self.branches = branches
        self.debug = debug
        nc = self.nc = bass.Bass("TRN2", target_bir_lowering=False)
        es = self.es = ExitStack()
        self.P = Prog(nc)
        self.dbg_outs = {}
        di = lambda n, s: nc.dram_tensor(n, s, F32, kind="ExternalInput").ap()
        self.x = di("x", [nseq, L, D])
        self.out = nc.dram_tensor("out", [nseq, L, D], F32, kind="ExternalOutput").ap()
        self.xres = nc.dram_tensor("xres", [nseq, L, D], F32).ap()
        self.w_in = di("w_in", [2, D, N_IN])
        self.w_perm = di("w_perm", [2, D, 672])
        self.norm_g = di("norm_g", [3, D])
        self.qk_g = di("qk_g", [2, 64, 4])
        self.w_q_b = di("w_q_b", [2, 384, 768])
        self.w_qb_perm = di("w_qb_perm", [2, 384, 256])
        self.w_kv_b = di("w_kv_b", [2, 256, 1024])
        self.lat_g = di("lat_g", [2, 128, 5])
        self.w_gate_up = di("w_gate_up", [2, 2, 16, 256])
        self.b_gate = di("b_gate", [2, 2, 256])
        self.gla_g = di("gla_g", [2, 128, 4])
        self.conv_p = di("conv_p", [2, 128, 12, 6])
        self.a_log = di("a_log", [2, 32])
        self.dt_bias = di("dt_bias", [2, 32])
        self.d_skip = di("d_skip", [2, 16])
        self.ssd_norm_g = di("ssd_norm_g", [2, D])
        self.yb_d = nc.dram_tensor("yb_d", [L, D], F32).ap()
        self.w_br_a = di("w_br_a", [2, 1024, D])
        self.w_br_b = di("w_br_b", [2, 512, D])
        self.w_br_c = di("w_br_c", [2, 512, D])
        self.w_br_d = di("w_br_d", [2, 512, D])
        self.w_out = di("w_out", [2, D, D])
        self.rope_cos = di("rope_cos", [128, L])
        self.rope_sin = di("rope_sin", [128, L])
        sb = lambda n, s, d: es.enter_context(nc.sbuf_tensor(n, s, d))
        self.hT = sb("hT", [128, 8, L], BF16)
        self.COS = sb("COS", [128, L], F32)
        self.SIN = sb("SIN", [128, L], F32)
        self.ident = sb("ident", [128, 128], BF16)
        self.identf = sb("identf", [128, 128], F32)
        self.ones = sb("ones", [128, 128], BF16)
        self.onesf = sb("onesf", [128, 128], F32)
        self.triF = sb("triF", [128, 128], F32)
        self.triB = sb("triB", [128, 128], F32)
        self.LF = sb("LF", [128, 128], BF16)
        self.LB = sb("LB", [128, 128], BF16)
        self.TF = sb("TF", [128, 128], BF16)
        self.TB = sb("TB", [128, 128], BF16)
        self.TmFb = sb("TmFb", [128, 128], BF16)
        self.TmBb = sb("TmBb", [128, 128], BF16)
        self.ARW = 155 * 256
        self.AR = sb("AR", [128, self.ARW], F32)
        self.ps = es.enter_context(nc.psum_tensor("ps", [128, 8, 512], F32))
        self.mixT = self.view(0, [8, L], BF16)
        self.onT = self.view(32 * KB, [8, L], BF16)
        self.acur = 64 * KB
        self.stage = self.view(147 * KB, [4, 512], F32)
        self.nst = 0
        self.pre = {}
        self.alim = 147 * KB

    def view(self, off, shape, dt, p0=0, parts=128):
        es_ = _esz(dt)
        n = int(np.prod(shape))
        assert off % 4 == 0
        w0 = off // 4
        w1 = (off + n * es_ + 3) // 4
        assert w1 <= self.ARW, (off, shape)
        ap = self.AR[p0:p0 + parts, w0:w1]
        if dt != F32:
            ap = ap.bitcast(dt)
        if len(shape) == 2:
            ap = ap.rearrange("p (a b) -> p a b", a=shape[0])
        elif len(shape) == 3:
            ap = ap.rearrange("p (a b c) -> p a b c", a=shape[0], b=shape[1])
        return ap

    def alloc(self, shape, dt, p0=0, parts=128):
        n = int(np.prod(shape)) * _esz(dt)
        n = (n + 63) // 64 * 64
        v = self.view(self.acur, shape, dt, p0, parts)
        self.acur += n
        assert self.acur <= self.alim or getattr(self, "allow_over", False), (self.acur, shape)
        return v

    def dbg(self, name, ap_sb, shape):
        if not self.debug:
            return
        d = self.nc.dram_tensor("dbg_" + name, shape, ap_sb.dtype, kind="ExternalOutput").ap()
        self.dbg_outs[name] = d
        self.P.dma(d, ap_sb)

    def wcols(self, src, l, c0, c1):
        return src[l].rearrange("(k p) c -> p k c", p=128)[:, :, c0:c1]

    def wload(self, dst, src, l, c0, c1, engs=("pool",)):
        P = self.P
        nk = dst.shape[1]
        C = c1 - c0
        for k in range(nk):
            for cc in range(0, C, 512):
                w = min(512, C - cc)
                st = self.stage[:, self.nst % 4, 0:w]
                self.nst += 1
                P.dma(st, src[l, k * 128:(k + 1) * 128, c0 + cc:c0 + cc + w])
                P.copy(engs[self.nst % len(engs)], dst[:, k, cc:cc + w], st)

    def setup(self):
        P = self.P
        P.dma(self.COS[:], self.rope_cos)
        P.dma(self.SIN[:], self.rope_sin)
        P.memset("pool", self.onesf[:], 1.0)
        P.memset("pool", self.ones[:], 1.0)
        P.memset("pool", self.identf[:], 0.0)
        idf = self.identf
        onf = self.onesf
        P.add("pool", lambda e: e.affine_select(idf[:], onf[:], [[-1, 128]], ALU.is_equal, 0.0,
                                                base=0, channel_multiplier=1), [onf[:]], [idf[:]])
        P.copy("dve", self.ident[:], self.identf[:])
        tf, tb = self.triF, self.triB
        P.add("pool", lambda e: e.affine_select(tf[:], onf[:], [[1, 128]], ALU.is_ge, 0.0,
                                                base=0, channel_multiplier=-1), [onf[:]], [tf[:]])
        P.add("pool", lambda e: e.affine_select(tb[:], onf[:], [[-1, 128]], ALU.is_ge, 0.0,
                                                base=0, channel_multiplier=1), [onf[:]], [tb[:]])
        P.ts("dve", self.TmFb[:], tf[:], -1.0 / 16, None, ALU.mult)
        P.copy("dve", self.TF[:], tf[:])
        P.copy("dve", self.TB[:], tb[:])
        P.tt("dve", self.LF[:], tb[:], self.identf[:], ALU.subtract)
        P.tt("dve", self.LB[:], tf[:], self.identf[:], ALU.subtract)
        P.ts("dve", self.TmBb[:], tb[:], -1.0 / 16, None, ALU.mult)

    def norm_to_hT(self, xt, t, gB, scr):
        P = self.P
        junk, ss, hb = scr
        i = t % 2
        P.actv(junk[:, :], xt, AF.Square, accum_out=ss[:, t:t + 1])
        P.actv(ss[:, 16 + t:17 + t], ss[:, t:t + 1], AF.Ln, bias=EPS, scale=1.0 / D)
        P.actv(ss[:, 32 + t:33 + t], ss[:, 16 + t:17 + t], AF.Exp, scale=-0.5)
        P.stt("dve", hb[:, i, :], xt, ss[:, 32 + t:33 + t], gB, ALU.mult, ALU.mult)
        pb = self.ps[:, 7, :].bitcast(BF16)
        for k in range(8):
            P.tr(pb[:, k * 128:(k + 1) * 128], hb[:, i, k * 128:(k + 1) * 128], self.ident[:])
        P.copy("act", self.hT[:, :, t * 128:(t + 1) * 128],
               pb.rearrange("p (k c) -> p k c", k=8))

    def phase_a(self, s, l):
        P = self.P
        self.P.phase = "A"
        self.pre["ssd"] = self.ssd_load(l)
        self.acur = 100 * KB
        xt = self.alloc([2, D], F32)
        gB = self.alloc([D], F32)
        junk = self.alloc([D], BF16)
        ss = self.alloc([48], F32)
        hb = self.alloc([2, D], BF16)
        P.dma(gB, bass.AP(self.norm_g.tensor, l * D, [[0, 128], [1, D]]))
        for t in range(NT):
            P.dma(xt[:, t % 2, :], self.x[s, t * 128:(t + 1) * 128, :])
            self.norm_to_hT(xt[:, t % 2, :], t, gB, (junk, ss, hb))

    def attn_units(self, kT_fn, qT, V_fn, ob, pT, scale):
        P = self.P
        ps = self.ps
        P.mm(ps[:, 0, :], kT_fn(0), qT)
        for kc in range(16):
            if kc + 1 < 16:
                P.mm(ps[:, (kc + 1) % 2, :], kT_fn(kc + 1), qT)
            P.actv(pT[:, kc % 2, :], ps[:, kc % 2, :], AF.Exp, scale=scale)
            P.mm(ps[:, ob, :], V_fn(kc), pT[:, kc % 2, :], start=(kc == 0), stop=(kc == 15))

    def qk_norm_rope(self, psA, psB, gtile, gc, out, b, tmp):
        P = self.P
        sq, rt, t1, t2 = tmp
        bl = slice(b * 512, (b + 1) * 512)
        P.actv(sq[0:64, :], psA, AF.Square)
        P.mm(self.ps[0:64, 6, :], self.ones[0:64, 0:64], sq[0:64, :])
        P.actv(rt[0:64, :], self.ps[0:64, 6, :], AF.Ln, bias=EPS, scale=1.0 / 64)
        P.actv(rt[0:64, :], rt[0:64, :], AF.Exp, scale=-0.5)
        P.stt("dve", t1[0:64, :], psA, gtile[0:64, gc:gc + 1], self.COS[0:64, bl], ALU.mult, ALU.mult)
        P.stt("dve", t2[0:64, :], psB, gtile[0:64, gc + 1:gc + 2], self.SIN[0:64, bl], ALU.mult, ALU.mult)
        P.tt("pool", t1[0:64, :], t1[0:64, :], t2[0:64, :], ALU.add)
        P.tt("pool", out, t1[0:64, :], rt[0:64, :], ALU.mult)

    def gqa_load(self, l):
        P = self.P
        self.acur = 108 * KB
        COSG = self.alloc([L], F32)
        SING = self.alloc([L], F32)
        wk2 = self.alloc([2, 8, 128], BF16)
        wkp2 = self.alloc([2, 8, 128], BF16)
        wv = self.alloc([8, 128], BF16)
        wq = self.alloc([2, 8, 128], BF16)
        wqp = self.alloc([2, 8, 128], BF16)
        wz = self.alloc([2, 8, 128], BF16)
        gt = self.alloc([4], F32)
        for g in range(2):
            for hlf in range(2):
                self.wload(wk2[:, g, :, 64 * hlf:64 * hlf + 64], self.w_in, l, OFF["kd"] + g * 64, OFF["kd"] + (g + 1) * 64, engs=BULK)
                self.wload(wkp2[:, g, :, 64 * hlf:64 * hlf + 64], self.w_perm, l, 544 + g * 64, 544 + (g + 1) * 64, engs=BULK)
        for hlf in range(2):
            rows = slice(64 * hlf, 64 * hlf + 64)
            P.dma(COSG[rows, :], self.rope_cos[0:64, :])
            P.dma(SING[rows, :], self.rope_sin[0:64, :])
            P.dma(gt[rows, :], self.qk_g[l])
        self.wload(wv, self.w_in, l, OFF["vd"], OFF["vd"] + 128, engs=BULK)
        self.wload(wq[:, 0], self.w_in, l, OFF["qd"], OFF["qd"] + 128, engs=BULK)
        self.wload(wqp[:, 0], self.w_perm, l, 32, 32 + 128, engs=BULK)
        self.wload(wz[:, 0], self.w_in, l, OFF["zd"], OFF["zd"] + 128, engs=BULK)
        return (COSG, SING, wk2, wkp2, wv, wq, wqp, wz, gt)

    def phase_gqa(self, l):
        P = self.P
        self.P.phase = "gqa_prep"
        ps = self.ps
        hT = self.hT
        onT = self.onT
        self.acur = 64 * KB
        BD = self.alloc([128], BF16)
        kT2 = self.alloc([2, L], BF16)
        Va = self.alloc([2, 16, 192], BF16)
        qT2 = self.alloc([2, 512], BF16)
        pT = self.alloc([2, 2, 512], BF16)
        sq = self.alloc([512], BF16)
        rt = self.alloc([512], F32)
        t1 = self.alloc([512], F32)
        t2 = self.alloc([512], F32)
        sz = self.alloc([2, 512], F32)
        ez = self.alloc([512], F32)
        rs = t1
        nt = t2
        osb = self.alloc([2, 512], F32)
        assert self.acur <= 108 * KB
        w = self.pre.pop("gqa", None) or self.gqa_load(l)
        COSG, SING, wk2, wkp2, wv, wq, wqp, wz, gt = w
        P.memset("pool", BD, 0.0)
        P.memset("pool", BD[0:64, 0:64], 1.0)
        P.memset("pool", BD[64:128, 64:128], 1.0)
        P.memset("pool", Va[:, :, :, 64:128], 1.0)

        def proj(wa, wb, bl):
            for k in range(8):
                P.mm(ps[:, 6, :], wa(k), hT[:, k, bl], k == 0, k == 7)
            for k in range(8):
                P.mm(ps[:, 7, :], wb(k), hT[:, k, bl], k == 0, k == 7)
            P.actv(sq, ps[:, 6, :], AF.Square)

        def rope1(gc, bl):
            P.stt("dve", t1, ps[:, 6, :], gt[:, gc:gc + 1], COSG[:, bl], ALU.mult, ALU.mult)
            P.stt("dve", t2, ps[:, 7, :], gt[:, gc + 1:gc + 2], SING[:, bl], ALU.mult, ALU.mult)

        def rope2(out):
            P.mm(ps[:, 7, :], BD, sq, True, True)
            P.actv(rt, ps[:, 7, :], AF.Ln, bias=EPS, scale=1.0 / 64)
            P.actv(rt, rt, AF.Exp, scale=-0.5)
            P.tt("pool", t1, t1, t2, ALU.add)
            P.tt("pool", out, t1, rt, ALU.mult)

        for g in range(2):
            for b in range(NB):
                bl = slice(b * 512, (b + 1) * 512)
                proj(lambda k: wk2[:, g, k, :], lambda k: wkp2[:, g, k, :], bl)
                rope1(2, bl)
                rope2(kT2[:, g, bl])
        for t in range(NT):
            for k in range(8):
                P.mm(ps[:, 4 + t % 2, 0:128], hT[:, k, t * 128:(t + 1) * 128], wv[:, k, :], k == 0, k == 7)
            for g in range(2):
                P.copy("act", Va[:, g, t, 0:64], ps[:, 4 + t % 2, g * 64:(g + 1) * 64])
                P.copy("dve", Va[:, g, t, 128:192], ps[:, 4 + t % 2, g * 64:(g + 1) * 64])
        P.phase = "gqa_heads"
        items = [(pr, b) for pr in range(4) for b in range(NB)]

        def prep_parts(it):
            pr, b = items[it]
            i = pr % 2
            bl = slice(b * 512, (b + 1) * 512)
            hooks = {}

            def add(kc, f):
                prev = hooks.get(kc)

                def both(prev=prev, f=f):
                    if prev is not None:
                        prev()
                    f()
                hooks[kc] = both

            def loads():
                if b == 0 and pr > 0:
                    self.wload(wq[:, i], self.w_in, l, OFF["qd"] + pr * 128, OFF["qd"] + (pr + 1) * 128)
                    self.wload(wqp[:, i], self.w_perm, l, 32 + pr * 128, 32 + (pr + 1) * 128)
                    self.wload(wz[:, i], self.w_in, l, OFF["zd"] + pr * 128, OFF["zd"] + (pr + 1) * 128)
            add(0, loads)

            def pk(k):
                P.mm(ps[:, 6, :], wq[:, i, k, :], hT[:, k, bl], k == 0, k == 7)
                P.mm(ps[:, 7, :], wqp[:, i, k, :], hT[:, k, bl], k == 0, k == 7)
            for k in range(8):
                add(k, lambda k=k: pk(k))

            def sq_rope1():
                P.actv(sq, ps[:, 6, :], AF.Square)
                rope1(0, bl)
            add(8, sq_rope1)
            add(9, lambda: rope2(qT2[:, it % 2, :]))
            for k in range(8):
                add(10 + min(k, 5), lambda k=k: P.mm(ps[:, 6, :], wz[:, i, k, :], hT[:, k, bl], k == 0, k == 7))

            def silu():
                P.actv(ez, ps[:, 6, :], AF.Exp, scale=-1.0)
                P.actv(ez, ez, AF.Ln, bias=1.0)
                P.actv(ez, ez, AF.Exp, scale=-1.0)
                P.tt("dve", sz[:, it % 2, :], ps[:, 6, :], ez, ALU.mult)
            add(15, silu)
            return hooks

        def units(it, hooks):
            pr, b = items[it]
            g = pr // 2
            bl = slice(b * 512, (b + 1) * 512)
            qb = qT2[:, it % 2, :]
            sbank = [[0, 1], [4, 5]]
            rows = [slice(0, 64), slice(64, 128)]
            vsel = [slice(0, 128), slice(64, 192)]
            for par in range(2):
                P.mm(ps[:, sbank[par][0], :], kT2[rows[par], g, 0:128], qb[rows[par], :])
            for kc in range(16):
                if kc + 1 < 16:
                    for par in range(2):
                        P.mm(ps[:, sbank[par][(kc + 1) % 2], :], kT2[rows[par], g, (kc + 1) * 128:(kc + 2) * 128], qb[rows[par], :])
                for par in range(2):
                    P.actv(pT[:, par, kc % 2, :], ps[:, sbank[par][kc % 2], :], AF.Exp, scale=0.125)
                for par in range(2):
                    P.mm(ps[:, 2 + par, :], Va[:, g, kc, vsel[par]], pT[:, par, kc % 2, :], kc == 0, kc == 15)
                if kc in hooks:
                    hooks[kc]()
            for par in range(2):
                P.copy("dve", osb[:, par, :], ps[:, 2 + par, :])
            for par in range(2):
                orow = rows[par]
                srow = rows[1 - par]
                P.recip(rs[orow, :], osb[srow, par, :])
                P.tt("dve", nt[orow, :], osb[orow, par, :], rs[orow, :], ALU.mult)
                P.tt("pool", onT[orow, pr, bl], nt[orow, :], sz[orow, it % 2, :], ALU.mult)

        h0 = prep_parts(0)
        for kc in sorted(h0):
            h0[kc]()
        for it in range(len(items)):
            hooks = prep_parts(it + 1) if it + 1 < len(items) else {}
            units(it, hooks)

    def mla_load(self, l):
        self.acur = 124 * KB
        wq = self.alloc([3, 768], BF16)
        wqp = self.alloc([3, 256], BF16)
        wkv = self.alloc([2, 1024], BF16)
        wql = self.alloc([8, 384], BF16)
        wkvl = self.alloc([8, 256], BF16)
        wkr = self.alloc([2, 8, 32], BF16)
        lg = self.alloc([5], F32)
        self.wload(wql, self.w_in, l, OFF["qlat"], OFF["qlat"] + 384, engs=BULK)
        self.wload(wkvl, self.w_in, l, OFF["kvlat"], OFF["kvlat"] + 256, engs=BULK)
        self.wload(wkr[:, 0], self.w_in, l, OFF["krope"], OFF["krope"] + 32, engs=BULK)
        self.wload(wkr[:, 1], self.w_perm, l, 0, 32, engs=BULK)
        self.wload(wq, self.w_q_b, l, 0, 768, engs=BULK)
        self.wload(wqp, self.w_qb_perm, l, 0, 256, engs=BULK)
        self.wload(wkv, self.w_kv_b, l, 0, 1024, engs=BULK)
        self.P.dma(lg, self.lat_g[l])
        return (wq, wqp, wkv, wql, wkvl, wkr, lg)

    def phase_mla(self, l):
        P = self.P
        self.P.phase = "mla_prep"
        ps = self.ps
        hT = self.hT
        onT = self.onT
        COS, SIN = self.COS, self.SIN
        self.acur = 64 * KB
        qln = self.alloc([3, L], BF16)
        kvn = self.alloc([2, L], BF16)
        krT = self.alloc([L], BF16)
        Va = self.alloc([16, 4, 192], BF16)
        qT = self.alloc([2, 512], BF16)
        pT = self.alloc([4, 512], BF16)
        sz = self.alloc([2, 512], F32)
        wz = self.alloc([2, 8, 64], BF16)
        assert self.acur <= 124 * KB
        self.acur = 48 * KB
        kT = self.alloc([2, L], BF16)
        t1 = self.alloc([512], F32)
        t2 = self.alloc([512], F32)
        sqc = self.alloc([2, 512], BF16)
        rt = self.alloc([512], F32)
        rs = t1
        assert self.acur <= 64 * KB
        w = self.pre.pop("mla", None) or self.mla_load(l)
        wq, wqp, wkv, wql, wkvl, wkr, lg = w
        P.memset("pool", Va[:, :, :, 64:128], 1.0)
        nsq = 0
        for b in range(NB):
            bl = slice(b * 512, (b + 1) * 512)
            for (wsrc, nch, bank0, sbank, dst, gofs, dim) in ((wql, 3, 0, 3, qln, 0, 384), (wkvl, 2, 4, 6, kvn, 3, 256)):
                for c in range(nch):
                    for k in range(8):
                        P.mm(ps[:, bank0 + c, :], wsrc[:, k, c * 128:(c + 1) * 128], hT[:, k, bl], k == 0, k == 7)
                for c in range(nch):
                    P.actv(sqc[:, nsq % 2, :], ps[:, bank0 + c, :], AF.Square)
                    P.mm(ps[:, sbank, :], self.ones[:, :], sqc[:, nsq % 2, :], c == 0, c == nch - 1)
                    nsq += 1
                P.actv(rt, ps[:, sbank, :], AF.Ln, bias=EPS, scale=1.0 / dim)
                P.actv(rt, rt, AF.Exp, scale=-0.5)
                for c in range(nch):
                    P.stt("dve", dst[:, c, bl], ps[:, bank0 + c, :], lg[:, gofs + c:gofs + c + 1], rt, ALU.mult, ALU.mult)
            for k in range(8):
                P.mm(ps[64:96, 7, :], wkr[:, 0, k, :], hT[:, k, bl], k == 0, k == 7)
            P.tt("dve", t1[64:96, :], ps[64:96, 7, :], COS[64:96, bl], ALU.mult)
            for k in range(8):
                P.mm(ps[64:96, 7, :], wkr[:, 1, k, :], hT[:, k, bl], k == 0, k == 7)
            P.tt("dve", t2[64:96, :], ps[64:96, 7, :], SIN[64:96, bl], ALU.mult)
            P.tt("pool", krT[64:96, bl], t1[64:96, :], t2[64:96, :], ALU.add)
        wv_view = wkv.rearrange("p c (h two d) -> p c h two d", h=8, two=2)
        for t in range(NT):
            tl = slice(t * 128, (t + 1) * 128)
            for c in range(2):
                P.mm(ps[:, 4 + t % 2, :].rearrange("p (h d) -> p h d", h=8), kvn[:, c, tl], wv_view[:, c, :, 1, :], c == 0, c == 1)
            pv = ps[:, 4 + t % 2, :].rearrange("p (j two d) -> p j two d", j=4, two=2)
            P.copy("act", Va[:, t, :, 0:64], pv[:, :, 0, :])
            P.copy("dve", Va[:, t, :, 128:192], pv[:, :, 1, :])
        scale = 96.0 ** -0.5
        P.phase = "mla_heads"
        items = [(h, b) for h in range(8) for b in range(NB)]
        ez = t2

        def head_prep(h):
            i = h % 2
            self.wload(wz[:, i], self.w_in, l, OFF["zb"] + h * 64, OFF["zb"] + (h + 1) * 64)
            for b in range(NB):
                bl = slice(b * 512, (b + 1) * 512)
                for c in range(2):
                    P.mm(ps[0:64, 6, :], wkv[:, c, h * 128:h * 128 + 64], kvn[:, c, bl], c == 0, c == 1)
                P.copy("dve", kT[0:64, i, bl], ps[0:64, 6, :])
            P.copy("pool", kT[64:96, i, :], krT[64:96, :])

        def prep_parts(it):
            h, b = items[it]
            i = h % 2
            par = h % 2
            orow = slice(64 * par, 64 * par + 64)
            bl = slice(b * 512, (b + 1) * 512)
            qb = qT[:, it % 2, :]

            def p0():
                if b == 0:
                    head_prep(h)

            def p1():
                for c in range(3):
                    P.mm(ps[0:96, 6, :], wq[:, c, h * 96:(h + 1) * 96], qln[:, c, bl], c == 0, c == 2)
                for c in range(3):
                    P.mm(ps[64:96, 7, :], wqp[:, c, h * 32:(h + 1) * 32], qln[:, c, bl], c == 0, c == 2)
                P.copy("dve", qb[0:64, :], ps[0:64, 6, :])
                P.tt("dve", t1[64:96, :], ps[64:96, 6, :], COS[64:96, bl], ALU.mult)
                P.tt("dve", t2[64:96, :], ps[64:96, 7, :], SIN[64:96, bl], ALU.mult)
                P.tt("pool", qb[64:96, :], t1[64:96, :], t2[64:96, :], ALU.add)

            def p3():
                for k in range(8):
                    P.mm(ps[orow, 7, :], wz[:, i, k, :], hT[:, k, bl], k == 0, k == 7)
                P.actv(ez[orow, :], ps[orow, 7, :], AF.Exp, scale=-1.0)
                P.actv(ez[orow, :], ez[orow, :], AF.Ln, bias=1.0)
                P.actv(ez[orow, :], ez[orow, :], AF.Exp, scale=-1.0)
                P.tt("dve", sz[orow, it % 2, :], ps[orow, 7, :], ez[orow, :], ALU.mult)

            return [p0, p1, p3]

        def units(it, hooks):
            h, b = items[it]
            i = h % 2
            par = h % 2
            pr = h // 2
            orow = slice(64 * par, 64 * par + 64)
            srow = slice(64 * (1 - par), 64 * (1 - par) + 64)
            vsel = slice(0, 128) if par == 0 else slice(64, 192)
            bl = slice(b * 512, (b + 1) * 512)
            qb = qT[:, it % 2, :]
            ob = 2 + it % 2
            sbanks = [[0, 1], [4, 5]]

            def S(kc):
                P.mm(ps[:, sbanks[(kc // 2) % 2][kc % 2], :], kT[0:96, i, kc * 128:(kc + 1) * 128], qb[0:96, :])
            S(0)
            S(1)
            for k2 in range(8):
                if k2 + 1 < 8:
                    S(2 * k2 + 2)
                    S(2 * k2 + 3)
                for kc in (2 * k2, 2 * k2 + 1):
                    P.actv(pT[:, kc % 4, :], ps[:, sbanks[(kc // 2) % 2][kc % 2], :], AF.Exp, scale=scale)
                for kc in (2 * k2, 2 * k2 + 1):
                    P.mm(ps[:, ob, :], Va[:, kc, pr, vsel], pT[:, kc % 4, :], kc == 0, kc == 15)
                if k2 in hooks:
                    hooks[k2]()
            P.recip(rs[orow, :], ps[srow, ob, :])
            P.tt("dve", rs[orow, :], ps[orow, ob, :], rs[orow, :], ALU.mult)
            P.tt("pool", onT[orow, pr, bl], rs[orow, :], sz[orow, it % 2, :], ALU.mult)

        for f in prep_parts(0):
            f()
        for it in range(len(items)):
            hooks = {}
            if it + 1 < len(items):
                pp = prep_parts(it + 1)
                hooks = {0: pp[0], 2: pp[1], 5: pp[2]}
            units(it, hooks)

    def gla_load(self, l):
        P = self.P
        self.acur = 48 * KB
        wvc = self.alloc([8, 512], BF16)
        wzc = self.view(48 * KB, [8, 512], BF16)
        wqc = self.alloc([8, 256], BF16)
        wkc = self.alloc([8, 256], BF16)
        assert self.acur <= 64 * KB
        self.acur = 139 * KB
        wgu = self.alloc([2, 256], BF16)
        bg = self.alloc([2, 256], BF16)
        wglr = self.alloc([2, 8, 32], BF16)
        gg = self.alloc([4], F32)
        self.wload(wqc, self.w_in, l, OFF["qc"], OFF["qc"] + 256, engs=BULK)
        self.wload(wkc, self.w_in, l, OFF["kc"], OFF["kc"] + 256, engs=BULK)
        self.wload(wvc, self.w_in, l, OFF["vc"], OFF["vc"] + 512, engs=BULK)
        self.wload(wglr[:, 0], self.w_in, l, OFF["glr"], OFF["glr"] + 32, engs=BULK)
        self.wload(wglr[:, 1, :, 0:16], self.w_in, l, OFF["glr"] + 16, OFF["glr"] + 32, engs=BULK)
        self.wload(wglr[:, 1, :, 16:32], self.w_in, l, OFF["glr"], OFF["glr"] + 16, engs=BULK)
        P.memset("pool", wgu[0:32], 0.0)
        st = self.stage[0:16, self.nst % 4, :]
        self.nst += 1
        P.dma(st.rearrange("p (d c) -> p d c", d=2), self.w_gate_up[l].rearrange("d r c -> r d c"))
        P.copy("pool", wgu[0:16], st.rearrange("p (d c) -> p d c", d=2))
        st = self.stage[0:1, self.nst % 4, :]
        self.nst += 1
        P.dma(st.rearrange("p (d c) -> p d c", d=2), self.b_gate[l:l + 1])
        P.copy("pool", bg[0:1], st.rearrange("p (d c) -> p d c", d=2))
        P.dma(gg, self.gla_g[l])
        return (wvc, wzc, wqc, wkc, wglr, wgu, bg, gg)

    def phase_gla(self, l):
        P = self.P
        self.P.phase = "gla_prep"
        ps = self.ps
        hT = self.hT
        onT = self.onT
        self.acur = 64 * KB
        qTc = self.alloc([2, L], BF16)
        kTc = self.alloc([2, L], BF16)
        Vc = self.alloc([16, 512], BF16)
        ob = self.alloc([4, L], BF16)
        glrT = self.alloc([2, L], BF16)
        oblk = self.alloc([4, 512], F32)
        S = self.alloc([2, 128], F32)
        Sbf = self.alloc([2, 128], BF16)
        gsp = self.alloc([256], F32)
        gh = self.alloc([256], BF16)
        gl = self.alloc([256], BF16)
        eg = self.alloc([2, 128], F32)
        eng = self.alloc([2, 128], F32)
        ek = self.alloc([2, 128], F32)
        qg = self.alloc([2, 128], BF16)
        kg = self.alloc([2, 128], BF16)
        kend = self.alloc([2, 128], BF16)
        attm = self.alloc([4, 128], BF16)
        kendT = self.alloc([2, 128], BF16)
        glast = self.alloc([2], F32)
        cd = self.alloc([2], F32)
        assert self.acur <= 139 * KB
        self.acur = 56 * KB
        sq = self.alloc([512], BF16)
        rt = self.alloc([512], F32)
        sz = self.alloc([512], F32)
        tmp = self.alloc([512], F32)
        assert self.acur <= 64 * KB
        w = self.pre.pop("gla", None) or self.gla_load(l)
        wvc, wzc, wqc, wkc, wglr, wgu, bg, gg = w
        n = 0
        for j in range(2):
            for (wsrc, dst) in ((wqc, qTc), (wkc, kTc)):
                for b in range(NB):
                    bl = slice(b * 512, (b + 1) * 512)
                    bank = 4 + n % 4
                    for k in range(8):
                        P.mm(ps[:, bank, :], wsrc[:, k, j * 128:(j + 1) * 128], hT[:, k, bl], k == 0, k == 7)
                    P.copy("act" if n % 2 else "dve", dst[:, j, bl], ps[:, bank, :])
                    n += 1
        for t in range(NT):
            tl = slice(t * 128, (t + 1) * 128)
            bank = 4 + n % 4
            for k in range(8):
                P.mm(ps[:, bank, :], hT[:, k, tl], wvc[:, k, :], k == 0, k == 7)
            P.copy("act" if n % 2 else "dve", Vc[:, t, :], ps[:, bank, :])
            n += 1
        for d in range(2):
            for b in range(NB):
                bl = slice(b * 512, (b + 1) * 512)
                bank = 4 + n % 4
                for k in range(8):
                    P.mm(ps[0:32, bank, :], wglr[:, d, k, :], hT[:, k, bl], k == 0, k == 7)
                P.copy("act" if n % 2 else "dve", glrT[0:32, d, bl], ps[0:32, bank, :])
                n += 1

        self.wload(wzc, self.w_in, l, OFF["zc"], OFF["zc"] + 512)

        def bank3(b, a):
            return ps[:, b, 0:a * 128].rearrange("p (a c) -> p a c", a=a)


        P.phase = "gla_scan"
        psC = bank3(1, 2)
        psA2 = [bank3(2, 2), bank3(3, 2)]
        psO2 = [bank3(4, 2), bank3(5, 2)]
        psT = ps[:, 6, :].bitcast(BF16)[:, 0:256].rearrange("p (a c) -> p a c", a=2)
        psU = bank3(7, 2)
        for d in (1, 0):
            Tm = self.TmFb if d == 0 else self.TmBb
            tri = self.triF if d == 0 else self.triB
            last = 127 if d == 0 else 0
            P.memset("pool", S, 0.0)
            P.memset("pool", Sbf, 0.0)
            order = range(NT) if d == 0 else range(NT - 1, -1, -1)
            for ci, t in enumerate(order):
                tl = slice(t * 128, (t + 1) * 128)
                P.mm(ps[:, 0, 0:256], glrT[0:32, d, tl], wgu[0:32, d, :], True, False)
                P.mm(ps[:, 0, 0:256], self.ones[0:1, 0:128], bg[0:1, d, :], False, True)
                P.actv(gsp, ps[:, 0, 0:256], AF.Exp, scale=-1.0)
                P.actv(gsp, gsp, AF.Ln, bias=1.0)
                P.copy("dve", gh, gsp)
                P.tt("dve", gl, gsp, gh, ALU.subtract)
                for j in range(2):
                    P.mm(psC[:, j, :], gh[:, j * 128:(j + 1) * 128], Tm[:, :], True, False)
                    P.mm(psC[:, j, :], gl[:, j * 128:(j + 1) * 128], Tm[:, :], False, True)
                P.copy("dve", glast, psC[:, :, last])
                P.actv(eg.rearrange("p a c -> p (a c)"), ps[:, 1, 0:256], AF.Exp)
                P.actv(eng.rearrange("p a c -> p (a c)"), ps[:, 1, 0:256], AF.Exp, scale=-1.0)
                for j in range(2):
                    P.actv(ek[:, j, :], psC[:, j, :], AF.Exp, scale=-1.0, bias=glast[:, j:j + 1])
                P.actv(cd, glast, AF.Exp)
                P.stt("dve", qg, qTc[:, :, tl], 0.125, eg, ALU.mult, ALU.mult)
                P.tt("dve", kg, kTc[:, :, tl], eng, ALU.mult)
                P.tt("dve", kend, kTc[:, :, tl], ek, ALU.mult)
                tri_b = bass.AP(tri, 0, [[128, 128], [0, 2], [1, 128]])
                obv = ob.rearrange("p (j two) t -> p j two t", two=2)
                oblv = oblk.rearrange("p (j two) t -> p j two t", two=2)
                attv = attm.rearrange("p (j two) c -> p j two c", two=2)
                for par in range(2):
                    r = slice(64 * par, 64 * par + 64)
                    for j in range(2):
                        P.mm(psA2[par][:, j, :], kg[r, j, :], qg[r, j, :], True, True)
                    P.tt("dve", attv[:, :, par, :], psA2[par], tri_b, ALU.mult)
                for par in range(2):
                    r = slice(64 * par, 64 * par + 64)
                    for j in range(2):
                        h = 2 * j + par
                        P.mm(psO2[par][:, j, :], Vc[:, t, h * 128:(h + 1) * 128], attm[:, h, :], True, ci == 0)
                        if ci > 0:
                            P.mm(psO2[par][:, j, :], Sbf[r, j, :], qg[r, j, :], False, True)
                    if d == 1:
                        P.copy("act", obv[:, :, par, tl], psO2[par])
                    else:
                        c4 = t % 4
                        P.tt("dve", oblv[:, :, par, c4 * 128:(c4 + 1) * 128], psO2[par], obv[:, :, par, tl], ALU.add)
                if ci < NT - 1:
                    for j in range(2):
                        P.tr(psT[:, j, :], kend[:, j, :], self.ident[:])
                    P.copy("act", kendT, psT)
                    for h in range(4):
                        j = h // 2
                        r = slice(64 * (h % 2), 64 * (h % 2) + 64)
                        P.mm(psU[r, j, :], kendT[:, j, 64 * (h % 2):64 * (h % 2) + 64], Vc[:, t, h * 128:(h + 1) * 128], True, True)
                    for j in range(2):
                        P.stt("dve", S[:, j, :], S[:, j, :], cd[:, j:j + 1], psU[:, j, :], ALU.mult, ALU.add)
                    P.copy("pool", Sbf, S)
                if d == 0 and t % 4 == 3:
                    b = t // 4
                    bl = slice(b * 512, (b + 1) * 512)
                    for h in range(4):
                        P.actv(sq, oblk[:, h, :], AF.Square)
                        P.mm(ps[:, 0, :], self.ones[:, :], sq, True, True)
                        P.actv(rt, ps[:, 0, :], AF.Ln, bias=EPS, scale=1.0 / 128)
                        P.actv(rt, rt, AF.Exp, scale=-0.5)
                        for k in range(8):
                            P.mm(ps[:, 1, :], wzc[:, k, h * 128:(h + 1) * 128], hT[:, k, bl], k == 0, k == 7)
                        P.actv(sz, ps[:, 1, :], AF.Silu)
                        P.stt("dve", tmp, oblk[:, h, :], gg[:, h:h + 1], rt, ALU.mult, ALU.mult)
                        P.tt("pool", onT[:, h, bl], tmp, sz, ALU.mult)


    def ssd_load(self, l):
        P = self.P
        self.acur = 64 * KB
        wz = self.alloc([8, D], BF16)
        wdt = self.alloc([8, 32], BF16)
        anb = self.alloc([32], F32)
        dtb = self.alloc([32], F32)
        dsk = self.alloc([16], F32)
        gnb = self.alloc([D], F32)
        cp = self.alloc([12, 6], F32)
        self.ssd_end = self.acur
        P.dma(cp, self.conv_p[l])
        P.dma(anb, bass.AP(self.a_log.tensor, l * 32, [[0, 128], [1, 32]]))
        P.dma(dtb, bass.AP(self.dt_bias.tensor, l * 32, [[0, 128], [1, 32]]))
        P.dma(dsk, bass.AP(self.d_skip.tensor, l * 16, [[0, 128], [1, 16]]))
        P.dma(gnb, bass.AP(self.ssd_norm_g.tensor, l * D, [[0, 128], [1, D]]))
        self.wload(wdt, self.w_in, l, OFF["dt"], OFF["dt"] + 32, engs=BULK)
        self.wload(wz, self.w_in, l, OFF["za"], OFF["za"] + D, engs=BULK)
        return (wz, wdt, cp, anb, dtb, dsk, gnb)

    def phase_ssd(self, l):
        P = self.P
        self.P.phase = "ssd_conv"
        ps = self.ps
        hT = self.hT
        onT = self.onT
        xs_tok = self.view(0, [16, D], BF16)
        w = self.pre.pop("ssd", None) or self.ssd_load(l)
        wz, wdt, cp, anb, dtb, dsk, gnb = w
        self.acur = self.ssd_end
        BT = self.alloc([2, L], BF16)
        CT = self.alloc([2, L], BF16)
        Btok = self.alloc([16, 256], BF16)
        dt_all = self.alloc([16, 32], F32)
        dta_hi = self.alloc([16, 32], BF16)
        dta_lo = self.alloc([16, 32], BF16)
        base = self.acur
        xb = self.alloc([L + 4], F32)
        acc = self.alloc([L], F32)
        cv = self.alloc([L], BF16)
        wx = self.alloc([2, 8, 128], BF16)
        P.actv(anb, anb, AF.Exp)
        P.ts("dve", anb, anb, -1.0, None, ALU.mult)
        P.memset("pool", xb[:, 0:2], 0.0)
        P.memset("pool", xb[:, L + 2:L + 4], 0.0)
        for c in range(12):
            i = c % 2
            self.wload(wx[:, i], self.w_in, l, OFF["xbc"] + c * 128, OFF["xbc"] + (c + 1) * 128)
            for b in range(NB):
                bank = 4 + (c * 4 + b) % 4
                for k in range(8):
                    P.mm(ps[:, bank, :], wx[:, i, k, :], hT[:, k, b * 512:(b + 1) * 512], k == 0, k == 7)
                P.copy("act", xb[:, 2 + b * 512:2 + (b + 1) * 512], ps[:, bank, :])
            eng = "dve"
            P.ts(eng, acc, xb[:, 0:L], cp[:, c, 0:1], cp[:, c, 5:6], ALU.mult, ALU.add)
            for j in range(1, 5):
                P.stt(eng, acc, xb[:, j:j + L], cp[:, c, j:j + 1], acc, ALU.mult, ALU.add)
            if c < 8:
                P.actv(cv, acc, AF.Silu)
                dst_tok = lambda t, c=c: xs_tok[:, t, c * 128:(c + 1) * 128]
                src = cv
            elif c < 10:
                P.actv(BT[:, c - 8, :], acc, AF.Silu)
                dst_tok = lambda t, c=c: Btok[:, t, (c - 8) * 128:(c - 7) * 128]
                src = BT[:, c - 8, :]
            else:
                P.actv(CT[:, c - 10, :], acc, AF.Silu)
                src = None
            if src is not None:
                for half in range(2):
                    pb = ps[:, 2 + half, :].bitcast(BF16)
                    for tt_ in range(8):
                        t = half * 8 + tt_
                        P.tr(pb[:, tt_ * 128:(tt_ + 1) * 128], src[:, t * 128:(t + 1) * 128], self.ident[:])
                    if c < 8:
                        P.copy("act" if half else "dve", xs_tok[:, half * 8:(half + 1) * 8, c * 128:(c + 1) * 128],
                               pb.rearrange("p (t c) -> p t c", t=8))
                    else:
                        P.copy("act" if half else "dve", Btok[:, half * 8:(half + 1) * 8, (c - 8) * 128:(c - 7) * 128],
                               pb.rearrange("p (t c) -> p t c", t=8))
        P.phase = "ssd_scan"
        for t in range(NT):
            for k in range(8):
                P.mm(ps[:, 7, t * 32:(t + 1) * 32], hT[:, k, t * 128:(t + 1) * 128], wdt[:, k, :], k == 0, k == 7)
        dtb_b = bass.AP(dtb.tensor, dtb.offset, [list(dtb.ap[0]), [0, 16], [1, 32]])
        anb_b = bass.AP(anb.tensor, anb.offset, [list(anb.ap[0]), [0, 16], [1, 32]])
        P.tt("dve", dt_all, ps[:, 7, :].rearrange("p (t c) -> p t c", t=16), dtb_b, ALU.add)
        P.actv(dt_all, dt_all, AF.Exp)
        P.actv(dt_all, dt_all, AF.Ln, bias=1.0)
        self.acur = base
        self.allow_over = True
        Ahi = self.alloc([1, 8, 128], BF16)
        Alo = self.alloc([1, 8, 128], BF16)
        E = self.alloc([1, 8, 128], BF16)
        W = self.alloc([1, 8, 128], BF16)
        Gm = self.alloc([2, 128], BF16)
        dlf = self.alloc([16, 32], F32)
        xd = self.alloc([D], BF16)
        xdd = self.alloc([D], BF16)
        S = self.alloc([D], F32)
        Sbf = self.alloc([D], BF16)
        ytmp = self.alloc([D], F32)
        y2 = self.alloc([2, D], F32)
        ybt = self.alloc([D], F32)
        eac = self.alloc([16], F32)
        cdb = self.alloc([16], F32)
        sz = self.alloc([D], F32)
        dta = sz[:, 0:512].rearrange("p (t c) -> p t c", t=16)
        junk = ytmp
        hb = xdd
        ss = self.alloc([4], F32)
        P.tt("dve", dta, dt_all, anb_b, ALU.mult)
        P.copy("dve", dta_hi, dta)
        P.tt("dve", dta_lo, dta, dta_hi, ALU.subtract)
        P.tt("dve", dlf, dta, dta_hi, ALU.subtract)

        def hv(ap2):
            return ap2.rearrange("p (h q) -> p h q", h=16)

        def bc_h(ap_col16, n):
            return bass.AP(ap_col16.tensor, ap_col16.offset, [list(ap_col16.ap[0]), [ap_col16.ap[1][0], 16], [0, n]])

        def bc_h2(ap16, g):
            return bass.AP(ap16.tensor, ap16.offset + g * 8, [list(ap16.ap[0]), [ap16.ap[1][0], 8], [0, 128]])

        def fin_stages(ci, t):
            y = y2[:, ci % 2, :]
            tl = slice(t * 128, (t + 1) * 128)
            pz = ps[:, 0:2, :].rearrange("p b c -> p (b c)")

            def f1a():
                P.tt("dve", hv(ytmp), hv(xs_tok[:, t, :]), bc_h(dsk, 64), ALU.mult)
                P.tt("dve", y, y, ytmp, ALU.add)

            def f1b():
                for hf in range(2):
                    for k in range(8):
                        P.mm(ps[:, hf, :], hT[:, k, tl], wz[:, k, hf * 512:(hf + 1) * 512], k == 0, k == 7)

            def f2():
                P.actv(sz, pz, AF.Silu)

            def f3():
                P.tt("dve", y, y, sz, ALU.mult)

            def f4():
                P.actv(junk, y, AF.Square, accum_out=ss[:, 0:1])
                P.actv(ss[:, 1:2], ss[:, 0:1], AF.Ln, bias=EPS, scale=1.0 / D)
                P.actv(ss[:, 2:3], ss[:, 1:2], AF.Exp, scale=-0.5)

            def f5():
                P.stt("dve", hb, y, ss[:, 2:3], gnb, ALU.mult, ALU.mult)
                pb = ps[:, 7, :].bitcast(BF16)
                for k in range(8):
                    P.tr(pb[:, k * 128:(k + 1) * 128], hb[:, k * 128:(k + 1) * 128], self.ident[:])
                P.copy("act", onT[:, :, tl], pb.rearrange("p (k c) -> p k c", k=8))

            return [f1a, f1b, f2, f3, f4, f5]

        def main_stages(d, ci, t):
            Lm = self.LF if d == 0 else self.LB
            Tm = self.TF if d == 0 else self.TB
            tri = self.triF if d == 0 else self.triB
            last = 127 if d == 0 else 0
            y = y2[:, ci % 2, :]
            tl = slice(t * 128, (t + 1) * 128)
            dh = dta_hi[:, t, d * 16:(d + 1) * 16]
            dl = dta_lo[:, t, d * 16:(d + 1) * 16]

            def m1():
                if d == 0:
                    P.dma(ybt, self.yb_d[tl, :])
                for g in range(2):
                    P.mm(ps[:, 4, g * 128:(g + 1) * 128], BT[:, g, tl], CT[:, g, tl], True, True)
                P.mm(ps[:, 4, 256:272], Tm[:, :], dh, True, False)
                P.mm(ps[:, 4, 256:272], Tm[:, :], dl, False, True)
                P.mm(ps[:, 4, 272:288], self.ones[:, :], dh, True, False)
                P.mm(ps[:, 4, 272:288], self.ones[:, :], dl, False, True)
                tri_b = bass.AP(tri, 0, [[128, 128], [0, 2], [1, 128]])
                P.tt("dve", Gm, ps[:, 4, 0:256].rearrange("p (g c) -> p g c", g=2), tri_b, ALU.mult)
                P.actv(eac, ps[:, 4, 256:272], AF.Exp)
                P.actv(cdb, ps[:, 4, 272:288], AF.Exp)
                P.tt("dve", hv(xd), hv(xs_tok[:, t, :]), bc_h(dt_all[:, t, d * 16:(d + 1) * 16], 64), ALU.mult)

            def mg(g):
                def f():
                    Lb = bass.AP(Lm, 0, [[128, 128], [0, 8], [1, 128]])
                    P.tt("pool", Ahi[:, 0], Lb, bc_h2(dh, g), ALU.mult)
                    for e in range(8):
                        P.actv(Alo[:, 0, e, :], Lm[:, :], AF.Copy, scale=dlf[:, t, d * 16 + g * 8 + e:d * 16 + g * 8 + e + 1])
                    for e in range(8):
                        out = ps[:, 2 * g + e // 4, (e % 4) * 128:(e % 4 + 1) * 128]
                        P.mm(out, Ahi[:, 0, e, :], Tm[:, :], True, False)
                        P.mm(out, Alo[:, 0, e, :], Tm[:, :], False, True)
                    P.actv(E[:, 0], ps[:, 2 * g:2 * g + 2, :].rearrange("p b (e c) -> p (b e) c", e=4), AF.Exp)
                    Gb = bass.AP(Gm.tensor, Gm.offset + g * 128, [list(Gm.ap[0]), [0, 8], [1, 128]])
                    P.tt("dve", W[:, 0], E[:, 0], Gb, ALU.mult)
                    Elast = bass.AP(E.tensor, E.offset + last, [list(E.ap[0]), [128, 8], [0, 64]])
                    P.tt("dve", hv(xdd)[:, g * 8:(g + 1) * 8, :], hv(xd)[:, g * 8:(g + 1) * 8, :], Elast, ALU.mult)
                    for e in range(8):
                        h = g * 8 + e
                        P.mm(ps[:, 5 + g, e * 64:(e + 1) * 64], W[:, 0, e, :], xd[:, h * 64:(h + 1) * 64], True, True)
                return f

            def m4():
                if ci > 0:
                    for g in range(2):
                        P.mm(ps[:, g, :], CT[:, g, tl], Sbf[:, g * 512:(g + 1) * 512], True, True)
                    P.tt("dve", hv(ytmp), ps[:, 0:2, :].rearrange("p b (e q) -> p (b e) q", e=8), bc_h(eac, 64), ALU.mult)
                    P.tt("dve", y, ytmp, ps[:, 5:7, :].rearrange("p b c -> p (b c)"), ALU.add)
                else:
                    P.copy("act", y, ps[:, 5:7, :].rearrange("p b c -> p (b c)"))

            def m5():
                if ci < NT - 1:
                    for g in range(2):
                        P.mm(ps[:, 2 + g, :], Btok[:, t, g * 128:(g + 1) * 128], xdd[:, g * 512:(g + 1) * 512], True, True)
                    if ci > 0:
                        P.tt("pool", hv(S), hv(S), bc_h(cdb, 64), ALU.mult)
                        P.tt("dve", S, S, ps[:, 2:4, :].rearrange("p b c -> p (b c)"), ALU.add)
                    else:
                        P.copy("act", S, ps[:, 2:4, :].rearrange("p b c -> p (b c)"))
                    P.copy("act", Sbf, S)

            def m6():
                if d == 1:
                    P.dma(self.yb_d[tl, :], y)
                else:
                    P.tt("dve", y, y, ybt, ALU.add)

            return [m1, mg(0), mg(1), m4, m5, m6]

        for d in (1, 0):
            order = range(NT) if d == 0 else range(NT - 1, -1, -1)
            pending = None
            for ci, t in enumerate(order):
                ms = main_stages(d, ci, t)
                fs = fin_stages(*pending) if pending is not None else []
                for i in range(6):
                    ms[i]()
                    if i < len(fs):
                        fs[i]()
                if d == 0:
                    pending = (ci, t)
            if pending is not None:
                for f in fin_stages(*pending):
                    f()
        self.allow_over = False

    def merge(self, l, bi, nk, wbr_src, first, pre=None):
        P = self.P
        self.P.phase = "merge"
        ps = self.ps
        hT = self.hT
        onT = self.onT
        self.acur = 64 * KB
        wbr = self.alloc([nk, D], BF16)
        wg = self.alloc([2, 8, 128], BF16)
        sg = self.alloc([2, 512], F32)
        tm = self.alloc([2, 512], F32)
        self.wload(wbr, wbr_src, l, 0, D, engs=BULK)
        self.wload(wg[:, 0], self.w_in, l, bi * D, bi * D + 128)
        if pre is not None:
            save = self.acur
            pre()
            self.acur = save
        n = 0
        for m in range(8):
            if m > 0:
                self.wload(wg[:, m % 2], self.w_in, l, bi * D + m * 128, bi * D + (m + 1) * 128)
            for b in range(NB):
                bl = slice(b * 512, (b + 1) * 512)
                yb = 4 + n % 2
                gb = 6 + n % 2
                for k in range(nk):
                    P.mm(ps[:, yb, :], wbr[:, k, m * 128:(m + 1) * 128], onT[:, k, bl], k == 0, k == nk - 1)
                for k in range(8):
                    P.mm(ps[:, gb, :], wg[:, m % 2, k, :], hT[:, k, bl], k == 0, k == 7)
                P.actv(sg[:, n % 2, :], ps[:, gb, :], AF.Sigmoid)
                if first:
                    P.tt("dve", self.mixT[:, m, bl], ps[:, yb, :], sg[:, n % 2, :], ALU.mult)
                else:
                    P.tt("dve", tm[:, n % 2, :], ps[:, yb, :], sg[:, n % 2, :], ALU.mult)
                    P.tt("pool", self.mixT[:, m, bl], self.mixT[:, m, bl], tm[:, n % 2, :], ALU.add)
                n += 1

    def outproj_load(self, l, last):
        self.acur = 96 * KB
        wo = self.alloc([8, D], BF16)
        gB = self.alloc([D], F32)
        self.wload(wo, self.w_out, l, 0, D, engs=BULK)
        grow = 2 if last else l + 1
        self.P.dma(gB, bass.AP(self.norm_g.tensor, grow * D, [[0, 128], [1, D]]))
        return (wo, gB)

    def outproj(self, s, l, last):
        P = self.P
        self.P.phase = "outproj"
        ps = self.ps
        w = self.pre.pop("outproj", None) or self.outproj_load(l, last)
        wo, gB = w
        self.acur = 116 * KB
        xt = self.alloc([2, D], F32)
        xn = self.alloc([2, D], F32)
        junk = self.alloc([D], BF16)
        ss = self.alloc([48], F32)
        hb = self.alloc([2, D], BF16)
        yo = self.alloc([2, D], F32)
        if not last:
            self.pre["ssd"] = self.ssd_load(self.layers[self.layers.index(l) + 1])
        xsrc = self.x if l == 0 else self.xres
        def finish(t):
            i = t % 2
            tl = slice(t * 128, (t + 1) * 128)
            if not last:
                P.dma(self.xres[s, tl, :], xn[:, i, :])
                self.norm_to_hT(xn[:, i, :], t, gB, (junk, ss, hb))
            else:
                P.actv(junk[:, :], xn[:, i, :], AF.Square, accum_out=ss[:, t:t + 1])
                P.actv(ss[:, 16 + t:17 + t], ss[:, t:t + 1], AF.Ln, bias=EPS, scale=1.0 / D)
                P.actv(ss[:, 32 + t:33 + t], ss[:, 16 + t:17 + t], AF.Exp, scale=-0.5)
                P.stt("dve", yo[:, i, :], xn[:, i, :], ss[:, 32 + t:33 + t], gB, ALU.mult, ALU.mult)
                P.dma(self.out[s, tl, :], yo[:, i, :])

        P.dma(xt[:, 0, :], xsrc[s, 0:128, :])
        for t in range(NT):
            i = t % 2
            tl = slice(t * 128, (t + 1) * 128)
            if t + 1 < NT:
                P.dma(xt[:, (t + 1) % 2, :], xsrc[s, (t + 1) * 128:(t + 2) * 128, :])
            for hf in range(2):
                bank = 4 + hf
                for k in range(8):
                    P.mm(ps[:, bank, :], self.mixT[:, k, tl], wo[:, k, hf * 512:(hf + 1) * 512], k == 0, k == 7)
                P.tt("dve", xn[:, i, hf * 512:(hf + 1) * 512], ps[:, bank, :], xt[:, i, hf * 512:(hf + 1) * 512], ALU.add)
            if t >= 1:
                finish(t - 1)
        finish(NT - 1)

    def build(self):
        self.setup()
        for s in range(self.nseq):
            nl = len(self.layers)
            for li, l in enumerate(self.layers):
                if li == 0:
                    self.phase_a(s, l)
                last = (li == nl - 1)
                order = [b for b in "abcd" if b in self.branches]
                phase = {"a": self.phase_ssd, "b": self.phase_mla, "c": self.phase_gla, "d": self.phase_gqa}
                loader = {"b": ("mla", self.mla_load), "c": ("gla", self.gla_load), "d": ("gqa", self.gqa_load)}
                mrg = {"a": (0, 8, self.w_br_a), "b": (1, 4, self.w_br_b), "c": (2, 4, self.w_br_c), "d": (3, 4, self.w_br_d)}
                for bi_, br in enumerate(order):
                    phase[br](l)
                    if bi_ + 1 < len(order):
                        key, fn = loader[order[bi_ + 1]]
                        pre = (lambda key=key, fn=fn: self.pre.__setitem__(key, fn(l)))
                    else:
                        pre = (lambda: self.pre.__setitem__("outproj", self.outproj_load(l, last)))
                    i_, nk_, wsrc_ = mrg[br]
                    self.merge(l, i_, nk_, wsrc_, bi_ == 0, pre=pre)
                self.outproj(s, l, last=(li == nl - 1))
        cnt = self.P.emit(self.es)
        return cnt


def _prep_inputs(inp):
    f = lambda a: np.ascontiguousarray(np.asarray(a, dtype=np.float32))
    w_in = f(inp["w_in"])
    p32, _ = _rope_perm_sign(32)
    p64, _ = _rope_perm_sign(64)
    kr = w_in[:, :, OFF["krope"]:OFF["krope"] + 32][:, :, p32]
    qd = w_in[:, :, OFF["qd"]:OFF["qd"] + 512].reshape(2, D, 8, 64)[:, :, :, p64].reshape(2, D, 512)
    kd = w_in[:, :, OFF["kd"]:OFF["kd"] + 128].reshape(2, D, 2, 64)[:, :, :, p64].reshape(2, D, 128)
    w_perm = np.ascontiguousarray(np.concatenate([kr, qd, kd], axis=2))
    qg = f(inp["q_norm_g"])
    kg = f(inp["k_norm_g"])
    qk_g = np.ascontiguousarray(np.stack([qg, qg[:, p64], kg, kg[:, p64]], axis=2))
    cos64, sin64 = _rope_tables(64)
    cos32, sin32 = _rope_tables(32)
    rope_cos = np.ones((128, L), np.float32)
    rope_sin = np.zeros((128, L), np.float32)
    rope_cos[0:64] = cos64
    rope_sin[0:64] = sin64
    rope_cos[64:96] = cos32
    rope_sin[64:96] = sin32
    wqb = f(inp["w_q_b"])
    w_qb_perm = np.ascontiguousarray(wqb.reshape(2, 384, 8, 96)[:, :, :, 64:96][:, :, :, p32].reshape(2, 384, 256))
    qlg = f(inp["q_lat_norm_g"]).reshape(2, 3, 128).transpose(0, 2, 1)
    kvg = f(inp["kv_lat_norm_g"]).reshape(2, 2, 128).transpose(0, 2, 1)
    lat_g = np.ascontiguousarray(np.concatenate([qlg, kvg], axis=2))
    gla_g = np.ascontiguousarray(f(inp["gla_norm_g"]).reshape(2, 4, 128).transpose(0, 2, 1))
    cw = f(inp["conv_w"]).reshape(2, 5, 12, 128).transpose(0, 3, 2, 1)
    cb = f(inp["conv_b"]).reshape(2, 12, 128).transpose(0, 2, 1)[..., None]
    conv_p = np.ascontiguousarray(np.concatenate([cw, cb], axis=3))
    shared = dict(
        conv_p=conv_p, a_log=f(inp["a_log"]).reshape(2, 32), dt_bias=f(inp["dt_bias"]).reshape(2, 32),
        d_skip=f(inp["d_skip"]), ssd_norm_g=f(inp["ssd_norm_g"]),
        w_gate_up=f(inp["w_gate_up"]), b_gate=f(inp["b_gate"]), gla_g=gla_g,
        w_in=w_in, w_perm=w_perm, w_q_b=wqb, w_qb_perm=w_qb_perm, w_kv_b=f(inp["w_kv_b"]), lat_g=lat_g,
        norm_g=np.ascontiguousarray(np.concatenate([f(inp["norm_g"]), f(inp["final_g"])[None, :]], axis=0)),
        qk_g=qk_g,
        w_br_a=f(inp["w_br_a"]), w_br_b=f(inp["w_br_b"]), w_br_c=f(inp["w_br_c"]), w_br_d=f(inp["w_br_d"]),
        w_out=f(inp["w_out"]), rope_cos=rope_cos, rope_sin=rope_sin,
    )
    return shared


_CACHE = {}


def kernel(**inputs):
    x = np.ascontiguousarray(np.asarray(inputs["x"], dtype=np.float32))
    n_cores = 8
    nseq = x.shape[0] // n_cores
    shared = _prep_inputs(inputs)
    mk = MK(nseq=nseq, layers=(0, 1), branches="abcd")
    mk.build()
    in_maps = []
    for c in range(n_cores):
        m = dict(shared)
        m["x"] = np.ascontiguousarray(x[c * nseq:(c + 1) * nseq])
        in_maps.append(m)
    res = run_bass_kernel_spmd(mk.nc, in_maps, core_ids=list(range(n_cores)))
    out = np.concatenate([np.asarray(r["out"]) for r in res.results], axis=0)
    return out.astype(np.float32)
```

```python
import sys
import numpy as np
from contextlib import ExitStack
import concourse.bass as bass
import concourse.mybir as mybir
from concourse.bass_utils import run_bass_kernel_spmd

F32 = mybir.dt.float32
BF16 = mybir.dt.bfloat16
ALU = mybir.AluOpType
AF = mybir.ActivationFunctionType
AX = mybir.AxisListType

_ESZ = {F32: 4, BF16: 2}


def _esz(dt):
    return _ESZ.get(dt, 4)


def region(ap):
    a = ap.ap
    off = int(ap.offset)
    es = _esz(ap.dtype)
    name = ap.tensor.name
    sp = str(ap.space)
    if sp == "DRAM":
        ext = sum((c - 1) * abs(s) for s, c in a) + 1
        return (name, 0, 1, off * es, (off + ext) * es)
    pstep, pcnt = a[0]
    if pstep == 0:
        pstep = 1 << 40
    p0 = off // pstep
    f0 = off % pstep
    ext = sum((c - 1) * abs(s) for s, c in a[1:]) + 1
    if sp == "PSUM":
        b0 = (f0 * es) // 2048 * 2048
        b1 = ((f0 + ext) * es + 2047) // 2048 * 2048
        return (name, p0 // 32 * 32, (p0 + pcnt + 31) // 32 * 32, b0, b1)
    return (name, p0, p0 + pcnt, f0 * es, (f0 + ext) * es)


def _ovl(r, s):
    return r[1] < s[2] and s[1] < r[2] and r[3] < s[4] and s[3] < r[4]


def _covers(r, s):
    return r[1] <= s[1] and r[2] >= s[2] and r[3] <= s[3] and r[4] >= s[4]


class Op:
    __slots__ = ("eng", "fn", "deps", "signal", "dma", "eidx", "rank", "waits", "line", "phase")

    def __init__(self, eng, fn):
        self.eng = eng
        self.fn = fn
        self.deps = set()
        self.signal = False
        self.dma = None
        self.waits = []


ENGS = ("pe", "act", "dve", "pool", "sp")
_WRAPPERS = ("mm", "tr", "actv", "tt", "ts", "stt", "copy", "memset", "recip", "dma")
NDMASEM = 8
EPOCH = 20000


class Prog:
    def __init__(self, nc):
        self.nc = nc
        self.ops = []
        self.acc = {}
        self.ndma = 0
        self.phase = ""

    def add(self, eng, fn, reads, writes, dma=False):
        op = Op(eng, fn)
        op.phase = self.phase
        try:
            fr = sys._getframe(1)
            if fr.f_code.co_filename == __file__ and fr.f_code.co_name in _WRAPPERS:
                fr = fr.f_back
            op.line = fr.f_lineno
        except Exception:
            op.line = 0
        rec_eng = "dma" if dma else eng
        idx = len(self.ops)
        ops = self.ops
        for ap in reads:
            r = region(ap)
            lst = self.acc.setdefault(r[0], [])
            done = False
            is_psum = (r[0] == "ps")
            for rec in lst:
                if rec[2]:
                    if _ovl(rec[0], r):
                        op.deps.add(rec[1])
                elif (not done) and (not dma) and rec[3] == rec_eng and rec[0] == r:
                    rec[1] = idx
                    done = True
                elif is_psum and rec[3] != rec_eng and _ovl(rec[0], r):
                    op.deps.add(rec[1])
            if not done:
                lst.append([r, idx, False, rec_eng])
        for ap in writes:
            r = region(ap)
            lst = self.acc.setdefault(r[0], [])
            keep = []
            for rec in lst:
                if _ovl(rec[0], r):
                    if rec[1] != idx:
                        op.deps.add(rec[1])
                    if _covers(r, rec[0]) and rec[1] != idx:
                        continue
                keep.append(rec)
            keep.append([r, idx, True, rec_eng])
            self.acc[r[0]] = keep
        if eng == "pe":
            op.deps = {d for d in op.deps if ops[d].eng != "pe"}
        if dma:
            op.dma = self.ndma
            self.ndma += 1
        ops.append(op)
        return op

    def mm(self, out, lhsT, rhs, start=True, stop=True):
        self.add("pe", lambda e: e.matmul(out, lhsT, rhs, start=start, stop=stop),
                 [lhsT, rhs], [out])

    def tr(self, out, in_, ident):
        self.add("pe", lambda e: e.transpose(out, in_, ident), [in_, ident], [out])

    def actv(self, out, in_, func, bias=None, scale=None, accum_out=None):
        kw = {}
        rd = [in_]
        wr = [out]
        if bias is not None:
            kw["bias"] = bias
            if not isinstance(bias, (int, float)):
                rd.append(bias)
        if scale is not None:
            kw["scale"] = scale
            if not isinstance(scale, (int, float)):
                rd.append(scale)
        if accum_out is not None:
            kw["accum_out"] = accum_out
            wr.append(accum_out)
        self.add("act", lambda e: e.activation(out, in_, func, **kw), rd, wr)

    def _veng(self, eng):
        return eng

    def tt(self, eng, out, in0, in1, op):
        self.add(eng, lambda e: e.tensor_tensor(out, in0, in1, op), [in0, in1], [out])

    def ts(self, eng, out, in0, s1, s2, op0, op1=None, accum_out=None):
        rd = [in0]
        if not isinstance(s1, (int, float)):
            rd.append(s1)
        if s2 is not None and not isinstance(s2, (int, float)):
            rd.append(s2)
        wr = [out]
        kw = {}
        if accum_out is not None:
            kw["accum_out"] = accum_out
            wr.append(accum_out)
        if op1 is None:
            self.add(eng, lambda e: e.tensor_scalar(out, in0, s1, None, op0, **kw), rd, wr)
        else:
            self.add(eng, lambda e: e.tensor_scalar(out, in0, s1, s2, op0, op1, **kw), rd, wr)

    def stt(self, eng, out, in0, scalar, in1, op0, op1):
        rd = [in0, in1]
        if not isinstance(scalar, (int, float)):
            rd.append(scalar)
        self.add(eng, lambda e: e.scalar_tensor_tensor(out, in0, scalar, in1, op0, op1), rd, [out])

    def copy(self, eng, out, in_):
        if eng == "act":
            self.add("act", lambda e: e.activation(out, in_, AF.Copy), [in_], [out])
        else:
            self.add(eng, lambda e: e.tensor_copy(out, in_), [in_], [out])

    def memset(self, eng, out, val):
        self.add(eng, lambda e: e.memset(out, val), [], [out])

    def recip(self, out, in_):
        self.add("dve", lambda e: e.reciprocal(out, in_), [in_], [out])

    def dma(self, out, in_, eng="sp", **kw):
        self.add(eng, lambda e: e.dma_start(out, in_, **kw), [in_], [out], dma=True)

    def emit(self, es, final_wait_all=True):
        nc = self.nc
        ops = self.ops
        cnt = {e: 0 for e in ENGS}
        for op in ops:
            op.eidx = cnt[op.eng]
            cnt[op.eng] += 1
        for i, op in enumerate(ops):
            for d in op.deps:
                dop = ops[d]
                if dop.dma is None:
                    if dop.eng == op.eng and op.eng != "sp":
                        pass
                    dop.signal = True
        last = {}
        for i, op in enumerate(ops):
            if op.dma is None:
                last[op.eng] = i
        for e, i in last.items():
            ops[i].signal = True
        rk = {e: 0 for e in ENGS}
        for op in ops:
            if op.dma is None and op.signal:
                rk[op.eng] += 1
                op.rank = rk[op.eng]
        nep = {e: (rk[e] + EPOCH - 1) // EPOCH + 1 for e in ENGS}
        sems = {}
        for e in ENGS:
            if e == "sp":
                continue
            sems[e] = [es.enter_context(nc.semaphore(f"s_{e}_{k}")) for k in range(nep[e])]
        dsem = [es.enter_context(nc.semaphore(f"s_dma_{k}")) for k in range(NDMASEM)]

        import os as _os
        simmode = bool(_os.environ.get("SIMMODE"))
        dinfo = {}
        m = 0
        for op in ops:
            if op.dma is None:
                continue
            if simmode and op.eng == "pool":
                sem = es.enter_context(nc.semaphore(f"s_u_{op.dma}"))
                dinfo[op.dma] = (("udma", op.dma), sem, 16)
            else:
                k = m % NDMASEM
                dinfo[op.dma] = (("dma", k), dsem[k], 16 * (m // NDMASEM + 1))
                m += 1

        def target(dop):
            if dop.dma is not None:
                return dinfo[dop.dma]
            r = dop.rank - 1
            return ((dop.eng, r // EPOCH), sems[dop.eng][r // EPOCH], r % EPOCH + 1)

        waited = {e: {} for e in ENGS}
        for i, op in enumerate(ops):
            w = waited[op.eng]
            need = {}
            for d in op.deps:
                key, sem, val = target(ops[d])
                if w.get(key, 0) >= val:
                    continue
                if key not in need or need[key][1] < val:
                    need[key] = (sem, val)
            if op.dma is not None:
                key, sem, val = dinfo[op.dma]
                val -= 16
                if val > 0 and w.get(key, 0) < val and (key not in need or need[key][1] < val):
                    need[key] = (sem, val)
            for key, (sem, val) in need.items():
                w[key] = val
                op.waits.append((sem, val))
        final_waits = []
        for e, i in last.items():
            key, sem, val = target(ops[i])
            final_waits.append((sem, val))
        fin = {}
        for key, sem, val in dinfo.values():
            if key not in fin or fin[key][1] < val:
                fin[key] = (sem, val)
        final_waits.extend(fin.values())

        byeng = {e: [op for op in ops if op.eng == e] for e in ENGS}

        annotate = bool(_os.environ.get("SIMMODE"))

        def run(e, name):
            for op in byeng[name]:
                for sem, val in op.waits:
                    e.wait_ge(sem, val)
                ins = op.fn(e)
                if annotate:
                    ins.annotate(f"L{op.line}")
                if op.dma is not None:
                    ins.then_inc(dinfo[op.dma][1], 16)
                elif op.signal:
                    r = op.rank - 1
                    ins.then_inc(sems[name][r // EPOCH], 1)
            if name == "sp":
                for sem, val in final_waits:
                    e.wait_ge(sem, val)

        with nc.Block() as block:
            @block.tensor
            def _(e):
                run(e, "pe")

            @block.scalar
            def _(e):
                run(e, "act")

            @block.vector
            def _(e):
                run(e, "dve")

            @block.gpsimd
            def _(e):
                run(e, "pool")

            @block.sync
            def _(e):
                run(e, "sp")
        return cnt


L = 2048
D = 1024
NT = 16
NB = 4
EPS = 1e-6
OFF = dict(g=0, za=4096, xbc=5120, dt=6656, zb=6688, qlat=7200, kvlat=7584, krope=7840,
           zc=7872, qc=8384, kc=8640, vc=8896, glr=9408, zd=9440, qd=9952, kd=10464, vd=10592)
N_IN = 10720
KB = 1024
BULK = ("dve", "dve", "pool")


def _rope_perm_sign(d):
    m = d // 2
    hm = m // 2
    perm = np.zeros(d, np.int64)
    sign = np.zeros(d, np.float32)
    for j in range(d):
        jj = j % m
        if jj < hm:
            perm[j] = j + hm
            sign[j] = -1.0
        else:
            perm[j] = j - hm
            sign[j] = 1.0
    return perm, sign


def _rope_tables(d):
    rows = L // 64
    row = np.repeat(np.arange(rows), 64).astype(np.float32)
    col = np.tile(np.arange(64), rows).astype(np.float32)
    m = d // 2
    inv = (np.float32(10000.0) ** (-np.arange(0, m, 2, dtype=np.float32) / np.float32(m))).astype(np.float32)
    ang_r = row[:, None] * inv
    ang_c = col[:, None] * inv
    ang = np.concatenate([ang_r, ang_r, ang_c, ang_c], axis=-1).astype(np.float32)
    _, sign = _rope_perm_sign(d)
    cos = np.cos(ang).astype(np.float32).T
    sin = (np.sin(ang).astype(np.float32) * sign[None, :]).T
    return np.ascontiguousarray(cos), np.ascontiguousarray(sin)


class MK:
    def __init__(self, nseq=2, layers=(0, 1), branches="abcd", debug=False):
        self.nseq = nseq
        self.layers = layers
        self.branches = branches
        self.debug = debug
        nc = self.nc = bass.Bass("TRN2", target_bir_lowering=False)
        es = self.es = ExitStack()
        self.P = Prog(nc)
        self.dbg_outs = {}
        di = lambda n, s: nc.dram_tensor(n, s, F32, kind="ExternalInput").ap()
        self.x = di("x", [nseq, L, D])
        self.out = nc.dram_tensor("out", [nseq, L, D], F32, kind="ExternalOutput").ap()
        self.xres = nc.dram_tensor("xres", [nseq, L, D], F32).ap()
        self.w_in = di("w_in", [2, D, N_IN])
        self.w_perm = di("w_perm", [2, D, 672])
        self.norm_g = di("norm_g", [3, D])
        self.qk_g = di("qk_g", [2, 64, 4])
        self.w_q_b = di("w_q_b", [2, 384, 768])
        self.w_qb_perm = di("w_qb_perm", [2, 384, 256])
        self.w_kv_b = di("w_kv_b", [2, 256, 1024])
        self.lat_g = di("lat_g", [2, 128, 5])
        self.w_gate_up = di("w_gate_up", [2, 2, 16, 256])
        self.b_gate = di("b_gate", [2, 2, 256])
        self.gla_g = di("gla_g", [2, 128, 4])
        self.conv_p = di("conv_p", [2, 128, 12, 6])
        self.a_log = di("a_log", [2, 32])
        self.dt_bias = di("dt_bias", [2, 32])
        self.d_skip = di("d_skip", [2, 16])
        self.ssd_norm_g = di("ssd_norm_g", [2, D])
        self.yb_d = nc.dram_tensor("yb_d", [L, D], F32).ap()
        self.w_br_a = di("w_br_a", [2, 1024, D])
        self.w_br_b = di("w_br_b", [2, 512, D])
        self.w_br_c = di("w_br_c", [2, 512, D])
        self.w_br_d = di("w_br_d", [2, 512, D])
        self.w_out = di("w_out", [2, D, D])
        self.rope_cos = di("rope_cos", [128, L])
        self.rope_sin = di("rope_sin", [128, L])
        sb = lambda n, s, d: es.enter_context(nc.sbuf_tensor(n, s, d))
        self.hT = sb("hT", [128, 8, L], BF16)
        self.COS = sb("COS", [128, L], F32)
        self.SIN = sb("SIN", [128, L], F32)
        self.ident = sb("ident", [128, 128], BF16)
        self.identf = sb("identf", [128, 128], F32)
        self.ones = sb("ones", [128, 128], BF16)
        self.onesf = sb("onesf", [128, 128], F32)
        self.triF = sb("triF", [128, 128], F32)
        self.triB = sb("triB", [128, 128], F32)
        self.LF = sb("LF", [128, 128], BF16)
        self.LB = sb("LB", [128, 128], BF16)
        self.TF = sb("TF", [128, 128], BF16)
        self.TB = sb("TB", [128, 128], BF16)
        self.TmFb = sb("TmFb", [128, 128], BF16)
        self.TmBb = sb("TmBb", [128, 128], BF16)
        self.ARW = 155 * 256
        self.AR = sb("AR", [128, self.ARW], F32)
        self.ps = es.enter_context(nc.psum_tensor("ps", [128, 8, 512], F32))
        self.mixT = self.view(0, [8, L], BF16)
        self.onT = self.view(32 * KB, [8, L], BF16)
        self.acur = 64 * KB
        self.stage = self.view(147 * KB, [4, 512], F32)
        self.nst = 0
        self.pre = {}
        self.alim = 147 * KB

    def view(self, off, shape, dt, p0=0, parts=128):
        es_ = _esz(dt)
        n = int(np.prod(shape))
        assert off % 4 == 0
        w0 = off // 4
        w1 = (off + n * es_ + 3) // 4
        assert w1 <= self.ARW, (off, shape)
        ap = self.AR[p0:p0 + parts, w0:w1]
        if dt != F32:
            ap = ap.bitcast(dt)
        if len(shape) == 2:
            ap = ap.rearrange("p (a b) -> p a b", a=shape[0])
        elif len(shape) == 3:
            ap = ap.rearrange("p (a b c) -> p a b c", a=shape[0], b=shape[1])
        return ap

    def alloc(self, shape, dt, p0=0, parts=128):
        n = int(np.prod(shape)) * _esz(dt)
        n = (n + 63) // 64 * 64
        v = self.view(self.acur, shape, dt, p0, parts)
        self.acur += n
        assert self.acur <= self.alim or getattr(self, "allow_over", False), (self.acur, shape)
        return v

    def dbg(self, name, ap_sb, shape):
        if not self.debug:
            return
        d = self.nc.dram_tensor("dbg_" + name, shape, ap_sb.dtype, kind="ExternalOutput").ap()
        self.dbg_outs[name] = d
        self.P.dma(d, ap_sb)

    def wcols(self, src, l, c0, c1):
        return src[l].rearrange("(k p) c -> p k c", p=128)[:, :, c0:c1]

    def wload(self, dst, src, l, c0, c1, engs=("pool",)):
        P = self.P
        nk = dst.shape[1]
        C = c1 - c0
        for k in range(nk):
            for cc in range(0, C, 512):
                w = min(512, C - cc)
                st = self.stage[:, self.nst % 4, 0:w]
                self.nst += 1
                P.dma(st, src[l, k * 128:(k + 1) * 128, c0 + cc:c0 + cc + w])
                P.copy(engs[self.nst % len(engs)], dst[:, k, cc:cc + w], st)

    def setup(self):
        P = self.P
        P.dma(self.COS[:], self.rope_cos)
        P.dma(self.SIN[:], self.rope_sin)
        P.memset("pool", self.onesf[:], 1.0)
        P.memset("pool", self.ones[:], 1.0)
        P.memset("pool", self.identf[:], 0.0)
        idf = self.identf
        onf = self.onesf
        P.add("pool", lambda e: e.affine_select(idf[:], onf[:], [[-1, 128]], ALU.is_equal, 0.0,
                                                base=0, channel_multiplier=1), [onf[:]], [idf[:]])
        P.copy("dve", self.ident[:], self.identf[:])
        tf, tb = self.triF, self.triB
        P.add("pool", lambda e: e.affine_select(tf[:], onf[:], [[1, 128]], ALU.is_ge, 0.0,
                                                base=0, channel_multiplier=-1), [onf[:]], [tf[:]])
        P.add("pool", lambda e: e.affine_select(tb[:], onf[:], [[-1, 128]], ALU.is_ge, 0.0,
                                                base=0, channel_multiplier=1), [onf[:]], [tb[:]])
        P.ts("dve", self.TmFb[:], tf[:], -1.0 / 16, None, ALU.mult)
        P.copy("dve", self.TF[:], tf[:])
        P.copy("dve", self.TB[:], tb[:])
        P.tt("dve", self.LF[:], tb[:], self.identf[:], ALU.subtract)
        P.tt("dve", self.LB[:], tf[:], self.identf[:], ALU.subtract)
        P.ts("dve", self.TmBb[:], tb[:], -1.0 / 16, None, ALU.mult)

    def norm_to_hT(self, xt, t, gB, scr):
        P = self.P
        junk, ss, hb = scr
        i = t % 2
        P.actv(junk[:, :], xt, AF.Square, accum_out=ss[:, t:t + 1])
        P.actv(ss[:, 16 + t:17 + t], ss[:, t:t + 1], AF.Ln, bias=EPS, scale=1.0 / D)
        P.actv(ss[:, 32 + t:33 + t], ss[:, 16 + t:17 + t], AF.Exp, scale=-0.5)
        P.stt("dve", hb[:, i, :], xt, ss[:, 32 + t:33 + t], gB, ALU.mult, ALU.mult)
        pb = self.ps[:, 7, :].bitcast(BF16)
        for k in range(8):
            P.tr(pb[:, k * 128:(k + 1) * 128], hb[:, i, k * 128:(k + 1) * 128], self.ident[:])
        P.copy("act", self.hT[:, :, t * 128:(t + 1) * 128],
               pb.rearrange("p (k c) -> p k c", k=8))

    def phase_a(self, s, l):
        P = self.P
        self.P.phase = "A"
        self.pre["ssd"] = self.ssd_load(l)
        self.acur = 100 * KB
        xt = self.alloc([2, D], F32)
        gB = self.alloc([D], F32)
        junk = self.alloc([D], BF16)
        ss = self.alloc([48], F32)
        hb = self.alloc([2, D], BF16)
        P.dma(gB, bass.AP(self.norm_g.tensor, l * D, [[0, 128], [1, D]]))
        for t in range(NT):
            P.dma(xt[:, t % 2, :], self.x[s, t * 128:(t + 1) * 128, :])
            self.norm_to_hT(xt[:, t % 2, :], t, gB, (junk, ss, hb))

    def attn_units(self, kT_fn, qT, V_fn, ob, pT, scale):
        P = self.P
        ps = self.ps
        P.mm(ps[:, 0, :], kT_fn(0), qT)
        for kc in range(16):
            if kc + 1 < 16:
                P.mm(ps[:, (kc + 1) % 2, :], kT_fn(kc + 1), qT)
            P.actv(pT[:, kc % 2, :], ps[:, kc % 2, :], AF.Exp, scale=scale)
            P.mm(ps[:, ob, :], V_fn(kc), pT[:, kc % 2, :], start=(kc == 0), stop=(kc == 15))

    def qk_norm_rope(self, psA, psB, gtile, gc, out, b, tmp):
        P = self.P
        sq, rt, t1, t2 = tmp
        bl = slice(b * 512, (b + 1) * 512)
        P.actv(sq[0:64, :], psA, AF.Square)
        P.mm(self.ps[0:64, 6, :], self.ones[0:64, 0:64], sq[0:64, :])
        P.actv(rt[0:64, :], self.ps[0:64, 6, :], AF.Ln, bias=EPS, scale=1.0 / 64)
        P.actv(rt[0:64, :], rt[0:64, :], AF.Exp, scale=-0.5)
        P.stt("dve", t1[0:64, :], psA, gtile[0:64, gc:gc + 1], self.COS[0:64, bl], ALU.mult, ALU.mult)
        P.stt("dve", t2[0:64, :], psB, gtile[0:64, gc + 1:gc + 2], self.SIN[0:64, bl], ALU.mult, ALU.mult)
        P.tt("pool", t1[0:64, :], t1[0:64, :], t2[0:64, :], ALU.add)
        P.tt("pool", out, t1[0:64, :], rt[0:64, :], ALU.mult)

    def gqa_load(self, l):
        P = self.P
        self.acur = 108 * KB
        COSG = self.alloc([L], F32)
        SING = self.alloc([L], F32)
        wk2 = self.alloc([2, 8, 128], BF16)
        wkp2 = self.alloc([2, 8, 128], BF16)
        wv = self.alloc([8, 128], BF16)
        wq = self.alloc([2, 8, 128], BF16)
        wqp = self.alloc([2, 8, 128], BF16)
        wz = self.alloc([2, 8, 128], BF16)
        gt = self.alloc([4], F32)
        for g in range(2):
            for hlf in range(2):
                self.wload(wk2[:, g, :, 64 * hlf:64 * hlf + 64], self.w_in, l, OFF["kd"] + g * 64, OFF["kd"] + (g + 1) * 64, engs=BULK)
                self.wload(wkp2[:, g, :, 64 * hlf:64 * hlf + 64], self.w_perm, l, 544 + g * 64, 544 + (g + 1) * 64, engs=BULK)
        for hlf in range(2):
            rows = slice(64 * hlf, 64 * hlf + 64)
            P.dma(COSG[rows, :], self.rope_cos[0:64, :])
            P.dma(SING[rows, :], self.rope_sin[0:64, :])
            P.dma(gt[rows, :], self.qk_g[l])
        self.wload(wv, self.w_in, l, OFF["vd"], OFF["vd"] + 128, engs=BULK)
        self.wload(wq[:, 0], self.w_in, l, OFF["qd"], OFF["qd"] + 128, engs=BULK)
        self.wload(wqp[:, 0], self.w_perm, l, 32, 32 + 128, engs=BULK)
        self.wload(wz[:, 0], self.w_in, l, OFF["zd"], OFF["zd"] + 128, engs=BULK)
        return (COSG, SING, wk2, wkp2, wv, wq, wqp, wz, gt)

    def phase_gqa(self, l):
        P = self.P
        self.P.phase = "gqa_prep"
        ps = self.ps
        hT = self.hT
        onT = self.onT
        self.acur = 64 * KB
        BD = self.alloc([128], BF16)
        kT2 = self.alloc([2, L], BF16)
        Va = self.alloc([2, 16, 192], BF16)
        qT2 = self.alloc([2, 512], BF16)
        pT = self.alloc([2, 2, 512], BF16)
        sq = self.alloc([512], BF16)
        rt = self.alloc([512], F32)
        t1 = self.alloc([512], F32)
        t2 = self.alloc([512], F32)
        sz = self.alloc([2, 512], F32)
        ez = self.alloc([512], F32)
        rs = t1
        nt = t2
        osb = self.alloc([2, 512], F32)
        assert self.acur <= 108 * KB
        w = self.pre.pop("gqa", None) or self.gqa_load(l)
        COSG, SING, wk2, wkp2, wv, wq, wqp, wz, gt = w
        P.memset("pool", BD, 0.0)
        P.memset("pool", BD[0:64, 0:64], 1.0)
        P.memset("pool", BD[64:128, 64:128], 1.0)
        P.memset("pool", Va[:, :, :, 64:128], 1.0)

        def proj(wa, wb, bl):
            for k in range(8):
                P.mm(ps[:, 6, :], wa(k), hT[:, k, bl], k == 0, k == 7)
            for k in range(8):
                P.mm(ps[:, 7, :], wb(k), hT[:, k, bl], k == 0, k == 7)
            P.actv(sq, ps[:, 6, :], AF.Square)

        def rope1(gc, bl):
            P.stt("dve", t1, ps[:, 6, :], gt[:, gc:gc + 1], COSG[:, bl], ALU.mult, ALU.mult)
            P.stt("dve", t2, ps[:, 7, :], gt[:, gc + 1:gc + 2], SING[:, bl], ALU.mult, ALU.mult)

        def rope2(out):
            P.mm(ps[:, 7, :], BD, sq, True, True)
            P.actv(rt, ps[:, 7, :], AF.Ln, bias=EPS, scale=1.0 / 64)
            P.actv(rt, rt, AF.Exp, scale=-0.5)
            P.tt("pool", t1, t1, t2, ALU.add)
            P.tt("pool", out, t1, rt, ALU.mult)

        for g in range(2):
            for b in range(NB):
                bl = slice(b * 512, (b + 1) * 512)
                proj(lambda k: wk2[:, g, k, :], lambda k: wkp2[:, g, k, :], bl)
                rope1(2, bl)
                rope2(kT2[:, g, bl])
        for t in range(NT):
            for k in range(8):
                P.mm(ps[:, 4 + t % 2, 0:128], hT[:, k, t * 128:(t + 1) * 128], wv[:, k, :], k == 0, k == 7)
            for g in range(2):
                P.copy("act", Va[:, g, t, 0:64], ps[:, 4 + t % 2, g * 64:(g + 1) * 64])
                P.copy("dve", Va[:, g, t, 128:192], ps[:, 4 + t % 2, g * 64:(g + 1) * 64])
        P.phase = "gqa_heads"
        items = [(pr, b) for pr in range(4) for b in range(NB)]

        def prep_parts(it):
            pr, b = items[it]
            i = pr % 2
            bl = slice(b * 512, (b + 1) * 512)
            hooks = {}

            def add(kc, f):
                prev = hooks.get(kc)

                def both(prev=prev, f=f):
                    if prev is not None:
                        prev()
                    f()
                hooks[kc] = both

            def loads():
                if b == 0 and pr > 0:
                    self.wload(wq[:, i], self.w_in, l, OFF["qd"] + pr * 128, OFF["qd"] + (pr + 1) * 128)
                    self.wload(wqp[:, i], self.w_perm, l, 32 + pr * 128, 32 + (pr + 1) * 128)
                    self.wload(wz[:, i], self.w_in, l, OFF["zd"] + pr * 128, OFF["zd"] + (pr + 1) * 128)
            add(0, loads)

            def pk(k):
                P.mm(ps[:, 6, :], wq[:, i, k, :], hT[:, k, bl], k == 0, k == 7)
                P.mm(ps[:, 7, :], wqp[:, i, k, :], hT[:, k, bl], k == 0, k == 7)
            for k in range(8):
                add(k, lambda k=k: pk(k))

            def sq_rope1():
                P.actv(sq, ps[:, 6, :], AF.Square)
                rope1(0, bl)
            add(8, sq_rope1)
            add(9, lambda: rope2(qT2[:, it % 2, :]))
            for k in range(8):
                add(10 + min(k, 5), lambda k=k: P.mm(ps[:, 6, :], wz[:, i, k, :], hT[:, k, bl], k == 0, k == 7))

            def silu():
                P.actv(ez, ps[:, 6, :], AF.Exp, scale=-1.0)
                P.actv(ez, ez, AF.Ln, bias=1.0)
                P.actv(ez, ez, AF.Exp, scale=-1.0)
                P.tt("dve", sz[:, it % 2, :], ps[:, 6, :], ez, ALU.mult)
            add(15, silu)
            return hooks

        def units(it, hooks):
            pr, b = items[it]
            g = pr // 2
            bl = slice(b * 512, (b + 1) * 512)
            qb = qT2[:, it % 2, :]
            sbank = [[0, 1], [4, 5]]
            rows = [slice(0, 64), slice(64, 128)]
            vsel = [slice(0, 128), slice(64, 192)]
            for par in range(2):
                P.mm(ps[:, sbank[par][0], :], kT2[rows[par], g, 0:128], qb[rows[par], :])
            for kc in range(16):
                if kc + 1 < 16:
                    for par in range(2):
                        P.mm(ps[:, sbank[par][(kc + 1) % 2], :], kT2[rows[par], g, (kc + 1) * 128:(kc + 2) * 128], qb[rows[par], :])
                for par in range(2):
                    P.actv(pT[:, par, kc % 2, :], ps[:, sbank[par][kc % 2], :], AF.Exp, scale=0.125)
                for par in range(2):
                    P.mm(ps[:, 2 + par, :], Va[:, g, kc, vsel[par]], pT[:, par, kc % 2, :], kc == 0, kc == 15)
                if kc in hooks:
                    hooks[kc]()
            for par in range(2):
                P.copy("dve", osb[:, par, :], ps[:, 2 + par, :])
            for par in range(2):
                orow = rows[par]
                srow = rows[1 - par]
                P.recip(rs[orow, :], osb[srow, par, :])
                P.tt("dve", nt[orow, :], osb[orow, par, :], rs[orow, :], ALU.mult)
                P.tt("pool", onT[orow, pr, bl], nt[orow, :], sz[orow, it % 2, :], ALU.mult)

        h0 = prep_parts(0)
        for kc in sorted(h0):
            h0[kc]()
        for it in range(len(items)):
            hooks = prep_parts(it + 1) if it + 1 < len(items) else {}
            units(it, hooks)

    def mla_load(self, l):
        self.acur = 124 * KB
        wq = self.alloc([3, 768], BF16)
        wqp = self.alloc([3, 256], BF16)
        wkv = self.alloc([2, 1024], BF16)
        wql = self.alloc([8, 384], BF16)
        wkvl = self.alloc([8, 256], BF16)
        wkr = self.alloc([2, 8, 32], BF16)
        lg = self.alloc([5], F32)
        self.wload(wql, self.w_in, l, OFF["qlat"], OFF["qlat"] + 384, engs=BULK)
        self.wload(wkvl, self.w_in, l, OFF["kvlat"], OFF["kvlat"] + 256, engs=BULK)
        self.wload(wkr[:, 0], self.w_in, l, OFF["krope"], OFF["krope"] + 32, engs=BULK)
        self.wload(wkr[:, 1], self.w_perm, l, 0, 32, engs=BULK)
        self.wload(wq, self.w_q_b, l, 0, 768, engs=BULK)
        self.wload(wqp, self.w_qb_perm, l, 0, 256, engs=BULK)
        self.wload(wkv, self.w_kv_b, l, 0, 1024, engs=BULK)
        self.P.dma(lg, self.lat_g[l])
        return (wq, wqp, wkv, wql, wkvl, wkr, lg)

    def phase_mla(self, l):
        P = self.P
        self.P.phase = "mla_prep"
        ps = self.ps
        hT = self.hT
        onT = self.onT
        COS, SIN = self.COS, self.SIN
        self.acur = 64 * KB
        qln = self.alloc([3, L], BF16)
        kvn = self.alloc([2, L], BF16)
        krT = self.alloc([L], BF16)
        Va = self.alloc([16, 4, 192], BF16)
        qT = self.alloc([2, 512], BF16)
        pT = self.alloc([4, 512], BF16)
        sz = self.alloc([2, 512], F32)
        wz = self.alloc([2, 8, 64], BF16)
        assert self.acur <= 124 * KB
        self.acur = 48 * KB
        kT = self.alloc([2, L], BF16)
        t1 = self.alloc([512], F32)
        t2 = self.alloc([512], F32)
        sqc = self.alloc([2, 512], BF16)
        rt = self.alloc([512], F32)
        rs = t1
        assert self.acur <= 64 * KB
        w = self.pre.pop("mla", None) or self.mla_load(l)
        wq, wqp, wkv, wql, wkvl, wkr, lg = w
        P.memset("pool", Va[:, :, :, 64:128], 1.0)
        nsq = 0
        for b in range(NB):
            bl = slice(b * 512, (b + 1) * 512)
            for (wsrc, nch, bank0, sbank, dst, gofs, dim) in ((wql, 3, 0, 3, qln, 0, 384), (wkvl, 2, 4, 6, kvn, 3, 256)):
                for c in range(nch):
                    for k in range(8):
                        P.mm(ps[:, bank0 + c, :], wsrc[:, k, c * 128:(c + 1) * 128], hT[:, k, bl], k == 0, k == 7)
                for c in range(nch):
                    P.actv(sqc[:, nsq % 2, :], ps[:, bank0 + c, :], AF.Square)
                    P.mm(ps[:, sbank, :], self.ones[:, :], sqc[:, nsq % 2, :], c == 0, c == nch - 1)
                    nsq += 1
                P.actv(rt, ps[:, sbank, :], AF.Ln, bias=EPS, scale=1.0 / dim)
                P.actv(rt, rt, AF.Exp, scale=-0.5)
                for c in range(nch):
                    P.stt("dve", dst[:, c, bl], ps[:, bank0 + c, :], lg[:, gofs + c:gofs + c + 1], rt, ALU.mult, ALU.mult)
            for k in range(8):
                P.mm(ps[64:96, 7, :], wkr[:, 0, k, :], hT[:, k, bl], k == 0, k == 7)
            P.tt("dve", t1[64:96, :], ps[64:96, 7, :], COS[64:96, bl], ALU.mult)
            for k in range(8):
                P.mm(ps[64:96, 7, :], wkr[:, 1, k, :], hT[:, k, bl], k == 0, k == 7)
            P.tt("dve", t2[64:96, :], ps[64:96, 7, :], SIN[64:96, bl], ALU.mult)
            P.tt("pool", krT[64:96, bl], t1[64:96, :], t2[64:96, :], ALU.add)
        wv_view = wkv.rearrange("p c (h two d) -> p c h two d", h=8, two=2)
        for t in range(NT):
            tl = slice(t * 128, (t + 1) * 128)
            for c in range(2):
                P.mm(ps[:, 4 + t % 2, :].rearrange("p (h d) -> p h d", h=8), kvn[:, c, tl], wv_view[:, c, :, 1, :], c == 0, c == 1)
            pv = ps[:, 4 + t % 2, :].rearrange("p (j two d) -> p j two d", j=4, two=2)
            P.copy("act", Va[:, t, :, 0:64], pv[:, :, 0, :])
            P.copy("dve", Va[:, t, :, 128:192], pv[:, :, 1, :])
        scale = 96.0 ** -0.5
        P.phase = "mla_heads"
        items = [(h, b) for h in range(8) for b in range(NB)]
        ez = t2

        def head_prep(h):
            i = h % 2
            self.wload(wz[:, i], self.w_in, l, OFF["zb"] + h * 64, OFF["zb"] + (h + 1) * 64)
            for b in range(NB):
                bl = slice(b * 512, (b + 1) * 512)
                for c in range(2):
                    P.mm(ps[0:64, 6, :], wkv[:, c, h * 128:h * 128 + 64], kvn[:, c, bl], c == 0, c == 1)
                P.copy("dve", kT[0:64, i, bl], ps[0:64, 6, :])
            P.copy("pool", kT[64:96, i, :], krT[64:96, :])

        def prep_parts(it):
            h, b = items[it]
            i = h % 2
            par = h % 2
            orow = slice(64 * par, 64 * par + 64)
            bl = slice(b * 512, (b + 1) * 512)
            qb = qT[:, it % 2, :]

            def p0():
                if b == 0:
                    head_prep(h)

            def p1():
                for c in range(3):
                    P.mm(ps[0:96, 6, :], wq[:, c, h * 96:(h + 1) * 96], qln[:, c, bl], c == 0, c == 2)
                for c in range(3):
                    P.mm(ps[64:96, 7, :], wqp[:, c, h * 32:(h + 1) * 32], qln[:, c, bl], c == 0, c == 2)
                P.copy("dve", qb[0:64, :], ps[0:64, 6, :])
                P.tt("dve", t1[64:96, :], ps[64:96, 6, :], COS[64:96, bl], ALU.mult)
                P.tt("dve", t2[64:96, :], ps[64:96, 7, :], SIN[64:96, bl], ALU.mult)
                P.tt("pool", qb[64:96, :], t1[64:96, :], t2[64:96, :], ALU.add)

            def p3():
                for k in range(8):
                    P.mm(ps[orow, 7, :], wz[:, i, k, :], hT[:, k, bl], k == 0, k == 7)
                P.actv(ez[orow, :], ps[orow, 7, :], AF.Exp, scale=-1.0)
                P.actv(ez[orow, :], ez[orow, :], AF.Ln, bias=1.0)
                P.actv(ez[orow, :], ez[orow, :], AF.Exp, scale=-1.0)
                P.tt("dve", sz[orow, it % 2, :], ps[orow, 7, :], ez[orow, :], ALU.mult)

            return [p0, p1, p3]

        def units(it, hooks):
            h, b = items[it]
            i = h % 2
            par = h % 2
            pr = h // 2
            orow = slice(64 * par, 64 * par + 64)
            srow = slice(64 * (1 - par), 64 * (1 - par) + 64)
            vsel = slice(0, 128) if par == 0 else slice(64, 192)
            bl = slice(b * 512, (b + 1) * 512)
            qb = qT[:, it % 2, :]
            ob = 2 + it % 2
            sbanks = [[0, 1], [4, 5]]

            def S(kc):
                P.mm(ps[:, sbanks[(kc // 2) % 2][kc % 2], :], kT[0:96, i, kc * 128:(kc + 1) * 128], qb[0:96, :])
            S(0)
            S(1)
            for k2 in range(8):
                if k2 + 1 < 8:
                    S(2 * k2 + 2)
                    S(2 * k2 + 3)
                for kc in (2 * k2, 2 * k2 + 1):
                    P.actv(pT[:, kc % 4, :], ps[:, sbanks[(kc // 2) % 2][kc % 2], :], AF.Exp, scale=scale)
                for kc in (2 * k2, 2 * k2 + 1):
                    P.mm(ps[:, ob, :], Va[:, kc, pr, vsel], pT[:, kc % 4, :], kc == 0, kc == 15)
                if k2 in hooks:
                    hooks[k2]()
            P.recip(rs[orow, :], ps[srow, ob, :])
            P.tt("dve", rs[orow, :], ps[orow, ob, :], rs[orow, :], ALU.mult)
            P.tt("pool", onT[orow, pr, bl], rs[orow, :], sz[orow, it % 2, :], ALU.mult)

        for f in prep_parts(0):
            f()
        for it in range(len(items)):
            hooks = {}
            if it + 1 < len(items):
                pp = prep_parts(it + 1)
                hooks = {0: pp[0], 2: pp[1], 5: pp[2]}
            units(it, hooks)

    def gla_load(self, l):
        P = self.P
        self.acur = 48 * KB
        wvc = self.alloc([8, 512], BF16)
        wzc = self.view(48 * KB, [8, 512], BF16)
        wqc = self.alloc([8, 256], BF16)
        wkc = self.alloc([8, 256], BF16)
        assert self.acur <= 64 * KB
        self.acur = 143 * KB
        wgu = self.alloc([2, 256], BF16)
        bg = self.alloc([2, 256], BF16)
        wglr = self.alloc([2, 8, 32], BF16)
        gg = self.alloc([4], F32)
        self.wload(wqc, self.w_in, l, OFF["qc"], OFF["qc"] + 256, engs=BULK)
        self.wload(wkc, self.w_in, l, OFF["kc"], OFF["kc"] + 256, engs=BULK)
        self.wload(wvc, self.w_in, l, OFF["vc"], OFF["vc"] + 512, engs=BULK)
        self.wload(wglr[:, 0], self.w_in, l, OFF["glr"], OFF["glr"] + 32, engs=BULK)
        self.wload(wglr[:, 1, :, 0:16], self.w_in, l, OFF["glr"] + 16, OFF["glr"] + 32, engs=BULK)
        self.wload(wglr[:, 1, :, 16:32], self.w_in, l, OFF["glr"], OFF["glr"] + 16, engs=BULK)
        P.memset("pool", wgu[0:32], 0.0)
        st = self.stage[0:16, self.nst % 4, :]
        self.nst += 1
        P.dma(st.rearrange("p (d c) -> p d c", d=2), self.w_gate_up[l].rearrange("d r c -> r d c"))
        P.copy("pool", wgu[0:16], st.rearrange("p (d c) -> p d c", d=2))
        st = self.stage[0:1, self.nst % 4, :]
        self.nst += 1
        P.dma(st.rearrange("p (d c) -> p d c", d=2), self.b_gate[l:l + 1])
        P.copy("pool", bg[0:1], st.rearrange("p (d c) -> p d c", d=2))
        P.dma(gg, self.gla_g[l])
        return (wvc, wzc, wqc, wkc, wglr, wgu, bg, gg)

    def phase_gla(self, l):
        P = self.P
        self.P.phase = "gla_prep"
        ps = self.ps
        hT = self.hT
        onT = self.onT
        self.acur = 64 * KB
        qTc = self.alloc([2, L], BF16)
        kTc = self.alloc([2, L], BF16)
        Vc = self.alloc([16, 512], BF16)
        ob = self.alloc([4, L], BF16)
        glrT = self.alloc([2, L], BF16)
        oblk = self.alloc([4, 512], F32)
        S = self.alloc([2, 128], F32)
        Sbf = self.alloc([2, 128], BF16)
        gsp = self.alloc([256], F32)
        gh = self.alloc([256], BF16)
        gl = self.alloc([256], BF16)
        eg = self.alloc([2, 128], F32)
        eng = self.alloc([2, 128], F32)
        ek = self.alloc([2, 128], F32)
        qg = self.alloc([2, 2, 128], BF16)
        kg = self.alloc([2, 128], BF16)
        kend = self.alloc([2, 2, 128], BF16)
        attm = self.alloc([2, 4, 128], BF16)
        kendT = self.alloc([2, 128], BF16)
        glast = self.alloc([2], F32)
        cd = self.alloc([2, 2], F32)
        assert self.acur <= 143 * KB
        self.acur = 56 * KB
        sq = self.alloc([512], BF16)
        rt = self.alloc([512], F32)
        sz = self.alloc([512], F32)
        tmp = self.alloc([512], F32)
        assert self.acur <= 64 * KB
        w = self.pre.pop("gla", None) or self.gla_load(l)
        wvc, wzc, wqc, wkc, wglr, wgu, bg, gg = w
        n = 0
        for j in range(2):
            for (wsrc, dst) in ((wqc, qTc), (wkc, kTc)):
                for b in range(NB):
                    bl = slice(b * 512, (b + 1) * 512)
                    bank = 4 + n % 4
                    for k in range(8):
                        P.mm(ps[:, bank, :], wsrc[:, k, j * 128:(j + 1) * 128], hT[:, k, bl], k == 0, k == 7)
                    P.copy("act" if n % 2 else "dve", dst[:, j, bl], ps[:, bank, :])
                    n += 1
        for t in range(NT):
            tl = slice(t * 128, (t + 1) * 128)
            bank = 4 + n % 4
            for k in range(8):
                P.mm(ps[:, bank, :], hT[:, k, tl], wvc[:, k, :], k == 0, k == 7)
            P.copy("act" if n % 2 else "dve", Vc[:, t, :], ps[:, bank, :])
            n += 1
        for d in range(2):
            for b in range(NB):
                bl = slice(b * 512, (b + 1) * 512)
                bank = 4 + n % 4
                for k in range(8):
                    P.mm(ps[0:32, bank, :], wglr[:, d, k, :], hT[:, k, bl], k == 0, k == 7)
                P.copy("act" if n % 2 else "dve", glrT[0:32, d, bl], ps[0:32, bank, :])
                n += 1

        self.wload(wzc, self.w_in, l, OFF["zc"], OFF["zc"] + 512)

        def bank3(b, a):
            return ps[:, b, 0:a * 128].rearrange("p (a c) -> p a c", a=a)


        P.phase = "gla_scan"
        psC = bank3(1, 2)
        psA2 = [bank3(2, 2), bank3(3, 2)]
        psO2 = [bank3(4, 2), bank3(5, 2)]
        psT = ps[:, 6, :].bitcast(BF16)[:, 0:256].rearrange("p (a c) -> p a c", a=2)
        psU = bank3(7, 2)
        obv = ob.rearrange("p (j two) t -> p j two t", two=2)
        oblv = oblk.rearrange("p (j two) t -> p j two t", two=2)

        def front(d, ci, t):
            Tm = self.TmFb if d == 0 else self.TmBb
            tri = self.triF if d == 0 else self.triB
            last = 127 if d == 0 else 0
            x = ci % 2
            tl = slice(t * 128, (t + 1) * 128)
            P.mm(ps[:, 0, 0:256], glrT[0:32, d, tl], wgu[0:32, d, :], True, False)
            P.mm(ps[:, 0, 0:256], self.ones[0:1, 0:128], bg[0:1, d, :], False, True)
            P.actv(gsp, ps[:, 0, 0:256], AF.Exp, scale=-1.0)
            P.actv(gsp, gsp, AF.Ln, bias=1.0)
            P.copy("dve", gh, gsp)
            P.tt("dve", gl, gsp, gh, ALU.subtract)
            for j in range(2):
                P.mm(psC[:, j, :], gh[:, j * 128:(j + 1) * 128], Tm[:, :], True, False)
                P.mm(psC[:, j, :], gl[:, j * 128:(j + 1) * 128], Tm[:, :], False, True)
            P.copy("dve", glast, psC[:, :, last])
            P.actv(eg.rearrange("p a c -> p (a c)"), ps[:, 1, 0:256], AF.Exp)
            P.actv(eng.rearrange("p a c -> p (a c)"), ps[:, 1, 0:256], AF.Exp, scale=-1.0)
            for j in range(2):
                P.actv(ek[:, j, :], psC[:, j, :], AF.Exp, scale=-1.0, bias=glast[:, j:j + 1])
            P.actv(cd[:, x, :], glast, AF.Exp)
            P.stt("dve", qg[:, x], qTc[:, :, tl], 0.125, eg, ALU.mult, ALU.mult)
            P.tt("dve", kg, kTc[:, :, tl], eng, ALU.mult)
            P.tt("dve", kend[:, x], kTc[:, :, tl], ek, ALU.mult)
            tri_b = bass.AP(tri, 0, [[128, 128], [0, 2], [1, 128]])
            attv = attm[:, x].rearrange("p (j two) c -> p j two c", two=2)
            for par in range(2):
                r = slice(64 * par, 64 * par + 64)
                for j in range(2):
                    P.mm(psA2[par][:, j, :], kg[r, j, :], qg[r, x, j, :], True, True)
                P.tt("dve", attv[:, :, par, :], psA2[par], tri_b, ALU.mult)

        def back(d, ci, t):
            x = ci % 2
            tl = slice(t * 128, (t + 1) * 128)
            for par in range(2):
                r = slice(64 * par, 64 * par + 64)
                for j in range(2):
                    h = 2 * j + par
                    P.mm(psO2[par][:, j, :], Vc[:, t, h * 128:(h + 1) * 128], attm[:, x, h, :], True, ci == 0)
                    if ci > 0:
                        P.mm(psO2[par][:, j, :], Sbf[r, j, :], qg[r, x, j, :], False, True)
                if d == 1:
                    P.copy("act", obv[:, :, par, tl], psO2[par])
                else:
                    c4 = t % 4
                    P.tt("dve", oblv[:, :, par, c4 * 128:(c4 + 1) * 128], psO2[par], obv[:, :, par, tl], ALU.add)
            if ci < NT - 1:
                for j in range(2):
                    P.tr(psT[:, j, :], kend[:, x, j, :], self.ident[:])
                P.copy("act", kendT, psT)
                for h in range(4):
                    j = h // 2
                    r = slice(64 * (h % 2), 64 * (h % 2) + 64)
                    P.mm(psU[r, j, :], kendT[:, j, 64 * (h % 2):64 * (h % 2) + 64], Vc[:, t, h * 128:(h + 1) * 128], True, True)
                for j in range(2):
                    if ci > 0:
                        P.stt("dve", S[:, j, :], S[:, j, :], cd[:, x, j:j + 1], psU[:, j, :], ALU.mult, ALU.add)
                    else:
                        P.copy("dve", S[:, j, :], psU[:, j, :])
                P.copy("pool", Sbf, S)
            if d == 0 and t % 4 == 3:
                b = t // 4
                bl = slice(b * 512, (b + 1) * 512)
                for h in range(4):
                    P.actv(sq, oblk[:, h, :], AF.Square)
                    P.mm(ps[:, 0, :], self.ones[:, :], sq, True, True)
                    P.actv(rt, ps[:, 0, :], AF.Ln, bias=EPS, scale=1.0 / 128)
                    P.actv(rt, rt, AF.Exp, scale=-0.5)
                    for k in range(8):
                        P.mm(ps[:, 1, :], wzc[:, k, h * 128:(h + 1) * 128], hT[:, k, bl], k == 0, k == 7)
                    P.actv(sz, ps[:, 1, :], AF.Silu)
                    P.stt("dve", tmp, oblk[:, h, :], gg[:, h:h + 1], rt, ALU.mult, ALU.mult)
                    P.tt("pool", onT[:, h, bl], tmp, sz, ALU.mult)

        for d in (1, 0):
            order = list(range(NT)) if d == 0 else list(range(NT - 1, -1, -1))
            front(d, 0, order[0])
            for ci, t in enumerate(order):
                if ci + 1 < NT:
                    front(d, ci + 1, order[ci + 1])
                back(d, ci, t)

    def ssd_load(self, l):
        P = self.P
        self.acur = 64 * KB
        wz = self.alloc([8, D], BF16)
        wdt = self.alloc([8, 32], BF16)
        anb = self.alloc([32], F32)
        dtb = self.alloc([32], F32)
        dsk = self.alloc([16], F32)
        gnb = self.alloc([D], F32)
        cp = self.alloc([12, 6], F32)
        self.ssd_end = self.acur
        P.dma(cp, self.conv_p[l])
        P.dma(anb, bass.AP(self.a_log.tensor, l * 32, [[0, 128], [1, 32]]))
        P.dma(dtb, bass.AP(self.dt_bias.tensor, l * 32, [[0, 128], [1, 32]]))
        P.dma(dsk, bass.AP(self.d_skip.tensor, l * 16, [[0, 128], [1, 16]]))
        P.dma(gnb, bass.AP(self.ssd_norm_g.tensor, l * D, [[0, 128], [1, D]]))
        self.wload(wdt, self.w_in, l, OFF["dt"], OFF["dt"] + 32, engs=BULK)
        self.wload(wz, self.w_in, l, OFF["za"], OFF["za"] + D, engs=BULK)
        return (wz, wdt, cp, anb, dtb, dsk, gnb)

    def phase_ssd(self, l):
        P = self.P
        self.P.phase = "ssd_conv"
        ps = self.ps
        hT = self.hT
        onT = self.onT
        xs_tok = self.view(0, [16, D], BF16)
        w = self.pre.pop("ssd", None) or self.ssd_load(l)
        wz, wdt, cp, anb, dtb, dsk, gnb = w
        self.acur = self.ssd_end
        BT = self.alloc([2, L], BF16)
        CT = self.alloc([2, L], BF16)
        Btok = self.alloc([16, 256], BF16)
        dt_all = self.alloc([16, 32], F32)
        dta_hi = self.alloc([16, 32], BF16)
        dta_lo = self.alloc([16, 32], BF16)
        base = self.acur
        xb = self.alloc([L + 4], F32)
        acc = self.alloc([L], F32)
        cv = self.alloc([L], BF16)
        wx = self.alloc([2, 8, 128], BF16)
        P.actv(anb, anb, AF.Exp)
        P.ts("dve", anb, anb, -1.0, None, ALU.mult)
        P.memset("pool", xb[:, 0:2], 0.0)
        P.memset("pool", xb[:, L + 2:L + 4], 0.0)
        for c in range(12):
            i = c % 2
            self.wload(wx[:, i], self.w_in, l, OFF["xbc"] + c * 128, OFF["xbc"] + (c + 1) * 128)
            for b in range(NB):
                bank = 4 + (c * 4 + b) % 4
                for k in range(8):
                    P.mm(ps[:, bank, :], wx[:, i, k, :], hT[:, k, b * 512:(b + 1) * 512], k == 0, k == 7)
                P.copy("act", xb[:, 2 + b * 512:2 + (b + 1) * 512], ps[:, bank, :])
            eng = "dve"
            P.ts(eng, acc, xb[:, 0:L], cp[:, c, 0:1], cp[:, c, 5:6], ALU.mult, ALU.add)
            for j in range(1, 5):
                P.stt(eng, acc, xb[:, j:j + L], cp[:, c, j:j + 1], acc, ALU.mult, ALU.add)
            if c < 8:
                P.actv(cv, acc, AF.Silu)
                dst_tok = lambda t, c=c: xs_tok[:, t, c * 128:(c + 1) * 128]
                src = cv
            elif c < 10:
                P.actv(BT[:, c - 8, :], acc, AF.Silu)
                dst_tok = lambda t, c=c: Btok[:, t, (c - 8) * 128:(c - 7) * 128]
                src = BT[:, c - 8, :]
            else:
                P.actv(CT[:, c - 10, :], acc, AF.Silu)
                src = None
            if src is not None:
                for half in range(2):
                    pb = ps[:, 2 + half, :].bitcast(BF16)
                    for tt_ in range(8):
                        t = half * 8 + tt_
                        P.tr(pb[:, tt_ * 128:(tt_ + 1) * 128], src[:, t * 128:(t + 1) * 128], self.ident[:])
                    if c < 8:
                        P.copy("act" if half else "dve", xs_tok[:, half * 8:(half + 1) * 8, c * 128:(c + 1) * 128],
                               pb.rearrange("p (t c) -> p t c", t=8))
                    else:
                        P.copy("act" if half else "dve", Btok[:, half * 8:(half + 1) * 8, (c - 8) * 128:(c - 7) * 128],
                               pb.rearrange("p (t c) -> p t c", t=8))
        P.phase = "ssd_scan"
        for t in range(NT):
            for k in range(8):
                P.mm(ps[:, 7, t * 32:(t + 1) * 32], hT[:, k, t * 128:(t + 1) * 128], wdt[:, k, :], k == 0, k == 7)
        dtb_b = bass.AP(dtb.tensor, dtb.offset, [list(dtb.ap[0]), [0, 16], [1, 32]])
        anb_b = bass.AP(anb.tensor, anb.offset, [list(anb.ap[0]), [0, 16], [1, 32]])
        P.tt("dve", dt_all, ps[:, 7, :].rearrange("p (t c) -> p t c", t=16), dtb_b, ALU.add)
        P.actv(dt_all, dt_all, AF.Exp)
        P.actv(dt_all, dt_all, AF.Ln, bias=1.0)
        self.acur = base
        self.allow_over = True
        Ahi = self.alloc([1, 8, 128], BF16)
        Alo = self.alloc([1, 8, 128], BF16)
        E = self.alloc([1, 8, 128], BF16)
        W = self.alloc([1, 8, 128], BF16)
        Gm = self.alloc([2, 128], BF16)
        dlf = self.alloc([16, 32], F32)
        xd = self.alloc([D], BF16)
        xdd = self.alloc([D], BF16)
        S = self.alloc([D], F32)
        Sbf = self.alloc([D], BF16)
        ytmp = self.alloc([D], F32)
        y2 = self.alloc([2, D], F32)
        ybt = self.alloc([D], F32)
        eac = self.alloc([16], F32)
        cdb = self.alloc([16], F32)
        sz = self.alloc([D], F32)
        dta = sz[:, 0:512].rearrange("p (t c) -> p t c", t=16)
        junk = ytmp
        hb = xdd
        ss = self.alloc([4], F32)
        P.tt("dve", dta, dt_all, anb_b, ALU.mult)
        P.copy("dve", dta_hi, dta)
        P.tt("dve", dta_lo, dta, dta_hi, ALU.subtract)
        P.tt("dve", dlf, dta, dta_hi, ALU.subtract)

        def hv(ap2):
            return ap2.rearrange("p (h q) -> p h q", h=16)

        def bc_h(ap_col16, n):
            return bass.AP(ap_col16.tensor, ap_col16.offset, [list(ap_col16.ap[0]), [ap_col16.ap[1][0], 16], [0, n]])

        def bc_h2(ap16, g):
            return bass.AP(ap16.tensor, ap16.offset + g * 8, [list(ap16.ap[0]), [ap16.ap[1][0], 8], [0, 128]])

        def fin_stages(ci, t):
            y = y2[:, ci % 2, :]
            tl = slice(t * 128, (t + 1) * 128)
            pz = ps[:, 0:2, :].rearrange("p b c -> p (b c)")

            def f1a():
                P.tt("dve", hv(ytmp), hv(xs_tok[:, t, :]), bc_h(dsk, 64), ALU.mult)
                P.tt("dve", y, y, ytmp, ALU.add)

            def f1b():
                for hf in range(2):
                    for k in range(8):
                        P.mm(ps[:, hf, :], hT[:, k, tl], wz[:, k, hf * 512:(hf + 1) * 512], k == 0, k == 7)

            def f2():
                P.actv(sz, pz, AF.Silu)

            def f3():
                P.tt("dve", y, y, sz, ALU.mult)

            def f4():
                P.actv(junk, y, AF.Square, accum_out=ss[:, 0:1])
                P.actv(ss[:, 1:2], ss[:, 0:1], AF.Ln, bias=EPS, scale=1.0 / D)
                P.actv(ss[:, 2:3], ss[:, 1:2], AF.Exp, scale=-0.5)

            def f5():
                P.stt("dve", hb, y, ss[:, 2:3], gnb, ALU.mult, ALU.mult)
                pb = ps[:, 7, :].bitcast(BF16)
                for k in range(8):
                    P.tr(pb[:, k * 128:(k + 1) * 128], hb[:, k * 128:(k + 1) * 128], self.ident[:])
                P.copy("act", onT[:, :, tl], pb.rearrange("p (k c) -> p k c", k=8))

            return [f1a, f1b, f2, f3, f4, f5]

        def main_stages(d, ci, t):
            Lm = self.LF if d == 0 else self.LB
            Tm = self.TF if d == 0 else self.TB
            tri = self.triF if d == 0 else self.triB
            last = 127 if d == 0 else 0
            y = y2[:, ci % 2, :]
            tl = slice(t * 128, (t + 1) * 128)
            dh = dta_hi[:, t, d * 16:(d + 1) * 16]
            dl = dta_lo[:, t, d * 16:(d + 1) * 16]

            def m1():
                if d == 0:
                    P.dma(ybt, self.yb_d[tl, :])
                for g in range(2):
                    P.mm(ps[:, 4, g * 128:(g + 1) * 128], BT[:, g, tl], CT[:, g, tl], True, True)
                P.mm(ps[:, 4, 256:272], Tm[:, :], dh, True, False)
                P.mm(ps[:, 4, 256:272], Tm[:, :], dl, False, True)
                P.mm(ps[:, 4, 272:288], self.ones[:, :], dh, True, False)
                P.mm(ps[:, 4, 272:288], self.ones[:, :], dl, False, True)
                tri_b = bass.AP(tri, 0, [[128, 128], [0, 2], [1, 128]])
                P.tt("dve", Gm, ps[:, 4, 0:256].rearrange("p (g c) -> p g c", g=2), tri_b, ALU.mult)
                P.actv(eac, ps[:, 4, 256:272], AF.Exp)
                P.actv(cdb, ps[:, 4, 272:288], AF.Exp)
                P.tt("dve", hv(xd), hv(xs_tok[:, t, :]), bc_h(dt_all[:, t, d * 16:(d + 1) * 16], 64), ALU.mult)

            def mg(g):
                def f():
                    Lb = bass.AP(Lm, 0, [[128, 128], [0, 8], [1, 128]])
                    P.tt("pool", Ahi[:, 0], Lb, bc_h2(dh, g), ALU.mult)
                    for e in range(8):
                        P.actv(Alo[:, 0, e, :], Lm[:, :], AF.Copy, scale=dlf[:, t, d * 16 + g * 8 + e:d * 16 + g * 8 + e + 1])
                    for e in range(8):
                        out = ps[:, 2 * g + e // 4, (e % 4) * 128:(e % 4 + 1) * 128]
                        P.mm(out, Ahi[:, 0, e, :], Tm[:, :], True, False)
                        P.mm(out, Alo[:, 0, e, :], Tm[:, :], False, True)
                    P.actv(E[:, 0], ps[:, 2 * g:2 * g + 2, :].rearrange("p b (e c) -> p (b e) c", e=4), AF.Exp)
                    Gb = bass.AP(Gm.tensor, Gm.offset + g * 128, [list(Gm.ap[0]), [0, 8], [1, 128]])
                    P.tt("dve", W[:, 0], E[:, 0], Gb, ALU.mult)
                    Elast = bass.AP(E.tensor, E.offset + last, [list(E.ap[0]), [128, 8], [0, 64]])
                    P.tt("dve", hv(xdd)[:, g * 8:(g + 1) * 8, :], hv(xd)[:, g * 8:(g + 1) * 8, :], Elast, ALU.mult)
                    for e in range(8):
                        h = g * 8 + e
                        P.mm(ps[:, 5 + g, e * 64:(e + 1) * 64], W[:, 0, e, :], xd[:, h * 64:(h + 1) * 64], True, True)
                return f

            def m4():
                if ci > 0:
                    for g in range(2):
                        P.mm(ps[:, g, :], CT[:, g, tl], Sbf[:, g * 512:(g + 1) * 512], True, True)
                    P.tt("dve", hv(ytmp), ps[:, 0:2, :].rearrange("p b (e q) -> p (b e) q", e=8), bc_h(eac, 64), ALU.mult)
                    P.tt("dve", y, ytmp, ps[:, 5:7, :].rearrange("p b c -> p (b c)"), ALU.add)
                else:
                    P.copy("act", y, ps[:, 5:7, :].rearrange("p b c -> p (b c)"))

            def m5():
                if ci < NT - 1:
                    for g in range(2):
                        P.mm(ps[:, 2 + g, :], Btok[:, t, g * 128:(g + 1) * 128], xdd[:, g * 512:(g + 1) * 512], True, True)
                    if ci > 0:
                        P.tt("pool", hv(S), hv(S), bc_h(cdb, 64), ALU.mult)
                        P.tt("dve", S, S, ps[:, 2:4, :].rearrange("p b c -> p (b c)"), ALU.add)
                    else:
                        P.copy("act", S, ps[:, 2:4, :].rearrange("p b c -> p (b c)"))
                    P.copy("act", Sbf, S)

            def m6():
                if d == 1:
                    P.dma(self.yb_d[tl, :], y)
                else:
                    P.tt("dve", y, y, ybt, ALU.add)

            return [m1, mg(0), mg(1), m4, m5, m6]

        for d in (1, 0):
            order = range(NT) if d == 0 else range(NT - 1, -1, -1)
            pending = None
            for ci, t in enumerate(order):
                ms = main_stages(d, ci, t)
                fs = fin_stages(*pending) if pending is not None else []
                for i in range(6):
                    ms[i]()
                    if i < len(fs):
                        fs[i]()
                if d == 0:
                    pending = (ci, t)
            if pending is not None:
                for f in fin_stages(*pending):
                    f()
        self.allow_over = False

    def merge(self, l, bi, nk, wbr_src, first, pre=None):
        P = self.P
        self.P.phase = "merge"
        ps = self.ps
        hT = self.hT
        onT = self.onT
        self.acur = 64 * KB
        wbr = self.alloc([nk, D], BF16)
        wg = self.alloc([2, 8, 128], BF16)
        sg = self.alloc([2, 512], F32)
        tm = self.alloc([2, 512], F32)
        self.wload(wbr, wbr_src, l, 0, D, engs=BULK)
        self.wload(wg[:, 0], self.w_in, l, bi * D, bi * D + 128)
        if pre is not None:
            save = self.acur
            pre()
            self.acur = save
        n = 0
        for m in range(8):
            if m > 0:
                self.wload(wg[:, m % 2], self.w_in, l, bi * D + m * 128, bi * D + (m + 1) * 128)
            for b in range(NB):
                bl = slice(b * 512, (b + 1) * 512)
                yb = 4 + n % 2
                gb = 6 + n % 2
                for k in range(nk):
                    P.mm(ps[:, yb, :], wbr[:, k, m * 128:(m + 1) * 128], onT[:, k, bl], k == 0, k == nk - 1)
                for k in range(8):
                    P.mm(ps[:, gb, :], wg[:, m % 2, k, :], hT[:, k, bl], k == 0, k == 7)
                P.actv(sg[:, n % 2, :], ps[:, gb, :], AF.Sigmoid)
                if first:
                    P.tt("dve", self.mixT[:, m, bl], ps[:, yb, :], sg[:, n % 2, :], ALU.mult)
                else:
                    P.tt("dve", tm[:, n % 2, :], ps[:, yb, :], sg[:, n % 2, :], ALU.mult)
                    P.tt("pool", self.mixT[:, m, bl], self.mixT[:, m, bl], tm[:, n % 2, :], ALU.add)
                n += 1

    def outproj_load(self, l, last):
        self.acur = 96 * KB
        wo = self.alloc([8, D], BF16)
        gB = self.alloc([D], F32)
        self.wload(wo, self.w_out, l, 0, D, engs=BULK)
        grow = 2 if last else l + 1
        self.P.dma(gB, bass.AP(self.norm_g.tensor, grow * D, [[0, 128], [1, D]]))
        return (wo, gB)

    def outproj(self, s, l, last):
        P = self.P
        self.P.phase = "outproj"
        ps = self.ps
        w = self.pre.pop("outproj", None) or self.outproj_load(l, last)
        wo, gB = w
        self.acur = 116 * KB
        xt = self.alloc([2, D], F32)
        xn = self.alloc([2, D], F32)
        junk = self.alloc([D], BF16)
        ss = self.alloc([48], F32)
        hb = self.alloc([2, D], BF16)
        yo = self.alloc([2, D], F32)
        if not last:
            self.pre["ssd"] = self.ssd_load(self.layers[self.layers.index(l) + 1])
        xsrc = self.x if l == 0 else self.xres
        def finish(t):
            i = t % 2
            tl = slice(t * 128, (t + 1) * 128)
            if not last:
                P.dma(self.xres[s, tl, :], xn[:, i, :])
                self.norm_to_hT(xn[:, i, :], t, gB, (junk, ss, hb))
            else:
                P.actv(junk[:, :], xn[:, i, :], AF.Square, accum_out=ss[:, t:t + 1])
                P.actv(ss[:, 16 + t:17 + t], ss[:, t:t + 1], AF.Ln, bias=EPS, scale=1.0 / D)
                P.actv(ss[:, 32 + t:33 + t], ss[:, 16 + t:17 + t], AF.Exp, scale=-0.5)
                P.stt("dve", yo[:, i, :], xn[:, i, :], ss[:, 32 + t:33 + t], gB, ALU.mult, ALU.mult)
                P.dma(self.out[s, tl, :], yo[:, i, :])

        P.dma(xt[:, 0, :], xsrc[s, 0:128, :])
        for t in range(NT):
            i = t % 2
            tl = slice(t * 128, (t + 1) * 128)
            if t + 1 < NT:
                P.dma(xt[:, (t + 1) % 2, :], xsrc[s, (t + 1) * 128:(t + 2) * 128, :])
            for hf in range(2):
                bank = 4 + hf
                for k in range(8):
                    P.mm(ps[:, bank, :], self.mixT[:, k, tl], wo[:, k, hf * 512:(hf + 1) * 512], k == 0, k == 7)
                P.tt("dve", xn[:, i, hf * 512:(hf + 1) * 512], ps[:, bank, :], xt[:, i, hf * 512:(hf + 1) * 512], ALU.add)
            if t >= 1:
                finish(t - 1)
        finish(NT - 1)

    def build(self):
        self.setup()
        for s in range(self.nseq):
            nl = len(self.layers)
            for li, l in enumerate(self.layers):
                if li == 0:
                    self.phase_a(s, l)
                last = (li == nl - 1)
                order = [b for b in "abcd" if b in self.branches]
                phase = {"a": self.phase_ssd, "b": self.phase_mla, "c": self.phase_gla, "d": self.phase_gqa}
                loader = {"b": ("mla", self.mla_load), "c": ("gla", self.gla_load), "d": ("gqa", self.gqa_load)}
                mrg = {"a": (0, 8, self.w_br_a), "b": (1, 4, self.w_br_b), "c": (2, 4, self.w_br_c), "d": (3, 4, self.w_br_d)}
                for bi_, br in enumerate(order):
                    phase[br](l)
                    if bi_ + 1 < len(order):
                        key, fn = loader[order[bi_ + 1]]
                        pre = (lambda key=key, fn=fn: self.pre.__setitem__(key, fn(l)))
                    else:
                        pre = (lambda: self.pre.__setitem__("outproj", self.outproj_load(l, last)))
                    i_, nk_, wsrc_ = mrg[br]
                    self.merge(l, i_, nk_, wsrc_, bi_ == 0, pre=pre)
                self.outproj(s, l, last=(li == nl - 1))
        cnt = self.P.emit(self.es)
        return cnt


def _prep_inputs(inp):
    f = lambda a: np.ascontiguousarray(np.asarray(a, dtype=np.float32))
    w_in = f(inp["w_in"])
    p32, _ = _rope_perm_sign(32)
    p64, _ = _rope_perm_sign(64)
    kr = w_in[:, :, OFF["krope"]:OFF["krope"] + 32][:, :, p32]
    qd = w_in[:, :, OFF["qd"]:OFF["qd"] + 512].reshape(2, D, 8, 64)[:, :, :, p64].reshape(2, D, 512)
    kd = w_in[:, :, OFF["kd"]:OFF["kd"] + 128].reshape(2, D, 2, 64)[:, :, :, p64].reshape(2, D, 128)
    w_perm = np.ascontiguousarray(np.concatenate([kr, qd, kd], axis=2))
    qg = f(inp["q_norm_g"])
    kg = f(inp["k_norm_g"])
    qk_g = np.ascontiguousarray(np.stack([qg, qg[:, p64], kg, kg[:, p64]], axis=2))
    cos64, sin64 = _rope_tables(64)
    cos32, sin32 = _rope_tables(32)
    rope_cos = np.ones((128, L), np.float32)
    rope_sin = np.zeros((128, L), np.float32)
    rope_cos[0:64] = cos64
    rope_sin[0:64] = sin64
    rope_cos[64:96] = cos32
    rope_sin[64:96] = sin32
    wqb = f(inp["w_q_b"])
    w_qb_perm = np.ascontiguousarray(wqb.reshape(2, 384, 8, 96)[:, :, :, 64:96][:, :, :, p32].reshape(2, 384, 256))
    qlg = f(inp["q_lat_norm_g"]).reshape(2, 3, 128).transpose(0, 2, 1)
    kvg = f(inp["kv_lat_norm_g"]).reshape(2, 2, 128).transpose(0, 2, 1)
    lat_g = np.ascontiguousarray(np.concatenate([qlg, kvg], axis=2))
    gla_g = np.ascontiguousarray(f(inp["gla_norm_g"]).reshape(2, 4, 128).transpose(0, 2, 1))
    cw = f(inp["conv_w"]).reshape(2, 5, 12, 128).transpose(0, 3, 2, 1)
    cb = f(inp["conv_b"]).reshape(2, 12, 128).transpose(0, 2, 1)[..., None]
    conv_p = np.ascontiguousarray(np.concatenate([cw, cb], axis=3))
    shared = dict(
        conv_p=conv_p, a_log=f(inp["a_log"]).reshape(2, 32), dt_bias=f(inp["dt_bias"]).reshape(2, 32),
        d_skip=f(inp["d_skip"]), ssd_norm_g=f(inp["ssd_norm_g"]),
        w_gate_up=f(inp["w_gate_up"]), b_gate=f(inp["b_gate"]), gla_g=gla_g,
        w_in=w_in, w_perm=w_perm, w_q_b=wqb, w_qb_perm=w_qb_perm, w_kv_b=f(inp["w_kv_b"]), lat_g=lat_g,
        norm_g=np.ascontiguousarray(np.concatenate([f(inp["norm_g"]), f(inp["final_g"])[None, :]], axis=0)),
        qk_g=qk_g,
        w_br_a=f(inp["w_br_a"]), w_br_b=f(inp["w_br_b"]), w_br_c=f(inp["w_br_c"]), w_br_d=f(inp["w_br_d"]),
        w_out=f(inp["w_out"]), rope_cos=rope_cos, rope_sin=rope_sin,
    )
    return shared


_CACHE = {}


def kernel(**inputs):
    x = np.ascontiguousarray(np.asarray(inputs["x"], dtype=np.float32))
    n_cores = 8
    nseq = x.shape[0] // n_cores
    shared = _prep_inputs(inputs)
    mk = MK(nseq=nseq, layers=(0, 1), branches="abcd")
    mk.build()
    in_maps = []
    for c in range(n_cores):
        m = dict(shared)
        m["x"] = np.ascontiguousarray(x[c * nseq:(c + 1) * nseq])
        in_maps.append(m)
    res = run_bass_kernel_spmd(mk.nc, in_maps, core_ids=list(range(n_cores)))
    out = np.concatenate([np.asarray(r["out"]) for r in res.results], axis=0)
    return out.astype(np.float32)
```

```python
import sys
import numpy as np
from contextlib import ExitStack
import concourse.bass as bass
import concourse.mybir as mybir
from concourse.bass_utils import run_bass_kernel_spmd

F32 = mybir.dt.float32
BF16 = mybir.dt.bfloat16
ALU = mybir.AluOpType
AF = mybir.ActivationFunctionType
AX = mybir.AxisListType

_ESZ = {F32: 4, BF16: 2}


def _esz(dt):
    return _ESZ.get(dt, 4)


def region(ap):
    a = ap.ap
    off = int(ap.offset)
    es = _esz(ap.dtype)
    name = ap.tensor.name
    sp = str(ap.space)
    if sp == "DRAM":
        ext = sum((c - 1) * abs(s) for s, c in a) + 1
        return (name, 0, 1, off * es, (off + ext) * es)
    pstep, pcnt = a[0]
    if pstep == 0:
        pstep = 1 << 40
    p0 = off // pstep
    f0 = off % pstep
    ext = sum((c - 1) * abs(s) for s, c in a[1:]) + 1
    if sp == "PSUM":
        b0 = (f0 * es) // 2048 * 2048
        b1 = ((f0 + ext) * es + 2047) // 2048 * 2048
        return (name, p0 // 32 * 32, (p0 + pcnt + 31) // 32 * 32, b0, b1)
    return (name, p0, p0 + pcnt, f0 * es, (f0 + ext) * es)


def _ovl(r, s):
    return r[1] < s[2] and s[1] < r[2] and r[3] < s[4] and s[3] < r[4]


def _covers(r, s):
    return r[1] <= s[1] and r[2] >= s[2] and r[3] <= s[3] and r[4] >= s[4]


class Op:
    __slots__ = ("eng", "fn", "deps", "signal", "dma", "eidx", "rank", "waits", "line", "phase")

    def __init__(self, eng, fn):
        self.eng = eng
        self.fn = fn
        self.deps = set()
        self.signal = False
        self.dma = None
        self.waits = []


ENGS = ("pe", "act", "dve", "pool", "sp")
_WRAPPERS = ("mm", "tr", "actv", "tt", "ts", "stt", "copy", "memset", "recip", "dma")
NDMASEM = 8
EPOCH = 20000


class Prog:
    def __init__(self, nc):
        self.nc = nc
        self.ops = []
        self.acc = {}
        self.ndma = 0
        self.phase = ""

    def add(self, eng, fn, reads, writes, dma=False):
        op = Op(eng, fn)
        op.phase = self.phase
        try:
            fr = sys._getframe(1)
            if fr.f_code.co_filename == __file__ and fr.f_code.co_name in _WRAPPERS:
                fr = fr.f_back
            op.line = fr.f_lineno
        except Exception:
            op.line = 0
        rec_eng = "dma" if dma else eng
        idx = len(self.ops)
        ops = self.ops
        for ap in reads:
            r = region(ap)
            lst = self.acc.setdefault(r[0], [])
            done = False
            is_psum = (r[0] == "ps")
            for rec in lst:
                if rec[2]:
                    if _ovl(rec[0], r):
                        op.deps.add(rec[1])
                elif (not done) and (not dma) and rec[3] == rec_eng and rec[0] == r:
                    rec[1] = idx
                    done = True
                elif is_psum and rec[3] != rec_eng and _ovl(rec[0], r):
                    op.deps.add(rec[1])
            if not done:
                lst.append([r, idx, False, rec_eng])
        for ap in writes:
            r = region(ap)
            lst = self.acc.setdefault(r[0], [])
            keep = []
            for rec in lst:
                if _ovl(rec[0], r):
                    if rec[1] != idx:
                        op.deps.add(rec[1])
                    if _covers(r, rec[0]) and rec[1] != idx:
                        continue
                keep.append(rec)
            keep.append([r, idx, True, rec_eng])
            self.acc[r[0]] = keep
        if eng == "pe":
            op.deps = {d for d in op.deps if ops[d].eng != "pe"}
        if dma:
            op.dma = self.ndma
            self.ndma += 1
        ops.append(op)
        return op

    def mm(self, out, lhsT, rhs, start=True, stop=True):
        self.add("pe", lambda e: e.matmul(out, lhsT, rhs, start=start, stop=stop),
                 [lhsT, rhs], [out])

    def tr(self, out, in_, ident):
        self.add("pe", lambda e: e.transpose(out, in_, ident), [in_, ident], [out])

    def actv(self, out, in_, func, bias=None, scale=None, accum_out=None):
        kw = {}
        rd = [in_]
        wr = [out]
        if bias is not None:
            kw["bias"] = bias
            if not isinstance(bias, (int, float)):
                rd.append(bias)
        if scale is not None:
            kw["scale"] = scale
            if not isinstance(scale, (int, float)):
                rd.append(scale)
        if accum_out is not None:
            kw["accum_out"] = accum_out
            wr.append(accum_out)
        self.add("act", lambda e: e.activation(out, in_, func, **kw), rd, wr)

    def _veng(self, eng):
        return eng

    def tt(self, eng, out, in0, in1, op):
        self.add(eng, lambda e: e.tensor_tensor(out, in0, in1, op), [in0, in1], [out])

    def ts(self, eng, out, in0, s1, s2, op0, op1=None, accum_out=None):
        rd = [in0]
        if not isinstance(s1, (int, float)):
            rd.append(s1)
        if s2 is not None and not isinstance(s2, (int, float)):
            rd.append(s2)
        wr = [out]
        kw = {}
        if accum_out is not None:
            kw["accum_out"] = accum_out
            wr.append(accum_out)
        if op1 is None:
            self.add(eng, lambda e: e.tensor_scalar(out, in0, s1, None, op0, **kw), rd, wr)
        else:
            self.add(eng, lambda e: e.tensor_scalar(out, in0, s1, s2, op0, op1, **kw), rd, wr)

    def stt(self, eng, out, in0, scalar, in1, op0, op1):
        rd = [in0, in1]
        if not isinstance(scalar, (int, float)):
            rd.append(scalar)
        self.add(eng, lambda e: e.scalar_tensor_tensor(out, in0, scalar, in1, op0, op1), rd, [out])

    def copy(self, eng, out, in_):
        if eng == "act":
            self.add("act", lambda e: e.activation(out, in_, AF.Copy), [in_], [out])
        else:
            self.add(eng, lambda e: e.tensor_copy(out, in_), [in_], [out])

    def memset(self, eng, out, val):
        self.add(eng, lambda e: e.memset(out, val), [], [out])

    def recip(self, out, in_):
        self.add("dve", lambda e: e.reciprocal(out, in_), [in_], [out])

    def dma(self, out, in_, eng="sp", **kw):
        self.add(eng, lambda e: e.dma_start(out, in_, **kw), [in_], [out], dma=True)

    def emit(self, es, final_wait_all=True):
        nc = self.nc
        ops = self.ops
        cnt = {e: 0 for e in ENGS}
        for op in ops:
            op.eidx = cnt[op.eng]
            cnt[op.eng] += 1
        for i, op in enumerate(ops):
            for d in op.deps:
                dop = ops[d]
                if dop.dma is None:
                    if dop.eng == op.eng and op.eng != "sp":
                        pass
                    dop.signal = True
        last = {}
        for i, op in enumerate(ops):
            if op.dma is None:
                last[op.eng] = i
        for e, i in last.items():
            ops[i].signal = True
        rk = {e: 0 for e in ENGS}
        for op in ops:
            if op.dma is None and op.signal:
                rk[op.eng] += 1
                op.rank = rk[op.eng]
        nep = {e: (rk[e] + EPOCH - 1) // EPOCH + 1 for e in ENGS}
        sems = {}
        for e in ENGS:
            if e == "sp":
                continue
            sems[e] = [es.enter_context(nc.semaphore(f"s_{e}_{k}")) for k in range(nep[e])]
        dsem = [es.enter_context(nc.semaphore(f"s_dma_{k}")) for k in range(NDMASEM)]

        import os as _os
        simmode = bool(_os.environ.get("SIMMODE"))
        dinfo = {}
        m = 0
        for op in ops:
            if op.dma is None:
                continue
            if simmode and op.eng == "pool":
                sem = es.enter_context(nc.semaphore(f"s_u_{op.dma}"))
                dinfo[op.dma] = (("udma", op.dma), sem, 16)
            else:
                k = m % NDMASEM
                dinfo[op.dma] = (("dma", k), dsem[k], 16 * (m // NDMASEM + 1))
                m += 1

        def target(dop):
            if dop.dma is not None:
                return dinfo[dop.dma]
            r = dop.rank - 1
            return ((dop.eng, r // EPOCH), sems[dop.eng][r // EPOCH], r % EPOCH + 1)

        waited = {e: {} for e in ENGS}
        for i, op in enumerate(ops):
            w = waited[op.eng]
            need = {}
            for d in op.deps:
                key, sem, val = target(ops[d])
                if w.get(key, 0) >= val:
                    continue
                if key not in need or need[key][1] < val:
                    need[key] = (sem, val)
            if op.dma is not None:
                key, sem, val = dinfo[op.dma]
                val -= 16
                if val > 0 and w.get(key, 0) < val and (key not in need or need[key][1] < val):
                    need[key] = (sem, val)
            for key, (sem, val) in need.items():
                w[key] = val
                op.waits.append((sem, val))
        final_waits = []
        for e, i in last.items():
            key, sem, val = target(ops[i])
            final_waits.append((sem, val))
        fin = {}
        for key, sem, val in dinfo.values():
            if key not in fin or fin[key][1] < val:
                fin[key] = (sem, val)
        final_waits.extend(fin.values())

        byeng = {e: [op for op in ops if op.eng == e] for e in ENGS}

        annotate = bool(_os.environ.get("SIMMODE"))

        def run(e, name):
            for op in byeng[name]:
                for sem, val in op.waits:
                    e.wait_ge(sem, val)
                ins = op.fn(e)
                if annotate:
                    ins.annotate(f"L{op.line}")
                if op.dma is not None:
                    ins.then_inc(dinfo[op.dma][1], 16)
                elif op.signal:
                    r = op.rank - 1
                    ins.then_inc(sems[name][r // EPOCH], 1)
            if name == "sp":
                for sem, val in final_waits:
                    e.wait_ge(sem, val)

        with nc.Block() as block:
            @block.tensor
            def _(e):
                run(e, "pe")

            @block.scalar
            def _(e):
                run(e, "act")

            @block.vector
            def _(e):
                run(e, "dve")

            @block.gpsimd
            def _(e):
                run(e, "pool")

            @block.sync
            def _(e):
                run(e, "sp")
        return cnt


L = 2048
D = 1024
NT = 16
NB = 4
EPS = 1e-6
OFF = dict(g=0, za=4096, xbc=5120, dt=6656, zb=6688, qlat=7200, kvlat=7584, krope=7840,
           zc=7872, qc=8384, kc=8640, vc=8896, glr=9408, zd=9440, qd=9952, kd=10464, vd=10592)
N_IN = 10720
KB = 1024
BULK = ("dve", "dve", "pool")


def _rope_perm_sign(d):
    m = d // 2
    hm = m // 2
    perm = np.zeros(d, np.int64)
    sign = np.zeros(d, np.float32)
    for j in range(d):
        jj = j % m
        if jj < hm:
            perm[j] = j + hm
            sign[j] = -1.0
        else:
            perm[j] = j - hm
            sign[j] = 1.0
    return perm, sign


def _rope_tables(d):
    rows = L // 64
    row = np.repeat(np.arange(rows), 64).astype(np.float32)
    col = np.tile(np.arange(64), rows).astype(np.float32)
    m = d // 2
    inv = (np.float32(10000.0) ** (-np.arange(0, m, 2, dtype=np.float32) / np.float32(m))).astype(np.float32)
    ang_r = row[:, None] * inv
    ang_c = col[:, None] * inv
    ang = np.concatenate([ang_r, ang_r, ang_c, ang_c], axis=-1).astype(np.float32)
    _, sign = _rope_perm_sign(d)
    cos = np.cos(ang).astype(np.float32).T
    sin = (np.sin(ang).astype(np.float32) * sign[None, :]).T
    return np.ascontiguousarray(cos), np.ascontiguousarray(sin)


class MK:
    def __init__(self, nseq=2, layers=(0, 1), branches="abcd", debug=False):
        self.nseq = nseq
        self.layers = layers
        self.branches = branches
        self.debug = debug
        nc = self.nc = bass.Bass("TRN2", target_bir_lowering=False)
        es = self.es = ExitStack()
        self.P = Prog(nc)
        self.dbg_outs = {}
        di = lambda n, s: nc.dram_tensor(n, s, F32, kind="ExternalInput").ap()
        self.x = di("x", [nseq, L, D])
        self.out = nc.dram_tensor("out", [nseq, L, D], F32, kind="ExternalOutput").ap()
        self.xres = nc.dram_tensor("xres", [nseq, L, D], F32).ap()
        self.w_in = di("w_in", [2, D, N_IN])
        self.w_perm = di("w_perm", [2, D, 672])
        self.norm_g = di("norm_g", [3, D])
        self.qk_g = di("qk_g", [2, 64, 4])
        self.w_q_b = di("w_q_b", [2, 384, 768])
        self.w_qb_perm = di("w_qb_perm", [2, 384, 256])
        self.w_kv_b = di("w_kv_b", [2, 256, 1024])
        self.lat_g = di("lat_g", [2, 128, 5])
        self.w_gate_up = di("w_gate_up", [2, 2, 16, 256])
        self.b_gate = di("b_gate", [2, 2, 256])
        self.gla_g = di("gla_g", [2, 128, 4])
        self.conv_p = di("conv_p", [2, 128, 12, 6])
        self.a_log = di("a_log", [2, 32])
        self.dt_bias = di("dt_bias", [2, 32])
        self.d_skip = di("d_skip", [2, 16])
        self.ssd_norm_g = di("ssd_norm_g", [2, D])
        self.yb_d = nc.dram_tensor("yb_d", [L, D], F32).ap()
        self.w_br_a = di("w_br_a", [2, 1024, D])
        self.w_br_b = di("w_br_b", [2, 512, D])
        self.w_br_c = di("w_br_c", [2, 512, D])
        self.w_br_d = di("w_br_d", [2, 512, D])
        self.w_out = di("w_out", [2, D, D])
        self.rope_cos = di("rope_cos", [128, L])
        self.rope_sin = di("rope_sin", [128, L])
        sb = lambda n, s, d: es.enter_context(nc.sbuf_tensor(n, s, d))
        self.hT = sb("hT", [128, 8, L], BF16)
        self.COS = sb("COS", [128, L], F32)
        self.SIN = sb("SIN", [128, L], F32)
        self.ident = sb("ident", [128, 128], BF16)
        self.identf = sb("identf", [128, 128], F32)
        self.ones = sb("ones", [128, 128], BF16)
        self.onesf = sb("onesf", [128, 128], F32)
        self.triF = sb("triF", [128, 128], F32)
        self.triB = sb("triB", [128, 128], F32)
        self.LF = sb("LF", [128, 128], F32)
        self.LB = sb("LB", [128, 128], F32)
        self.TmFb = sb("TmFb", [128, 128], BF16)
        self.TmBb = sb("TmBb", [128, 128], BF16)
        self.ARW = 155 * 256
        self.AR = sb("AR", [128, self.ARW], F32)
        self.ps = es.enter_context(nc.psum_tensor("ps", [128, 8, 512], F32))
        self.mixT = self.view(0, [8, L], BF16)
        self.onT = self.view(32 * KB, [8, L], BF16)
        self.acur = 64 * KB
        self.stage = self.view(147 * KB, [4, 512], F32)
        self.nst = 0
        self.pre = {}
        self.alim = 147 * KB

    def view(self, off, shape, dt, p0=0, parts=128):
        es_ = _esz(dt)
        n = int(np.prod(shape))
        assert off % 4 == 0
        w0 = off // 4
        w1 = (off + n * es_ + 3) // 4
        assert w1 <= self.ARW, (off, shape)
        ap = self.AR[p0:p0 + parts, w0:w1]
        if dt != F32:
            ap = ap.bitcast(dt)
        if len(shape) == 2:
            ap = ap.rearrange("p (a b) -> p a b", a=shape[0])
        elif len(shape) == 3:
            ap = ap.rearrange("p (a b c) -> p a b c", a=shape[0], b=shape[1])
        return ap

    def alloc(self, shape, dt, p0=0, parts=128):
        n = int(np.prod(shape)) * _esz(dt)
        n = (n + 63) // 64 * 64
        v = self.view(self.acur, shape, dt, p0, parts)
        self.acur += n
        assert self.acur <= self.alim or getattr(self, "allow_over", False), (self.acur, shape)
        return v

    def dbg(self, name, ap_sb, shape):
        if not self.debug:
            return
        d = self.nc.dram_tensor("dbg_" + name, shape, ap_sb.dtype, kind="ExternalOutput").ap()
        self.dbg_outs[name] = d
        self.P.dma(d, ap_sb)

    def wcols(self, src, l, c0, c1):
        return src[l].rearrange("(k p) c -> p k c", p=128)[:, :, c0:c1]

    def wload(self, dst, src, l, c0, c1, engs=("pool",)):
        P = self.P
        nk = dst.shape[1]
        C = c1 - c0
        for k in range(nk):
            for cc in range(0, C, 512):
                w = min(512, C - cc)
                st = self.stage[:, self.nst % 4, 0:w]
                self.nst += 1
                P.dma(st, src[l, k * 128:(k + 1) * 128, c0 + cc:c0 + cc + w])
                P.copy(engs[self.nst % len(engs)], dst[:, k, cc:cc + w], st)

    def setup(self):
        P = self.P
        P.dma(self.COS[:], self.rope_cos)
        P.dma(self.SIN[:], self.rope_sin)
        P.memset("pool", self.onesf[:], 1.0)
        P.memset("pool", self.ones[:], 1.0)
        P.memset("pool", self.identf[:], 0.0)
        idf = self.identf
        onf = self.onesf
        P.add("pool", lambda e: e.affine_select(idf[:], onf[:], [[-1, 128]], ALU.is_equal, 0.0,
                                                base=0, channel_multiplier=1), [onf[:]], [idf[:]])
        P.copy("dve", self.ident[:], self.identf[:])
        tf, tb = self.triF, self.triB
        P.add("pool", lambda e: e.affine_select(tf[:], onf[:], [[1, 128]], ALU.is_ge, 0.0,
                                                base=0, channel_multiplier=-1), [onf[:]], [tf[:]])
        P.add("pool", lambda e: e.affine_select(tb[:], onf[:], [[-1, 128]], ALU.is_ge, 0.0,
                                                base=0, channel_multiplier=1), [onf[:]], [tb[:]])
        P.ts("dve", self.TmFb[:], tf[:], -1.0 / 16, None, ALU.mult)
        P.tt("dve", self.LF[:], tb[:], self.identf[:], ALU.subtract)
        P.tt("dve", self.LB[:], tf[:], self.identf[:], ALU.subtract)
        P.ts("dve", self.TmBb[:], tb[:], -1.0 / 16, None, ALU.mult)

    def norm_to_hT(self, xt, t, gB, scr):
        P = self.P
        junk, ss, hb = scr
        i = t % 2
        P.actv(junk[:, :], xt, AF.Square, accum_out=ss[:, t:t + 1])
        P.actv(ss[:, 16 + t:17 + t], ss[:, t:t + 1], AF.Ln, bias=EPS, scale=1.0 / D)
        P.actv(ss[:, 32 + t:33 + t], ss[:, 16 + t:17 + t], AF.Exp, scale=-0.5)
        P.stt("dve", hb[:, i, :], xt, ss[:, 32 + t:33 + t], gB, ALU.mult, ALU.mult)
        pb = self.ps[:, 7, :].bitcast(BF16)
        for k in range(8):
            P.tr(pb[:, k * 128:(k + 1) * 128], hb[:, i, k * 128:(k + 1) * 128], self.ident[:])
        P.copy("act", self.hT[:, :, t * 128:(t + 1) * 128],
               pb.rearrange("p (k c) -> p k c", k=8))

    def phase_a(self, s, l):
        P = self.P
        self.P.phase = "A"
        self.pre["ssd"] = self.ssd_load(l)
        self.acur = 100 * KB
        xt = self.alloc([2, D], F32)
        gB = self.alloc([D], F32)
        junk = self.alloc([D], BF16)
        ss = self.alloc([48], F32)
        hb = self.alloc([2, D], BF16)
        P.dma(gB, bass.AP(self.norm_g.tensor, l * D, [[0, 128], [1, D]]))
        for t in range(NT):
            P.dma(xt[:, t % 2, :], self.x[s, t * 128:(t + 1) * 128, :])
            self.norm_to_hT(xt[:, t % 2, :], t, gB, (junk, ss, hb))

    def attn_units(self, kT_fn, qT, V_fn, ob, pT, scale):
        P = self.P
        ps = self.ps
        P.mm(ps[:, 0, :], kT_fn(0), qT)
        for kc in range(16):
            if kc + 1 < 16:
                P.mm(ps[:, (kc + 1) % 2, :], kT_fn(kc + 1), qT)
            P.actv(pT[:, kc % 2, :], ps[:, kc % 2, :], AF.Exp, scale=scale)
            P.mm(ps[:, ob, :], V_fn(kc), pT[:, kc % 2, :], start=(kc == 0), stop=(kc == 15))

    def qk_norm_rope(self, psA, psB, gtile, gc, out, b, tmp):
        P = self.P
        sq, rt, t1, t2 = tmp
        bl = slice(b * 512, (b + 1) * 512)
        P.actv(sq[0:64, :], psA, AF.Square)
        P.mm(self.ps[0:64, 6, :], self.ones[0:64, 0:64], sq[0:64, :])
        P.actv(rt[0:64, :], self.ps[0:64, 6, :], AF.Ln, bias=EPS, scale=1.0 / 64)
        P.actv(rt[0:64, :], rt[0:64, :], AF.Exp, scale=-0.5)
        P.stt("dve", t1[0:64, :], psA, gtile[0:64, gc:gc + 1], self.COS[0:64, bl], ALU.mult, ALU.mult)
        P.stt("dve", t2[0:64, :], psB, gtile[0:64, gc + 1:gc + 2], self.SIN[0:64, bl], ALU.mult, ALU.mult)
        P.tt("pool", t1[0:64, :], t1[0:64, :], t2[0:64, :], ALU.add)
        P.tt("pool", out, t1[0:64, :], rt[0:64, :], ALU.mult)

    def gqa_load(self, l):
        P = self.P
        self.acur = 108 * KB
        COSG = self.alloc([L], F32)
        SING = self.alloc([L], F32)
        wk2 = self.alloc([2, 8, 128], BF16)
        wkp2 = self.alloc([2, 8, 128], BF16)
        wv = self.alloc([8, 128], BF16)
        wq = self.alloc([2, 8, 128], BF16)
        wqp = self.alloc([2, 8, 128], BF16)
        wz = self.alloc([2, 8, 128], BF16)
        gt = self.alloc([4], F32)
        for g in range(2):
            for hlf in range(2):
                self.wload(wk2[:, g, :, 64 * hlf:64 * hlf + 64], self.w_in, l, OFF["kd"] + g * 64, OFF["kd"] + (g + 1) * 64, engs=BULK)
                self.wload(wkp2[:, g, :, 64 * hlf:64 * hlf + 64], self.w_perm, l, 544 + g * 64, 544 + (g + 1) * 64, engs=BULK)
        for hlf in range(2):
            rows = slice(64 * hlf, 64 * hlf + 64)
            P.dma(COSG[rows, :], self.rope_cos[0:64, :])
            P.dma(SING[rows, :], self.rope_sin[0:64, :])
            P.dma(gt[rows, :], self.qk_g[l])
        self.wload(wv, self.w_in, l, OFF["vd"], OFF["vd"] + 128, engs=BULK)
        self.wload(wq[:, 0], self.w_in, l, OFF["qd"], OFF["qd"] + 128, engs=BULK)
        self.wload(wqp[:, 0], self.w_perm, l, 32, 32 + 128, engs=BULK)
        self.wload(wz[:, 0], self.w_in, l, OFF["zd"], OFF["zd"] + 128, engs=BULK)
        return (COSG, SING, wk2, wkp2, wv, wq, wqp, wz, gt)

    def phase_gqa(self, l):
        P = self.P
        self.P.phase = "gqa_prep"
        ps = self.ps
        hT = self.hT
        onT = self.onT
        self.acur = 64 * KB
        BD = self.alloc([128], BF16)
        kT2 = self.alloc([2, L], BF16)
        Va = self.alloc([2, 16, 192], BF16)
        qT2 = self.alloc([2, 512], BF16)
        pT = self.alloc([2, 2, 512], BF16)
        sq = self.alloc([512], BF16)
        rt = self.alloc([512], F32)
        t1 = self.alloc([512], F32)
        t2 = self.alloc([512], F32)
        sz = self.alloc([2, 512], F32)
        ez = self.alloc([512], F32)
        rs = t1
        nt = t2
        osb = self.alloc([2, 512], F32)
        assert self.acur <= 108 * KB
        w = self.pre.pop("gqa", None) or self.gqa_load(l)
        COSG, SING, wk2, wkp2, wv, wq, wqp, wz, gt = w
        P.memset("pool", BD, 0.0)
        P.memset("pool", BD[0:64, 0:64], 1.0)
        P.memset("pool", BD[64:128, 64:128], 1.0)
        P.memset("pool", Va[:, :, :, 64:128], 1.0)

        def proj(wa, wb, bl):
            for k in range(8):
                P.mm(ps[:, 6, :], wa(k), hT[:, k, bl], k == 0, k == 7)
            for k in range(8):
                P.mm(ps[:, 7, :], wb(k), hT[:, k, bl], k == 0, k == 7)
            P.actv(sq, ps[:, 6, :], AF.Square)

        def rope1(gc, bl):
            P.stt("dve", t1, ps[:, 6, :], gt[:, gc:gc + 1], COSG[:, bl], ALU.mult, ALU.mult)
            P.stt("dve", t2, ps[:, 7, :], gt[:, gc + 1:gc + 2], SING[:, bl], ALU.mult, ALU.mult)

        def rope2(out):
            P.mm(ps[:, 7, :], BD, sq, True, True)
            P.actv(rt, ps[:, 7, :], AF.Ln, bias=EPS, scale=1.0 / 64)
            P.actv(rt, rt, AF.Exp, scale=-0.5)
            P.tt("pool", t1, t1, t2, ALU.add)
            P.tt("pool", out, t1, rt, ALU.mult)

        for g in range(2):
            for b in range(NB):
                bl = slice(b * 512, (b + 1) * 512)
                proj(lambda k: wk2[:, g, k, :], lambda k: wkp2[:, g, k, :], bl)
                rope1(2, bl)
                rope2(kT2[:, g, bl])
        for t in range(NT):
            for k in range(8):
                P.mm(ps[:, 4 + t % 2, 0:128], hT[:, k, t * 128:(t + 1) * 128], wv[:, k, :], k == 0, k == 7)
            for g in range(2):
                P.copy("act", Va[:, g, t, 0:64], ps[:, 4 + t % 2, g * 64:(g + 1) * 64])
                P.copy("dve", Va[:, g, t, 128:192], ps[:, 4 + t % 2, g * 64:(g + 1) * 64])
        P.phase = "gqa_heads"
        items = [(pr, b) for pr in range(4) for b in range(NB)]

        def prep_parts(it):
            pr, b = items[it]
            i = pr % 2
            bl = slice(b * 512, (b + 1) * 512)
            hooks = {}

            def add(kc, f):
                prev = hooks.get(kc)

                def both(prev=prev, f=f):
                    if prev is not None:
                        prev()
                    f()
                hooks[kc] = both

            def loads():
                if b == 0 and pr > 0:
                    self.wload(wq[:, i], self.w_in, l, OFF["qd"] + pr * 128, OFF["qd"] + (pr + 1) * 128)
                    self.wload(wqp[:, i], self.w_perm, l, 32 + pr * 128, 32 + (pr + 1) * 128)
                    self.wload(wz[:, i], self.w_in, l, OFF["zd"] + pr * 128, OFF["zd"] + (pr + 1) * 128)
            add(0, loads)

            def pk(k):
                P.mm(ps[:, 6, :], wq[:, i, k, :], hT[:, k, bl], k == 0, k == 7)
                P.mm(ps[:, 7, :], wqp[:, i, k, :], hT[:, k, bl], k == 0, k == 7)
            for k in range(8):
                add(k, lambda k=k: pk(k))

            def sq_rope1():
                P.actv(sq, ps[:, 6, :], AF.Square)
                rope1(0, bl)
            add(8, sq_rope1)
            add(9, lambda: rope2(qT2[:, it % 2, :]))
            for k in range(8):
                add(10 + min(k, 5), lambda k=k: P.mm(ps[:, 6, :], wz[:, i, k, :], hT[:, k, bl], k == 0, k == 7))

            def silu():
                P.actv(ez, ps[:, 6, :], AF.Exp, scale=-1.0)
                P.actv(ez, ez, AF.Ln, bias=1.0)
                P.actv(ez, ez, AF.Exp, scale=-1.0)
                P.tt("dve", sz[:, it % 2, :], ps[:, 6, :], ez, ALU.mult)
            add(15, silu)
            return hooks

        def units(it, hooks):
            pr, b = items[it]
            g = pr // 2
            bl = slice(b * 512, (b + 1) * 512)
            qb = qT2[:, it % 2, :]
            sbank = [[0, 1], [4, 5]]
            rows = [slice(0, 64), slice(64, 128)]
            vsel = [slice(0, 128), slice(64, 192)]
            for par in range(2):
                P.mm(ps[:, sbank[par][0], :], kT2[rows[par], g, 0:128], qb[rows[par], :])
            for kc in range(16):
                if kc + 1 < 16:
                    for par in range(2):
                        P.mm(ps[:, sbank[par][(kc + 1) % 2], :], kT2[rows[par], g, (kc + 1) * 128:(kc + 2) * 128], qb[rows[par], :])
                for par in range(2):
                    P.actv(pT[:, par, kc % 2, :], ps[:, sbank[par][kc % 2], :], AF.Exp, scale=0.125)
                for par in range(2):
                    P.mm(ps[:, 2 + par, :], Va[:, g, kc, vsel[par]], pT[:, par, kc % 2, :], kc == 0, kc == 15)
                if kc in hooks:
                    hooks[kc]()
            for par in range(2):
                P.copy("dve", osb[:, par, :], ps[:, 2 + par, :])
            for par in range(2):
                orow = rows[par]
                srow = rows[1 - par]
                P.recip(rs[orow, :], osb[srow, par, :])
                P.tt("dve", nt[orow, :], osb[orow, par, :], rs[orow, :], ALU.mult)
                P.tt("pool", onT[orow, pr, bl], nt[orow, :], sz[orow, it % 2, :], ALU.mult)

        h0 = prep_parts(0)
        for kc in sorted(h0):
            h0[kc]()
        for it in range(len(items)):
            hooks = prep_parts(it + 1) if it + 1 < len(items) else {}
            units(it, hooks)

    def mla_load(self, l):
        self.acur = 124 * KB
        wq = self.alloc([3, 768], BF16)
        wqp = self.alloc([3, 256], BF16)
        wkv = self.alloc([2, 1024], BF16)
        wql = self.alloc([8, 384], BF16)
        wkvl = self.alloc([8, 256], BF16)
        wkr = self.alloc([2, 8, 32], BF16)
        lg = self.alloc([5], F32)
        self.wload(wql, self.w_in, l, OFF["qlat"], OFF["qlat"] + 384, engs=BULK)
        self.wload(wkvl, self.w_in, l, OFF["kvlat"], OFF["kvlat"] + 256, engs=BULK)
        self.wload(wkr[:, 0], self.w_in, l, OFF["krope"], OFF["krope"] + 32, engs=BULK)
        self.wload(wkr[:, 1], self.w_perm, l, 0, 32, engs=BULK)
        self.wload(wq, self.w_q_b, l, 0, 768, engs=BULK)
        self.wload(wqp, self.w_qb_perm, l, 0, 256, engs=BULK)
        self.wload(wkv, self.w_kv_b, l, 0, 1024, engs=BULK)
        self.P.dma(lg, self.lat_g[l])
        return (wq, wqp, wkv, wql, wkvl, wkr, lg)

    def phase_mla(self, l):
        P = self.P
        self.P.phase = "mla_prep"
        ps = self.ps
        hT = self.hT
        onT = self.onT
        COS, SIN = self.COS, self.SIN
        self.acur = 64 * KB
        qln = self.alloc([3, L], BF16)
        kvn = self.alloc([2, L], BF16)
        krT = self.alloc([L], BF16)
        Va = self.alloc([16, 4, 192], BF16)
        qT = self.alloc([2, 512], BF16)
        pT = self.alloc([4, 512], BF16)
        sz = self.alloc([2, 512], F32)
        wz = self.alloc([2, 8, 64], BF16)
        assert self.acur <= 124 * KB
        self.acur = 48 * KB
        kT = self.alloc([2, L], BF16)
        t1 = self.alloc([512], F32)
        t2 = self.alloc([512], F32)
        sqc = self.alloc([2, 512], BF16)
        rt = self.alloc([512], F32)
        rs = t1
        assert self.acur <= 64 * KB
        w = self.pre.pop("mla", None) or self.mla_load(l)
        wq, wqp, wkv, wql, wkvl, wkr, lg = w
        P.memset("pool", Va[:, :, :, 64:128], 1.0)
        nsq = 0
        for b in range(NB):
            bl = slice(b * 512, (b + 1) * 512)
            for (wsrc, nch, bank0, sbank, dst, gofs, dim) in ((wql, 3, 0, 3, qln, 0, 384), (wkvl, 2, 4, 6, kvn, 3, 256)):
                for c in range(nch):
                    for k in range(8):
                        P.mm(ps[:, bank0 + c, :], wsrc[:, k, c * 128:(c + 1) * 128], hT[:, k, bl], k == 0, k == 7)
                for c in range(nch):
                    P.actv(sqc[:, nsq % 2, :], ps[:, bank0 + c, :], AF.Square)
                    P.mm(ps[:, sbank, :], self.ones[:, :], sqc[:, nsq % 2, :], c == 0, c == nch - 1)
                    nsq += 1
                P.actv(rt, ps[:, sbank, :], AF.Ln, bias=EPS, scale=1.0 / dim)
                P.actv(rt, rt, AF.Exp, scale=-0.5)
                for c in range(nch):
                    P.stt("dve", dst[:, c, bl], ps[:, bank0 + c, :], lg[:, gofs + c:gofs + c + 1], rt, ALU.mult, ALU.mult)
            for k in range(8):
                P.mm(ps[64:96, 7, :], wkr[:, 0, k, :], hT[:, k, bl], k == 0, k == 7)
            P.tt("dve", t1[64:96, :], ps[64:96, 7, :], COS[64:96, bl], ALU.mult)
            for k in range(8):
                P.mm(ps[64:96, 7, :], wkr[:, 1, k, :], hT[:, k, bl], k == 0, k == 7)
            P.tt("dve", t2[64:96, :], ps[64:96, 7, :], SIN[64:96, bl], ALU.mult)
            P.tt("pool", krT[64:96, bl], t1[64:96, :], t2[64:96, :], ALU.add)
        wv_view = wkv.rearrange("p c (h two d) -> p c h two d", h=8, two=2)
        for t in range(NT):
            tl = slice(t * 128, (t + 1) * 128)
            for c in range(2):
                P.mm(ps[:, 4 + t % 2, :].rearrange("p (h d) -> p h d", h=8), kvn[:, c, tl], wv_view[:, c, :, 1, :], c == 0, c == 1)
            pv = ps[:, 4 + t % 2, :].rearrange("p (j two d) -> p j two d", j=4, two=2)
            P.copy("act", Va[:, t, :, 0:64], pv[:, :, 0, :])
            P.copy("dve", Va[:, t, :, 128:192], pv[:, :, 1, :])
        scale = 96.0 ** -0.5
        P.phase = "mla_heads"
        items = [(h, b) for h in range(8) for b in range(NB)]
        ez = t2

        def head_prep(h):
            i = h % 2
            self.wload(wz[:, i], self.w_in, l, OFF["zb"] + h * 64, OFF["zb"] + (h + 1) * 64)
            for b in range(NB):
                bl = slice(b * 512, (b + 1) * 512)
                for c in range(2):
                    P.mm(ps[0:64, 6, :], wkv[:, c, h * 128:h * 128 + 64], kvn[:, c, bl], c == 0, c == 1)
                P.copy("dve", kT[0:64, i, bl], ps[0:64, 6, :])
            P.copy("pool", kT[64:96, i, :], krT[64:96, :])

        def prep_parts(it):
            h, b = items[it]
            i = h % 2
            par = h % 2
            orow = slice(64 * par, 64 * par + 64)
            bl = slice(b * 512, (b + 1) * 512)
            qb = qT[:, it % 2, :]

            def p0():
                if b == 0:
                    head_prep(h)

            def p1():
                for c in range(3):
                    P.mm(ps[0:96, 6, :], wq[:, c, h * 96:(h + 1) * 96], qln[:, c, bl], c == 0, c == 2)
                for c in range(3):
                    P.mm(ps[64:96, 7, :], wqp[:, c, h * 32:(h + 1) * 32], qln[:, c, bl], c == 0, c == 2)
                P.copy("dve", qb[0:64, :], ps[0:64, 6, :])
                P.tt("dve", t1[64:96, :], ps[64:96, 6, :], COS[64:96, bl], ALU.mult)
                P.tt("dve", t2[64:96, :], ps[64:96, 7, :], SIN[64:96, bl], ALU.mult)
                P.tt("pool", qb[64:96, :], t1[64:96, :], t2[64:96, :], ALU.add)

            def p3():
                for k in range(8):
                    P.mm(ps[orow, 7, :], wz[:, i, k, :], hT[:, k, bl], k == 0, k == 7)
                P.actv(ez[orow, :], ps[orow, 7, :], AF.Exp, scale=-1.0)
                P.actv(ez[orow, :], ez[orow, :], AF.Ln, bias=1.0)
                P.actv(ez[orow, :], ez[orow, :], AF.Exp, scale=-1.0)
                P.tt("dve", sz[orow, it % 2, :], ps[orow, 7, :], ez[orow, :], ALU.mult)

            return [p0, p1, p3]

        def units(it, hooks):
            h, b = items[it]
            i = h % 2
            par = h % 2
            pr = h // 2
            orow = slice(64 * par, 64 * par + 64)
            srow = slice(64 * (1 - par), 64 * (1 - par) + 64)
            vsel = slice(0, 128) if par == 0 else slice(64, 192)
            bl = slice(b * 512, (b + 1) * 512)
            qb = qT[:, it % 2, :]
            ob = 2 + it % 2
            sbanks = [[0, 1], [4, 5]]

            def S(kc):
                P.mm(ps[:, sbanks[(kc // 2) % 2][kc % 2], :], kT[0:96, i, kc * 128:(kc + 1) * 128], qb[0:96, :])
            S(0)
            S(1)
            for k2 in range(8):
                if k2 + 1 < 8:
                    S(2 * k2 + 2)
                    S(2 * k2 + 3)
                for kc in (2 * k2, 2 * k2 + 1):
                    P.actv(pT[:, kc % 4, :], ps[:, sbanks[(kc // 2) % 2][kc % 2], :], AF.Exp, scale=scale)
                for kc in (2 * k2, 2 * k2 + 1):
                    P.mm(ps[:, ob, :], Va[:, kc, pr, vsel], pT[:, kc % 4, :], kc == 0, kc == 15)
                if k2 in hooks:
                    hooks[k2]()
            P.recip(rs[orow, :], ps[srow, ob, :])
            P.tt("dve", rs[orow, :], ps[orow, ob, :], rs[orow, :], ALU.mult)
            P.tt("pool", onT[orow, pr, bl], rs[orow, :], sz[orow, it % 2, :], ALU.mult)

        for f in prep_parts(0):
            f()
        for it in range(len(items)):
            hooks = {}
            if it + 1 < len(items):
                pp = prep_parts(it + 1)
                hooks = {0: pp[0], 2: pp[1], 5: pp[2]}
            units(it, hooks)

    def gla_load(self, l):
        P = self.P
        self.acur = 48 * KB
        wvc = self.alloc([8, 512], BF16)
        wzc = self.view(48 * KB, [8, 512], BF16)
        wqc = self.alloc([8, 256], BF16)
        wkc = self.alloc([8, 256], BF16)
        assert self.acur <= 64 * KB
        self.acur = 143 * KB
        wgu = self.alloc([2, 256], BF16)
        bg = self.alloc([2, 256], BF16)
        wglr = self.alloc([2, 8, 32], BF16)
        gg = self.alloc([4], F32)
        self.wload(wqc, self.w_in, l, OFF["qc"], OFF["qc"] + 256, engs=BULK)
        self.wload(wkc, self.w_in, l, OFF["kc"], OFF["kc"] + 256, engs=BULK)
        self.wload(wvc, self.w_in, l, OFF["vc"], OFF["vc"] + 512, engs=BULK)
        self.wload(wglr[:, 0], self.w_in, l, OFF["glr"], OFF["glr"] + 32, engs=BULK)
        self.wload(wglr[:, 1, :, 0:16], self.w_in, l, OFF["glr"] + 16, OFF["glr"] + 32, engs=BULK)
        self.wload(wglr[:, 1, :, 16:32], self.w_in, l, OFF["glr"], OFF["glr"] + 16, engs=BULK)
        P.memset("pool", wgu[0:32], 0.0)
        st = self.stage[0:16, self.nst % 4, :]
        self.nst += 1
        P.dma(st.rearrange("p (d c) -> p d c", d=2), self.w_gate_up[l].rearrange("d r c -> r d c"))
        P.copy("pool", wgu[0:16], st.rearrange("p (d c) -> p d c", d=2))
        st = self.stage[0:1, self.nst % 4, :]
        self.nst += 1
        P.dma(st.rearrange("p (d c) -> p d c", d=2), self.b_gate[l:l + 1])
        P.copy("pool", bg[0:1], st.rearrange("p (d c) -> p d c", d=2))
        P.dma(gg, self.gla_g[l])
        return (wvc, wzc, wqc, wkc, wglr, wgu, bg, gg)

    def phase_gla(self, l):
        P = self.P
        self.P.phase = "gla_prep"
        ps = self.ps
        hT = self.hT
        onT = self.onT
        self.acur = 64 * KB
        qTc = self.alloc([2, L], BF16)
        kTc = self.alloc([2, L], BF16)
        Vc = self.alloc([16, 512], BF16)
        ob = self.alloc([4, L], BF16)
        glrT = self.alloc([2, L], BF16)
        oblk = self.alloc([4, 512], F32)
        S = self.alloc([2, 128], F32)
        Sbf = self.alloc([2, 128], BF16)
        gsp = self.alloc([256], F32)
        gh = self.alloc([256], BF16)
        gl = self.alloc([256], BF16)
        eg = self.alloc([2, 128], F32)
        eng = self.alloc([2, 128], F32)
        ek = self.alloc([2, 128], F32)
        qg = self.alloc([2, 2, 128], BF16)
        kg = self.alloc([2, 128], BF16)
        kend = self.alloc([2, 2, 128], BF16)
        attm = self.alloc([2, 4, 128], BF16)
        kendT = self.alloc([2, 128], BF16)
        glast = self.alloc([2], F32)
        cd = self.alloc([2, 2], F32)
        assert self.acur <= 143 * KB
        self.acur = 56 * KB
        sq = self.alloc([512], BF16)
        rt = self.alloc([512], F32)
        sz = self.alloc([512], F32)
        tmp = self.alloc([512], F32)
        assert self.acur <= 64 * KB
        w = self.pre.pop("gla", None) or self.gla_load(l)
        wvc, wzc, wqc, wkc, wglr, wgu, bg, gg = w
        n = 0
        for j in range(2):
            for (wsrc, dst) in ((wqc, qTc), (wkc, kTc)):
                for b in range(NB):
                    bl = slice(b * 512, (b + 1) * 512)
                    bank = 4 + n % 4
                    for k in range(8):
                        P.mm(ps[:, bank, :], wsrc[:, k, j * 128:(j + 1) * 128], hT[:, k, bl], k == 0, k == 7)
                    P.copy("act" if n % 2 else "dve", dst[:, j, bl], ps[:, bank, :])
                    n += 1
        for t in range(NT):
            tl = slice(t * 128, (t + 1) * 128)
            bank = 4 + n % 4
            for k in range(8):
                P.mm(ps[:, bank, :], hT[:, k, tl], wvc[:, k, :], k == 0, k == 7)
            P.copy("act" if n % 2 else "dve", Vc[:, t, :], ps[:, bank, :])
            n += 1
        for d in range(2):
            for b in range(NB):
                bl = slice(b * 512, (b + 1) * 512)
                bank = 4 + n % 4
                for k in range(8):
                    P.mm(ps[0:32, bank, :], wglr[:, d, k, :], hT[:, k, bl], k == 0, k == 7)
                P.copy("act" if n % 2 else "dve", glrT[0:32, d, bl], ps[0:32, bank, :])
                n += 1

        self.wload(wzc, self.w_in, l, OFF["zc"], OFF["zc"] + 512)

        def bank3(b, a):
            return ps[:, b, 0:a * 128].rearrange("p (a c) -> p a c", a=a)


        P.phase = "gla_scan"
        psC = bank3(1, 2)
        psA2 = [bank3(2, 2), bank3(3, 2)]
        psO2 = [bank3(4, 2), bank3(5, 2)]
        psT = ps[:, 6, :].bitcast(BF16)[:, 0:256].rearrange("p (a c) -> p a c", a=2)
        psU = bank3(7, 2)
        obv = ob.rearrange("p (j two) t -> p j two t", two=2)
        oblv = oblk.rearrange("p (j two) t -> p j two t", two=2)

        def front(d, ci, t):
            Tm = self.TmFb if d == 0 else self.TmBb
            tri = self.triF if d == 0 else self.triB
            last = 127 if d == 0 else 0
            x = ci % 2
            tl = slice(t * 128, (t + 1) * 128)
            P.mm(ps[:, 0, 0:256], glrT[0:32, d, tl], wgu[0:32, d, :], True, False)
            P.mm(ps[:, 0, 0:256], self.ones[0:1, 0:128], bg[0:1, d, :], False, True)
            P.actv(gsp, ps[:, 0, 0:256], AF.Exp, scale=-1.0)
            P.actv(gsp, gsp, AF.Ln, bias=1.0)
            P.copy("dve", gh, gsp)
            P.tt("dve", gl, gsp, gh, ALU.subtract)
            for j in range(2):
                P.mm(psC[:, j, :], gh[:, j * 128:(j + 1) * 128], Tm[:, :], True, False)
                P.mm(psC[:, j, :], gl[:, j * 128:(j + 1) * 128], Tm[:, :], False, True)
            P.copy("dve", glast, psC[:, :, last])
            P.actv(eg.rearrange("p a c -> p (a c)"), ps[:, 1, 0:256], AF.Exp)
            P.actv(eng.rearrange("p a c -> p (a c)"), ps[:, 1, 0:256], AF.Exp, scale=-1.0)
            for j in range(2):
                P.actv(ek[:, j, :], psC[:, j, :], AF.Exp, scale=-1.0, bias=glast[:, j:j + 1])
            P.actv(cd[:, x, :], glast, AF.Exp)
            P.stt("dve", qg[:, x], qTc[:, :, tl], 0.125, eg, ALU.mult, ALU.mult)
            P.tt("dve", kg, kTc[:, :, tl], eng, ALU.mult)
            P.tt("dve", kend[:, x], kTc[:, :, tl], ek, ALU.mult)
            tri_b = bass.AP(tri, 0, [[128, 128], [0, 2], [1, 128]])
            attv = attm[:, x].rearrange("p (j two) c -> p j two c", two=2)
            for par in range(2):
                r = slice(64 * par, 64 * par + 64)
                for j in range(2):
                    P.mm(psA2[par][:, j, :], kg[r, j, :], qg[r, x, j, :], True, True)
                P.tt("dve", attv[:, :, par, :], psA2[par], tri_b, ALU.mult)

        def back(d, ci, t):
            x = ci % 2
            tl = slice(t * 128, (t + 1) * 128)
            for par in range(2):
                r = slice(64 * par, 64 * par + 64)
                for j in range(2):
                    h = 2 * j + par
                    P.mm(psO2[par][:, j, :], Vc[:, t, h * 128:(h + 1) * 128], attm[:, x, h, :], True, ci == 0)
                    if ci > 0:
                        P.mm(psO2[par][:, j, :], Sbf[r, j, :], qg[r, x, j, :], False, True)
                if d == 1:
                    P.copy("act", obv[:, :, par, tl], psO2[par])
                else:
                    c4 = t % 4
                    P.tt("dve", oblv[:, :, par, c4 * 128:(c4 + 1) * 128], psO2[par], obv[:, :, par, tl], ALU.add)
            if ci < NT - 1:
                for j in range(2):
                    P.tr(psT[:, j, :], kend[:, x, j, :], self.ident[:])
                P.copy("act", kendT, psT)
                for h in range(4):
                    j = h // 2
                    r = slice(64 * (h % 2), 64 * (h % 2) + 64)
                    P.mm(psU[r, j, :], kendT[:, j, 64 * (h % 2):64 * (h % 2) + 64], Vc[:, t, h * 128:(h + 1) * 128], True, True)
                for j in range(2):
                    if ci > 0:
                        P.stt("dve", S[:, j, :], S[:, j, :], cd[:, x, j:j + 1], psU[:, j, :], ALU.mult, ALU.add)
                    else:
                        P.copy("dve", S[:, j, :], psU[:, j, :])
                P.copy("pool", Sbf, S)
            if d == 0 and t % 4 == 3:
                b = t // 4
                bl = slice(b * 512, (b + 1) * 512)
                for h in range(4):
                    P.actv(sq, oblk[:, h, :], AF.Square)
                    P.mm(ps[:, 0, :], self.ones[:, :], sq, True, True)
                    P.actv(rt, ps[:, 0, :], AF.Ln, bias=EPS, scale=1.0 / 128)
                    P.actv(rt, rt, AF.Exp, scale=-0.5)
                    for k in range(8):
                        P.mm(ps[:, 1, :], wzc[:, k, h * 128:(h + 1) * 128], hT[:, k, bl], k == 0, k == 7)
                    P.actv(sz, ps[:, 1, :], AF.Silu)
                    P.stt("dve", tmp, oblk[:, h, :], gg[:, h:h + 1], rt, ALU.mult, ALU.mult)
                    P.tt("pool", onT[:, h, bl], tmp, sz, ALU.mult)

        for d in (1, 0):
            order = list(range(NT)) if d == 0 else list(range(NT - 1, -1, -1))
            front(d, 0, order[0])
            for ci, t in enumerate(order):
                if ci + 1 < NT:
                    front(d, ci + 1, order[ci + 1])
                back(d, ci, t)

    def ssd_load(self, l):
        P = self.P
        self.acur = 64 * KB
        wz = self.alloc([8, D], BF16)
        wdt = self.alloc([8, 32], BF16)
        anb = self.alloc([32], F32)
        dtb = self.alloc([32], F32)
        dsk = self.alloc([16], F32)
        gnb = self.alloc([D], F32)
        cp = self.alloc([12, 6], F32)
        self.ssd_end = self.acur
        P.dma(cp, self.conv_p[l])
        P.dma(anb, bass.AP(self.a_log.tensor, l * 32, [[0, 128], [1, 32]]))
        P.dma(dtb, bass.AP(self.dt_bias.tensor, l * 32, [[0, 128], [1, 32]]))
        P.dma(dsk, bass.AP(self.d_skip.tensor, l * 16, [[0, 128], [1, 16]]))
        P.dma(gnb, bass.AP(self.ssd_norm_g.tensor, l * D, [[0, 128], [1, D]]))
        self.wload(wdt, self.w_in, l, OFF["dt"], OFF["dt"] + 32, engs=BULK)
        self.wload(wz, self.w_in, l, OFF["za"], OFF["za"] + D, engs=BULK)
        return (wz, wdt, cp, anb, dtb, dsk, gnb)

    def phase_ssd(self, l):
        P = self.P
        self.P.phase = "ssd_conv"
        ps = self.ps
        hT = self.hT
        onT = self.onT
        xs_tok = self.view(0, [16, D], BF16)
        w = self.pre.pop("ssd", None) or self.ssd_load(l)
        wz, wdt, cp, anb, dtb, dsk, gnb = w
        self.acur = self.ssd_end
        BT = self.alloc([2, L], BF16)
        CT = self.alloc([2, L], BF16)
        Btok = self.alloc([16, 256], BF16)
        dt_all = self.alloc([16, 32], F32)
        dtaf = self.alloc([16, 32], F32)
        base = self.acur
        xb = self.alloc([L + 4], F32)
        acc = self.alloc([L], F32)
        cv = self.alloc([L], BF16)
        wx = self.alloc([2, 8, 128], BF16)
        P.actv(anb, anb, AF.Exp)
        P.ts("dve", anb, anb, -1.0, None, ALU.mult)
        P.memset("pool", xb[:, 0:2], 0.0)
        P.memset("pool", xb[:, L + 2:L + 4], 0.0)
        for c in range(12):
            i = c % 2
            self.wload(wx[:, i], self.w_in, l, OFF["xbc"] + c * 128, OFF["xbc"] + (c + 1) * 128)
            for b in range(NB):
                bank = 4 + (c * 4 + b) % 4
                for k in range(8):
                    P.mm(ps[:, bank, :], wx[:, i, k, :], hT[:, k, b * 512:(b + 1) * 512], k == 0, k == 7)
                P.copy("act", xb[:, 2 + b * 512:2 + (b + 1) * 512], ps[:, bank, :])
            eng = "dve"
            P.ts(eng, acc, xb[:, 0:L], cp[:, c, 0:1], cp[:, c, 5:6], ALU.mult, ALU.add)
            for j in range(1, 5):
                P.stt(eng, acc, xb[:, j:j + L], cp[:, c, j:j + 1], acc, ALU.mult, ALU.add)
            if c < 8:
                P.actv(cv, acc, AF.Silu)
                dst_tok = lambda t, c=c: xs_tok[:, t, c * 128:(c + 1) * 128]
                src = cv
            elif c < 10:
                P.actv(BT[:, c - 8, :], acc, AF.Silu)
                dst_tok = lambda t, c=c: Btok[:, t, (c - 8) * 128:(c - 7) * 128]
                src = BT[:, c - 8, :]
            else:
                P.actv(CT[:, c - 10, :], acc, AF.Silu)
                src = None
            if src is not None:
                for half in range(2):
                    pb = ps[:, 2 + half, :].bitcast(BF16)
                    for tt_ in range(8):
                        t = half * 8 + tt_
                        P.tr(pb[:, tt_ * 128:(tt_ + 1) * 128], src[:, t * 128:(t + 1) * 128], self.ident[:])
                    if c < 8:
                        P.copy("act" if half else "dve", xs_tok[:, half * 8:(half + 1) * 8, c * 128:(c + 1) * 128],
                               pb.rearrange("p (t c) -> p t c", t=8))
                    else:
                        P.copy("act" if half else "dve", Btok[:, half * 8:(half + 1) * 8, (c - 8) * 128:(c - 7) * 128],
                               pb.rearrange("p (t c) -> p t c", t=8))
        P.phase = "ssd_scan"
        for t in range(NT):
            for k in range(8):
                P.mm(ps[:, 7, t * 32:(t + 1) * 32], hT[:, k, t * 128:(t + 1) * 128], wdt[:, k, :], k == 0, k == 7)
        dtb_b = bass.AP(dtb.tensor, dtb.offset, [list(dtb.ap[0]), [0, 16], [1, 32]])
        anb_b = bass.AP(anb.tensor, anb.offset, [list(anb.ap[0]), [0, 16], [1, 32]])
        P.tt("dve", dt_all, ps[:, 7, :].rearrange("p (t c) -> p t c", t=16), dtb_b, ALU.add)
        P.actv(dt_all, dt_all, AF.Exp)
        P.actv(dt_all, dt_all, AF.Ln, bias=1.0)
        self.acur = base
        self.allow_over = True
        Af = self.alloc([1, 8, 128], F32)
        E = self.alloc([1, 8, 128], BF16)
        W = self.alloc([1, 8, 128], BF16)
        Gm = self.alloc([2, 128], BF16)
        xd = self.alloc([D], BF16)
        xdd = self.alloc([D], BF16)
        S = self.alloc([D], F32)
        Sbf = self.alloc([D], BF16)
        ytmp = self.alloc([D], F32)
        y2 = self.alloc([2, D], F32)
        ybt = self.alloc([D], F32)
        eac = self.alloc([16], F32)
        cdb = self.alloc([16], F32)
        sz = self.alloc([D], F32)
        dta = sz[:, 0:512].rearrange("p (t c) -> p t c", t=16)
        junk = ytmp
        hb = xdd
        ss = self.alloc([4], F32)
        P.tt("dve", dta, dt_all, anb_b, ALU.mult)
        P.copy("dve", dtaf, dta)

        def hv(ap2):
            return ap2.rearrange("p (h q) -> p h q", h=16)

        def bc_h(ap_col16, n):
            return bass.AP(ap_col16.tensor, ap_col16.offset, [list(ap_col16.ap[0]), [ap_col16.ap[1][0], 16], [0, n]])

        def bc_h2(ap16, g):
            return bass.AP(ap16.tensor, ap16.offset + g * 8, [list(ap16.ap[0]), [ap16.ap[1][0], 8], [0, 128]])

        def fin_stages(ci, t):
            y = y2[:, ci % 2, :]
            tl = slice(t * 128, (t + 1) * 128)
            pz = ps[:, 0:2, :].rearrange("p b c -> p (b c)")

            def f1a():
                P.tt("dve", hv(ytmp), hv(xs_tok[:, t, :]), bc_h(dsk, 64), ALU.mult)
                P.tt("dve", y, y, ytmp, ALU.add)

            def f1b():
                for hf in range(2):
                    for k in range(8):
                        P.mm(ps[:, hf, :], hT[:, k, tl], wz[:, k, hf * 512:(hf + 1) * 512], k == 0, k == 7)

            def f2():
                P.actv(sz, pz, AF.Silu)

            def f3():
                P.tt("dve", y, y, sz, ALU.mult)

            def f4():
                P.actv(junk, y, AF.Square, accum_out=ss[:, 0:1])
                P.actv(ss[:, 1:2], ss[:, 0:1], AF.Ln, bias=EPS, scale=1.0 / D)
                P.actv(ss[:, 2:3], ss[:, 1:2], AF.Exp, scale=-0.5)

            def f5():
                P.stt("dve", hb, y, ss[:, 2:3], gnb, ALU.mult, ALU.mult)
                pb = ps[:, 7, :].bitcast(BF16)
                for k in range(8):
                    P.tr(pb[:, k * 128:(k + 1) * 128], hb[:, k * 128:(k + 1) * 128], self.ident[:])
                P.copy("act", onT[:, :, tl], pb.rearrange("p (k c) -> p k c", k=8))

            return [f1a, f1b, f2, f3, f4, f5]

        def main_stages(d, ci, t):
            Lm = self.LF if d == 0 else self.LB
            Tm = self.triF if d == 0 else self.triB
            tri = self.triF if d == 0 else self.triB
            last = 127 if d == 0 else 0
            y = y2[:, ci % 2, :]
            tl = slice(t * 128, (t + 1) * 128)
            dh = dtaf[:, t, d * 16:(d + 1) * 16]

            def m1():
                if d == 0:
                    P.dma(ybt, self.yb_d[tl, :])
                for g in range(2):
                    P.mm(ps[:, 4, g * 128:(g + 1) * 128], BT[:, g, tl], CT[:, g, tl], True, True)
                P.mm(ps[:, 4, 256:272], Tm[:, :], dh, True, True)
                P.mm(ps[:, 4, 272:288], self.onesf[:, :], dh, True, True)
                tri_b = bass.AP(tri, 0, [[128, 128], [0, 2], [1, 128]])
                P.tt("dve", Gm, ps[:, 4, 0:256].rearrange("p (g c) -> p g c", g=2), tri_b, ALU.mult)
                P.actv(eac, ps[:, 4, 256:272], AF.Exp)
                P.actv(cdb, ps[:, 4, 272:288], AF.Exp)
                P.tt("dve", hv(xd), hv(xs_tok[:, t, :]), bc_h(dt_all[:, t, d * 16:(d + 1) * 16], 64), ALU.mult)

            def mg(g):
                def f():
                    Lb = bass.AP(Lm, 0, [[128, 128], [0, 8], [1, 128]])
                    P.tt("pool", Af[:, 0], Lb, bc_h2(dh, g), ALU.mult)
                    for e in range(8):
                        out = ps[:, 2 * g + e // 4, (e % 4) * 128:(e % 4 + 1) * 128]
                        P.mm(out, Af[:, 0, e, :], Tm[:, :], True, True)
                    P.actv(E[:, 0], ps[:, 2 * g:2 * g + 2, :].rearrange("p b (e c) -> p (b e) c", e=4), AF.Exp)
                    Gb = bass.AP(Gm.tensor, Gm.offset + g * 128, [list(Gm.ap[0]), [0, 8], [1, 128]])
                    P.tt("dve", W[:, 0], E[:, 0], Gb, ALU.mult)
                    Elast = bass.AP(E.tensor, E.offset + last, [list(E.ap[0]), [128, 8], [0, 64]])
                    P.tt("dve", hv(xdd)[:, g * 8:(g + 1) * 8, :], hv(xd)[:, g * 8:(g + 1) * 8, :], Elast, ALU.mult)
                    for e in range(8):
                        h = g * 8 + e
                        P.mm(ps[:, 5 + g, e * 64:(e + 1) * 64], W[:, 0, e, :], xd[:, h * 64:(h + 1) * 64], True, True)
                return f

            def m4():
                if ci > 0:
                    for g in range(2):
                        P.mm(ps[:, g, :], CT[:, g, tl], Sbf[:, g * 512:(g + 1) * 512], True, True)
                    P.tt("dve", hv(ytmp), ps[:, 0:2, :].rearrange("p b (e q) -> p (b e) q", e=8), bc_h(eac, 64), ALU.mult)
                    P.tt("dve", y, ytmp, ps[:, 5:7, :].rearrange("p b c -> p (b c)"), ALU.add)
                else:
                    P.copy("act", y, ps[:, 5:7, :].rearrange("p b c -> p (b c)"))

            def m5():
                if ci < NT - 1:
                    for g in range(2):
                        P.mm(ps[:, 2 + g, :], Btok[:, t, g * 128:(g + 1) * 128], xdd[:, g * 512:(g + 1) * 512], True, True)
                    if ci > 0:
                        P.tt("pool", hv(S), hv(S), bc_h(cdb, 64), ALU.mult)
                        P.tt("dve", S, S, ps[:, 2:4, :].rearrange("p b c -> p (b c)"), ALU.add)
                    else:
                        P.copy("act", S, ps[:, 2:4, :].rearrange("p b c -> p (b c)"))
                    P.copy("act", Sbf, S)

            def m6():
                if d == 1:
                    P.dma(self.yb_d[tl, :], y)
                else:
                    P.tt("dve", y, y, ybt, ALU.add)

            return [m1, mg(0), mg(1), m4, m5, m6]

        for d in (1, 0):
            order = range(NT) if d == 0 else range(NT - 1, -1, -1)
            pending = None
            for ci, t in enumerate(order):
                ms = main_stages(d, ci, t)
                fs = fin_stages(*pending) if pending is not None else []
                for i in range(6):
                    ms[i]()
                    if i < len(fs):
                        fs[i]()
                if d == 0:
                    pending = (ci, t)
            if pending is not None:
                for f in fin_stages(*pending):
                    f()
        self.allow_over = False

    def merge(self, l, bi, nk, wbr_src, first, pre=None):
        P = self.P
        self.P.phase = "merge"
        ps = self.ps
        hT = self.hT
        onT = self.onT
        self.acur = 64 * KB
        wbr = self.alloc([nk, D], BF16)
        wg = self.alloc([2, 8, 128], BF16)
        sg = self.alloc([2, 512], F32)
        tm = self.alloc([2, 512], F32)
        self.wload(wbr, wbr_src, l, 0, D, engs=BULK)
        self.wload(wg[:, 0], self.w_in, l, bi * D, bi * D + 128)
        if pre is not None:
            save = self.acur
            pre()
            self.acur = save
        n = 0
        for m in range(8):
            if m > 0:
                self.wload(wg[:, m % 2], self.w_in, l, bi * D + m * 128, bi * D + (m + 1) * 128)
            for b in range(NB):
                bl = slice(b * 512, (b + 1) * 512)
                yb = 4 + n % 2
                gb = 6 + n % 2
                for k in range(nk):
                    P.mm(ps[:, yb, :], wbr[:, k, m * 128:(m + 1) * 128], onT[:, k, bl], k == 0, k == nk - 1)
                for k in range(8):
                    P.mm(ps[:, gb, :], wg[:, m % 2, k, :], hT[:, k, bl], k == 0, k == 7)
                P.actv(sg[:, n % 2, :], ps[:, gb, :], AF.Sigmoid)
                if first:
                    P.tt("dve", self.mixT[:, m, bl], ps[:, yb, :], sg[:, n % 2, :], ALU.mult)
                else:
                    P.tt("dve", tm[:, n % 2, :], ps[:, yb, :], sg[:, n % 2, :], ALU.mult)
                    P.tt("pool", self.mixT[:, m, bl], self.mixT[:, m, bl], tm[:, n % 2, :], ALU.add)
                n += 1

    def outproj_load(self, l, last):
        self.acur = 96 * KB
        wo = self.alloc([8, D], BF16)
        gB = self.alloc([D], F32)
        self.wload(wo, self.w_out, l, 0, D, engs=BULK)
        grow = 2 if last else l + 1
        self.P.dma(gB, bass.AP(self.norm_g.tensor, grow * D, [[0, 128], [1, D]]))
        return (wo, gB)

    def outproj(self, s, l, last):
        P = self.P
        self.P.phase = "outproj"
        ps = self.ps
        w = self.pre.pop("outproj", None) or self.outproj_load(l, last)
        wo, gB = w
        self.acur = 116 * KB
        xt = self.alloc([2, D], F32)
        xn = self.alloc([2, D], F32)
        junk = self.alloc([D], BF16)
        ss = self.alloc([48], F32)
        hb = self.alloc([2, D], BF16)
        yo = self.alloc([2, D], F32)
        if not last:
            self.pre["ssd"] = self.ssd_load(self.layers[self.layers.index(l) + 1])
        xsrc = self.x if l == 0 else self.xres
        def finish(t):
            i = t % 2
            tl = slice(t * 128, (t + 1) * 128)
            if not last:
                P.dma(self.xres[s, tl, :], xn[:, i, :])
                self.norm_to_hT(xn[:, i, :], t, gB, (junk, ss, hb))
            else:
                P.actv(junk[:, :], xn[:, i, :], AF.Square, accum_out=ss[:, t:t + 1])
                P.actv(ss[:, 16 + t:17 + t], ss[:, t:t + 1], AF.Ln, bias=EPS, scale=1.0 / D)
                P.actv(ss[:, 32 + t:33 + t], ss[:, 16 + t:17 + t], AF.Exp, scale=-0.5)
                P.stt("dve", yo[:, i, :], xn[:, i, :], ss[:, 32 + t:33 + t], gB, ALU.mult, ALU.mult)
                P.dma(self.out[s, tl, :], yo[:, i, :])

        P.dma(xt[:, 0, :], xsrc[s, 0:128, :])
        for t in range(NT):
            i = t % 2
            tl = slice(t * 128, (t + 1) * 128)
            if t + 1 < NT:
                P.dma(xt[:, (t + 1) % 2, :], xsrc[s, (t + 1) * 128:(t + 2) * 128, :])
            for hf in range(2):
                bank = 4 + hf
                for k in range(8):
                    P.mm(ps[:, bank, :], self.mixT[:, k, tl], wo[:, k, hf * 512:(hf + 1) * 512], k == 0, k == 7)
                P.tt("dve", xn[:, i, hf * 512:(hf + 1) * 512], ps[:, bank, :], xt[:, i, hf * 512:(hf + 1) * 512], ALU.add)
            if t >= 1:
                finish(t - 1)
        finish(NT - 1)

    def build(self):
        self.setup()
        for s in range(self.nseq):
            nl = len(self.layers)
            for li, l in enumerate(self.layers):
                if li == 0:
                    self.phase_a(s, l)
                last = (li == nl - 1)
                order = [b for b in "abcd" if b in self.branches]
                phase = {"a": self.phase_ssd, "b": self.phase_mla, "c": self.phase_gla, "d": self.phase_gqa}
                loader = {"b": ("mla", self.mla_load), "c": ("gla", self.gla_load), "d": ("gqa", self.gqa_load)}
                mrg = {"a": (0, 8, self.w_br_a), "b": (1, 4, self.w_br_b), "c": (2, 4, self.w_br_c), "d": (3, 4, self.w_br_d)}
                for bi_, br in enumerate(order):
                    phase[br](l)
                    if bi_ + 1 < len(order):
                        key, fn = loader[order[bi_ + 1]]
                        pre = (lambda key=key, fn=fn: self.pre.__setitem__(key, fn(l)))
                    else:
                        pre = (lambda: self.pre.__setitem__("outproj", self.outproj_load(l, last)))
                    i_, nk_, wsrc_ = mrg[br]
                    self.merge(l, i_, nk_, wsrc_, bi_ == 0, pre=pre)
                self.outproj(s, l, last=(li == nl - 1))
        cnt = self.P.emit(self.es)
        return cnt


def _prep_inputs(inp):
    f = lambda a: np.ascontiguousarray(np.asarray(a, dtype=np.float32))
    w_in = f(inp["w_in"])
    p32, _ = _rope_perm_sign(32)
    p64, _ = _rope_perm_sign(64)
    kr = w_in[:, :, OFF["krope"]:OFF["krope"] + 32][:, :, p32]
    qd = w_in[:, :, OFF["qd"]:OFF["qd"] + 512].reshape(2, D, 8, 64)[:, :, :, p64].reshape(2, D, 512)
    kd = w_in[:, :, OFF["kd"]:OFF["kd"] + 128].reshape(2, D, 2, 64)[:, :, :, p64].reshape(2, D, 128)
    w_perm = np.ascontiguousarray(np.concatenate([kr, qd, kd], axis=2))
    qg = f(inp["q_norm_g"])
    kg = f(inp["k_norm_g"])
    qk_g = np.ascontiguousarray(np.stack([qg, qg[:, p64], kg, kg[:, p64]], axis=2))
    cos64, sin64 = _rope_tables(64)
    cos32, sin32 = _rope_tables(32)
    rope_cos = np.ones((128, L), np.float32)
    rope_sin = np.zeros((128, L), np.float32)
    rope_cos[0:64] = cos64
    rope_sin[0:64] = sin64
    rope_cos[64:96] = cos32
    rope_sin[64:96] = sin32
    wqb = f(inp["w_q_b"])
    w_qb_perm = np.ascontiguousarray(wqb.reshape(2, 384, 8, 96)[:, :, :, 64:96][:, :, :, p32].reshape(2, 384, 256))
    qlg = f(inp["q_lat_norm_g"]).reshape(2, 3, 128).transpose(0, 2, 1)
    kvg = f(inp["kv_lat_norm_g"]).reshape(2, 2, 128).transpose(0, 2, 1)
    lat_g = np.ascontiguousarray(np.concatenate([qlg, kvg], axis=2))
    gla_g = np.ascontiguousarray(f(inp["gla_norm_g"]).reshape(2, 4, 128).transpose(0, 2, 1))
    cw = f(inp["conv_w"]).reshape(2, 5, 12, 128).transpose(0, 3, 2, 1)
    cb = f(inp["conv_b"]).reshape(2, 12, 128).transpose(0, 2, 1)[..., None]
    conv_p = np.ascontiguousarray(np.concatenate([cw, cb], axis=3))
    shared = dict(
        conv_p=conv_p, a_log=f(inp["a_log"]).reshape(2, 32), dt_bias=f(inp["dt_bias"]).reshape(2, 32),
        d_skip=f(inp["d_skip"]), ssd_norm_g=f(inp["ssd_norm_g"]),
        w_gate_up=f(inp["w_gate_up"]), b_gate=f(inp["b_gate"]), gla_g=gla_g,
        w_in=w_in, w_perm=w_perm, w_q_b=wqb, w_qb_perm=w_qb_perm, w_kv_b=f(inp["w_kv_b"]), lat_g=lat_g,
        norm_g=np.ascontiguousarray(np.concatenate([f(inp["norm_g"]), f(inp["final_g"])[None, :]], axis=0)),
        qk_g=qk_g,
        w_br_a=f(inp["w_br_a"]), w_br_b=f(inp["w_br_b"]), w_br_c=f(inp["w_br_c"]), w_br_d=f(inp["w_br_d"]),
        w_out=f(inp["w_out"]), rope_cos=rope_cos, rope_sin=rope_sin,
    )
    return shared


_CACHE = {}


def kernel(**inputs):
    x = np.ascontiguousarray(np.asarray(inputs["x"], dtype=np.float32))
    n_cores = 8
    nseq = x.shape[0] // n_cores
    shared = _prep_inputs(inputs)
    mk = MK(nseq=nseq, layers=(0, 1), branches="abcd")
    mk.build()
    in_maps = []
    for c in range(n_cores):
        m = dict(shared)
        m["x"] = np.ascontiguousarray(x[c * nseq:(c + 1) * nseq])
        in_maps.append(m)
    res = run_bass_kernel_spmd(mk.nc, in_maps, core_ids=list(range(n_cores)))
    out = np.concatenate([np.asarray(r["out"]) for r in res.results], axis=0)
    return out.astype(np.float32)
```

```python
import sys
import numpy as np
from contextlib import ExitStack
import concourse.bass as bass
import concourse.mybir as mybir
from concourse.bass_utils import run_bass_kernel_spmd

F32 = mybir.dt.float32
BF16 = mybir.dt.bfloat16
ALU = mybir.AluOpType
AF = mybir.ActivationFunctionType
AX = mybir.AxisListType

_ESZ = {F32: 4, BF16: 2}


def _esz(dt):
    return _ESZ.get(dt, 4)


def region(ap):
    a = ap.ap
    off = int(ap.offset)
    es = _esz(ap.dtype)
    name = ap.tensor.name
    sp = str(ap.space)
    if sp == "DRAM":
        ext = sum((c - 1) * abs(s) for s, c in a) + 1
        return (name, 0, 1, off * es, (off + ext) * es)
    pstep, pcnt = a[0]
    if pstep == 0:
        pstep = 1 << 40
    p0 = off // pstep
    f0 = off % pstep
    ext = sum((c - 1) * abs(s) for s, c in a[1:]) + 1
    if sp == "PSUM":
        b0 = (f0 * es) // 2048 * 2048
        b1 = ((f0 + ext) * es + 2047) // 2048 * 2048
        return (name, p0 // 32 * 32, (p0 + pcnt + 31) // 32 * 32, b0, b1)
    return (name, p0, p0 + pcnt, f0 * es, (f0 + ext) * es)


def _ovl(r, s):
    return r[1] < s[2] and s[1] < r[2] and r[3] < s[4] and s[3] < r[4]


def _covers(r, s):
    return r[1] <= s[1] and r[2] >= s[2] and r[3] <= s[3] and r[4] >= s[4]


class Op:
    __slots__ = ("eng", "fn", "deps", "signal", "dma", "eidx", "rank", "waits", "line", "phase")

    def __init__(self, eng, fn):
        self.eng = eng
        self.fn = fn
        self.deps = set()
        self.signal = False
        self.dma = None
        self.waits = []


ENGS = ("pe", "act", "dve", "pool", "sp")
_WRAPPERS = ("mm", "tr", "actv", "tt", "ts", "stt", "copy", "memset", "recip", "dma")
NDMASEM = 8
EPOCH = 20000


class Prog:
    def __init__(self, nc):
        self.nc = nc
        self.ops = []
        self.acc = {}
        self.ndma = 0
        self.phase = ""

    def add(self, eng, fn, reads, writes, dma=False):
        op = Op(eng, fn)
        op.phase = self.phase
        try:
            fr = sys._getframe(1)
            if fr.f_code.co_filename == __file__ and fr.f_code.co_name in _WRAPPERS:
                fr = fr.f_back
            op.line = fr.f_lineno
        except Exception:
            op.line = 0
        rec_eng = "dma" if dma else eng
        idx = len(self.ops)
        ops = self.ops
        for ap in reads:
            r = region(ap)
            lst = self.acc.setdefault(r[0], [])
            done = False
            is_psum = (r[0] == "ps")
            for rec in lst:
                if rec[2]:
                    if _ovl(rec[0], r):
                        op.deps.add(rec[1])
                elif (not done) and (not dma) and rec[3] == rec_eng and rec[0] == r:
                    rec[1] = idx
                    done = True
                elif is_psum and rec[3] != rec_eng and _ovl(rec[0], r):
                    op.deps.add(rec[1])
            if not done:
                lst.append([r, idx, False, rec_eng])
        for ap in writes:
            r = region(ap)
            lst = self.acc.setdefault(r[0], [])
            keep = []
            for rec in lst:
                if _ovl(rec[0], r):
                    if rec[1] != idx:
                        op.deps.add(rec[1])
                    if _covers(r, rec[0]) and rec[1] != idx:
                        continue
                keep.append(rec)
            keep.append([r, idx, True, rec_eng])
            self.acc[r[0]] = keep
        if eng == "pe":
            op.deps = {d for d in op.deps if ops[d].eng != "pe"}
        if dma:
            op.dma = self.ndma
            self.ndma += 1
        ops.append(op)
        return op

    def mm(self, out, lhsT, rhs, start=True, stop=True):
        self.add("pe", lambda e: e.matmul(out, lhsT, rhs, start=start, stop=stop),
                 [lhsT, rhs], [out])

    def tr(self, out, in_, ident):
        self.add("pe", lambda e: e.transpose(out, in_, ident), [in_, ident], [out])

    def actv(self, out, in_, func, bias=None, scale=None, accum_out=None):
        kw = {}
        rd = [in_]
        wr = [out]
        if bias is not None:
            kw["bias"] = bias
            if not isinstance(bias, (int, float)):
                rd.append(bias)
        if scale is not None:
            kw["scale"] = scale
            if not isinstance(scale, (int, float)):
                rd.append(scale)
        if accum_out is not None:
            kw["accum_out"] = accum_out
            wr.append(accum_out)
        self.add("act", lambda e: e.activation(out, in_, func, **kw), rd, wr)

    def _veng(self, eng):
        return eng

    def tt(self, eng, out, in0, in1, op):
        self.add(eng, lambda e: e.tensor_tensor(out, in0, in1, op), [in0, in1], [out])

    def ts(self, eng, out, in0, s1, s2, op0, op1=None, accum_out=None):
        rd = [in0]
        if not isinstance(s1, (int, float)):
            rd.append(s1)
        if s2 is not None and not isinstance(s2, (int, float)):
            rd.append(s2)
        wr = [out]
        kw = {}
        if accum_out is not None:
            kw["accum_out"] = accum_out
            wr.append(accum_out)
        if op1 is None:
            self.add(eng, lambda e: e.tensor_scalar(out, in0, s1, None, op0, **kw), rd, wr)
        else:
            self.add(eng, lambda e: e.tensor_scalar(out, in0, s1, s2, op0, op1, **kw), rd, wr)

    def stt(self, eng, out, in0, scalar, in1, op0, op1):
        rd = [in0, in1]
        if not isinstance(scalar, (int, float)):
            rd.append(scalar)
        self.add(eng, lambda e: e.scalar_tensor_tensor(out, in0, scalar, in1, op0, op1), rd, [out])

    def copy(self, eng, out, in_):
        if eng == "act":
            self.add("act", lambda e: e.activation(out, in_, AF.Copy), [in_], [out])
        else:
            self.add(eng, lambda e: e.tensor_copy(out, in_), [in_], [out])

    def memset(self, eng, out, val):
        self.add(eng, lambda e: e.memset(out, val), [], [out])

    def recip(self, out, in_):
        self.add("dve", lambda e: e.reciprocal(out, in_), [in_], [out])

    def dma(self, out, in_, eng="sp", **kw):
        self.add(eng, lambda e: e.dma_start(out, in_, **kw), [in_], [out], dma=True)

    def emit(self, es, final_wait_all=True):
        nc = self.nc
        ops = self.ops
        cnt = {e: 0 for e in ENGS}
        for op in ops:
            op.eidx = cnt[op.eng]
            cnt[op.eng] += 1
        for i, op in enumerate(ops):
            for d in op.deps:
                dop = ops[d]
                if dop.dma is None:
                    if dop.eng == op.eng and op.eng != "sp":
                        pass
                    dop.signal = True
        last = {}
        for i, op in enumerate(ops):
            if op.dma is None:
                last[op.eng] = i
        for e, i in last.items():
            ops[i].signal = True
        rk = {e: 0 for e in ENGS}
        for op in ops:
            if op.dma is None and op.signal:
                rk[op.eng] += 1
                op.rank = rk[op.eng]
        nep = {e: (rk[e] + EPOCH - 1) // EPOCH + 1 for e in ENGS}
        sems = {}
        for e in ENGS:
            if e == "sp":
                continue
            sems[e] = [es.enter_context(nc.semaphore(f"s_{e}_{k}")) for k in range(nep[e])]
        dsem = [es.enter_context(nc.semaphore(f"s_dma_{k}")) for k in range(NDMASEM)]

        import os as _os
        simmode = bool(_os.environ.get("SIMMODE"))
        dinfo = {}
        m = 0
        for op in ops:
            if op.dma is None:
                continue
            if simmode and op.eng == "pool":
                sem = es.enter_context(nc.semaphore(f"s_u_{op.dma}"))
                dinfo[op.dma] = (("udma", op.dma), sem, 16)
            else:
                k = m % NDMASEM
                dinfo[op.dma] = (("dma", k), dsem[k], 16 * (m // NDMASEM + 1))
                m += 1

        def target(dop):
            if dop.dma is not None:
                return dinfo[dop.dma]
            r = dop.rank - 1
            return ((dop.eng, r // EPOCH), sems[dop.eng][r // EPOCH], r % EPOCH + 1)

        waited = {e: {} for e in ENGS}
        for i, op in enumerate(ops):
            w = waited[op.eng]
            need = {}
            for d in op.deps:
                key, sem, val = target(ops[d])
                if w.get(key, 0) >= val:
                    continue
                if key not in need or need[key][1] < val:
                    need[key] = (sem, val)
            if op.dma is not None:
                key, sem, val = dinfo[op.dma]
                val -= 16
                if val > 0 and w.get(key, 0) < val and (key not in need or need[key][1] < val):
                    need[key] = (sem, val)
            for key, (sem, val) in need.items():
                w[key] = val
                op.waits.append((sem, val))
        final_waits = []
        for e, i in last.items():
            key, sem, val = target(ops[i])
            final_waits.append((sem, val))
        fin = {}
        for key, sem, val in dinfo.values():
            if key not in fin or fin[key][1] < val:
                fin[key] = (sem, val)
        final_waits.extend(fin.values())

        byeng = {e: [op for op in ops if op.eng == e] for e in ENGS}

        annotate = bool(_os.environ.get("SIMMODE"))

        def run(e, name):
            for op in byeng[name]:
                for sem, val in op.waits:
                    e.wait_ge(sem, val)
                ins = op.fn(e)
                if annotate:
                    ins.annotate(f"L{op.line}")
                if op.dma is not None:
                    ins.then_inc(dinfo[op.dma][1], 16)
                elif op.signal:
                    r = op.rank - 1
                    ins.then_inc(sems[name][r // EPOCH], 1)
            if name == "sp":
                for sem, val in final_waits:
                    e.wait_ge(sem, val)

        with nc.Block() as block:
            @block.tensor
            def _(e):
                run(e, "pe")

            @block.scalar
            def _(e):
                run(e, "act")

            @block.vector
            def _(e):
                run(e, "dve")

            @block.gpsimd
            def _(e):
                run(e, "pool")

            @block.sync
            def _(e):
                run(e, "sp")
        return cnt


L = 2048
D = 1024
NT = 16
NB = 4
EPS = 1e-6
OFF = dict(g=0, za=4096, xbc=5120, dt=6656, zb=6688, qlat=7200, kvlat=7584, krope=7840,
           zc=7872, qc=8384, kc=8640, vc=8896, glr=9408, zd=9440, qd=9952, kd=10464, vd=10592)
N_IN = 10720
KB = 1024
BULK = ("dve", "dve", "pool")


def _rope_perm_sign(d):
    m = d // 2
    hm = m // 2
    perm = np.zeros(d, np.int64)
    sign = np.zeros(d, np.float32)
    for j in range(d):
        jj = j % m
        if jj < hm:
            perm[j] = j + hm
            sign[j] = -1.0
        else:
            perm[j] = j - hm
            sign[j] = 1.0
    return perm, sign


def _rope_tables(d):
    rows = L // 64
    row = np.repeat(np.arange(rows), 64).astype(np.float32)
    col = np.tile(np.arange(64), rows).astype(np.float32)
    m = d // 2
    inv = (np.float32(10000.0) ** (-np.arange(0, m, 2, dtype=np.float32) / np.float32(m))).astype(np.float32)
    ang_r = row[:, None] * inv
    ang_c = col[:, None] * inv
    ang = np.concatenate([ang_r, ang_r, ang_c, ang_c], axis=-1).astype(np.float32)
    _, sign = _rope_perm_sign(d)
    cos = np.cos(ang).astype(np.float32).T
    sin = (np.sin(ang).astype(np.float32) * sign[None, :]).T
    return np.ascontiguousarray(cos), np.ascontiguousarray(sin)


class MK:
    def __init__(self, nseq=2, layers=(0, 1), branches="abcd", debug=False):
        self.nseq = nseq
        self.layers = layers
        self.branches = branches
        self.debug = debug
        nc = self.nc = bass.Bass("TRN2", target_bir_lowering=False)
        es = self.es = ExitStack()
        self.P = Prog(nc)
        self.dbg_outs = {}
        di = lambda n, s: nc.dram_tensor(n, s, F32, kind="ExternalInput").ap()
        self.x = di("x", [nseq, L, D])
        self.out = nc.dram_tensor("out", [nseq, L, D], F32, kind="ExternalOutput").ap()
        self.xres = nc.dram_tensor("xres", [nseq, L, D], F32).ap()
        self.w_in = di("w_in", [2, D, N_IN])
        self.w_perm = di("w_perm", [2, D, 672])
        self.norm_g = di("norm_g", [3, D])
        self.qk_g = di("qk_g", [2, 64, 4])
        self.w_q_b = di("w_q_b", [2, 384, 768])
        self.w_qb_perm = di("w_qb_perm", [2, 384, 256])
        self.w_kv_b = di("w_kv_b", [2, 256, 1024])
        self.lat_g = di("lat_g", [2, 128, 5])
        self.w_gate_up = di("w_gate_up", [2, 2, 16, 256])
        self.b_gate = di("b_gate", [2, 2, 256])
        self.gla_g = di("gla_g", [2, 128, 4])
        self.conv_p = di("conv_p", [2, 128, 12, 6])
        self.a_log = di("a_log", [2, 32])
        self.dt_bias = di("dt_bias", [2, 32])
        self.d_skip = di("d_skip", [2, 16])
        self.ssd_norm_g = di("ssd_norm_g", [2, D])
        self.yb_d = nc.dram_tensor("yb_d", [L, D], F32).ap()
        self.w_br_a = di("w_br_a", [2, 1024, D])
        self.w_br_b = di("w_br_b", [2, 512, D])
        self.w_br_c = di("w_br_c", [2, 512, D])
        self.w_br_d = di("w_br_d", [2, 512, D])
        self.w_out = di("w_out", [2, D, D])
        self.rope_cos = di("rope_cos", [128, L])
        self.rope_sin = di("rope_sin", [128, L])
        sb = lambda n, s, d: es.enter_context(nc.sbuf_tensor(n, s, d))
        self.hT = sb("hT", [128, 8, L], BF16)
        self.COS = sb("COS", [128, L], F32)
        self.SIN = sb("SIN", [128, L], F32)
        self.ident = sb("ident", [128, 128], BF16)
        self.identf = sb("identf", [128, 128], F32)
        self.ones = sb("ones", [128, 128], BF16)
        self.onesf = sb("onesf", [128, 128], F32)
        self.triF = sb("triF", [128, 128], F32)
        self.triB = sb("triB", [128, 128], F32)
        self.LF = sb("LF", [128, 128], F32)
        self.LB = sb("LB", [128, 128], F32)
        self.TmFb = sb("TmFb", [128, 128], BF16)
        self.TmBb = sb("TmBb", [128, 128], BF16)
        self.ARW = 155 * 256
        self.AR = sb("AR", [128, self.ARW], F32)
        self.ps = es.enter_context(nc.psum_tensor("ps", [128, 8, 512], F32))
        self.mixT = self.view(0, [8, L], BF16)
        self.onT = self.view(32 * KB, [8, L], BF16)
        self.acur = 64 * KB
        self.stage = self.view(147 * KB, [4, 512], F32)
        self.nst = 0
        self.pre = {}
        self.alim = 147 * KB

    def view(self, off, shape, dt, p0=0, parts=128):
        es_ = _esz(dt)
        n = int(np.prod(shape))
        assert off % 4 == 0
        w0 = off // 4
        w1 = (off + n * es_ + 3) // 4
        assert w1 <= self.ARW, (off, shape)
        ap = self.AR[p0:p0 + parts, w0:w1]
        if dt != F32:
            ap = ap.bitcast(dt)
        if len(shape) == 2:
            ap = ap.rearrange("p (a b) -> p a b", a=shape[0])
        elif len(shape) == 3:
            ap = ap.rearrange("p (a b c) -> p a b c", a=shape[0], b=shape[1])
        return ap

    def alloc(self, shape, dt, p0=0, parts=128):
        n = int(np.prod(shape)) * _esz(dt)
        n = (n + 63) // 64 * 64
        v = self.view(self.acur, shape, dt, p0, parts)
        self.acur += n
        assert self.acur <= self.alim or getattr(self, "allow_over", False), (self.acur, shape)
        return v

    def dbg(self, name, ap_sb, shape):
        if not self.debug:
            return
        d = self.nc.dram_tensor("dbg_" + name, shape, ap_sb.dtype, kind="ExternalOutput").ap()
        self.dbg_outs[name] = d
        self.P.dma(d, ap_sb)

    def wcols(self, src, l, c0, c1):
        return src[l].rearrange("(k p) c -> p k c", p=128)[:, :, c0:c1]

    def wload(self, dst, src, l, c0, c1, engs=("pool",)):
        P = self.P
        nk = dst.shape[1]
        C = c1 - c0
        for k in range(nk):
            for cc in range(0, C, 512):
                w = min(512, C - cc)
                st = self.stage[:, self.nst % 4, 0:w]
                self.nst += 1
                P.dma(st, src[l, k * 128:(k + 1) * 128, c0 + cc:c0 + cc + w])
                P.copy(engs[self.nst % len(engs)], dst[:, k, cc:cc + w], st)

    def setup(self):
        P = self.P
        P.dma(self.COS[:], self.rope_cos)
        P.dma(self.SIN[:], self.rope_sin)
        P.memset("pool", self.onesf[:], 1.0)
        P.memset("pool", self.ones[:], 1.0)
        P.memset("pool", self.identf[:], 0.0)
        idf = self.identf
        onf = self.onesf
        P.add("pool", lambda e: e.affine_select(idf[:], onf[:], [[-1, 128]], ALU.is_equal, 0.0,
                                                base=0, channel_multiplier=1), [onf[:]], [idf[:]])
        P.copy("dve", self.ident[:], self.identf[:])
        tf, tb = self.triF, self.triB
        P.add("pool", lambda e: e.affine_select(tf[:], onf[:], [[1, 128]], ALU.is_ge, 0.0,
                                                base=0, channel_multiplier=-1), [onf[:]], [tf[:]])
        P.add("pool", lambda e: e.affine_select(tb[:], onf[:], [[-1, 128]], ALU.is_ge, 0.0,
                                                base=0, channel_multiplier=1), [onf[:]], [tb[:]])
        P.ts("dve", self.TmFb[:], tf[:], -1.0 / 16, None, ALU.mult)
        P.tt("dve", self.LF[:], tb[:], self.identf[:], ALU.subtract)
        P.tt("dve", self.LB[:], tf[:], self.identf[:], ALU.subtract)
        P.ts("dve", self.TmBb[:], tb[:], -1.0 / 16, None, ALU.mult)

    def norm_to_hT(self, xt, t, gB, scr):
        P = self.P
        junk, ss, hb = scr
        i = t % 2
        P.actv(junk[:, :], xt, AF.Square, accum_out=ss[:, t:t + 1])
        P.actv(ss[:, 16 + t:17 + t], ss[:, t:t + 1], AF.Ln, bias=EPS, scale=1.0 / D)
        P.actv(ss[:, 32 + t:33 + t], ss[:, 16 + t:17 + t], AF.Exp, scale=-0.5)
        P.stt("dve", hb[:, i, :], xt, ss[:, 32 + t:33 + t], gB, ALU.mult, ALU.mult)
        pb = self.ps[:, 7, :].bitcast(BF16)
        for k in range(8):
            P.tr(pb[:, k * 128:(k + 1) * 128], hb[:, i, k * 128:(k + 1) * 128], self.ident[:])
        P.copy("act", self.hT[:, :, t * 128:(t + 1) * 128],
               pb.rearrange("p (k c) -> p k c", k=8))

    def phase_a(self, s, l):
        P = self.P
        self.P.phase = "A"
        self.pre["ssd"] = self.ssd_load(l)
        self.acur = 100 * KB
        xt = self.alloc([2, D], F32)
        gB = self.alloc([D], F32)
        junk = self.alloc([D], BF16)
        ss = self.alloc([48], F32)
        hb = self.alloc([2, D], BF16)
        P.dma(gB, bass.AP(self.norm_g.tensor, l * D, [[0, 128], [1, D]]))
        for t in range(NT):
            P.dma(xt[:, t % 2, :], self.x[s, t * 128:(t + 1) * 128, :])
            self.norm_to_hT(xt[:, t % 2, :], t, gB, (junk, ss, hb))

    def attn_units(self, kT_fn, qT, V_fn, ob, pT, scale):
        P = self.P
        ps = self.ps
        P.mm(ps[:, 0, :], kT_fn(0), qT)
        for kc in range(16):
            if kc + 1 < 16:
                P.mm(ps[:, (kc + 1) % 2, :], kT_fn(kc + 1), qT)
            P.actv(pT[:, kc % 2, :], ps[:, kc % 2, :], AF.Exp, scale=scale)
            P.mm(ps[:, ob, :], V_fn(kc), pT[:, kc % 2, :], start=(kc == 0), stop=(kc == 15))

    def qk_norm_rope(self, psA, psB, gtile, gc, out, b, tmp):
        P = self.P
        sq, rt, t1, t2 = tmp
        bl = slice(b * 512, (b + 1) * 512)
        P.actv(sq[0:64, :], psA, AF.Square)
        P.mm(self.ps[0:64, 6, :], self.ones[0:64, 0:64], sq[0:64, :])
        P.actv(rt[0:64, :], self.ps[0:64, 6, :], AF.Ln, bias=EPS, scale=1.0 / 64)
        P.actv(rt[0:64, :], rt[0:64, :], AF.Exp, scale=-0.5)
        P.stt("dve", t1[0:64, :], psA, gtile[0:64, gc:gc + 1], self.COS[0:64, bl], ALU.mult, ALU.mult)
        P.stt("dve", t2[0:64, :], psB, gtile[0:64, gc + 1:gc + 2], self.SIN[0:64, bl], ALU.mult, ALU.mult)
        P.tt("pool", t1[0:64, :], t1[0:64, :], t2[0:64, :], ALU.add)
        P.tt("pool", out, t1[0:64, :], rt[0:64, :], ALU.mult)

    def gqa_load(self, l):
        P = self.P
        self.acur = 108 * KB
        COSG = self.alloc([L], F32)
        SING = self.alloc([L], F32)
        wk2 = self.alloc([2, 8, 128], BF16)
        wkp2 = self.alloc([2, 8, 128], BF16)
        wv = self.alloc([8, 128], BF16)
        wq = self.alloc([2, 8, 128], BF16)
        wqp = self.alloc([2, 8, 128], BF16)
        wz = self.alloc([2, 8, 128], BF16)
        gt = self.alloc([4], F32)
        for g in range(2):
            for hlf in range(2):
                self.wload(wk2[:, g, :, 64 * hlf:64 * hlf + 64], self.w_in, l, OFF["kd"] + g * 64, OFF["kd"] + (g + 1) * 64, engs=BULK)
                self.wload(wkp2[:, g, :, 64 * hlf:64 * hlf + 64], self.w_perm, l, 544 + g * 64, 544 + (g + 1) * 64, engs=BULK)
        for hlf in range(2):
            rows = slice(64 * hlf, 64 * hlf + 64)
            P.dma(COSG[rows, :], self.rope_cos[0:64, :])
            P.dma(SING[rows, :], self.rope_sin[0:64, :])
            P.dma(gt[rows, :], self.qk_g[l])
        self.wload(wv, self.w_in, l, OFF["vd"], OFF["vd"] + 128, engs=BULK)
        self.wload(wq[:, 0], self.w_in, l, OFF["qd"], OFF["qd"] + 128, engs=BULK)
        self.wload(wqp[:, 0], self.w_perm, l, 32, 32 + 128, engs=BULK)
        self.wload(wz[:, 0], self.w_in, l, OFF["zd"], OFF["zd"] + 128, engs=BULK)
        return (COSG, SING, wk2, wkp2, wv, wq, wqp, wz, gt)

    def phase_gqa(self, l):
        P = self.P
        self.P.phase = "gqa_prep"
        ps = self.ps
        hT = self.hT
        onT = self.onT
        self.acur = 64 * KB
        BD = self.alloc([128], BF16)
        kT2 = self.alloc([2, L], BF16)
        Va = self.alloc([2, 16, 192], BF16)
        qT2 = self.alloc([2, 512], BF16)
        pT = self.alloc([2, 2, 512], BF16)
        sq = self.alloc([512], BF16)
        rt = self.alloc([512], F32)
        t1 = self.alloc([512], F32)
        t2 = self.alloc([512], F32)
        sz = self.alloc([2, 512], F32)
        ez = self.alloc([512], F32)
        rs = t1
        nt = t2
        osb = self.alloc([2, 512], F32)
        assert self.acur <= 108 * KB
        w = self.pre.pop("gqa", None) or self.gqa_load(l)
        COSG, SING, wk2, wkp2, wv, wq, wqp, wz, gt = w
        P.memset("pool", BD, 0.0)
        P.memset("pool", BD[0:64, 0:64], 1.0)
        P.memset("pool", BD[64:128, 64:128], 1.0)
        P.memset("pool", Va[:, :, :, 64:128], 1.0)

        def proj(wa, wb, bl):
            for k in range(8):
                P.mm(ps[:, 6, :], wa(k), hT[:, k, bl], k == 0, k == 7)
            for k in range(8):
                P.mm(ps[:, 7, :], wb(k), hT[:, k, bl], k == 0, k == 7)
            P.actv(sq, ps[:, 6, :], AF.Square)

        def rope1(gc, bl):
            P.stt("dve", t1, ps[:, 6, :], gt[:, gc:gc + 1], COSG[:, bl], ALU.mult, ALU.mult)
            P.stt("dve", t2, ps[:, 7, :], gt[:, gc + 1:gc + 2], SING[:, bl], ALU.mult, ALU.mult)

        def rope2(out):
            P.mm(ps[:, 7, :], BD, sq, True, True)
            P.actv(rt, ps[:, 7, :], AF.Ln, bias=EPS, scale=1.0 / 64)
            P.actv(rt, rt, AF.Exp, scale=-0.5)
            P.tt("pool", t1, t1, t2, ALU.add)
            P.tt("pool", out, t1, rt, ALU.mult)

        for g in range(2):
            for b in range(NB):
                bl = slice(b * 512, (b + 1) * 512)
                proj(lambda k: wk2[:, g, k, :], lambda k: wkp2[:, g, k, :], bl)
                rope1(2, bl)
                rope2(kT2[:, g, bl])
        for t in range(NT):
            for k in range(8):
                P.mm(ps[:, 4 + t % 2, 0:128], hT[:, k, t * 128:(t + 1) * 128], wv[:, k, :], k == 0, k == 7)
            for g in range(2):
                P.copy("act", Va[:, g, t, 0:64], ps[:, 4 + t % 2, g * 64:(g + 1) * 64])
                P.copy("dve", Va[:, g, t, 128:192], ps[:, 4 + t % 2, g * 64:(g + 1) * 64])
        P.phase = "gqa_heads"
        items = [(pr, b) for pr in range(4) for b in range(NB)]

        def prep_parts(it):
            pr, b = items[it]
            i = pr % 2
            bl = slice(b * 512, (b + 1) * 512)
            hooks = {}

            def add(kc, f):
                prev = hooks.get(kc)

                def both(prev=prev, f=f):
                    if prev is not None:
                        prev()
                    f()
                hooks[kc] = both

            def loads():
                if b == 0 and pr > 0:
                    self.wload(wq[:, i], self.w_in, l, OFF["qd"] + pr * 128, OFF["qd"] + (pr + 1) * 128)
                    self.wload(wqp[:, i], self.w_perm, l, 32 + pr * 128, 32 + (pr + 1) * 128)
                    self.wload(wz[:, i], self.w_in, l, OFF["zd"] + pr * 128, OFF["zd"] + (pr + 1) * 128)
            add(0, loads)

            def pk(k):
                P.mm(ps[:, 6, :], wq[:, i, k, :], hT[:, k, bl], k == 0, k == 7)
                P.mm(ps[:, 7, :], wqp[:, i, k, :], hT[:, k, bl], k == 0, k == 7)
            for k in range(8):
                add(k, lambda k=k: pk(k))

            def sq_rope1():
                P.actv(sq, ps[:, 6, :], AF.Square)
                rope1(0, bl)
            add(8, sq_rope1)
            add(9, lambda: rope2(qT2[:, it % 2, :]))
            for k in range(8):
                add(10 + min(k, 5), lambda k=k: P.mm(ps[:, 6, :], wz[:, i, k, :], hT[:, k, bl], k == 0, k == 7))

            def silu():
                P.actv(ez, ps[:, 6, :], AF.Exp, scale=-1.0)
                P.actv(ez, ez, AF.Ln, bias=1.0)
                P.actv(ez, ez, AF.Exp, scale=-1.0)
                P.tt("dve", sz[:, it % 2, :], ps[:, 6, :], ez, ALU.mult)
            add(15, silu)
            return hooks

        def units(it, hooks):
            pr, b = items[it]
            g = pr // 2
            bl = slice(b * 512, (b + 1) * 512)
            qb = qT2[:, it % 2, :]
            sbank = [[0, 1], [4, 5]]
            rows = [slice(0, 64), slice(64, 128)]
            vsel = [slice(0, 128), slice(64, 192)]
            for par in range(2):
                P.mm(ps[:, sbank[par][0], :], kT2[rows[par], g, 0:128], qb[rows[par], :])
            for kc in range(16):
                if kc + 1 < 16:
                    for par in range(2):
                        P.mm(ps[:, sbank[par][(kc + 1) % 2], :], kT2[rows[par], g, (kc + 1) * 128:(kc + 2) * 128], qb[rows[par], :])
                for par in range(2):
                    P.actv(pT[:, par, kc % 2, :], ps[:, sbank[par][kc % 2], :], AF.Exp, scale=0.125)
                for par in range(2):
                    P.mm(ps[:, 2 + par, :], Va[:, g, kc, vsel[par]], pT[:, par, kc % 2, :], kc == 0, kc == 15)
                if kc in hooks:
                    hooks[kc]()
            for par in range(2):
                P.copy("dve", osb[:, par, :], ps[:, 2 + par, :])
            for par in range(2):
                orow = rows[par]
                srow = rows[1 - par]
                P.recip(rs[orow, :], osb[srow, par, :])
                P.tt("dve", nt[orow, :], osb[orow, par, :], rs[orow, :], ALU.mult)
                P.tt("pool", onT[orow, pr, bl], nt[orow, :], sz[orow, it % 2, :], ALU.mult)

        h0 = prep_parts(0)
        for kc in sorted(h0):
            h0[kc]()
        for it in range(len(items)):
            hooks = prep_parts(it + 1) if it + 1 < len(items) else {}
            units(it, hooks)

    def mla_load(self, l):
        self.acur = 124 * KB
        wq = self.alloc([3, 768], BF16)
        wqp = self.alloc([3, 256], BF16)
        wkv = self.alloc([2, 1024], BF16)
        wql = self.alloc([8, 384], BF16)
        wkvl = self.alloc([8, 256], BF16)
        wkr = self.alloc([2, 8, 32], BF16)
        lg = self.alloc([5], F32)
        self.wload(wql, self.w_in, l, OFF["qlat"], OFF["qlat"] + 384, engs=BULK)
        self.wload(wkvl, self.w_in, l, OFF["kvlat"], OFF["kvlat"] + 256, engs=BULK)
        self.wload(wkr[:, 0], self.w_in, l, OFF["krope"], OFF["krope"] + 32, engs=BULK)
        self.wload(wkr[:, 1], self.w_perm, l, 0, 32, engs=BULK)
        self.wload(wq, self.w_q_b, l, 0, 768, engs=BULK)
        self.wload(wqp, self.w_qb_perm, l, 0, 256, engs=BULK)
        self.wload(wkv, self.w_kv_b, l, 0, 1024, engs=BULK)
        self.P.dma(lg, self.lat_g[l])
        return (wq, wqp, wkv, wql, wkvl, wkr, lg)

    def phase_mla(self, l):
        P = self.P
        self.P.phase = "mla_prep"
        ps = self.ps
        hT = self.hT
        onT = self.onT
        COS, SIN = self.COS, self.SIN
        self.acur = 64 * KB
        qln = self.alloc([3, L], BF16)
        kvn = self.alloc([2, L], BF16)
        krT = self.alloc([L], BF16)
        Va = self.alloc([16, 4, 192], BF16)
        qT = self.alloc([2, 512], BF16)
        pT = self.alloc([4, 512], BF16)
        sz = self.alloc([2, 512], F32)
        wz = self.alloc([2, 8, 64], BF16)
        assert self.acur <= 124 * KB
        self.acur = 48 * KB
        kT = self.alloc([2, L], BF16)
        t1 = self.alloc([512], F32)
        t2 = self.alloc([512], F32)
        sqc = self.alloc([2, 512], BF16)
        rt = self.alloc([512], F32)
        rs = t1
        assert self.acur <= 64 * KB
        w = self.pre.pop("mla", None) or self.mla_load(l)
        wq, wqp, wkv, wql, wkvl, wkr, lg = w
        P.memset("pool", Va[:, :, :, 64:128], 1.0)
        nsq = 0
        for b in range(NB):
            bl = slice(b * 512, (b + 1) * 512)
            for (wsrc, nch, bank0, sbank, dst, gofs, dim) in ((wql, 3, 0, 3, qln, 0, 384), (wkvl, 2, 4, 6, kvn, 3, 256)):
                for c in range(nch):
                    for k in range(8):
                        P.mm(ps[:, bank0 + c, :], wsrc[:, k, c * 128:(c + 1) * 128], hT[:, k, bl], k == 0, k == 7)
                for c in range(nch):
                    P.actv(sqc[:, nsq % 2, :], ps[:, bank0 + c, :], AF.Square)
                    P.mm(ps[:, sbank, :], self.ones[:, :], sqc[:, nsq % 2, :], c == 0, c == nch - 1)
                    nsq += 1
                P.actv(rt, ps[:, sbank, :], AF.Ln, bias=EPS, scale=1.0 / dim)
                P.actv(rt, rt, AF.Exp, scale=-0.5)
                for c in range(nch):
                    P.stt("dve", dst[:, c, bl], ps[:, bank0 + c, :], lg[:, gofs + c:gofs + c + 1], rt, ALU.mult, ALU.mult)
            for k in range(8):
                P.mm(ps[64:96, 7, :], wkr[:, 0, k, :], hT[:, k, bl], k == 0, k == 7)
            P.tt("dve", t1[64:96, :], ps[64:96, 7, :], COS[64:96, bl], ALU.mult)
            for k in range(8):
                P.mm(ps[64:96, 7, :], wkr[:, 1, k, :], hT[:, k, bl], k == 0, k == 7)
            P.tt("dve", t2[64:96, :], ps[64:96, 7, :], SIN[64:96, bl], ALU.mult)
            P.tt("pool", krT[64:96, bl], t1[64:96, :], t2[64:96, :], ALU.add)
        wv_view = wkv.rearrange("p c (h two d) -> p c h two d", h=8, two=2)
        for t in range(NT):
            tl = slice(t * 128, (t + 1) * 128)
            for c in range(2):
                P.mm(ps[:, 4 + t % 2, :].rearrange("p (h d) -> p h d", h=8), kvn[:, c, tl], wv_view[:, c, :, 1, :], c == 0, c == 1)
            pv = ps[:, 4 + t % 2, :].rearrange("p (j two d) -> p j two d", j=4, two=2)
            P.copy("act", Va[:, t, :, 0:64], pv[:, :, 0, :])
            P.copy("dve", Va[:, t, :, 128:192], pv[:, :, 1, :])
        scale = 96.0 ** -0.5
        P.phase = "mla_heads"
        items = [(h, b) for h in range(8) for b in range(NB)]
        ez = t2

        def head_prep(h):
            i = h % 2
            self.wload(wz[:, i], self.w_in, l, OFF["zb"] + h * 64, OFF["zb"] + (h + 1) * 64)
            for b in range(NB):
                bl = slice(b * 512, (b + 1) * 512)
                for c in range(2):
                    P.mm(ps[0:64, 6, :], wkv[:, c, h * 128:h * 128 + 64], kvn[:, c, bl], c == 0, c == 1)
                P.copy("dve", kT[0:64, i, bl], ps[0:64, 6, :])
            P.copy("pool", kT[64:96, i, :], krT[64:96, :])

        def prep_parts(it):
            h, b = items[it]
            i = h % 2
            par = h % 2
            orow = slice(64 * par, 64 * par + 64)
            bl = slice(b * 512, (b + 1) * 512)
            qb = qT[:, it % 2, :]

            def p0():
                if b == 0:
                    head_prep(h)

            def p1():
                for c in range(3):
                    P.mm(ps[0:96, 6, :], wq[:, c, h * 96:(h + 1) * 96], qln[:, c, bl], c == 0, c == 2)
                for c in range(3):
                    P.mm(ps[64:96, 7, :], wqp[:, c, h * 32:(h + 1) * 32], qln[:, c, bl], c == 0, c == 2)
                P.copy("dve", qb[0:64, :], ps[0:64, 6, :])
                P.tt("dve", t1[64:96, :], ps[64:96, 6, :], COS[64:96, bl], ALU.mult)
                P.tt("dve", t2[64:96, :], ps[64:96, 7, :], SIN[64:96, bl], ALU.mult)
                P.tt("pool", qb[64:96, :], t1[64:96, :], t2[64:96, :], ALU.add)

            def p3():
                for k in range(8):
                    P.mm(ps[orow, 7, :], wz[:, i, k, :], hT[:, k, bl], k == 0, k == 7)
                P.actv(ez[orow, :], ps[orow, 7, :], AF.Exp, scale=-1.0)
                P.actv(ez[orow, :], ez[orow, :], AF.Ln, bias=1.0)
                P.actv(ez[orow, :], ez[orow, :], AF.Exp, scale=-1.0)
                P.tt("dve", sz[orow, it % 2, :], ps[orow, 7, :], ez[orow, :], ALU.mult)

            return [p0, p1, p3]

        def units(it, hooks):
            h, b = items[it]
            i = h % 2
            par = h % 2
            pr = h // 2
            orow = slice(64 * par, 64 * par + 64)
            srow = slice(64 * (1 - par), 64 * (1 - par) + 64)
            vsel = slice(0, 128) if par == 0 else slice(64, 192)
            bl = slice(b * 512, (b + 1) * 512)
            qb = qT[:, it % 2, :]
            ob = 2 + it % 2
            sbanks = [[0, 1], [4, 5]]

            def S(kc):
                P.mm(ps[:, sbanks[(kc // 2) % 2][kc % 2], :], kT[0:96, i, kc * 128:(kc + 1) * 128], qb[0:96, :])
            S(0)
            S(1)
            for k2 in range(8):
                if k2 + 1 < 8:
                    S(2 * k2 + 2)
                    S(2 * k2 + 3)
                for kc in (2 * k2, 2 * k2 + 1):
                    P.actv(pT[:, kc % 4, :], ps[:, sbanks[(kc // 2) % 2][kc % 2], :], AF.Exp, scale=scale)
                for kc in (2 * k2, 2 * k2 + 1):
                    P.mm(ps[:, ob, :], Va[:, kc, pr, vsel], pT[:, kc % 4, :], kc == 0, kc == 15)
                if k2 in hooks:
                    hooks[k2]()
            P.recip(rs[orow, :], ps[srow, ob, :])
            P.tt("dve", rs[orow, :], ps[orow, ob, :], rs[orow, :], ALU.mult)
            P.tt("pool", onT[orow, pr, bl], rs[orow, :], sz[orow, it % 2, :], ALU.mult)

        for f in prep_parts(0):
            f()
        for it in range(len(items)):
            hooks = {}
            if it + 1 < len(items):
                pp = prep_parts(it + 1)
                hooks = {0: pp[0], 2: pp[1], 5: pp[2]}
            units(it, hooks)

    def gla_load(self, l):
        P = self.P
        self.acur = 48 * KB
        wvc = self.alloc([8, 512], BF16)
        wzc = self.view(48 * KB, [8, 512], BF16)
        wqc = self.alloc([8, 256], BF16)
        wkc = self.alloc([8, 256], BF16)
        assert self.acur <= 64 * KB
        self.acur = 143 * KB
        wgu = self.alloc([2, 256], BF16)
        bg = self.alloc([2, 256], BF16)
        wglr = self.alloc([2, 8, 32], BF16)
        gg = self.alloc([4], F32)
        self.wload(wqc, self.w_in, l, OFF["qc"], OFF["qc"] + 256, engs=BULK)
        self.wload(wkc, self.w_in, l, OFF["kc"], OFF["kc"] + 256, engs=BULK)
        self.wload(wvc, self.w_in, l, OFF["vc"], OFF["vc"] + 512, engs=BULK)
        self.wload(wglr[:, 0], self.w_in, l, OFF["glr"], OFF["glr"] + 32, engs=BULK)
        self.wload(wglr[:, 1, :, 0:16], self.w_in, l, OFF["glr"] + 16, OFF["glr"] + 32, engs=BULK)
        self.wload(wglr[:, 1, :, 16:32], self.w_in, l, OFF["glr"], OFF["glr"] + 16, engs=BULK)
        P.memset("pool", wgu[0:32], 0.0)
        st = self.stage[0:16, self.nst % 4, :]
        self.nst += 1
        P.dma(st.rearrange("p (d c) -> p d c", d=2), self.w_gate_up[l].rearrange("d r c -> r d c"))
        P.copy("pool", wgu[0:16], st.rearrange("p (d c) -> p d c", d=2))
        st = self.stage[0:1, self.nst % 4, :]
        self.nst += 1
        P.dma(st.rearrange("p (d c) -> p d c", d=2), self.b_gate[l:l + 1])
        P.copy("pool", bg[0:1], st.rearrange("p (d c) -> p d c", d=2))
        P.dma(gg, self.gla_g[l])
        return (wvc, wzc, wqc, wkc, wglr, wgu, bg, gg)

    def phase_gla(self, l):
        P = self.P
        self.P.phase = "gla_prep"
        ps = self.ps
        hT = self.hT
        onT = self.onT
        self.acur = 64 * KB
        qTc = self.alloc([2, L], BF16)
        kTc = self.alloc([2, L], BF16)
        Vc = self.alloc([16, 512], BF16)
        ob = self.alloc([4, L], BF16)
        glrT = self.alloc([2, L], BF16)
        oblk = self.alloc([4, 512], F32)
        S = self.alloc([2, 128], F32)
        Sbf = self.alloc([2, 128], BF16)
        gsp = self.alloc([256], F32)
        gh = self.alloc([256], BF16)
        gl = self.alloc([256], BF16)
        eg = self.alloc([2, 128], F32)
        eng = self.alloc([2, 128], F32)
        ek = self.alloc([2, 128], F32)
        qg = self.alloc([2, 2, 128], BF16)
        kg = self.alloc([2, 128], BF16)
        kend = self.alloc([2, 2, 128], BF16)
        attm = self.alloc([2, 4, 128], BF16)
        kendT = self.alloc([2, 128], BF16)
        glast = self.alloc([2], F32)
        cd = self.alloc([2, 2], F32)
        assert self.acur <= 143 * KB
        self.acur = 56 * KB
        sq = self.alloc([512], BF16)
        rt = self.alloc([512], F32)
        sz = self.alloc([512], F32)
        tmp = self.alloc([512], F32)
        assert self.acur <= 64 * KB
        w = self.pre.pop("gla", None) or self.gla_load(l)
        wvc, wzc, wqc, wkc, wglr, wgu, bg, gg = w
        n = 0
        for j in range(2):
            for (wsrc, dst) in ((wqc, qTc), (wkc, kTc)):
                for b in range(NB):
                    bl = slice(b * 512, (b + 1) * 512)
                    bank = 4 + n % 4
                    for k in range(8):
                        P.mm(ps[:, bank, :], wsrc[:, k, j * 128:(j + 1) * 128], hT[:, k, bl], k == 0, k == 7)
                    P.copy("act" if n % 2 else "dve", dst[:, j, bl], ps[:, bank, :])
                    n += 1
        for t in range(NT):
            tl = slice(t * 128, (t + 1) * 128)
            bank = 4 + n % 4
            for k in range(8):
                P.mm(ps[:, bank, :], hT[:, k, tl], wvc[:, k, :], k == 0, k == 7)
            P.copy("act" if n % 2 else "dve", Vc[:, t, :], ps[:, bank, :])
            n += 1
        for d in range(2):
            for b in range(NB):
                bl = slice(b * 512, (b + 1) * 512)
                bank = 4 + n % 4
                for k in range(8):
                    P.mm(ps[0:32, bank, :], wglr[:, d, k, :], hT[:, k, bl], k == 0, k == 7)
                P.copy("act" if n % 2 else "dve", glrT[0:32, d, bl], ps[0:32, bank, :])
                n += 1

        self.wload(wzc, self.w_in, l, OFF["zc"], OFF["zc"] + 512)

        def bank3(b, a):
            return ps[:, b, 0:a * 128].rearrange("p (a c) -> p a c", a=a)


        P.phase = "gla_scan"
        psC = bank3(1, 2)
        psA2 = [bank3(2, 2), bank3(3, 2)]
        psO2 = [bank3(4, 2), bank3(5, 2)]
        psT = ps[:, 6, :].bitcast(BF16)[:, 0:256].rearrange("p (a c) -> p a c", a=2)
        psU = bank3(7, 2)
        obv = ob.rearrange("p (j two) t -> p j two t", two=2)
        oblv = oblk.rearrange("p (j two) t -> p j two t", two=2)

        def front(d, ci, t):
            Tm = self.TmFb if d == 0 else self.TmBb
            tri = self.triF if d == 0 else self.triB
            last = 127 if d == 0 else 0
            x = ci % 2
            tl = slice(t * 128, (t + 1) * 128)
            P.mm(ps[:, 0, 0:256], glrT[0:32, d, tl], wgu[0:32, d, :], True, False)
            P.mm(ps[:, 0, 0:256], self.ones[0:1, 0:128], bg[0:1, d, :], False, True)
            P.actv(gsp, ps[:, 0, 0:256], AF.Exp, scale=-1.0)
            P.actv(gsp, gsp, AF.Ln, bias=1.0)
            P.copy("dve", gh, gsp)
            P.tt("dve", gl, gsp, gh, ALU.subtract)
            for j in range(2):
                P.mm(psC[:, j, :], gh[:, j * 128:(j + 1) * 128], Tm[:, :], True, False)
                P.mm(psC[:, j, :], gl[:, j * 128:(j + 1) * 128], Tm[:, :], False, True)
            P.copy("dve", glast, psC[:, :, last])
            P.actv(eg.rearrange("p a c -> p (a c)"), ps[:, 1, 0:256], AF.Exp)
            P.actv(eng.rearrange("p a c -> p (a c)"), ps[:, 1, 0:256], AF.Exp, scale=-1.0)
            for j in range(2):
                P.actv(ek[:, j, :], psC[:, j, :], AF.Exp, scale=-1.0, bias=glast[:, j:j + 1])
            P.actv(cd[:, x, :], glast, AF.Exp)
            P.stt("dve", qg[:, x], qTc[:, :, tl], 0.125, eg, ALU.mult, ALU.mult)
            P.tt("dve", kg, kTc[:, :, tl], eng, ALU.mult)
            P.tt("dve", kend[:, x], kTc[:, :, tl], ek, ALU.mult)
            tri_b = bass.AP(tri, 0, [[128, 128], [0, 2], [1, 128]])
            attv = attm[:, x].rearrange("p (j two) c -> p j two c", two=2)
            for par in range(2):
                r = slice(64 * par, 64 * par + 64)
                for j in range(2):
                    P.mm(psA2[par][:, j, :], kg[r, j, :], qg[r, x, j, :], True, True)
                P.tt("dve", attv[:, :, par, :], psA2[par], tri_b, ALU.mult)

        def back(d, ci, t):
            x = ci % 2
            tl = slice(t * 128, (t + 1) * 128)
            for par in range(2):
                r = slice(64 * par, 64 * par + 64)
                for j in range(2):
                    h = 2 * j + par
                    P.mm(psO2[par][:, j, :], Vc[:, t, h * 128:(h + 1) * 128], attm[:, x, h, :], True, ci == 0)
                    if ci > 0:
                        P.mm(psO2[par][:, j, :], Sbf[r, j, :], qg[r, x, j, :], False, True)
                if d == 1:
                    P.copy("act", obv[:, :, par, tl], psO2[par])
                else:
                    c4 = t % 4
                    P.tt("dve", oblv[:, :, par, c4 * 128:(c4 + 1) * 128], psO2[par], obv[:, :, par, tl], ALU.add)
            if ci < NT - 1:
                for j in range(2):
                    P.tr(psT[:, j, :], kend[:, x, j, :], self.ident[:])
                P.copy("act", kendT, psT)
                for h in range(4):
                    j = h // 2
                    r = slice(64 * (h % 2), 64 * (h % 2) + 64)
                    P.mm(psU[r, j, :], kendT[:, j, 64 * (h % 2):64 * (h % 2) + 64], Vc[:, t, h * 128:(h + 1) * 128], True, True)
                for j in range(2):
                    if ci > 0:
                        P.stt("dve", S[:, j, :], S[:, j, :], cd[:, x, j:j + 1], psU[:, j, :], ALU.mult, ALU.add)
                    else:
                        P.copy("dve", S[:, j, :], psU[:, j, :])
                P.copy("pool", Sbf, S)
            if d == 0 and t % 4 == 3:
                b = t // 4
                bl = slice(b * 512, (b + 1) * 512)
                for h in range(4):
                    P.actv(sq, oblk[:, h, :], AF.Square)
                    P.mm(ps[:, 0, :], self.ones[:, :], sq, True, True)
                    P.actv(rt, ps[:, 0, :], AF.Ln, bias=EPS, scale=1.0 / 128)
                    P.actv(rt, rt, AF.Exp, scale=-0.5)
                    for k in range(8):
                        P.mm(ps[:, 1, :], wzc[:, k, h * 128:(h + 1) * 128], hT[:, k, bl], k == 0, k == 7)
                    P.actv(sz, ps[:, 1, :], AF.Silu)
                    P.stt("dve", tmp, oblk[:, h, :], gg[:, h:h + 1], rt, ALU.mult, ALU.mult)
                    P.tt("pool", onT[:, h, bl], tmp, sz, ALU.mult)

        for d in (1, 0):
            order = list(range(NT)) if d == 0 else list(range(NT - 1, -1, -1))
            front(d, 0, order[0])
            for ci, t in enumerate(order):
                if ci + 1 < NT:
                    front(d, ci + 1, order[ci + 1])
                back(d, ci, t)

    def ssd_load(self, l):
        P = self.P
        self.acur = 64 * KB
        wz = self.alloc([8, D], BF16)
        wdt = self.alloc([8, 32], BF16)
        anb = self.alloc([32], F32)
        dtb = self.alloc([32], F32)
        dsk = self.alloc([16], F32)
        gnb = self.alloc([D], F32)
        cp = self.alloc([12, 6], F32)
        self.ssd_end = self.acur
        P.dma(cp, self.conv_p[l])
        P.dma(anb, bass.AP(self.a_log.tensor, l * 32, [[0, 128], [1, 32]]))
        P.dma(dtb, bass.AP(self.dt_bias.tensor, l * 32, [[0, 128], [1, 32]]))
        P.dma(dsk, bass.AP(self.d_skip.tensor, l * 16, [[0, 128], [1, 16]]))
        P.dma(gnb, bass.AP(self.ssd_norm_g.tensor, l * D, [[0, 128], [1, D]]))
        self.wload(wdt, self.w_in, l, OFF["dt"], OFF["dt"] + 32, engs=BULK)
        self.wload(wz, self.w_in, l, OFF["za"], OFF["za"] + D, engs=BULK)
        return (wz, wdt, cp, anb, dtb, dsk, gnb)

    def phase_ssd(self, l):
        P = self.P
        self.P.phase = "ssd_conv"
        ps = self.ps
        hT = self.hT
        onT = self.onT
        xs_tok = self.view(0, [16, D], BF16)
        w = self.pre.pop("ssd", None) or self.ssd_load(l)
        wz, wdt, cp, anb, dtb, dsk, gnb = w
        self.acur = self.ssd_end
        BT = self.alloc([2, L], BF16)
        CT = self.alloc([2, L], BF16)
        Btok = self.alloc([16, 256], BF16)
        dt_all = self.alloc([16, 32], F32)
        dtaf = self.alloc([16, 32], F32)
        base = self.acur
        xb = self.alloc([L + 4], F32)
        acc = self.alloc([L], F32)
        cv = self.alloc([L], BF16)
        wx = self.alloc([2, 8, 128], BF16)
        P.actv(anb, anb, AF.Exp)
        P.ts("dve", anb, anb, -1.0, None, ALU.mult)
        P.memset("pool", xb[:, 0:2], 0.0)
        P.memset("pool", xb[:, L + 2:L + 4], 0.0)
        for c in range(12):
            i = c % 2
            self.wload(wx[:, i], self.w_in, l, OFF["xbc"] + c * 128, OFF["xbc"] + (c + 1) * 128)
            for b in range(NB):
                bank = 4 + (c * 4 + b) % 4
                for k in range(8):
                    P.mm(ps[:, bank, :], wx[:, i, k, :], hT[:, k, b * 512:(b + 1) * 512], k == 0, k == 7)
                P.copy("act", xb[:, 2 + b * 512:2 + (b + 1) * 512], ps[:, bank, :])
            eng = "dve"
            P.ts(eng, acc, xb[:, 0:L], cp[:, c, 0:1], cp[:, c, 5:6], ALU.mult, ALU.add)
            for j in range(1, 5):
                P.stt(eng, acc, xb[:, j:j + L], cp[:, c, j:j + 1], acc, ALU.mult, ALU.add)
            if c < 8:
                P.actv(cv, acc, AF.Silu)
                dst_tok = lambda t, c=c: xs_tok[:, t, c * 128:(c + 1) * 128]
                src = cv
            elif c < 10:
                P.actv(BT[:, c - 8, :], acc, AF.Silu)
                dst_tok = lambda t, c=c: Btok[:, t, (c - 8) * 128:(c - 7) * 128]
                src = BT[:, c - 8, :]
            else:
                P.actv(CT[:, c - 10, :], acc, AF.Silu)
                src = None
            if src is not None:
                for half in range(2):
                    pb = ps[:, 2 + half, :].bitcast(BF16)
                    for tt_ in range(8):
                        t = half * 8 + tt_
                        P.tr(pb[:, tt_ * 128:(tt_ + 1) * 128], src[:, t * 128:(t + 1) * 128], self.ident[:])
                    if c < 8:
                        P.copy("act" if half else "dve", xs_tok[:, half * 8:(half + 1) * 8, c * 128:(c + 1) * 128],
                               pb.rearrange("p (t c) -> p t c", t=8))
                    else:
                        P.copy("act" if half else "dve", Btok[:, half * 8:(half + 1) * 8, (c - 8) * 128:(c - 7) * 128],
                               pb.rearrange("p (t c) -> p t c", t=8))
        P.phase = "ssd_scan"
        for t in range(NT):
            for k in range(8):
                P.mm(ps[:, 7, t * 32:(t + 1) * 32], hT[:, k, t * 128:(t + 1) * 128], wdt[:, k, :], k == 0, k == 7)
        dtb_b = bass.AP(dtb.tensor, dtb.offset, [list(dtb.ap[0]), [0, 16], [1, 32]])
        anb_b = bass.AP(anb.tensor, anb.offset, [list(anb.ap[0]), [0, 16], [1, 32]])
        P.tt("dve", dt_all, ps[:, 7, :].rearrange("p (t c) -> p t c", t=16), dtb_b, ALU.add)
        P.actv(dt_all, dt_all, AF.Exp)
        P.actv(dt_all, dt_all, AF.Ln, bias=1.0)
        self.acur = base
        self.allow_over = True
        Af = self.alloc([1, 8, 128], F32)
        E = self.alloc([1, 8, 128], BF16)
        W = self.alloc([1, 8, 128], BF16)
        Gm = self.alloc([2, 128], BF16)
        xd = self.alloc([D], BF16)
        xdd = self.alloc([D], BF16)
        S = self.alloc([D], F32)
        Sbf = self.alloc([D], BF16)
        ytmp = self.alloc([D], F32)
        y2 = self.alloc([2, D], F32)
        ybt = self.alloc([D], F32)
        eac = self.alloc([16], F32)
        cdb = self.alloc([16], F32)
        sz = self.alloc([D], F32)
        dta = sz[:, 0:512].rearrange("p (t c) -> p t c", t=16)
        junk = ytmp
        hb = xdd
        ss = self.alloc([4], F32)
        P.tt("dve", dta, dt_all, anb_b, ALU.mult)
        P.copy("dve", dtaf, dta)

        def hv(ap2):
            return ap2.rearrange("p (h q) -> p h q", h=16)

        def bc_h(ap_col16, n):
            return bass.AP(ap_col16.tensor, ap_col16.offset, [list(ap_col16.ap[0]), [ap_col16.ap[1][0], 16], [0, n]])

        def bc_h2(ap16, g):
            return bass.AP(ap16.tensor, ap16.offset + g * 8, [list(ap16.ap[0]), [ap16.ap[1][0], 8], [0, 128]])

        def fin_stages(ci, t):
            y = y2[:, ci % 2, :]
            tl = slice(t * 128, (t + 1) * 128)
            pz = ps[:, 0:2, :].rearrange("p b c -> p (b c)")

            def f1a():
                P.tt("dve", hv(ytmp), hv(xs_tok[:, t, :]), bc_h(dsk, 64), ALU.mult)
                P.tt("dve", y, y, ytmp, ALU.add)

            def f1b():
                for hf in range(2):
                    for k in range(8):
                        P.mm(ps[:, hf, :], hT[:, k, tl], wz[:, k, hf * 512:(hf + 1) * 512], k == 0, k == 7)

            def f2():
                P.actv(sz, pz, AF.Silu)

            def f3():
                P.tt("dve", y, y, sz, ALU.mult)

            def f4():
                P.actv(junk, y, AF.Square, accum_out=ss[:, 0:1])
                P.actv(ss[:, 1:2], ss[:, 0:1], AF.Ln, bias=EPS, scale=1.0 / D)
                P.actv(ss[:, 2:3], ss[:, 1:2], AF.Exp, scale=-0.5)

            def f5():
                P.stt("dve", hb, y, ss[:, 2:3], gnb, ALU.mult, ALU.mult)
                pb = ps[:, 7, :].bitcast(BF16)
                for k in range(8):
                    P.tr(pb[:, k * 128:(k + 1) * 128], hb[:, k * 128:(k + 1) * 128], self.ident[:])
                P.copy("act", onT[:, :, tl], pb.rearrange("p (k c) -> p k c", k=8))

            return [f1a, f1b, f2, f3, f4, f5]

        def main_stages(d, ci, t):
            Lm = self.LF if d == 0 else self.LB
            Tm = self.triF if d == 0 else self.triB
            tri = self.triF if d == 0 else self.triB
            last = 127 if d == 0 else 0
            y = y2[:, ci % 2, :]
            tl = slice(t * 128, (t + 1) * 128)
            dh = dtaf[:, t, d * 16:(d + 1) * 16]

            def m1():
                if d == 0:
                    P.dma(ybt, self.yb_d[tl, :])
                for g in range(2):
                    P.mm(ps[:, 4, g * 128:(g + 1) * 128], BT[:, g, tl], CT[:, g, tl], True, True)
                P.mm(ps[:, 4, 256:272], Tm[:, :], dh, True, True)
                P.mm(ps[:, 4, 272:288], self.onesf[:, :], dh, True, True)
                tri_b = bass.AP(tri, 0, [[128, 128], [0, 2], [1, 128]])
                P.tt("dve", Gm, ps[:, 4, 0:256].rearrange("p (g c) -> p g c", g=2), tri_b, ALU.mult)
                P.actv(eac, ps[:, 4, 256:272], AF.Exp)
                P.actv(cdb, ps[:, 4, 272:288], AF.Exp)
                P.tt("dve", hv(xd), hv(xs_tok[:, t, :]), bc_h(dt_all[:, t, d * 16:(d + 1) * 16], 64), ALU.mult)

            def mg(g):
                def f():
                    Lb = bass.AP(Lm, 0, [[128, 128], [0, 8], [1, 128]])
                    P.tt("pool", Af[:, 0], Lb, bc_h2(dh, g), ALU.mult)
                    for e in range(8):
                        out = ps[:, 2 * g + e // 4, (e % 4) * 128:(e % 4 + 1) * 128]
                        P.mm(out, Af[:, 0, e, :], Tm[:, :], True, True)
                    P.actv(E[:, 0], ps[:, 2 * g:2 * g + 2, :].rearrange("p b (e c) -> p (b e) c", e=4), AF.Exp)
                    Gb = bass.AP(Gm.tensor, Gm.offset + g * 128, [list(Gm.ap[0]), [0, 8], [1, 128]])
                    P.tt("dve", W[:, 0], E[:, 0], Gb, ALU.mult)
                    Elast = bass.AP(E.tensor, E.offset + last, [list(E.ap[0]), [128, 8], [0, 64]])
                    P.tt("dve", hv(xdd)[:, g * 8:(g + 1) * 8, :], hv(xd)[:, g * 8:(g + 1) * 8, :], Elast, ALU.mult)
                    for e in range(8):
                        h = g * 8 + e
                        P.mm(ps[:, 5 + g, e * 64:(e + 1) * 64], W[:, 0, e, :], xd[:, h * 64:(h + 1) * 64], True, True)
                return f

            def m4():
                if ci > 0:
                    for g in range(2):
                        P.mm(ps[:, g, :], CT[:, g, tl], Sbf[:, g * 512:(g + 1) * 512], True, True)
                    P.tt("dve", hv(ytmp), ps[:, 0:2, :].rearrange("p b (e q) -> p (b e) q", e=8), bc_h(eac, 64), ALU.mult)
                    P.tt("dve", y, ytmp, ps[:, 5:7, :].rearrange("p b c -> p (b c)"), ALU.add)
                else:
                    P.copy("act", y, ps[:, 5:7, :].rearrange("p b c -> p (b c)"))

            def m5():
                if ci < NT - 1:
                    for g in range(2):
                        P.mm(ps[:, 2 + g, :], Btok[:, t, g * 128:(g + 1) * 128], xdd[:, g * 512:(g + 1) * 512], True, True)
                    if ci > 0:
                        P.tt("pool", hv(S), hv(S), bc_h(cdb, 64), ALU.mult)
                        P.tt("dve", S, S, ps[:, 2:4, :].rearrange("p b c -> p (b c)"), ALU.add)
                    else:
                        P.copy("act", S, ps[:, 2:4, :].rearrange("p b c -> p (b c)"))
                    P.copy("act", Sbf, S)

            def m6():
                if d == 1:
                    P.dma(self.yb_d[tl, :], y)
                else:
                    P.tt("dve", y, y, ybt, ALU.add)

            return [m1, mg(0), mg(1), m4, m5, m6]

        for d in (1, 0):
            order = range(NT) if d == 0 else range(NT - 1, -1, -1)
            pending = None
            for ci, t in enumerate(order):
                ms = main_stages(d, ci, t)
                fs = fin_stages(*pending) if pending is not None else []
                for i in range(6):
                    ms[i]()
                    if i < len(fs):
                        fs[i]()
                if d == 0:
                    pending = (ci, t)
            if pending is not None:
                for f in fin_stages(*pending):
                    f()
        self.allow_over = False

    def merge(self, l, bi, nk, wbr_src, first, pre=None):
        P = self.P
        self.P.phase = "merge"
        ps = self.ps
        hT = self.hT
        onT = self.onT
        self.acur = 64 * KB
        wbr = self.alloc([nk, D], BF16)
        wg = self.alloc([3, 8, 128], BF16)
        sg = self.alloc([2, 512], F32)
        tm = self.alloc([2, 512], F32)
        self.wload(wbr, wbr_src, l, 0, D, engs=BULK)
        self.wload(wg[:, 0], self.w_in, l, bi * D, bi * D + 128)
        self.wload(wg[:, 1], self.w_in, l, bi * D + 128, bi * D + 256)
        if pre is not None:
            save = self.acur
            pre()
            self.acur = save
        n = 0
        for m in range(8):
            if m + 2 < 8:
                self.wload(wg[:, (m + 2) % 3], self.w_in, l, bi * D + (m + 2) * 128, bi * D + (m + 3) * 128)
            for b in range(NB):
                bl = slice(b * 512, (b + 1) * 512)
                yb = 4 + n % 2
                gb = 6 + n % 2
                for k in range(nk):
                    P.mm(ps[:, yb, :], wbr[:, k, m * 128:(m + 1) * 128], onT[:, k, bl], k == 0, k == nk - 1)
                for k in range(8):
                    P.mm(ps[:, gb, :], wg[:, m % 3, k, :], hT[:, k, bl], k == 0, k == 7)
                P.actv(sg[:, n % 2, :], ps[:, gb, :], AF.Sigmoid)
                if first:
                    P.tt("dve", self.mixT[:, m, bl], ps[:, yb, :], sg[:, n % 2, :], ALU.mult)
                else:
                    P.tt("dve", tm[:, n % 2, :], ps[:, yb, :], sg[:, n % 2, :], ALU.mult)
                    P.tt("pool", self.mixT[:, m, bl], self.mixT[:, m, bl], tm[:, n % 2, :], ALU.add)
                n += 1

    def outproj_load(self, l, last):
        self.acur = 96 * KB
        wo = self.alloc([8, D], BF16)
        gB = self.alloc([D], F32)
        self.wload(wo, self.w_out, l, 0, D, engs=BULK)
        grow = 2 if last else l + 1
        self.P.dma(gB, bass.AP(self.norm_g.tensor, grow * D, [[0, 128], [1, D]]))
        return (wo, gB)

    def outproj(self, s, l, last):
        P = self.P
        self.P.phase = "outproj"
        ps = self.ps
        w = self.pre.pop("outproj", None) or self.outproj_load(l, last)
        wo, gB = w
        self.acur = 116 * KB
        xt = self.alloc([2, D], F32)
        xn = self.alloc([2, D], F32)
        junk = self.alloc([D], BF16)
        ss = self.alloc([48], F32)
        hb = self.alloc([2, D], BF16)
        yo = self.alloc([2, D], F32)
        if not last:
            self.pre["ssd"] = self.ssd_load(self.layers[self.layers.index(l) + 1])
        xsrc = self.x if l == 0 else self.xres
        def finish(t):
            i = t % 2
            tl = slice(t * 128, (t + 1) * 128)
            if not last:
                P.dma(self.xres[s, tl, :], xn[:, i, :])
                self.norm_to_hT(xn[:, i, :], t, gB, (junk, ss, hb))
            else:
                P.actv(junk[:, :], xn[:, i, :], AF.Square, accum_out=ss[:, t:t + 1])
                P.actv(ss[:, 16 + t:17 + t], ss[:, t:t + 1], AF.Ln, bias=EPS, scale=1.0 / D)
                P.actv(ss[:, 32 + t:33 + t], ss[:, 16 + t:17 + t], AF.Exp, scale=-0.5)
                P.stt("dve", yo[:, i, :], xn[:, i, :], ss[:, 32 + t:33 + t], gB, ALU.mult, ALU.mult)
                P.dma(self.out[s, tl, :], yo[:, i, :])

        P.dma(xt[:, 0, :], xsrc[s, 0:128, :])
        for t in range(NT):
            i = t % 2
            tl = slice(t * 128, (t + 1) * 128)
            if t + 1 < NT:
                P.dma(xt[:, (t + 1) % 2, :], xsrc[s, (t + 1) * 128:(t + 2) * 128, :])
            for hf in range(2):
                bank = 4 + hf
                for k in range(8):
                    P.mm(ps[:, bank, :], self.mixT[:, k, tl], wo[:, k, hf * 512:(hf + 1) * 512], k == 0, k == 7)
                P.tt("dve", xn[:, i, hf * 512:(hf + 1) * 512], ps[:, bank, :], xt[:, i, hf * 512:(hf + 1) * 512], ALU.add)
            if t >= 1:
                finish(t - 1)
        finish(NT - 1)

    def build(self):
        self.setup()
        for s in range(self.nseq):
            nl = len(self.layers)
            for li, l in enumerate(self.layers):
                if li == 0:
                    self.phase_a(s, l)
                last = (li == nl - 1)
                order = [b for b in "abcd" if b in self.branches]
                phase = {"a": self.phase_ssd, "b": self.phase_mla, "c": self.phase_gla, "d": self.phase_gqa}
                loader = {"b": ("mla", self.mla_load), "c": ("gla", self.gla_load), "d": ("gqa", self.gqa_load)}
                mrg = {"a": (0, 8, self.w_br_a), "b": (1, 4, self.w_br_b), "c": (2, 4, self.w_br_c), "d": (3, 4, self.w_br_d)}
                for bi_, br in enumerate(order):
                    phase[br](l)
                    if bi_ + 1 < len(order):
                        key, fn = loader[order[bi_ + 1]]
                        pre = (lambda key=key, fn=fn: self.pre.__setitem__(key, fn(l)))
                    else:
                        pre = (lambda: self.pre.__setitem__("outproj", self.outproj_load(l, last)))
                    i_, nk_, wsrc_ = mrg[br]
                    self.merge(l, i_, nk_, wsrc_, bi_ == 0, pre=pre)
                self.outproj(s, l, last=(li == nl - 1))
        cnt = self.P.emit(self.es)
        return cnt


def _prep_inputs(inp):
    f = lambda a: np.ascontiguousarray(np.asarray(a, dtype=np.float32))
    w_in = f(inp["w_in"])
    p32, _ = _rope_perm_sign(32)
    p64, _ = _rope_perm_sign(64)
    kr = w_in[:, :, OFF["krope"]:OFF["krope"] + 32][:, :, p32]
    qd = w_in[:, :, OFF["qd"]:OFF["qd"] + 512].reshape(2, D, 8, 64)[:, :, :, p64].reshape(2, D, 512)
    kd = w_in[:, :, OFF["kd"]:OFF["kd"] + 128].reshape(2, D, 2, 64)[:, :, :, p64].reshape(2, D, 128)
    w_perm = np.ascontiguousarray(np.concatenate([kr, qd, kd], axis=2))
    qg = f(inp["q_norm_g"])
    kg = f(inp["k_norm_g"])
    qk_g = np.ascontiguousarray(np.stack([qg, qg[:, p64], kg, kg[:, p64]], axis=2))
    cos64, sin64 = _rope_tables(64)
    cos32, sin32 = _rope_tables(32)
    rope_cos = np.ones((128, L), np.float32)
    rope_sin = np.zeros((128, L), np.float32)
    rope_cos[0:64] = cos64
    rope_sin[0:64] = sin64
    rope_cos[64:96] = cos32
    rope_sin[64:96] = sin32
    wqb = f(inp["w_q_b"])
    w_qb_perm = np.ascontiguousarray(wqb.reshape(2, 384, 8, 96)[:, :, :, 64:96][:, :, :, p32].reshape(2, 384, 256))
    qlg = f(inp["q_lat_norm_g"]).reshape(2, 3, 128).transpose(0, 2, 1)
    kvg = f(inp["kv_lat_norm_g"]).reshape(2, 2, 128).transpose(0, 2, 1)
    lat_g = np.ascontiguousarray(np.concatenate([qlg, kvg], axis=2))
    gla_g = np.ascontiguousarray(f(inp["gla_norm_g"]).reshape(2, 4, 128).transpose(0, 2, 1))
    cw = f(inp["conv_w"]).reshape(2, 5, 12, 128).transpose(0, 3, 2, 1)
    cb = f(inp["conv_b"]).reshape(2, 12, 128).transpose(0, 2, 1)[..., None]
    conv_p = np.ascontiguousarray(np.concatenate([cw, cb], axis=3))
    shared = dict(
        conv_p=conv_p, a_log=f(inp["a_log"]).reshape(2, 32), dt_bias=f(inp["dt_bias"]).reshape(2, 32),
        d_skip=f(inp["d_skip"]), ssd_norm_g=f(inp["ssd_norm_g"]),
        w_gate_up=f(inp["w_gate_up"]), b_gate=f(inp["b_gate"]), gla_g=gla_g,
        w_in=w_in, w_perm=w_perm, w_q_b=wqb, w_qb_perm=w_qb_perm, w_kv_b=f(inp["w_kv_b"]), lat_g=lat_g,
        norm_g=np.ascontiguousarray(np.concatenate([f(inp["norm_g"]), f(inp["final_g"])[None, :]], axis=0)),
        qk_g=qk_g,
        w_br_a=f(inp["w_br_a"]), w_br_b=f(inp["w_br_b"]), w_br_c=f(inp["w_br_c"]), w_br_d=f(inp["w_br_d"]),
        w_out=f(inp["w_out"]), rope_cos=rope_cos, rope_sin=rope_sin,
    )
    return shared


_CACHE = {}


def kernel(**inputs):
    x = np.ascontiguousarray(np.asarray(inputs["x"], dtype=np.float32))
    n_cores = 8
    nseq = x.shape[0] // n_cores
    shared = _prep_inputs(inputs)
    mk = MK(nseq=nseq, layers=(0, 1), branches="abcd")
    mk.build()
    in_maps = []
    for c in range(n_cores):
        m = dict(shared)
        m["x"] = np.ascontiguousarray(x[c * nseq:(c + 1) * nseq])
        in_maps.append(m)
    res = run_bass_kernel_spmd(mk.nc, in_maps, core_ids=list(range(n_cores)))
    out = np.concatenate([np.asarray(r["out"]) for r in res.results], axis=0)
    return out.astype(np.float32)
```

```python
import sys
import numpy as np
from contextlib import ExitStack
import concourse.bass as bass
import concourse.mybir as mybir
from concourse.bass_utils import run_bass_kernel_spmd

F32 = mybir.dt.float32
BF16 = mybir.dt.bfloat16
ALU = mybir.AluOpType
AF = mybir.ActivationFunctionType
AX = mybir.AxisListType

_ESZ = {F32: 4, BF16: 2}


def _esz(dt):
    return _ESZ.get(dt, 4)


def region(ap):
    a = ap.ap
    off = int(ap.offset)
    es = _esz(ap.dtype)
    name = ap.tensor.name
    sp = str(ap.space)
    if sp == "DRAM":
        ext = sum((c - 1) * abs(s) for s, c in a) + 1
        return (name, 0, 1, off * es, (off + ext) * es)
    pstep, pcnt = a[0]
    if pstep == 0:
        pstep = 1 << 40
    p0 = off // pstep
    f0 = off % pstep
    ext = sum((c - 1) * abs(s) for s, c in a[1:]) + 1
    if sp == "PSUM":
        b0 = (f0 * es) // 2048 * 2048
        b1 = ((f0 + ext) * es + 2047) // 2048 * 2048
        return (name, p0 // 32 * 32, (p0 + pcnt + 31) // 32 * 32, b0, b1)
    return (name, p0, p0 + pcnt, f0 * es, (f0 + ext) * es)


def _ovl(r, s):
    return r[1] < s[2] and s[1] < r[2] and r[3] < s[4] and s[3] < r[4]


def _covers(r, s):
    return r[1] <= s[1] and r[2] >= s[2] and r[3] <= s[3] and r[4] >= s[4]


class Op:
    __slots__ = ("eng", "fn", "deps", "signal", "dma", "eidx", "rank", "waits", "line", "phase")

    def __init__(self, eng, fn):
        self.eng = eng
        self.fn = fn
        self.deps = set()
        self.signal = False
        self.dma = None
        self.waits = []


ENGS = ("pe", "act", "dve", "pool", "sp")
_WRAPPERS = ("mm", "tr", "actv", "tt", "ts", "stt", "copy", "memset", "recip", "dma")
NDMASEM = 8
EPOCH = 20000


class Prog:
    def __init__(self, nc):
        self.nc = nc
        self.ops = []
        self.acc = {}
        self.ndma = 0
        self.phase = ""

    def add(self, eng, fn, reads, writes, dma=False):
        op = Op(eng, fn)
        op.phase = self.phase
        try:
            fr = sys._getframe(1)
            if fr.f_code.co_filename == __file__ and fr.f_code.co_name in _WRAPPERS:
                fr = fr.f_back
            op.line = fr.f_lineno
        except Exception:
            op.line = 0
        rec_eng = "dma" if dma else eng
        idx = len(self.ops)
        ops = self.ops
        for ap in reads:
            r = region(ap)
            lst = self.acc.setdefault(r[0], [])
            done = False
            is_psum = (r[0] == "ps")
            for rec in lst:
                if rec[2]:
                    if _ovl(rec[0], r):
                        op.deps.add(rec[1])
                elif (not done) and (not dma) and rec[3] == rec_eng and rec[0] == r:
                    rec[1] = idx
                    done = True
                elif is_psum and rec[3] != rec_eng and _ovl(rec[0], r):
                    op.deps.add(rec[1])
            if not done:
                lst.append([r, idx, False, rec_eng])
        for ap in writes:
            r = region(ap)
            lst = self.acc.setdefault(r[0], [])
            keep = []
            for rec in lst:
                if _ovl(rec[0], r):
                    if rec[1] != idx:
                        op.deps.add(rec[1])
                    if _covers(r, rec[0]) and rec[1] != idx:
                        continue
                keep.append(rec)
            keep.append([r, idx, True, rec_eng])
            self.acc[r[0]] = keep
        if eng == "pe":
            op.deps = {d for d in op.deps if ops[d].eng != "pe"}
        if dma:
            op.dma = self.ndma
            self.ndma += 1
        ops.append(op)
        return op

    def mm(self, out, lhsT, rhs, start=True, stop=True):
        self.add("pe", lambda e: e.matmul(out, lhsT, rhs, start=start, stop=stop),
                 [lhsT, rhs], [out])

    def tr(self, out, in_, ident):
        self.add("pe", lambda e: e.transpose(out, in_, ident), [in_, ident], [out])

    def actv(self, out, in_, func, bias=None, scale=None, accum_out=None):
        kw = {}
        rd = [in_]
        wr = [out]
        if bias is not None:
            kw["bias"] = bias
            if not isinstance(bias, (int, float)):
                rd.append(bias)
        if scale is not None:
            kw["scale"] = scale
            if not isinstance(scale, (int, float)):
                rd.append(scale)
        if accum_out is not None:
            kw["accum_out"] = accum_out
            wr.append(accum_out)
        self.add("act", lambda e: e.activation(out, in_, func, **kw), rd, wr)

    def _veng(self, eng):
        return eng

    def tt(self, eng, out, in0, in1, op):
        self.add(eng, lambda e: e.tensor_tensor(out, in0, in1, op), [in0, in1], [out])

    def ts(self, eng, out, in0, s1, s2, op0, op1=None, accum_out=None):
        rd = [in0]
        if not isinstance(s1, (int, float)):
            rd.append(s1)
        if s2 is not None and not isinstance(s2, (int, float)):
            rd.append(s2)
        wr = [out]
        kw = {}
        if accum_out is not None:
            kw["accum_out"] = accum_out
            wr.append(accum_out)
        if op1 is None:
            self.add(eng, lambda e: e.tensor_scalar(out, in0, s1, None, op0, **kw), rd, wr)
        else:
            self.add(eng, lambda e: e.tensor_scalar(out, in0, s1, s2, op0, op1, **kw), rd, wr)

    def stt(self, eng, out, in0, scalar, in1, op0, op1):
        rd = [in0, in1]
        if not isinstance(scalar, (int, float)):
            rd.append(scalar)
        self.add(eng, lambda e: e.scalar_tensor_tensor(out, in0, scalar, in1, op0, op1), rd, [out])

    def copy(self, eng, out, in_):
        if eng == "act":
            self.add("act", lambda e: e.activation(out, in_, AF.Copy), [in_], [out])
        else:
            self.add(eng, lambda e: e.tensor_copy(out, in_), [in_], [out])

    def memset(self, eng, out, val):
        self.add(eng, lambda e: e.memset(out, val), [], [out])

    def recip(self, out, in_):
        self.add("dve", lambda e: e.reciprocal(out, in_), [in_], [out])

    def dma(self, out, in_, eng="sp", **kw):
        self.add(eng, lambda e: e.dma_start(out, in_, **kw), [in_], [out], dma=True)

    def emit(self, es, final_wait_all=True):
        nc = self.nc
        ops = self.ops
        cnt = {e: 0 for e in ENGS}
        for op in ops:
            op.eidx = cnt[op.eng]
            cnt[op.eng] += 1
        for i, op in enumerate(ops):
            for d in op.deps:
                dop = ops[d]
                if dop.dma is None:
                    if dop.eng == op.eng and op.eng != "sp":
                        pass
                    dop.signal = True
        last = {}
        for i, op in enumerate(ops):
            if op.dma is None:
                last[op.eng] = i
        for e, i in last.items():
            ops[i].signal = True
        rk = {e: 0 for e in ENGS}
        for op in ops:
            if op.dma is None and op.signal:
                rk[op.eng] += 1
                op.rank = rk[op.eng]
        nep = {e: (rk[e] + EPOCH - 1) // EPOCH + 1 for e in ENGS}
        sems = {}
        for e in ENGS:
            if e == "sp":
                continue
            sems[e] = [es.enter_context(nc.semaphore(f"s_{e}_{k}")) for k in range(nep[e])]
        dsem = [es.enter_context(nc.semaphore(f"s_dma_{k}")) for k in range(NDMASEM)]

        import os as _os
        simmode = bool(_os.environ.get("SIMMODE"))
        dinfo = {}
        m = 0
        for op in ops:
            if op.dma is None:
                continue
            if simmode and op.eng == "pool":
                sem = es.enter_context(nc.semaphore(f"s_u_{op.dma}"))
                dinfo[op.dma] = (("udma", op.dma), sem, 16)
            else:
                k = m % NDMASEM
                dinfo[op.dma] = (("dma", k), dsem[k], 16 * (m // NDMASEM + 1))
                m += 1

        def target(dop):
            if dop.dma is not None:
                return dinfo[dop.dma]
            r = dop.rank - 1
            return ((dop.eng, r // EPOCH), sems[dop.eng][r // EPOCH], r % EPOCH + 1)

        waited = {e: {} for e in ENGS}
        for i, op in enumerate(ops):
            w = waited[op.eng]
            need = {}
            for d in op.deps:
                key, sem, val = target(ops[d])
                if w.get(key, 0) >= val:
                    continue
                if key not in need or need[key][1] < val:
                    need[key] = (sem, val)
            if op.dma is not None:
                key, sem, val = dinfo[op.dma]
                val -= 16
                if val > 0 and w.get(key, 0) < val and (key not in need or need[key][1] < val):
                    need[key] = (sem, val)
            for key, (sem, val) in need.items():
                w[key] = val
                op.waits.append((sem, val))
        final_waits = []
        for e, i in last.items():
            key, sem, val = target(ops[i])
            final_waits.append((sem, val))
        fin = {}
        for key, sem, val in dinfo.values():
            if key not in fin or fin[key][1] < val:
                fin[key] = (sem, val)
        final_waits.extend(fin.values())

        byeng = {e: [op for op in ops if op.eng == e] for e in ENGS}

        annotate = bool(_os.environ.get("SIMMODE"))

        def run(e, name):
            for op in byeng[name]:
                for sem, val in op.waits:
                    e.wait_ge(sem, val)
                ins = op.fn(e)
                if annotate:
                    ins.annotate(f"L{op.line}")
                if op.dma is not None:
                    ins.then_inc(dinfo[op.dma][1], 16)
                elif op.signal:
                    r = op.rank - 1
                    ins.then_inc(sems[name][r // EPOCH], 1)
            if name == "sp":
                for sem, val in final_waits:
                    e.wait_ge(sem, val)

        with nc.Block() as block:
            @block.tensor
            def _(e):
                run(e, "pe")

            @block.scalar
            def _(e):
                run(e, "act")

            @block.vector
            def _(e):
                run(e, "dve")

            @block.gpsimd
            def _(e):
                run(e, "pool")

            @block.sync
            def _(e):
                run(e, "sp")
        return cnt


L = 2048
D = 1024
NT = 16
NB = 4
EPS = 1e-6
OFF = dict(g=0, za=4096, xbc=5120, dt=6656, zb=6688, qlat=7200, kvlat=7584, krope=7840,
           zc=7872, qc=8384, kc=8640, vc=8896, glr=9408, zd=9440, qd=9952, kd=10464, vd=10592)
N_IN = 10720
KB = 1024
BULK = ("dve", "dve", "pool")


def _rope_perm_sign(d):
    m = d // 2
    hm = m // 2
    perm = np.zeros(d, np.int64)
    sign = np.zeros(d, np.float32)
    for j in range(d):
        jj = j % m
        if jj < hm:
            perm[j] = j + hm
            sign[j] = -1.0
        else:
            perm[j] = j - hm
            sign[j] = 1.0
    return perm, sign


def _rope_tables(d):
    rows = L // 64
    row = np.repeat(np.arange(rows), 64).astype(np.float32)
    col = np.tile(np.arange(64), rows).astype(np.float32)
    m = d // 2
    inv = (np.float32(10000.0) ** (-np.arange(0, m, 2, dtype=np.float32) / np.float32(m))).astype(np.float32)
    ang_r = row[:, None] * inv
    ang_c = col[:, None] * inv
    ang = np.concatenate([ang_r, ang_r, ang_c, ang_c], axis=-1).astype(np.float32)
    _, sign = _rope_perm_sign(d)
    cos = np.cos(ang).astype(np.float32).T
    sin = (np.sin(ang).astype(np.float32) * sign[None, :]).T
    return np.ascontiguousarray(cos), np.ascontiguousarray(sin)


class MK:
    def __init__(self, nseq=2, layers=(0, 1), branches="abcd", debug=False):
        self.nseq = nseq
        self.layers = layers
        self.branches = branches
        self.debug = debug
        nc = self.nc = bass.Bass("TRN2", target_bir_lowering=False)
        es = self.es = ExitStack()
        self.P = Prog(nc)
        self.dbg_outs = {}
        di = lambda n, s: nc.dram_tensor(n, s, F32, kind="ExternalInput").ap()
        self.x = di("x", [nseq, L, D])
        self.out = nc.dram_tensor("out", [nseq, L, D], F32, kind="ExternalOutput").ap()
        self.xres = nc.dram_tensor("xres", [nseq, L, D], F32).ap()
        self.w_in = di("w_in", [2, D, N_IN])
        self.w_perm = di("w_perm", [2, D, 672])
        self.norm_g = di("norm_g", [3, D])
        self.qk_g = di("qk_g", [2, 64, 4])
        self.w_q_b = di("w_q_b", [2, 384, 768])
        self.w_qb_perm = di("w_qb_perm", [2, 384, 256])
        self.w_kv_b = di("w_kv_b", [2, 256, 1024])
        self.lat_g = di("lat_g", [2, 128, 5])
        self.w_gate_up = di("w_gate_up", [2, 2, 16, 256])
        self.b_gate = di("b_gate", [2, 2, 256])
        self.gla_g = di("gla_g", [2, 128, 4])
        self.conv_p = di("conv_p", [2, 128, 12, 6])
        self.a_log = di("a_log", [2, 32])
        self.dt_bias = di("dt_bias", [2, 32])
        self.d_skip = di("d_skip", [2, 16])
        self.ssd_norm_g = di("ssd_norm_g", [2, D])
        self.yb_d = nc.dram_tensor("yb_d", [L, D], F32).ap()
        self.w_br_a = di("w_br_a", [2, 1024, D])
        self.w_br_b = di("w_br_b", [2, 512, D])
        self.w_br_c = di("w_br_c", [2, 512, D])
        self.w_br_d = di("w_br_d", [2, 512, D])
        self.w_out = di("w_out", [2, D, D])
        self.rope_cos = di("rope_cos", [128, L])
        self.rope_sin = di("rope_sin", [128, L])
        sb = lambda n, s, d: es.enter_context(nc.sbuf_tensor(n, s, d))
        self.hT = sb("hT", [128, 8, L], BF16)
        self.COS = sb("COS", [128, L], F32)
        self.SIN = sb("SIN", [128, L], F32)
        self.ident = sb("ident", [128, 128], BF16)
        self.identf = sb("identf", [128, 128], F32)
        self.ones = sb("ones", [128, 128], BF16)
        self.onesf = sb("onesf", [128, 128], F32)
        self.triF = sb("triF", [128, 128], F32)
        self.triB = sb("triB", [128, 128], F32)
        self.LF = sb("LF", [128, 128], F32)
        self.LB = sb("LB", [128, 128], F32)
        self.TmFb = sb("TmFb", [128, 128], BF16)
        self.TmBb = sb("TmBb", [128, 128], BF16)
        self.ARW = 155 * 256
        self.AR = sb("AR", [128, self.ARW], F32)
        self.ps = es.enter_context(nc.psum_tensor("ps", [128, 8, 512], F32))
        self.mixT = self.view(0, [8, L], BF16)
        self.onT = self.view(32 * KB, [8, L], BF16)
        self.acur = 64 * KB
        self.stage = self.view(147 * KB, [4, 512], F32)
        self.nst = 0
        self.pre = {}
        self.alim = 147 * KB

    def view(self, off, shape, dt, p0=0, parts=128):
        es_ = _esz(dt)
        n = int(np.prod(shape))
        assert off % 4 == 0
        w0 = off // 4
        w1 = (off + n * es_ + 3) // 4
        assert w1 <= self.ARW, (off, shape)
        ap = self.AR[p0:p0 + parts, w0:w1]
        if dt != F32:
            ap = ap.bitcast(dt)
        if len(shape) == 2:
            ap = ap.rearrange("p (a b) -> p a b", a=shape[0])
        elif len(shape) == 3:
            ap = ap.rearrange("p (a b c) -> p a b c", a=shape[0], b=shape[1])
        return ap

    def alloc(self, shape, dt, p0=0, parts=128):
        n = int(np.prod(shape)) * _esz(dt)
        n = (n + 63) // 64 * 64
        v = self.view(self.acur, shape, dt, p0, parts)
        self.acur += n
        assert self.acur <= self.alim or getattr(self, "allow_over", False), (self.acur, shape)
        return v

    def dbg(self, name, ap_sb, shape):
        if not self.debug:
            return
        d = self.nc.dram_tensor("dbg_" + name, shape, ap_sb.dtype, kind="ExternalOutput").ap()
        self.dbg_outs[name] = d
        self.P.dma(d, ap_sb)

    def wcols(self, src, l, c0, c1):
        return src[l].rearrange("(k p) c -> p k c", p=128)[:, :, c0:c1]

    def wload(self, dst, src, l, c0, c1, engs=("pool",)):
        P = self.P
        nk = dst.shape[1]
        C = c1 - c0
        for k in range(nk):
            for cc in range(0, C, 512):
                w = min(512, C - cc)
                st = self.stage[:, self.nst % 4, 0:w]
                self.nst += 1
                P.dma(st, src[l, k * 128:(k + 1) * 128, c0 + cc:c0 + cc + w])
                P.copy(engs[self.nst % len(engs)], dst[:, k, cc:cc + w], st)

    def setup(self):
        P = self.P
        P.dma(self.COS[:], self.rope_cos)
        P.dma(self.SIN[:], self.rope_sin)
        P.memset("pool", self.onesf[:], 1.0)
        P.memset("pool", self.ones[:], 1.0)
        P.memset("pool", self.identf[:], 0.0)
        idf = self.identf
        onf = self.onesf
        P.add("pool", lambda e: e.affine_select(idf[:], onf[:], [[-1, 128]], ALU.is_equal, 0.0,
                                                base=0, channel_multiplier=1), [onf[:]], [idf[:]])
        P.copy("dve", self.ident[:], self.identf[:])
        tf, tb = self.triF, self.triB
        P.add("pool", lambda e: e.affine_select(tf[:], onf[:], [[1, 128]], ALU.is_ge, 0.0,
                                                base=0, channel_multiplier=-1), [onf[:]], [tf[:]])
        P.add("pool", lambda e: e.affine_select(tb[:], onf[:], [[-1, 128]], ALU.is_ge, 0.0,
                                                base=0, channel_multiplier=1), [onf[:]], [tb[:]])
        P.ts("dve", self.TmFb[:], tf[:], -1.0 / 16, None, ALU.mult)
        P.tt("dve", self.LF[:], tb[:], self.identf[:], ALU.subtract)
        P.tt("dve", self.LB[:], tf[:], self.identf[:], ALU.subtract)
        P.ts("dve", self.TmBb[:], tb[:], -1.0 / 16, None, ALU.mult)

    def norm_to_hT(self, xt, t, gB, scr):
        P = self.P
        junk, ss, hb = scr
        i = t % 2
        P.actv(junk[:, :], xt, AF.Square, accum_out=ss[:, t:t + 1])
        P.actv(ss[:, 16 + t:17 + t], ss[:, t:t + 1], AF.Ln, bias=EPS, scale=1.0 / D)
        P.actv(ss[:, 32 + t:33 + t], ss[:, 16 + t:17 + t], AF.Exp, scale=-0.5)
        P.stt("dve", hb[:, i, :], xt, ss[:, 32 + t:33 + t], gB, ALU.mult, ALU.mult)
        pb = self.ps[:, 7, :].bitcast(BF16)
        for k in range(8):
            P.tr(pb[:, k * 128:(k + 1) * 128], hb[:, i, k * 128:(k + 1) * 128], self.ident[:])
        P.copy("act", self.hT[:, :, t * 128:(t + 1) * 128],
               pb.rearrange("p (k c) -> p k c", k=8))

    def phase_a(self, s, l):
        P = self.P
        self.P.phase = "A"
        self.pre["ssd"] = self.ssd_load(l)
        self.acur = 100 * KB
        xt = self.alloc([2, D], F32)
        gB = self.alloc([D], F32)
        junk = self.alloc([D], BF16)
        ss = self.alloc([48], F32)
        hb = self.alloc([2, D], BF16)
        P.dma(gB, bass.AP(self.norm_g.tensor, l * D, [[0, 128], [1, D]]))
        for t in range(NT):
            P.dma(xt[:, t % 2, :], self.x[s, t * 128:(t + 1) * 128, :])
            self.norm_to_hT(xt[:, t % 2, :], t, gB, (junk, ss, hb))

    def attn_units(self, kT_fn, qT, V_fn, ob, pT, scale):
        P = self.P
        ps = self.ps
        P.mm(ps[:, 0, :], kT_fn(0), qT)
        for kc in range(16):
            if kc + 1 < 16:
                P.mm(ps[:, (kc + 1) % 2, :], kT_fn(kc + 1), qT)
            P.actv(pT[:, kc % 2, :], ps[:, kc % 2, :], AF.Exp, scale=scale)
            P.mm(ps[:, ob, :], V_fn(kc), pT[:, kc % 2, :], start=(kc == 0), stop=(kc == 15))

    def qk_norm_rope(self, psA, psB, gtile, gc, out, b, tmp):
        P = self.P
        sq, rt, t1, t2 = tmp
        bl = slice(b * 512, (b + 1) * 512)
        P.actv(sq[0:64, :], psA, AF.Square)
        P.mm(self.ps[0:64, 6, :], self.ones[0:64, 0:64], sq[0:64, :])
        P.actv(rt[0:64, :], self.ps[0:64, 6, :], AF.Ln, bias=EPS, scale=1.0 / 64)
        P.actv(rt[0:64, :], rt[0:64, :], AF.Exp, scale=-0.5)
        P.stt("dve", t1[0:64, :], psA, gtile[0:64, gc:gc + 1], self.COS[0:64, bl], ALU.mult, ALU.mult)
        P.stt("dve", t2[0:64, :], psB, gtile[0:64, gc + 1:gc + 2], self.SIN[0:64, bl], ALU.mult, ALU.mult)
        P.tt("pool", t1[0:64, :], t1[0:64, :], t2[0:64, :], ALU.add)
        P.tt("pool", out, t1[0:64, :], rt[0:64, :], ALU.mult)

    def gqa_load(self, l):
        P = self.P
        self.acur = 108 * KB
        COSG = self.alloc([L], F32)
        SING = self.alloc([L], F32)
        wk2 = self.alloc([2, 8, 128], BF16)
        wkp2 = self.alloc([2, 8, 128], BF16)
        wv = self.alloc([8, 128], BF16)
        wq = self.alloc([2, 8, 128], BF16)
        wqp = self.alloc([2, 8, 128], BF16)
        wz = self.alloc([2, 8, 128], BF16)
        gt = self.alloc([4], F32)
        for g in range(2):
            for hlf in range(2):
                self.wload(wk2[:, g, :, 64 * hlf:64 * hlf + 64], self.w_in, l, OFF["kd"] + g * 64, OFF["kd"] + (g + 1) * 64, engs=BULK)
                self.wload(wkp2[:, g, :, 64 * hlf:64 * hlf + 64], self.w_perm, l, 544 + g * 64, 544 + (g + 1) * 64, engs=BULK)
        for hlf in range(2):
            rows = slice(64 * hlf, 64 * hlf + 64)
            P.dma(COSG[rows, :], self.rope_cos[0:64, :])
            P.dma(SING[rows, :], self.rope_sin[0:64, :])
            P.dma(gt[rows, :], self.qk_g[l])
        self.wload(wv, self.w_in, l, OFF["vd"], OFF["vd"] + 128, engs=BULK)
        self.wload(wq[:, 0], self.w_in, l, OFF["qd"], OFF["qd"] + 128, engs=BULK)
        self.wload(wqp[:, 0], self.w_perm, l, 32, 32 + 128, engs=BULK)
        self.wload(wz[:, 0], self.w_in, l, OFF["zd"], OFF["zd"] + 128, engs=BULK)
        return (COSG, SING, wk2, wkp2, wv, wq, wqp, wz, gt)

    def phase_gqa(self, l):
        P = self.P
        self.P.phase = "gqa_prep"
        ps = self.ps
        hT = self.hT
        onT = self.onT
        self.acur = 64 * KB
        BD = self.alloc([128], BF16)
        kT2 = self.alloc([2, L], BF16)
        Va = self.alloc([2, 16, 192], BF16)
        qT2 = self.alloc([2, 512], BF16)
        pT = self.alloc([2, 2, 512], BF16)
        sq = self.alloc([512], BF16)
        rt = self.alloc([512], F32)
        t1 = self.alloc([512], F32)
        t2 = self.alloc([512], F32)
        sz = self.alloc([2, 512], F32)
        ez = self.alloc([512], F32)
        rs = t1
        nt = t2
        osb = self.alloc([2, 512], F32)
        assert self.acur <= 108 * KB
        w = self.pre.pop("gqa", None) or self.gqa_load(l)
        COSG, SING, wk2, wkp2, wv, wq, wqp, wz, gt = w
        P.memset("pool", BD, 0.0)
        P.memset("pool", BD[0:64, 0:64], 1.0)
        P.memset("pool", BD[64:128, 64:128], 1.0)
        P.memset("pool", Va[:, :, :, 64:128], 1.0)

        def proj(wa, wb, bl):
            for k in range(8):
                P.mm(ps[:, 6, :], wa(k), hT[:, k, bl], k == 0, k == 7)
            for k in range(8):
                P.mm(ps[:, 7, :], wb(k), hT[:, k, bl], k == 0, k == 7)
            P.actv(sq, ps[:, 6, :], AF.Square)

        def rope1(gc, bl):
            P.stt("dve", t1, ps[:, 6, :], gt[:, gc:gc + 1], COSG[:, bl], ALU.mult, ALU.mult)
            P.stt("dve", t2, ps[:, 7, :], gt[:, gc + 1:gc + 2], SING[:, bl], ALU.mult, ALU.mult)

        def rope2(out):
            P.mm(ps[:, 7, :], BD, sq, True, True)
            P.actv(rt, ps[:, 7, :], AF.Ln, bias=EPS, scale=1.0 / 64)
            P.actv(rt, rt, AF.Exp, scale=-0.5)
            P.tt("pool", t1, t1, t2, ALU.add)
            P.tt("pool", out, t1, rt, ALU.mult)

        for g in range(2):
            for b in range(NB):
                bl = slice(b * 512, (b + 1) * 512)
                proj(lambda k: wk2[:, g, k, :], lambda k: wkp2[:, g, k, :], bl)
                rope1(2, bl)
                rope2(kT2[:, g, bl])
        for t in range(NT):
            for k in range(8):
                P.mm(ps[:, 4 + t % 2, 0:128], hT[:, k, t * 128:(t + 1) * 128], wv[:, k, :], k == 0, k == 7)
            for g in range(2):
                P.copy("act", Va[:, g, t, 0:64], ps[:, 4 + t % 2, g * 64:(g + 1) * 64])
                P.copy("dve", Va[:, g, t, 128:192], ps[:, 4 + t % 2, g * 64:(g + 1) * 64])
        P.phase = "gqa_heads"
        items = [(pr, b) for pr in range(4) for b in range(NB)]

        def prep_parts(it):
            pr, b = items[it]
            i = pr % 2
            bl = slice(b * 512, (b + 1) * 512)
            hooks = {}

            def add(kc, f):
                prev = hooks.get(kc)

                def both(prev=prev, f=f):
                    if prev is not None:
                        prev()
                    f()
                hooks[kc] = both

            def loads():
                if b == NB - 1 and pr + 1 < 4:
                    p2 = pr + 1
                    i2 = p2 % 2
                    self.wload(wq[:, i2], self.w_in, l, OFF["qd"] + p2 * 128, OFF["qd"] + (p2 + 1) * 128)
                    self.wload(wqp[:, i2], self.w_perm, l, 32 + p2 * 128, 32 + (p2 + 1) * 128)
                    self.wload(wz[:, i2], self.w_in, l, OFF["zd"] + p2 * 128, OFF["zd"] + (p2 + 1) * 128)
            add(0, loads)

            def pk(k):
                P.mm(ps[:, 6, :], wq[:, i, k, :], hT[:, k, bl], k == 0, k == 7)
                P.mm(ps[:, 7, :], wqp[:, i, k, :], hT[:, k, bl], k == 0, k == 7)
            for k in range(8):
                add(k, lambda k=k: pk(k))

            def sq_rope1():
                P.actv(sq, ps[:, 6, :], AF.Square)
                rope1(0, bl)
            add(8, sq_rope1)
            add(9, lambda: rope2(qT2[:, it % 2, :]))
            for k in range(8):
                add(10 + min(k, 5), lambda k=k: P.mm(ps[:, 6, :], wz[:, i, k, :], hT[:, k, bl], k == 0, k == 7))

            def silu():
                P.actv(ez, ps[:, 6, :], AF.Exp, scale=-1.0)
                P.actv(ez, ez, AF.Ln, bias=1.0)
                P.actv(ez, ez, AF.Exp, scale=-1.0)
                P.tt("dve", sz[:, it % 2, :], ps[:, 6, :], ez, ALU.mult)
            add(15, silu)
            return hooks

        def units(it, hooks):
            pr, b = items[it]
            g = pr // 2
            bl = slice(b * 512, (b + 1) * 512)
            qb = qT2[:, it % 2, :]
            sbank = [[0, 1], [4, 5]]
            rows = [slice(0, 64), slice(64, 128)]
            vsel = [slice(0, 128), slice(64, 192)]
            for par in range(2):
                P.mm(ps[:, sbank[par][0], :], kT2[rows[par], g, 0:128], qb[rows[par], :])
            for kc in range(16):
                if kc + 1 < 16:
                    for par in range(2):
                        P.mm(ps[:, sbank[par][(kc + 1) % 2], :], kT2[rows[par], g, (kc + 1) * 128:(kc + 2) * 128], qb[rows[par], :])
                for par in range(2):
                    P.actv(pT[:, par, kc % 2, :], ps[:, sbank[par][kc % 2], :], AF.Exp, scale=0.125)
                for par in range(2):
                    P.mm(ps[:, 2 + par, :], Va[:, g, kc, vsel[par]], pT[:, par, kc % 2, :], kc == 0, kc == 15)
                if kc in hooks:
                    hooks[kc]()
            for par in range(2):
                P.copy("dve", osb[:, par, :], ps[:, 2 + par, :])
            for par in range(2):
                orow = rows[par]
                srow = rows[1 - par]
                P.recip(rs[orow, :], osb[srow, par, :])
                P.tt("dve", nt[orow, :], osb[orow, par, :], rs[orow, :], ALU.mult)
                P.tt("pool", onT[orow, pr, bl], nt[orow, :], sz[orow, it % 2, :], ALU.mult)

        h0 = prep_parts(0)
        for kc in sorted(h0):
            h0[kc]()
        for it in range(len(items)):
            hooks = prep_parts(it + 1) if it + 1 < len(items) else {}
            units(it, hooks)

    def mla_load(self, l):
        self.acur = 124 * KB
        wq = self.alloc([3, 768], BF16)
        wqp = self.alloc([3, 256], BF16)
        wkv = self.alloc([2, 1024], BF16)
        wql = self.alloc([8, 384], BF16)
        wkvl = self.alloc([8, 256], BF16)
        wkr = self.alloc([2, 8, 32], BF16)
        lg = self.alloc([5], F32)
        self.wload(wql, self.w_in, l, OFF["qlat"], OFF["qlat"] + 384, engs=BULK)
        self.wload(wkvl, self.w_in, l, OFF["kvlat"], OFF["kvlat"] + 256, engs=BULK)
        self.wload(wkr[:, 0], self.w_in, l, OFF["krope"], OFF["krope"] + 32, engs=BULK)
        self.wload(wkr[:, 1], self.w_perm, l, 0, 32, engs=BULK)
        self.wload(wq, self.w_q_b, l, 0, 768, engs=BULK)
        self.wload(wqp, self.w_qb_perm, l, 0, 256, engs=BULK)
        self.wload(wkv, self.w_kv_b, l, 0, 1024, engs=BULK)
        self.P.dma(lg, self.lat_g[l])
        return (wq, wqp, wkv, wql, wkvl, wkr, lg)

    def phase_mla(self, l):
        P = self.P
        self.P.phase = "mla_prep"
        ps = self.ps
        hT = self.hT
        onT = self.onT
        COS, SIN = self.COS, self.SIN
        self.acur = 64 * KB
        qln = self.alloc([3, L], BF16)
        kvn = self.alloc([2, L], BF16)
        krT = self.alloc([L], BF16)
        Va = self.alloc([16, 4, 192], BF16)
        qT = self.alloc([2, 512], BF16)
        pT = self.alloc([4, 512], BF16)
        sz = self.alloc([2, 512], F32)
        wz = self.alloc([2, 8, 64], BF16)
        assert self.acur <= 124 * KB
        self.acur = 48 * KB
        kT = self.alloc([2, L], BF16)
        t1 = self.alloc([512], F32)
        t2 = self.alloc([512], F32)
        sqc = self.alloc([2, 512], BF16)
        rt = self.alloc([512], F32)
        rs = t1
        assert self.acur <= 64 * KB
        w = self.pre.pop("mla", None) or self.mla_load(l)
        wq, wqp, wkv, wql, wkvl, wkr, lg = w
        P.memset("pool", Va[:, :, :, 64:128], 1.0)
        nsq = 0
        for b in range(NB):
            bl = slice(b * 512, (b + 1) * 512)
            for (wsrc, nch, bank0, sbank, dst, gofs, dim) in ((wql, 3, 0, 3, qln, 0, 384), (wkvl, 2, 4, 6, kvn, 3, 256)):
                for c in range(nch):
                    for k in range(8):
                        P.mm(ps[:, bank0 + c, :], wsrc[:, k, c * 128:(c + 1) * 128], hT[:, k, bl], k == 0, k == 7)
                for c in range(nch):
                    P.actv(sqc[:, nsq % 2, :], ps[:, bank0 + c, :], AF.Square)
                    P.mm(ps[:, sbank, :], self.ones[:, :], sqc[:, nsq % 2, :], c == 0, c == nch - 1)
                    nsq += 1
                P.actv(rt, ps[:, sbank, :], AF.Ln, bias=EPS, scale=1.0 / dim)
                P.actv(rt, rt, AF.Exp, scale=-0.5)
                for c in range(nch):
                    P.stt("dve", dst[:, c, bl], ps[:, bank0 + c, :], lg[:, gofs + c:gofs + c + 1], rt, ALU.mult, ALU.mult)
            for k in range(8):
                P.mm(ps[64:96, 7, :], wkr[:, 0, k, :], hT[:, k, bl], k == 0, k == 7)
            P.tt("dve", t1[64:96, :], ps[64:96, 7, :], COS[64:96, bl], ALU.mult)
            for k in range(8):
                P.mm(ps[64:96, 7, :], wkr[:, 1, k, :], hT[:, k, bl], k == 0, k == 7)
            P.tt("dve", t2[64:96, :], ps[64:96, 7, :], SIN[64:96, bl], ALU.mult)
            P.tt("pool", krT[64:96, bl], t1[64:96, :], t2[64:96, :], ALU.add)
        wv_view = wkv.rearrange("p c (h two d) -> p c h two d", h=8, two=2)
        for t in range(NT):
            tl = slice(t * 128, (t + 1) * 128)
            for c in range(2):
                P.mm(ps[:, 4 + t % 2, :].rearrange("p (h d) -> p h d", h=8), kvn[:, c, tl], wv_view[:, c, :, 1, :], c == 0, c == 1)
            pv = ps[:, 4 + t % 2, :].rearrange("p (j two d) -> p j two d", j=4, two=2)
            P.copy("act", Va[:, t, :, 0:64], pv[:, :, 0, :])
            P.copy("dve", Va[:, t, :, 128:192], pv[:, :, 1, :])
        scale = 96.0 ** -0.5
        P.phase = "mla_heads"
        items = [(h, b) for h in range(8) for b in range(NB)]
        ez = t2

        def head_prep(h):
            i = h % 2
            if h == 0:
                self.wload(wz[:, i], self.w_in, l, OFF["zb"] + h * 64, OFF["zb"] + (h + 1) * 64)
            for b in range(NB):
                bl = slice(b * 512, (b + 1) * 512)
                for c in range(2):
                    P.mm(ps[0:64, 6, :], wkv[:, c, h * 128:h * 128 + 64], kvn[:, c, bl], c == 0, c == 1)
                P.copy("dve", kT[0:64, i, bl], ps[0:64, 6, :])
            P.copy("pool", kT[64:96, i, :], krT[64:96, :])

        def prep_parts(it):
            h, b = items[it]
            i = h % 2
            par = h % 2
            orow = slice(64 * par, 64 * par + 64)
            bl = slice(b * 512, (b + 1) * 512)
            qb = qT[:, it % 2, :]

            def p0():
                if b == 0:
                    head_prep(h)
                if b == NB - 2 and h + 1 < 8:
                    h2 = h + 1
                    self.wload(wz[:, h2 % 2], self.w_in, l, OFF["zb"] + h2 * 64, OFF["zb"] + (h2 + 1) * 64)

            def p1():
                for c in range(3):
                    P.mm(ps[0:96, 6, :], wq[:, c, h * 96:(h + 1) * 96], qln[:, c, bl], c == 0, c == 2)
                for c in range(3):
                    P.mm(ps[64:96, 7, :], wqp[:, c, h * 32:(h + 1) * 32], qln[:, c, bl], c == 0, c == 2)
                P.copy("dve", qb[0:64, :], ps[0:64, 6, :])
                P.tt("dve", t1[64:96, :], ps[64:96, 6, :], COS[64:96, bl], ALU.mult)
                P.tt("dve", t2[64:96, :], ps[64:96, 7, :], SIN[64:96, bl], ALU.mult)
                P.tt("pool", qb[64:96, :], t1[64:96, :], t2[64:96, :], ALU.add)

            def p3():
                for k in range(8):
                    P.mm(ps[orow, 7, :], wz[:, i, k, :], hT[:, k, bl], k == 0, k == 7)
                P.actv(ez[orow, :], ps[orow, 7, :], AF.Exp, scale=-1.0)
                P.actv(ez[orow, :], ez[orow, :], AF.Ln, bias=1.0)
                P.actv(ez[orow, :], ez[orow, :], AF.Exp, scale=-1.0)
                P.tt("dve", sz[orow, it % 2, :], ps[orow, 7, :], ez[orow, :], ALU.mult)

            return [p0, p1, p3]

        def units(it, hooks):
            h, b = items[it]
            i = h % 2
            par = h % 2
            pr = h // 2
            orow = slice(64 * par, 64 * par + 64)
            srow = slice(64 * (1 - par), 64 * (1 - par) + 64)
            vsel = slice(0, 128) if par == 0 else slice(64, 192)
            bl = slice(b * 512, (b + 1) * 512)
            qb = qT[:, it % 2, :]
            ob = 2 + it % 2
            sbanks = [[0, 1], [4, 5]]

            def S(kc):
                P.mm(ps[:, sbanks[(kc // 2) % 2][kc % 2], :], kT[0:96, i, kc * 128:(kc + 1) * 128], qb[0:96, :])
            S(0)
            S(1)
            for k2 in range(8):
                if k2 + 1 < 8:
                    S(2 * k2 + 2)
                    S(2 * k2 + 3)
                for kc in (2 * k2, 2 * k2 + 1):
                    P.actv(pT[:, kc % 4, :], ps[:, sbanks[(kc // 2) % 2][kc % 2], :], AF.Exp, scale=scale)
                for kc in (2 * k2, 2 * k2 + 1):
                    P.mm(ps[:, ob, :], Va[:, kc, pr, vsel], pT[:, kc % 4, :], kc == 0, kc == 15)
                if k2 in hooks:
                    hooks[k2]()
            P.recip(rs[orow, :], ps[srow, ob, :])
            P.tt("dve", rs[orow, :], ps[orow, ob, :], rs[orow, :], ALU.mult)
            P.tt("pool", onT[orow, pr, bl], rs[orow, :], sz[orow, it % 2, :], ALU.mult)

        for f in prep_parts(0):
            f()
        for it in range(len(items)):
            hooks = {}
            if it + 1 < len(items):
                pp = prep_parts(it + 1)
                hooks = {0: pp[0], 2: pp[1], 5: pp[2]}
            units(it, hooks)

    def gla_load(self, l):
        P = self.P
        self.acur = 48 * KB
        wvc = self.alloc([8, 512], BF16)
        wzc = self.view(48 * KB, [8, 512], BF16)
        wqc = self.alloc([8, 256], BF16)
        wkc = self.alloc([8, 256], BF16)
        assert self.acur <= 64 * KB
        self.acur = 143 * KB
        wgu = self.alloc([2, 256], BF16)
        bg = self.alloc([2, 256], BF16)
        wglr = self.alloc([2, 8, 32], BF16)
        gg = self.alloc([4], F32)
        self.wload(wqc, self.w_in, l, OFF["qc"], OFF["qc"] + 256, engs=BULK)
        self.wload(wkc, self.w_in, l, OFF["kc"], OFF["kc"] + 256, engs=BULK)
        self.wload(wvc, self.w_in, l, OFF["vc"], OFF["vc"] + 512, engs=BULK)
        self.wload(wglr[:, 0], self.w_in, l, OFF["glr"], OFF["glr"] + 32, engs=BULK)
        self.wload(wglr[:, 1, :, 0:16], self.w_in, l, OFF["glr"] + 16, OFF["glr"] + 32, engs=BULK)
        self.wload(wglr[:, 1, :, 16:32], self.w_in, l, OFF["glr"], OFF["glr"] + 16, engs=BULK)
        P.memset("pool", wgu[0:32], 0.0)
        st = self.stage[0:16, self.nst % 4, :]
        self.nst += 1
        P.dma(st.rearrange("p (d c) -> p d c", d=2), self.w_gate_up[l].rearrange("d r c -> r d c"))
        P.copy("pool", wgu[0:16], st.rearrange("p (d c) -> p d c", d=2))
        st = self.stage[0:1, self.nst % 4, :]
        self.nst += 1
        P.dma(st.rearrange("p (d c) -> p d c", d=2), self.b_gate[l:l + 1])
        P.copy("pool", bg[0:1], st.rearrange("p (d c) -> p d c", d=2))
        P.dma(gg, self.gla_g[l])
        return (wvc, wzc, wqc, wkc, wglr, wgu, bg, gg)

    def phase_gla(self, l):
        P = self.P
        self.P.phase = "gla_prep"
        ps = self.ps
        hT = self.hT
        onT = self.onT
        self.acur = 64 * KB
        qTc = self.alloc([2, L], BF16)
        kTc = self.alloc([2, L], BF16)
        Vc = self.alloc([16, 512], BF16)
        ob = self.alloc([4, L], BF16)
        glrT = self.alloc([2, L], BF16)
        oblk = self.alloc([4, 512], F32)
        S = self.alloc([2, 128], F32)
        Sbf = self.alloc([2, 128], BF16)
        gsp = self.alloc([256], F32)
        gh = self.alloc([256], BF16)
        gl = self.alloc([256], BF16)
        eg = self.alloc([2, 128], F32)
        eng = self.alloc([2, 128], F32)
        ek = self.alloc([2, 128], F32)
        qg = self.alloc([2, 2, 128], BF16)
        kg = self.alloc([2, 128], BF16)
        kend = self.alloc([2, 2, 128], BF16)
        attm = self.alloc([2, 4, 128], BF16)
        kendT = self.alloc([2, 128], BF16)
        glast = self.alloc([2], F32)
        cd = self.alloc([2, 2], F32)
        assert self.acur <= 143 * KB
        self.acur = 56 * KB
        sq = self.alloc([512], BF16)
        rt = self.alloc([512], F32)
        sz = self.alloc([512], F32)
        tmp = self.alloc([512], F32)
        assert self.acur <= 64 * KB
        w = self.pre.pop("gla", None) or self.gla_load(l)
        wvc, wzc, wqc, wkc, wglr, wgu, bg, gg = w
        n = 0
        for j in range(2):
            for (wsrc, dst) in ((wqc, qTc), (wkc, kTc)):
                for b in range(NB):
                    bl = slice(b * 512, (b + 1) * 512)
                    bank = 4 + n % 4
                    for k in range(8):
                        P.mm(ps[:, bank, :], wsrc[:, k, j * 128:(j + 1) * 128], hT[:, k, bl], k == 0, k == 7)
                    P.copy("act" if n % 2 else "dve", dst[:, j, bl], ps[:, bank, :])
                    n += 1
        for t in range(NT):
            tl = slice(t * 128, (t + 1) * 128)
            bank = 4 + n % 4
            for k in range(8):
                P.mm(ps[:, bank, :], hT[:, k, tl], wvc[:, k, :], k == 0, k == 7)
            P.copy("act" if n % 2 else "dve", Vc[:, t, :], ps[:, bank, :])
            n += 1
        for d in range(2):
            for b in range(NB):
                bl = slice(b * 512, (b + 1) * 512)
                bank = 4 + n % 4
                for k in range(8):
                    P.mm(ps[0:32, bank, :], wglr[:, d, k, :], hT[:, k, bl], k == 0, k == 7)
                P.copy("act" if n % 2 else "dve", glrT[0:32, d, bl], ps[0:32, bank, :])
                n += 1

        self.wload(wzc, self.w_in, l, OFF["zc"], OFF["zc"] + 512)

        def bank3(b, a):
            return ps[:, b, 0:a * 128].rearrange("p (a c) -> p a c", a=a)


        P.phase = "gla_scan"
        psC = bank3(1, 2)
        psA2 = [bank3(2, 2), bank3(3, 2)]
        psO2 = [bank3(4, 2), bank3(5, 2)]
        psT = ps[:, 6, :].bitcast(BF16)[:, 0:256].rearrange("p (a c) -> p a c", a=2)
        psU = bank3(7, 2)
        obv = ob.rearrange("p (j two) t -> p j two t", two=2)
        oblv = oblk.rearrange("p (j two) t -> p j two t", two=2)

        def front(d, ci, t):
            Tm = self.TmFb if d == 0 else self.TmBb
            tri = self.triF if d == 0 else self.triB
            last = 127 if d == 0 else 0
            x = ci % 2
            tl = slice(t * 128, (t + 1) * 128)
            P.mm(ps[:, 0, 0:256], glrT[0:32, d, tl], wgu[0:32, d, :], True, False)
            P.mm(ps[:, 0, 0:256], self.ones[0:1, 0:128], bg[0:1, d, :], False, True)
            P.actv(gsp, ps[:, 0, 0:256], AF.Exp, scale=-1.0)
            P.actv(gsp, gsp, AF.Ln, bias=1.0)
            P.copy("dve", gh, gsp)
            P.tt("dve", gl, gsp, gh, ALU.subtract)
            for j in range(2):
                P.mm(psC[:, j, :], gh[:, j * 128:(j + 1) * 128], Tm[:, :], True, False)
                P.mm(psC[:, j, :], gl[:, j * 128:(j + 1) * 128], Tm[:, :], False, True)
            P.copy("dve", glast, psC[:, :, last])
            P.actv(eg.rearrange("p a c -> p (a c)"), ps[:, 1, 0:256], AF.Exp)
            P.actv(eng.rearrange("p a c -> p (a c)"), ps[:, 1, 0:256], AF.Exp, scale=-1.0)
            for j in range(2):
                P.actv(ek[:, j, :], psC[:, j, :], AF.Exp, scale=-1.0, bias=glast[:, j:j + 1])
            P.actv(cd[:, x, :], glast, AF.Exp)
            P.stt("dve", qg[:, x], qTc[:, :, tl], 0.125, eg, ALU.mult, ALU.mult)
            P.tt("dve", kg, kTc[:, :, tl], eng, ALU.mult)
            P.tt("dve", kend[:, x], kTc[:, :, tl], ek, ALU.mult)
            tri_b = bass.AP(tri, 0, [[128, 128], [0, 2], [1, 128]])
            attv = attm[:, x].rearrange("p (j two) c -> p j two c", two=2)
            for par in range(2):
                r = slice(64 * par, 64 * par + 64)
                for j in range(2):
                    P.mm(psA2[par][:, j, :], kg[r, j, :], qg[r, x, j, :], True, True)
                P.tt("dve", attv[:, :, par, :], psA2[par], tri_b, ALU.mult)

        def back(d, ci, t):
            x = ci % 2
            tl = slice(t * 128, (t + 1) * 128)
            for par in range(2):
                r = slice(64 * par, 64 * par + 64)
                for j in range(2):
                    h = 2 * j + par
                    P.mm(psO2[par][:, j, :], Vc[:, t, h * 128:(h + 1) * 128], attm[:, x, h, :], True, ci == 0)
                    if ci > 0:
                        P.mm(psO2[par][:, j, :], Sbf[r, j, :], qg[r, x, j, :], False, True)
                if d == 1:
                    P.copy("act", obv[:, :, par, tl], psO2[par])
                else:
                    c4 = t % 4
                    P.tt("dve", oblv[:, :, par, c4 * 128:(c4 + 1) * 128], psO2[par], obv[:, :, par, tl], ALU.add)
            if ci < NT - 1:
                for j in range(2):
                    P.tr(psT[:, j, :], kend[:, x, j, :], self.ident[:])
                P.copy("act", kendT, psT)
                for h in range(4):
                    j = h // 2
                    r = slice(64 * (h % 2), 64 * (h % 2) + 64)
                    P.mm(psU[r, j, :], kendT[:, j, 64 * (h % 2):64 * (h % 2) + 64], Vc[:, t, h * 128:(h + 1) * 128], True, True)
                for j in range(2):
                    if ci > 0:
                        P.stt("dve", S[:, j, :], S[:, j, :], cd[:, x, j:j + 1], psU[:, j, :], ALU.mult, ALU.add)
                    else:
                        P.copy("dve", S[:, j, :], psU[:, j, :])
                P.copy("pool", Sbf, S)
            if d == 0 and t % 4 == 3:
                b = t // 4
                bl = slice(b * 512, (b + 1) * 512)
                for h in range(4):
                    P.actv(sq, oblk[:, h, :], AF.Square)
                    P.mm(ps[:, 0, :], self.ones[:, :], sq, True, True)
                    P.actv(rt, ps[:, 0, :], AF.Ln, bias=EPS, scale=1.0 / 128)
                    P.actv(rt, rt, AF.Exp, scale=-0.5)
                    for k in range(8):
                        P.mm(ps[:, 1, :], wzc[:, k, h * 128:(h + 1) * 128], hT[:, k, bl], k == 0, k == 7)
                    P.actv(sz, ps[:, 1, :], AF.Silu)
                    P.stt("dve", tmp, oblk[:, h, :], gg[:, h:h + 1], rt, ALU.mult, ALU.mult)
                    P.tt("pool", onT[:, h, bl], tmp, sz, ALU.mult)

        for d in (1, 0):
            order = list(range(NT)) if d == 0 else list(range(NT - 1, -1, -1))
            front(d, 0, order[0])
            for ci, t in enumerate(order):
                if ci + 1 < NT:
                    front(d, ci + 1, order[ci + 1])
                back(d, ci, t)

    def ssd_load(self, l):
        P = self.P
        self.acur = 64 * KB
        wz = self.alloc([8, D], BF16)
        wdt = self.alloc([8, 32], BF16)
        anb = self.alloc([32], F32)
        dtb = self.alloc([32], F32)
        dsk = self.alloc([16], F32)
        gnb = self.alloc([D], F32)
        cp = self.alloc([12, 6], F32)
        self.ssd_end = self.acur
        P.dma(cp, self.conv_p[l])
        P.dma(anb, bass.AP(self.a_log.tensor, l * 32, [[0, 128], [1, 32]]))
        P.dma(dtb, bass.AP(self.dt_bias.tensor, l * 32, [[0, 128], [1, 32]]))
        P.dma(dsk, bass.AP(self.d_skip.tensor, l * 16, [[0, 128], [1, 16]]))
        P.dma(gnb, bass.AP(self.ssd_norm_g.tensor, l * D, [[0, 128], [1, D]]))
        self.wload(wdt, self.w_in, l, OFF["dt"], OFF["dt"] + 32, engs=BULK)
        self.wload(wz, self.w_in, l, OFF["za"], OFF["za"] + D, engs=BULK)
        return (wz, wdt, cp, anb, dtb, dsk, gnb)

    def phase_ssd(self, l):
        P = self.P
        self.P.phase = "ssd_conv"
        ps = self.ps
        hT = self.hT
        onT = self.onT
        xs_tok = self.view(0, [16, D], BF16)
        w = self.pre.pop("ssd", None) or self.ssd_load(l)
        wz, wdt, cp, anb, dtb, dsk, gnb = w
        self.acur = self.ssd_end
        BT = self.alloc([2, L], BF16)
        CT = self.alloc([2, L], BF16)
        Btok = self.alloc([16, 256], BF16)
        dt_all = self.alloc([16, 32], F32)
        dtaf = self.alloc([16, 32], F32)
        base = self.acur
        xb = self.alloc([L + 4], F32)
        acc = self.alloc([L], F32)
        cv = self.alloc([L], BF16)
        wx = self.alloc([3, 8, 128], BF16)
        P.actv(anb, anb, AF.Exp)
        P.ts("dve", anb, anb, -1.0, None, ALU.mult)
        P.memset("pool", xb[:, 0:2], 0.0)
        P.memset("pool", xb[:, L + 2:L + 4], 0.0)
        for c in range(2):
            self.wload(wx[:, c], self.w_in, l, OFF["xbc"] + c * 128, OFF["xbc"] + (c + 1) * 128)
        for c in range(12):
            i = c % 3
            if c + 2 < 12:
                self.wload(wx[:, (c + 2) % 3], self.w_in, l, OFF["xbc"] + (c + 2) * 128, OFF["xbc"] + (c + 3) * 128)
            for b in range(NB):
                bank = 4 + (c * 4 + b) % 4
                for k in range(8):
                    P.mm(ps[:, bank, :], wx[:, i, k, :], hT[:, k, b * 512:(b + 1) * 512], k == 0, k == 7)
                P.copy("act", xb[:, 2 + b * 512:2 + (b + 1) * 512], ps[:, bank, :])
            eng = "dve"
            P.ts(eng, acc, xb[:, 0:L], cp[:, c, 0:1], cp[:, c, 5:6], ALU.mult, ALU.add)
            for j in range(1, 5):
                P.stt(eng, acc, xb[:, j:j + L], cp[:, c, j:j + 1], acc, ALU.mult, ALU.add)
            if c < 8:
                P.actv(cv, acc, AF.Silu)
                dst_tok = lambda t, c=c: xs_tok[:, t, c * 128:(c + 1) * 128]
                src = cv
            elif c < 10:
                P.actv(BT[:, c - 8, :], acc, AF.Silu)
                dst_tok = lambda t, c=c: Btok[:, t, (c - 8) * 128:(c - 7) * 128]
                src = BT[:, c - 8, :]
            else:
                P.actv(CT[:, c - 10, :], acc, AF.Silu)
                src = None
            if src is not None:
                for half in range(2):
                    pb = ps[:, 2 + half, :].bitcast(BF16)
                    for tt_ in range(8):
                        t = half * 8 + tt_
                        P.tr(pb[:, tt_ * 128:(tt_ + 1) * 128], src[:, t * 128:(t + 1) * 128], self.ident[:])
                    if c < 8:
                        P.copy("act" if half else "dve", xs_tok[:, half * 8:(half + 1) * 8, c * 128:(c + 1) * 128],
                               pb.rearrange("p (t c) -> p t c", t=8))
                    else:
                        P.copy("act" if half else "dve", Btok[:, half * 8:(half + 1) * 8, (c - 8) * 128:(c - 7) * 128],
                               pb.rearrange("p (t c) -> p t c", t=8))
        P.phase = "ssd_scan"
        for t in range(NT):
            for k in range(8):
                P.mm(ps[:, 7, t * 32:(t + 1) * 32], hT[:, k, t * 128:(t + 1) * 128], wdt[:, k, :], k == 0, k == 7)
        dtb_b = bass.AP(dtb.tensor, dtb.offset, [list(dtb.ap[0]), [0, 16], [1, 32]])
        anb_b = bass.AP(anb.tensor, anb.offset, [list(anb.ap[0]), [0, 16], [1, 32]])
        P.tt("dve", dt_all, ps[:, 7, :].rearrange("p (t c) -> p t c", t=16), dtb_b, ALU.add)
        P.actv(dt_all, dt_all, AF.Exp)
        P.actv(dt_all, dt_all, AF.Ln, bias=1.0)
        self.acur = base
        self.allow_over = True
        Af = self.alloc([1, 8, 128], F32)
        E = self.alloc([1, 8, 128], BF16)
        W = self.alloc([1, 8, 128], BF16)
        Gm = self.alloc([2, 128], BF16)
        xd = self.alloc([D], BF16)
        xdd = self.alloc([D], BF16)
        S = self.alloc([D], F32)
        Sbf = self.alloc([D], BF16)
        ytmp = self.alloc([D], F32)
        y2 = self.alloc([2, D], F32)
        ybt = self.alloc([D], F32)
        eac = self.alloc([16], F32)
        cdb = self.alloc([16], F32)
        sz = self.alloc([D], F32)
        dta = sz[:, 0:512].rearrange("p (t c) -> p t c", t=16)
        junk = ytmp
        hb = xdd
        ss = self.alloc([4], F32)
        P.tt("dve", dta, dt_all, anb_b, ALU.mult)
        P.copy("dve", dtaf, dta)

        def hv(ap2):
            return ap2.rearrange("p (h q) -> p h q", h=16)

        def bc_h(ap_col16, n):
            return bass.AP(ap_col16.tensor, ap_col16.offset, [list(ap_col16.ap[0]), [ap_col16.ap[1][0], 16], [0, n]])

        def bc_h2(ap16, g):
            return bass.AP(ap16.tensor, ap16.offset + g * 8, [list(ap16.ap[0]), [ap16.ap[1][0], 8], [0, 128]])

        def fin_stages(ci, t):
            y = y2[:, ci % 2, :]
            tl = slice(t * 128, (t + 1) * 128)
            pz = ps[:, 0:2, :].rearrange("p b c -> p (b c)")

            def f1a():
                P.tt("dve", hv(ytmp), hv(xs_tok[:, t, :]), bc_h(dsk, 64), ALU.mult)
                P.tt("dve", y, y, ytmp, ALU.add)

            def f1b():
                for hf in range(2):
                    for k in range(8):
                        P.mm(ps[:, hf, :], hT[:, k, tl], wz[:, k, hf * 512:(hf + 1) * 512], k == 0, k == 7)

            def f2():
                P.actv(sz, pz, AF.Silu)

            def f3():
                P.tt("dve", y, y, sz, ALU.mult)

            def f4():
                P.actv(junk, y, AF.Square, accum_out=ss[:, 0:1])
                P.actv(ss[:, 1:2], ss[:, 0:1], AF.Ln, bias=EPS, scale=1.0 / D)
                P.actv(ss[:, 2:3], ss[:, 1:2], AF.Exp, scale=-0.5)

            def f5():
                P.stt("dve", hb, y, ss[:, 2:3], gnb, ALU.mult, ALU.mult)
                pb = ps[:, 7, :].bitcast(BF16)
                for k in range(8):
                    P.tr(pb[:, k * 128:(k + 1) * 128], hb[:, k * 128:(k + 1) * 128], self.ident[:])
                P.copy("act", onT[:, :, tl], pb.rearrange("p (k c) -> p k c", k=8))

            return [f1a, f1b, f2, f3, f4, f5]

        def main_stages(d, ci, t):
            Lm = self.LF if d == 0 else self.LB
            Tm = self.triF if d == 0 else self.triB
            tri = self.triF if d == 0 else self.triB
            last = 127 if d == 0 else 0
            y = y2[:, ci % 2, :]
            tl = slice(t * 128, (t + 1) * 128)
            dh = dtaf[:, t, d * 16:(d + 1) * 16]

            def m1():
                if d == 0:
                    P.dma(ybt, self.yb_d[tl, :])
                for g in range(2):
                    P.mm(ps[:, 4, g * 128:(g + 1) * 128], BT[:, g, tl], CT[:, g, tl], True, True)
                P.mm(ps[:, 4, 256:272], Tm[:, :], dh, True, True)
                P.mm(ps[:, 4, 272:288], self.onesf[:, :], dh, True, True)
                tri_b = bass.AP(tri, 0, [[128, 128], [0, 2], [1, 128]])
                P.tt("dve", Gm, ps[:, 4, 0:256].rearrange("p (g c) -> p g c", g=2), tri_b, ALU.mult)
                P.actv(eac, ps[:, 4, 256:272], AF.Exp)
                P.actv(cdb, ps[:, 4, 272:288], AF.Exp)
                P.tt("dve", hv(xd), hv(xs_tok[:, t, :]), bc_h(dt_all[:, t, d * 16:(d + 1) * 16], 64), ALU.mult)

            def mg(g):
                def f():
                    Lb = bass.AP(Lm, 0, [[128, 128], [0, 8], [1, 128]])
                    P.tt("pool", Af[:, 0], Lb, bc_h2(dh, g), ALU.mult)
                    for e in range(8):
                        out = ps[:, 2 * g + e // 4, (e % 4) * 128:(e % 4 + 1) * 128]
                        P.mm(out, Af[:, 0, e, :], Tm[:, :], True, True)
                    P.actv(E[:, 0], ps[:, 2 * g:2 * g + 2, :].rearrange("p b (e c) -> p (b e) c", e=4), AF.Exp)
                    Gb = bass.AP(Gm.tensor, Gm.offset + g * 128, [list(Gm.ap[0]), [0, 8], [1, 128]])
                    P.tt("dve", W[:, 0], E[:, 0], Gb, ALU.mult)
                    Elast = bass.AP(E.tensor, E.offset + last, [list(E.ap[0]), [128, 8], [0, 64]])
                    P.tt("dve", hv(xdd)[:, g * 8:(g + 1) * 8, :], hv(xd)[:, g * 8:(g + 1) * 8, :], Elast, ALU.mult)
                    for e in range(8):
                        h = g * 8 + e
                        P.mm(ps[:, 5 + g, e * 64:(e + 1) * 64], W[:, 0, e, :], xd[:, h * 64:(h + 1) * 64], True, True)
                return f

            def m4():
                if ci > 0:
                    for g in range(2):
                        P.mm(ps[:, g, :], CT[:, g, tl], Sbf[:, g * 512:(g + 1) * 512], True, True)
                    P.tt("dve", hv(ytmp), ps[:, 0:2, :].rearrange("p b (e q) -> p (b e) q", e=8), bc_h(eac, 64), ALU.mult)
                    P.tt("dve", y, ytmp, ps[:, 5:7, :].rearrange("p b c -> p (b c)"), ALU.add)
                else:
                    P.copy("act", y, ps[:, 5:7, :].rearrange("p b c -> p (b c)"))

            def m5():
                if ci < NT - 1:
                    for g in range(2):
                        P.mm(ps[:, 2 + g, :], Btok[:, t, g * 128:(g + 1) * 128], xdd[:, g * 512:(g + 1) * 512], True, True)
                    if ci > 0:
                        P.tt("pool", hv(S), hv(S), bc_h(cdb, 64), ALU.mult)
                        P.tt("dve", S, S, ps[:, 2:4, :].rearrange("p b c -> p (b c)"), ALU.add)
                    else:
                        P.copy("act", S, ps[:, 2:4, :].rearrange("p b c -> p (b c)"))
                    P.copy("act", Sbf, S)

            def m6():
                if d == 1:
                    P.dma(self.yb_d[tl, :], y)
                else:
                    P.tt("dve", y, y, ybt, ALU.add)

            return [m1, mg(0), mg(1), m4, m5, m6]

        for d in (1, 0):
            order = range(NT) if d == 0 else range(NT - 1, -1, -1)
            pending = None
            for ci, t in enumerate(order):
                ms = main_stages(d, ci, t)
                fs = fin_stages(*pending) if pending is not None else []
                for i in range(6):
                    ms[i]()
                    if i < len(fs):
                        fs[i]()
                if d == 0:
                    pending = (ci, t)
            if pending is not None:
                for f in fin_stages(*pending):
                    f()
        self.allow_over = False

    def merge(self, l, bi, nk, wbr_src, first, pre=None):
        P = self.P
        self.P.phase = "merge"
        ps = self.ps
        hT = self.hT
        onT = self.onT
        self.acur = 64 * KB
        wbr = self.alloc([nk, D], BF16)
        wg = self.alloc([3, 8, 128], BF16)
        sg = self.alloc([2, 512], F32)
        tm = self.alloc([2, 512], F32)
        self.wload(wbr, wbr_src, l, 0, D, engs=BULK)
        self.wload(wg[:, 0], self.w_in, l, bi * D, bi * D + 128)
        self.wload(wg[:, 1], self.w_in, l, bi * D + 128, bi * D + 256)
        if pre is not None:
            save = self.acur
            pre()
            self.acur = save
        n = 0
        for m in range(8):
            if m + 2 < 8:
                self.wload(wg[:, (m + 2) % 3], self.w_in, l, bi * D + (m + 2) * 128, bi * D + (m + 3) * 128)
            for b in range(NB):
                bl = slice(b * 512, (b + 1) * 512)
                yb = 4 + n % 2
                gb = 6 + n % 2
                for k in range(nk):
                    P.mm(ps[:, yb, :], wbr[:, k, m * 128:(m + 1) * 128], onT[:, k, bl], k == 0, k == nk - 1)
                for k in range(8):
                    P.mm(ps[:, gb, :], wg[:, m % 3, k, :], hT[:, k, bl], k == 0, k == 7)
                P.actv(sg[:, n % 2, :], ps[:, gb, :], AF.Sigmoid)
                if first:
                    P.tt("dve", self.mixT[:, m, bl], ps[:, yb, :], sg[:, n % 2, :], ALU.mult)
                else:
                    P.tt("dve", tm[:, n % 2, :], ps[:, yb, :], sg[:, n % 2, :], ALU.mult)
                    P.tt("pool", self.mixT[:, m, bl], self.mixT[:, m, bl], tm[:, n % 2, :], ALU.add)
                n += 1

    def outproj_load(self, l, last):
        self.acur = 96 * KB
        wo = self.alloc([8, D], BF16)
        gB = self.alloc([D], F32)
        self.wload(wo, self.w_out, l, 0, D, engs=BULK)
        grow = 2 if last else l + 1
        self.P.dma(gB, bass.AP(self.norm_g.tensor, grow * D, [[0, 128], [1, D]]))
        return (wo, gB)

    def outproj(self, s, l, last):
        P = self.P
        self.P.phase = "outproj"
        ps = self.ps
        w = self.pre.pop("outproj", None) or self.outproj_load(l, last)
        wo, gB = w
        self.acur = 116 * KB
        xt = self.alloc([2, D], F32)
        xn = self.alloc([2, D], F32)
        junk = self.alloc([D], BF16)
        ss = self.alloc([48], F32)
        hb = self.alloc([2, D], BF16)
        yo = self.alloc([2, D], F32)
        if not last:
            self.pre["ssd"] = self.ssd_load(self.layers[self.layers.index(l) + 1])
        xsrc = self.x if l == 0 else self.xres
        def finish(t):
            i = t % 2
            tl = slice(t * 128, (t + 1) * 128)
            if not last:
                P.dma(self.xres[s, tl, :], xn[:, i, :])
                self.norm_to_hT(xn[:, i, :], t, gB, (junk, ss, hb))
            else:
                P.actv(junk[:, :], xn[:, i, :], AF.Square, accum_out=ss[:, t:t + 1])
                P.actv(ss[:, 16 + t:17 + t], ss[:, t:t + 1], AF.Ln, bias=EPS, scale=1.0 / D)
                P.actv(ss[:, 32 + t:33 + t], ss[:, 16 + t:17 + t], AF.Exp, scale=-0.5)
                P.stt("dve", yo[:, i, :], xn[:, i, :], ss[:, 32 + t:33 + t], gB, ALU.mult, ALU.mult)
                P.dma(self.out[s, tl, :], yo[:, i, :])

        P.dma(xt[:, 0, :], xsrc[s, 0:128, :])
        for t in range(NT):
            i = t % 2
            tl = slice(t * 128, (t + 1) * 128)
            if t + 1 < NT:
                P.dma(xt[:, (t + 1) % 2, :], xsrc[s, (t + 1) * 128:(t + 2) * 128, :])
            for hf in range(2):
                bank = 4 + hf
                for k in range(8):
                    P.mm(ps[:, bank, :], self.mixT[:, k, tl], wo[:, k, hf * 512:(hf + 1) * 512], k == 0, k == 7)
                P.tt("dve", xn[:, i, hf * 512:(hf + 1) * 512], ps[:, bank, :], xt[:, i, hf * 512:(hf + 1) * 512], ALU.add)
            if t >= 1:
                finish(t - 1)
        finish(NT - 1)

    def build(self):
        self.setup()
        for s in range(self.nseq):
            nl = len(self.layers)
            for li, l in enumerate(self.layers):
                if li == 0:
                    self.phase_a(s, l)
                last = (li == nl - 1)
                order = [b for b in "abcd" if b in self.branches]
                phase = {"a": self.phase_ssd, "b": self.phase_mla, "c": self.phase_gla, "d": self.phase_gqa}
                loader = {"b": ("mla", self.mla_load), "c": ("gla", self.gla_load), "d": ("gqa", self.gqa_load)}
                mrg = {"a": (0, 8, self.w_br_a), "b": (1, 4, self.w_br_b), "c": (2, 4, self.w_br_c), "d": (3, 4, self.w_br_d)}
                for bi_, br in enumerate(order):
                    phase[br](l)
                    if bi_ + 1 < len(order):
                        key, fn = loader[order[bi_ + 1]]
                        pre = (lambda key=key, fn=fn: self.pre.__setitem__(key, fn(l)))
                    else:
                        pre = (lambda: self.pre.__setitem__("outproj", self.outproj_load(l, last)))
                    i_, nk_, wsrc_ = mrg[br]
                    self.merge(l, i_, nk_, wsrc_, bi_ == 0, pre=pre)
                self.outproj(s, l, last=(li == nl - 1))
        cnt = self.P.emit(self.es)
        return cnt


def _prep_inputs(inp):
    f = lambda a: np.ascontiguousarray(np.asarray(a, dtype=np.float32))
    w_in = f(inp["w_in"])
    p32, _ = _rope_perm_sign(32)
    p64, _ = _rope_perm_sign(64)
    kr = w_in[:, :, OFF["krope"]:OFF["krope"] + 32][:, :, p32]
    qd = w_in[:, :, OFF["qd"]:OFF["qd"] + 512].reshape(2, D, 8, 64)[:, :, :, p64].reshape(2, D, 512)
    kd = w_in[:, :, OFF["kd"]:OFF["kd"] + 128].reshape(2, D, 2, 64)[:, :, :, p64].reshape(2, D, 128)
    w_perm = np.ascontiguousarray(np.concatenate([kr, qd, kd], axis=2))
    qg = f(inp["q_norm_g"])
    kg = f(inp["k_norm_g"])
    qk_g = np.ascontiguousarray(np.stack([qg, qg[:, p64], kg, kg[:, p64]], axis=2))
    cos64, sin64 = _rope_tables(64)
    cos32, sin32 = _rope_tables(32)
    rope_cos = np.ones((128, L), np.float32)
    rope_sin = np.zeros((128, L), np.float32)
    rope_cos[0:64] = cos64
    rope_sin[0:64] = sin64
    rope_cos[64:96] = cos32
    rope_sin[64:96] = sin32
    wqb = f(inp["w_q_b"])
    w_qb_perm = np.ascontiguousarray(wqb.reshape(2, 384, 8, 96)[:, :, :, 64:96][:, :, :, p32].reshape(2, 384, 256))
    qlg = f(inp["q_lat_norm_g"]).reshape(2, 3, 128).transpose(0, 2, 1)
    kvg = f(inp["kv_lat_norm_g"]).reshape(2, 2, 128).transpose(0, 2, 1)
    lat_g = np.ascontiguousarray(np.concatenate([qlg, kvg], axis=2))
    gla_g = np.ascontiguousarray(f(inp["gla_norm_g"]).reshape(2, 4, 128).transpose(0, 2, 1))
    cw = f(inp["conv_w"]).reshape(2, 5, 12, 128).transpose(0, 3, 2, 1)
    cb = f(inp["conv_b"]).reshape(2, 12, 128).transpose(0, 2, 1)[..., None]
    conv_p = np.ascontiguousarray(np.concatenate([cw, cb], axis=3))
    shared = dict(
        conv_p=conv_p, a_log=f(inp["a_log"]).reshape(2, 32), dt_bias=f(inp["dt_bias"]).reshape(2, 32),
        d_skip=f(inp["d_skip"]), ssd_norm_g=f(inp["ssd_norm_g"]),
        w_gate_up=f(inp["w_gate_up"]), b_gate=f(inp["b_gate"]), gla_g=gla_g,
        w_in=w_in, w_perm=w_perm, w_q_b=wqb, w_qb_perm=w_qb_perm, w_kv_b=f(inp["w_kv_b"]), lat_g=lat_g,
        norm_g=np.ascontiguousarray(np.concatenate([f(inp["norm_g"]), f(inp["final_g"])[None, :]], axis=0)),
        qk_g=qk_g,
        w_br_a=f(inp["w_br_a"]), w_br_b=f(inp["w_br_b"]), w_br_c=f(inp["w_br_c"]), w_br_d=f(inp["w_br_d"]),
        w_out=f(inp["w_out"]), rope_cos=rope_cos, rope_sin=rope_sin,
    )
    return shared


_CACHE = {}


def kernel(**inputs):
    x = np.ascontiguousarray(np.asarray(inputs["x"], dtype=np.float32))
    n_cores = 8
    nseq = x.shape[0] // n_cores
    shared = _prep_inputs(inputs)
    mk = MK(nseq=nseq, layers=(0, 1), branches="abcd")
    mk.build()
    in_maps = []
    for c in range(n_cores):
        m = dict(shared)
        m["x"] = np.ascontiguousarray(x[c * nseq:(c + 1) * nseq])
        in_maps.append(m)
    res = run_bass_kernel_spmd(mk.nc, in_maps, core_ids=list(range(n_cores)))
    out = np.concatenate([np.asarray(r["out"]) for r in res.results], axis=0)
    return out.astype(np.float32)
```

```python
import sys
import numpy as np
from contextlib import ExitStack
import concourse.bass as bass
import concourse.mybir as mybir
from concourse.bass_utils import run_bass_kernel_spmd

F32 = mybir.dt.float32
BF16 = mybir.dt.bfloat16
ALU = mybir.AluOpType
AF = mybir.ActivationFunctionType
AX = mybir.AxisListType

_ESZ = {F32: 4, BF16: 2}


def _esz(dt):
    return _ESZ.get(dt, 4)


def region(ap):
    a = ap.ap
    off = int(ap.offset)
    es = _esz(ap.dtype)
    name = ap.tensor.name
    sp = str(ap.space)
    if sp == "DRAM":
        ext = sum((c - 1) * abs(s) for s, c in a) + 1
        return (name, 0, 1, off * es, (off + ext) * es)
    pstep, pcnt = a[0]
    if pstep == 0:
        pstep = 1 << 40
    p0 = off // pstep
    f0 = off % pstep
    ext = sum((c - 1) * abs(s) for s, c in a[1:]) + 1
    if sp == "PSUM":
        b0 = (f0 * es) // 2048 * 2048
        b1 = ((f0 + ext) * es + 2047) // 2048 * 2048
        return (name, p0 // 32 * 32, (p0 + pcnt + 31) // 32 * 32, b0, b1)
    return (name, p0, p0 + pcnt, f0 * es, (f0 + ext) * es)


def _ovl(r, s):
    return r[1] < s[2] and s[1] < r[2] and r[3] < s[4] and s[3] < r[4]


def _covers(r, s):
    return r[1] <= s[1] and r[2] >= s[2] and r[3] <= s[3] and r[4] >= s[4]


class Op:
    __slots__ = ("eng", "fn", "deps", "signal", "dma", "eidx", "rank", "waits", "line", "phase")

    def __init__(self, eng, fn):
        self.eng = eng
        self.fn = fn
        self.deps = set()
        self.signal = False
        self.dma = None
        self.waits = []


ENGS = ("pe", "act", "dve", "pool", "sp")
_WRAPPERS = ("mm", "tr", "actv", "tt", "ts", "stt", "copy", "memset", "recip", "dma")
NDMASEM = 8
EPOCH = 20000


class Prog:
    def __init__(self, nc):
        self.nc = nc
        self.ops = []
        self.acc = {}
        self.ndma = 0
        self.phase = ""

    def add(self, eng, fn, reads, writes, dma=False):
        op = Op(eng, fn)
        op.phase = self.phase
        try:
            fr = sys._getframe(1)
            if fr.f_code.co_filename == __file__ and fr.f_code.co_name in _WRAPPERS:
                fr = fr.f_back
            op.line = fr.f_lineno
        except Exception:
            op.line = 0
        rec_eng = "dma" if dma else eng
        idx = len(self.ops)
        ops = self.ops
        for ap in reads:
            r = region(ap)
            lst = self.acc.setdefault(r[0], [])
            done = False
            is_psum = (r[0] == "ps")
            for rec in lst:
                if rec[2]:
                    if _ovl(rec[0], r):
                        op.deps.add(rec[1])
                elif (not done) and (not dma) and rec[3] == rec_eng and rec[0] == r:
                    rec[1] = idx
                    done = True
                elif is_psum and rec[3] != rec_eng and _ovl(rec[0], r):
                    op.deps.add(rec[1])
            if not done:
                lst.append([r, idx, False, rec_eng])
        for ap in writes:
            r = region(ap)
            lst = self.acc.setdefault(r[0], [])
            keep = []
            for rec in lst:
                if _ovl(rec[0], r):
                    if rec[1] != idx:
                        op.deps.add(rec[1])
                    if _covers(r, rec[0]) and rec[1] != idx:
                        continue
                keep.append(rec)
            keep.append([r, idx, True, rec_eng])
            self.acc[r[0]] = keep
        if eng == "pe":
            op.deps = {d for d in op.deps if ops[d].eng != "pe"}
        if dma:
            op.dma = self.ndma
            self.ndma += 1
        ops.append(op)
        return op

    def mm(self, out, lhsT, rhs, start=True, stop=True):
        self.add("pe", lambda e: e.matmul(out, lhsT, rhs, start=start, stop=stop),
                 [lhsT, rhs], [out])

    def tr(self, out, in_, ident):
        self.add("pe", lambda e: e.transpose(out, in_, ident), [in_, ident], [out])

    def actv(self, out, in_, func, bias=None, scale=None, accum_out=None):
        kw = {}
        rd = [in_]
        wr = [out]
        if bias is not None:
            kw["bias"] = bias
            if not isinstance(bias, (int, float)):
                rd.append(bias)
        if scale is not None:
            kw["scale"] = scale
            if not isinstance(scale, (int, float)):
                rd.append(scale)
        if accum_out is not None:
            kw["accum_out"] = accum_out
            wr.append(accum_out)
        self.add("act", lambda e: e.activation(out, in_, func, **kw), rd, wr)

    def _veng(self, eng):
        return eng

    def tt(self, eng, out, in0, in1, op):
        self.add(eng, lambda e: e.tensor_tensor(out, in0, in1, op), [in0, in1], [out])

    def ts(self, eng, out, in0, s1, s2, op0, op1=None, accum_out=None):
        rd = [in0]
        if not isinstance(s1, (int, float)):
            rd.append(s1)
        if s2 is not None and not isinstance(s2, (int, float)):
            rd.append(s2)
        wr = [out]
        kw = {}
        if accum_out is not None:
            kw["accum_out"] = accum_out
            wr.append(accum_out)
        if op1 is None:
            self.add(eng, lambda e: e.tensor_scalar(out, in0, s1, None, op0, **kw), rd, wr)
        else:
            self.add(eng, lambda e: e.tensor_scalar(out, in0, s1, s2, op0, op1, **kw), rd, wr)

    def stt(self, eng, out, in0, scalar, in1, op0, op1):
        rd = [in0, in1]
        if not isinstance(scalar, (int, float)):
            rd.append(scalar)
        self.add(eng, lambda e: e.scalar_tensor_tensor(out, in0, scalar, in1, op0, op1), rd, [out])

    def copy(self, eng, out, in_):
        if eng == "act":
            self.add("act", lambda e: e.activation(out, in_, AF.Copy), [in_], [out])
        else:
            self.add(eng, lambda e: e.tensor_copy(out, in_), [in_], [out])

    def memset(self, eng, out, val):
        self.add(eng, lambda e: e.memset(out, val), [], [out])

    def recip(self, out, in_):
        self.add("dve", lambda e: e.reciprocal(out, in_), [in_], [out])

    def dma(self, out, in_, eng="sp", **kw):
        self.add(eng, lambda e: e.dma_start(out, in_, **kw), [in_], [out], dma=True)

    def emit(self, es, final_wait_all=True):
        nc = self.nc
        ops = self.ops
        cnt = {e: 0 for e in ENGS}
        for op in ops:
            op.eidx = cnt[op.eng]
            cnt[op.eng] += 1
        for i, op in enumerate(ops):
            for d in op.deps:
                dop = ops[d]
                if dop.dma is None:
                    if dop.eng == op.eng and op.eng != "sp":
                        pass
                    dop.signal = True
        last = {}
        for i, op in enumerate(ops):
            if op.dma is None:
                last[op.eng] = i
        for e, i in last.items():
            ops[i].signal = True
        rk = {e: 0 for e in ENGS}
        for op in ops:
            if op.dma is None and op.signal:
                rk[op.eng] += 1
                op.rank = rk[op.eng]
        nep = {e: (rk[e] + EPOCH - 1) // EPOCH + 1 for e in ENGS}
        sems = {}
        for e in ENGS:
            if e == "sp":
                continue
            sems[e] = [es.enter_context(nc.semaphore(f"s_{e}_{k}")) for k in range(nep[e])]
        dsem = [es.enter_context(nc.semaphore(f"s_dma_{k}")) for k in range(NDMASEM)]

        import os as _os
        simmode = bool(_os.environ.get("SIMMODE"))
        dinfo = {}
        m = 0
        for op in ops:
            if op.dma is None:
                continue
            if simmode and op.eng == "pool":
                sem = es.enter_context(nc.semaphore(f"s_u_{op.dma}"))
                dinfo[op.dma] = (("udma", op.dma), sem, 16)
            else:
                k = m % NDMASEM
                dinfo[op.dma] = (("dma", k), dsem[k], 16 * (m // NDMASEM + 1))
                m += 1

        def target(dop):
            if dop.dma is not None:
                return dinfo[dop.dma]
            r = dop.rank - 1
            return ((dop.eng, r // EPOCH), sems[dop.eng][r // EPOCH], r % EPOCH + 1)

        waited = {e: {} for e in ENGS}
        for i, op in enumerate(ops):
            w = waited[op.eng]
            need = {}
            for d in op.deps:
                key, sem, val = target(ops[d])
                if w.get(key, 0) >= val:
                    continue
                if key not in need or need[key][1] < val:
                    need[key] = (sem, val)
            if op.dma is not None:
                key, sem, val = dinfo[op.dma]
                val -= 16
                if val > 0 and w.get(key, 0) < val and (key not in need or need[key][1] < val):
                    need[key] = (sem, val)
            for key, (sem, val) in need.items():
                w[key] = val
                op.waits.append((sem, val))
        final_waits = []
        for e, i in last.items():
            key, sem, val = target(ops[i])
            final_waits.append((sem, val))
        fin = {}
        for key, sem, val in dinfo.values():
            if key not in fin or fin[key][1] < val:
                fin[key] = (sem, val)
        final_waits.extend(fin.values())

        byeng = {e: [op for op in ops if op.eng == e] for e in ENGS}

        annotate = bool(_os.environ.get("SIMMODE"))

        def run(e, name):
            for op in byeng[name]:
                for sem, val in op.waits:
                    e.wait_ge(sem, val)
                ins = op.fn(e)
                if annotate:
                    ins.annotate(f"L{op.line}")
                if op.dma is not None:
                    ins.then_inc(dinfo[op.dma][1], 16)
                elif op.signal:
                    r = op.rank - 1
                    ins.then_inc(sems[name][r // EPOCH], 1)
            if name == "sp":
                for sem, val in final_waits:
                    e.wait_ge(sem, val)

        with nc.Block() as block:
            @block.tensor
            def _(e):
                run(e, "pe")

            @block.scalar
            def _(e):
                run(e, "act")

            @block.vector
            def _(e):
                run(e, "dve")

            @block.gpsimd
            def _(e):
                run(e, "pool")

            @block.sync
            def _(e):
                run(e, "sp")
        return cnt


L = 2048
D = 1024
NT = 16
NB = 4
EPS = 1e-6
OFF = dict(g=0, za=4096, xbc=5120, dt=6656, zb=6688, qlat=7200, kvlat=7584, krope=7840,
           zc=7872, qc=8384, kc=8640, vc=8896, glr=9408, zd=9440, qd=9952, kd=10464, vd=10592)
N_IN = 10720
KB = 1024
BULK = ("dve", "dve", "pool")


def _rope_perm_sign(d):
    m = d // 2
    hm = m // 2
    perm = np.zeros(d, np.int64)
    sign = np.zeros(d, np.float32)
    for j in range(d):
        jj = j % m
        if jj < hm:
            perm[j] = j + hm
            sign[j] = -1.0
        else:
            perm[j] = j - hm
            sign[j] = 1.0
    return perm, sign


def _rope_tables(d):
    rows = L // 64
    row = np.repeat(np.arange(rows), 64).astype(np.float32)
    col = np.tile(np.arange(64), rows).astype(np.float32)
    m = d // 2
    inv = (np.float32(10000.0) ** (-np.arange(0, m, 2, dtype=np.float32) / np.float32(m))).astype(np.float32)
    ang_r = row[:, None] * inv
    ang_c = col[:, None] * inv
    ang = np.concatenate([ang_r, ang_r, ang_c, ang_c], axis=-1).astype(np.float32)
    _, sign = _rope_perm_sign(d)
    cos = np.cos(ang).astype(np.float32).T
    sin = (np.sin(ang).astype(np.float32) * sign[None, :]).T
    return np.ascontiguousarray(cos), np.ascontiguousarray(sin)


class MK:
    def __init__(self, nseq=2, layers=(0, 1), branches="abcd", debug=False):
        self.nseq = nseq
        self.layers = layers
        self.branches = branches
        self.debug = debug
        nc = self.nc = bass.Bass("TRN2", target_bir_lowering=False)
        es = self.es = ExitStack()
        self.P = Prog(nc)
        self.dbg_outs = {}
        di = lambda n, s: nc.dram_tensor(n, s, F32, kind="ExternalInput").ap()
        self.x = di("x", [nseq, L, D])
        self.out = nc.dram_tensor("out", [nseq, L, D], F32, kind="ExternalOutput").ap()
        self.xres = nc.dram_tensor("xres", [nseq, L, D], F32).ap()
        self.w_in = di("w_in", [2, D, N_IN])
        self.w_perm = di("w_perm", [2, D, 672])
        self.norm_g = di("norm_g", [3, D])
        self.qk_g = di("qk_g", [2, 64, 4])
        self.w_q_b = di("w_q_b", [2, 384, 768])
        self.w_qb_perm = di("w_qb_perm", [2, 384, 256])
        self.w_kv_b = di("w_kv_b", [2, 256, 1024])
        self.lat_g = di("lat_g", [2, 128, 5])
        self.w_gate_up = di("w_gate_up", [2, 2, 16, 256])
        self.b_gate = di("b_gate", [2, 2, 256])
        self.gla_g = di("gla_g", [2, 128, 4])
        self.conv_p = di("conv_p", [2, 128, 12, 6])
        self.a_log = di("a_log", [2, 32])
        self.dt_bias = di("dt_bias", [2, 32])
        self.d_skip = di("d_skip", [2, 16])
        self.ssd_norm_g = di("ssd_norm_g", [2, D])
        self.yb_d = nc.dram_tensor("yb_d", [L, D], F32).ap()
        self.w_br_a = di("w_br_a", [2, 1024, D])
        self.w_br_b = di("w_br_b", [2, 512, D])
        self.w_br_c = di("w_br_c", [2, 512, D])
        self.w_br_d = di("w_br_d", [2, 512, D])
        self.w_out = di("w_out", [2, D, D])
        self.rope_cos = di("rope_cos", [128, L])
        self.rope_sin = di("rope_sin", [128, L])
        sb = lambda n, s, d: es.enter_context(nc.sbuf_tensor(n, s, d))
        self.hT = sb("hT", [128, 8, L], BF16)
        self.COS = sb("COS", [128, L], F32)
        self.SIN = sb("SIN", [128, L], F32)
        self.ident = sb("ident", [128, 128], BF16)
        self.identf = sb("identf", [128, 128], F32)
        self.ones = sb("ones", [128, 128], BF16)
        self.onesf = sb("onesf", [128, 128], F32)
        self.triF = sb("triF", [128, 128], F32)
        self.triB = sb("triB", [128, 128], F32)
        self.LF = sb("LF", [128, 128], F32)
        self.LB = sb("LB", [128, 128], F32)
        self.TmFb = sb("TmFb", [128, 128], BF16)
        self.TmBb = sb("TmBb", [128, 128], BF16)
        self.ARW = 155 * 256
        self.AR = sb("AR", [128, self.ARW], F32)
        self.ps = es.enter_context(nc.psum_tensor("ps", [128, 8, 512], F32))
        self.mixT = self.view(0, [8, L], BF16)
        self.onT = self.view(32 * KB, [8, L], BF16)
        self.acur = 64 * KB
        self.stage = self.view(147 * KB, [4, 512], F32)
        self.nst = 0
        self.pre = {}
        self.alim = 147 * KB

    def view(self, off, shape, dt, p0=0, parts=128):
        es_ = _esz(dt)
        n = int(np.prod(shape))
        assert off % 4 == 0
        w0 = off // 4
        w1 = (off + n * es_ + 3) // 4
        assert w1 <= self.ARW, (off, shape)
        ap = self.AR[p0:p0 + parts, w0:w1]
        if dt != F32:
            ap = ap.bitcast(dt)
        if len(shape) == 2:
            ap = ap.rearrange("p (a b) -> p a b", a=shape[0])
        elif len(shape) == 3:
            ap = ap.rearrange("p (a b c) -> p a b c", a=shape[0], b=shape[1])
        return ap

    def alloc(self, shape, dt, p0=0, parts=128):
        n = int(np.prod(shape)) * _esz(dt)
        n = (n + 63) // 64 * 64
        v = self.view(self.acur, shape, dt, p0, parts)
        self.acur += n
        assert self.acur <= self.alim or getattr(self, "allow_over", False), (self.acur, shape)
        return v

    def dbg(self, name, ap_sb, shape):
        if not self.debug:
            return
        d = self.nc.dram_tensor("dbg_" + name, shape, ap_sb.dtype, kind="ExternalOutput").ap()
        self.dbg_outs[name] = d
        self.P.dma(d, ap_sb)

    def wcols(self, src, l, c0, c1):
        return src[l].rearrange("(k p) c -> p k c", p=128)[:, :, c0:c1]

    def wload(self, dst, src, l, c0, c1, engs=("pool",)):
        P = self.P
        nk = dst.shape[1]
        C = c1 - c0
        for cc in range(0, C, 512):
            for k in range(nk):
                w = min(512, C - cc)
                st = self.stage[:, self.nst % 4, 0:w]
                self.nst += 1
                P.dma(st, src[l, k * 128:(k + 1) * 128, c0 + cc:c0 + cc + w])
                P.copy(engs[self.nst % len(engs)], dst[:, k, cc:cc + w], st)

    def setup(self):
        P = self.P
        P.dma(self.COS[:], self.rope_cos)
        P.dma(self.SIN[:], self.rope_sin)
        P.memset("pool", self.onesf[:], 1.0)
        P.memset("pool", self.ones[:], 1.0)
        P.memset("pool", self.identf[:], 0.0)
        idf = self.identf
        onf = self.onesf
        P.add("pool", lambda e: e.affine_select(idf[:], onf[:], [[-1, 128]], ALU.is_equal, 0.0,
                                                base=0, channel_multiplier=1), [onf[:]], [idf[:]])
        P.copy("dve", self.ident[:], self.identf[:])
        tf, tb = self.triF, self.triB
        P.add("pool", lambda e: e.affine_select(tf[:], onf[:], [[1, 128]], ALU.is_ge, 0.0,
                                                base=0, channel_multiplier=-1), [onf[:]], [tf[:]])
        P.add("pool", lambda e: e.affine_select(tb[:], onf[:], [[-1, 128]], ALU.is_ge, 0.0,
                                                base=0, channel_multiplier=1), [onf[:]], [tb[:]])
        P.ts("dve", self.TmFb[:], tf[:], -1.0 / 16, None, ALU.mult)
        P.tt("dve", self.LF[:], tb[:], self.identf[:], ALU.subtract)
        P.tt("dve", self.LB[:], tf[:], self.identf[:], ALU.subtract)
        P.ts("dve", self.TmBb[:], tb[:], -1.0 / 16, None, ALU.mult)

    def norm_to_hT(self, xt, t, gB, scr):
        P = self.P
        junk, ss, hb = scr
        i = t % 2
        P.actv(junk[:, :], xt, AF.Square, accum_out=ss[:, t:t + 1])
        P.actv(ss[:, 16 + t:17 + t], ss[:, t:t + 1], AF.Ln, bias=EPS, scale=1.0 / D)
        P.actv(ss[:, 32 + t:33 + t], ss[:, 16 + t:17 + t], AF.Exp, scale=-0.5)
        P.stt("dve", hb[:, i, :], xt, ss[:, 32 + t:33 + t], gB, ALU.mult, ALU.mult)
        pb = self.ps[:, 7, :].bitcast(BF16)
        for k in range(8):
            P.tr(pb[:, k * 128:(k + 1) * 128], hb[:, i, k * 128:(k + 1) * 128], self.ident[:])
        P.copy("act", self.hT[:, :, t * 128:(t + 1) * 128],
               pb.rearrange("p (k c) -> p k c", k=8))

    def phase_a(self, s, l):
        P = self.P
        self.P.phase = "A"
        self.pre["ssd"] = self.ssd_load(l)
        self.acur = 100 * KB
        xt = self.alloc([2, D], F32)
        gB = self.alloc([D], F32)
        junk = self.alloc([D], BF16)
        ss = self.alloc([48], F32)
        hb = self.alloc([2, D], BF16)
        P.dma(gB, bass.AP(self.norm_g.tensor, l * D, [[0, 128], [1, D]]))
        for t in range(NT):
            P.dma(xt[:, t % 2, :], self.x[s, t * 128:(t + 1) * 128, :])
            self.norm_to_hT(xt[:, t % 2, :], t, gB, (junk, ss, hb))

    def attn_units(self, kT_fn, qT, V_fn, ob, pT, scale):
        P = self.P
        ps = self.ps
        P.mm(ps[:, 0, :], kT_fn(0), qT)
        for kc in range(16):
            if kc + 1 < 16:
                P.mm(ps[:, (kc + 1) % 2, :], kT_fn(kc + 1), qT)
            P.actv(pT[:, kc % 2, :], ps[:, kc % 2, :], AF.Exp, scale=scale)
            P.mm(ps[:, ob, :], V_fn(kc), pT[:, kc % 2, :], start=(kc == 0), stop=(kc == 15))

    def qk_norm_rope(self, psA, psB, gtile, gc, out, b, tmp):
        P = self.P
        sq, rt, t1, t2 = tmp
        bl = slice(b * 512, (b + 1) * 512)
        P.actv(sq[0:64, :], psA, AF.Square)
        P.mm(self.ps[0:64, 6, :], self.ones[0:64, 0:64], sq[0:64, :])
        P.actv(rt[0:64, :], self.ps[0:64, 6, :], AF.Ln, bias=EPS, scale=1.0 / 64)
        P.actv(rt[0:64, :], rt[0:64, :], AF.Exp, scale=-0.5)
        P.stt("dve", t1[0:64, :], psA, gtile[0:64, gc:gc + 1], self.COS[0:64, bl], ALU.mult, ALU.mult)
        P.stt("dve", t2[0:64, :], psB, gtile[0:64, gc + 1:gc + 2], self.SIN[0:64, bl], ALU.mult, ALU.mult)
        P.tt("pool", t1[0:64, :], t1[0:64, :], t2[0:64, :], ALU.add)
        P.tt("pool", out, t1[0:64, :], rt[0:64, :], ALU.mult)

    def gqa_load(self, l):
        P = self.P
        self.acur = 108 * KB
        COSG = self.alloc([L], F32)
        SING = self.alloc([L], F32)
        wk2 = self.alloc([2, 8, 128], BF16)
        wkp2 = self.alloc([2, 8, 128], BF16)
        wv = self.alloc([8, 128], BF16)
        wq = self.alloc([2, 8, 128], BF16)
        wqp = self.alloc([2, 8, 128], BF16)
        wz = self.alloc([2, 8, 128], BF16)
        gt = self.alloc([4], F32)
        for g in range(2):
            for hlf in range(2):
                self.wload(wk2[:, g, :, 64 * hlf:64 * hlf + 64], self.w_in, l, OFF["kd"] + g * 64, OFF["kd"] + (g + 1) * 64, engs=BULK)
                self.wload(wkp2[:, g, :, 64 * hlf:64 * hlf + 64], self.w_perm, l, 544 + g * 64, 544 + (g + 1) * 64, engs=BULK)
        for hlf in range(2):
            rows = slice(64 * hlf, 64 * hlf + 64)
            P.dma(COSG[rows, :], self.rope_cos[0:64, :])
            P.dma(SING[rows, :], self.rope_sin[0:64, :])
            P.dma(gt[rows, :], self.qk_g[l])
        self.wload(wv, self.w_in, l, OFF["vd"], OFF["vd"] + 128, engs=BULK)
        self.wload(wq[:, 0], self.w_in, l, OFF["qd"], OFF["qd"] + 128, engs=BULK)
        self.wload(wqp[:, 0], self.w_perm, l, 32, 32 + 128, engs=BULK)
        self.wload(wz[:, 0], self.w_in, l, OFF["zd"], OFF["zd"] + 128, engs=BULK)
        return (COSG, SING, wk2, wkp2, wv, wq, wqp, wz, gt)

    def phase_gqa(self, l):
        P = self.P
        self.P.phase = "gqa_prep"
        ps = self.ps
        hT = self.hT
        onT = self.onT
        self.acur = 64 * KB
        BD = self.alloc([128], BF16)
        kT2 = self.alloc([2, L], BF16)
        Va = self.alloc([2, 16, 192], BF16)
        qT2 = self.alloc([2, 512], BF16)
        pT = self.alloc([2, 2, 512], BF16)
        sq = self.alloc([512], BF16)
        rt = self.alloc([512], F32)
        t1 = self.alloc([512], F32)
        t2 = self.alloc([512], F32)
        sz = self.alloc([2, 512], F32)
        ez = self.alloc([512], F32)
        rs = t1
        nt = t2
        osb = self.alloc([2, 512], F32)
        assert self.acur <= 108 * KB
        w = self.pre.pop("gqa", None) or self.gqa_load(l)
        COSG, SING, wk2, wkp2, wv, wq, wqp, wz, gt = w
        P.memset("pool", BD, 0.0)
        P.memset("pool", BD[0:64, 0:64], 1.0)
        P.memset("pool", BD[64:128, 64:128], 1.0)
        P.memset("pool", Va[:, :, :, 64:128], 1.0)

        def proj(wa, wb, bl):
            for k in range(8):
                P.mm(ps[:, 6, :], wa(k), hT[:, k, bl], k == 0, k == 7)
            for k in range(8):
                P.mm(ps[:, 7, :], wb(k), hT[:, k, bl], k == 0, k == 7)
            P.actv(sq, ps[:, 6, :], AF.Square)

        def rope1(gc, bl):
            P.stt("dve", t1, ps[:, 6, :], gt[:, gc:gc + 1], COSG[:, bl], ALU.mult, ALU.mult)
            P.stt("dve", t2, ps[:, 7, :], gt[:, gc + 1:gc + 2], SING[:, bl], ALU.mult, ALU.mult)

        def rope2(out):
            P.mm(ps[:, 7, :], BD, sq, True, True)
            P.actv(rt, ps[:, 7, :], AF.Ln, bias=EPS, scale=1.0 / 64)
            P.actv(rt, rt, AF.Exp, scale=-0.5)
            P.tt("pool", t1, t1, t2, ALU.add)
            P.tt("pool", out, t1, rt, ALU.mult)

        for g in range(2):
            for b in range(NB):
                bl = slice(b * 512, (b + 1) * 512)
                proj(lambda k: wk2[:, g, k, :], lambda k: wkp2[:, g, k, :], bl)
                rope1(2, bl)
                rope2(kT2[:, g, bl])
        for t in range(NT):
            for k in range(8):
                P.mm(ps[:, 4 + t % 2, 0:128], hT[:, k, t * 128:(t + 1) * 128], wv[:, k, :], k == 0, k == 7)
            for g in range(2):
                P.copy("act", Va[:, g, t, 0:64], ps[:, 4 + t % 2, g * 64:(g + 1) * 64])
                P.copy("dve", Va[:, g, t, 128:192], ps[:, 4 + t % 2, g * 64:(g + 1) * 64])
        P.phase = "gqa_heads"
        items = [(pr, b) for pr in range(4) for b in range(NB)]

        def prep_parts(it):
            pr, b = items[it]
            i = pr % 2
            bl = slice(b * 512, (b + 1) * 512)
            hooks = {}

            def add(kc, f):
                prev = hooks.get(kc)

                def both(prev=prev, f=f):
                    if prev is not None:
                        prev()
                    f()
                hooks[kc] = both

            def loads():
                if b == NB - 1 and pr + 1 < 4:
                    p2 = pr + 1
                    i2 = p2 % 2
                    self.wload(wq[:, i2], self.w_in, l, OFF["qd"] + p2 * 128, OFF["qd"] + (p2 + 1) * 128)
                    self.wload(wqp[:, i2], self.w_perm, l, 32 + p2 * 128, 32 + (p2 + 1) * 128)
                    self.wload(wz[:, i2], self.w_in, l, OFF["zd"] + p2 * 128, OFF["zd"] + (p2 + 1) * 128)
            add(0, loads)

            def pk(k):
                P.mm(ps[:, 6, :], wq[:, i, k, :], hT[:, k, bl], k == 0, k == 7)
                P.mm(ps[:, 7, :], wqp[:, i, k, :], hT[:, k, bl], k == 0, k == 7)
            for k in range(8):
                add(k, lambda k=k: pk(k))

            def sq_rope1():
                P.actv(sq, ps[:, 6, :], AF.Square)
                rope1(0, bl)
            add(8, sq_rope1)
            add(9, lambda: rope2(qT2[:, it % 2, :]))
            for k in range(8):
                add(10 + min(k, 5), lambda k=k: P.mm(ps[:, 6, :], wz[:, i, k, :], hT[:, k, bl], k == 0, k == 7))

            def silu():
                P.actv(ez, ps[:, 6, :], AF.Exp, scale=-1.0)
                P.actv(ez, ez, AF.Ln, bias=1.0)
                P.actv(ez, ez, AF.Exp, scale=-1.0)
                P.tt("dve", sz[:, it % 2, :], ps[:, 6, :], ez, ALU.mult)
            add(15, silu)
            return hooks

        def units(it, hooks):
            pr, b = items[it]
            g = pr // 2
            bl = slice(b * 512, (b + 1) * 512)
            qb = qT2[:, it % 2, :]
            sbank = [[0, 1], [4, 5]]
            rows = [slice(0, 64), slice(64, 128)]
            vsel = [slice(0, 128), slice(64, 192)]
            for par in range(2):
                P.mm(ps[:, sbank[par][0], :], kT2[rows[par], g, 0:128], qb[rows[par], :])
            for kc in range(16):
                if kc + 1 < 16:
                    for par in range(2):
                        P.mm(ps[:, sbank[par][(kc + 1) % 2], :], kT2[rows[par], g, (kc + 1) * 128:(kc + 2) * 128], qb[rows[par], :])
                for par in range(2):
                    P.actv(pT[:, par, kc % 2, :], ps[:, sbank[par][kc % 2], :], AF.Exp, scale=0.125)
                for par in range(2):
                    P.mm(ps[:, 2 + par, :], Va[:, g, kc, vsel[par]], pT[:, par, kc % 2, :], kc == 0, kc == 15)
                if kc in hooks:
                    hooks[kc]()
            for par in range(2):
                P.copy("dve", osb[:, par, :], ps[:, 2 + par, :])
            for par in range(2):
                orow = rows[par]
                srow = rows[1 - par]
                P.recip(rs[orow, :], osb[srow, par, :])
                P.tt("dve", nt[orow, :], osb[orow, par, :], rs[orow, :], ALU.mult)
                P.tt("pool", onT[orow, pr, bl], nt[orow, :], sz[orow, it % 2, :], ALU.mult)

        h0 = prep_parts(0)
        for kc in sorted(h0):
            h0[kc]()
        for it in range(len(items)):
            hooks = prep_parts(it + 1) if it + 1 < len(items) else {}
            units(it, hooks)

    def mla_load(self, l):
        self.acur = 124 * KB
        wq = self.alloc([3, 768], BF16)
        wqp = self.alloc([3, 256], BF16)
        wkv = self.alloc([2, 1024], BF16)
        wql = self.alloc([8, 384], BF16)
        wkvl = self.alloc([8, 256], BF16)
        wkr = self.alloc([2, 8, 32], BF16)
        lg = self.alloc([5], F32)
        self.wload(wql, self.w_in, l, OFF["qlat"], OFF["qlat"] + 384, engs=BULK)
        self.wload(wkvl, self.w_in, l, OFF["kvlat"], OFF["kvlat"] + 256, engs=BULK)
        self.wload(wkr[:, 0], self.w_in, l, OFF["krope"], OFF["krope"] + 32, engs=BULK)
        self.wload(wkr[:, 1], self.w_perm, l, 0, 32, engs=BULK)
        self.wload(wq, self.w_q_b, l, 0, 768, engs=BULK)
        self.wload(wqp, self.w_qb_perm, l, 0, 256, engs=BULK)
        self.wload(wkv, self.w_kv_b, l, 0, 1024, engs=BULK)
        self.P.dma(lg, self.lat_g[l])
        return (wq, wqp, wkv, wql, wkvl, wkr, lg)

    def phase_mla(self, l):
        P = self.P
        self.P.phase = "mla_prep"
        ps = self.ps
        hT = self.hT
        onT = self.onT
        COS, SIN = self.COS, self.SIN
        self.acur = 64 * KB
        qln = self.alloc([3, L], BF16)
        kvn = self.alloc([2, L], BF16)
        krT = self.alloc([L], BF16)
        Va = self.alloc([16, 4, 192], BF16)
        qT = self.alloc([2, 512], BF16)
        pT = self.alloc([4, 512], BF16)
        sz = self.alloc([2, 512], F32)
        wz = self.alloc([2, 8, 64], BF16)
        assert self.acur <= 124 * KB
        self.acur = 48 * KB
        kT = self.alloc([2, L], BF16)
        t1 = self.alloc([512], F32)
        t2 = self.alloc([512], F32)
        sqc = self.alloc([2, 512], BF16)
        rt = self.alloc([512], F32)
        rs = t1
        assert self.acur <= 64 * KB
        w = self.pre.pop("mla", None) or self.mla_load(l)
        wq, wqp, wkv, wql, wkvl, wkr, lg = w
        P.memset("pool", Va[:, :, :, 64:128], 1.0)
        nsq = 0
        for b in range(NB):
            bl = slice(b * 512, (b + 1) * 512)
            for (wsrc, nch, bank0, sbank, dst, gofs, dim) in ((wql, 3, 0, 3, qln, 0, 384), (wkvl, 2, 4, 6, kvn, 3, 256)):
                for c in range(nch):
                    for k in range(8):
                        P.mm(ps[:, bank0 + c, :], wsrc[:, k, c * 128:(c + 1) * 128], hT[:, k, bl], k == 0, k == 7)
                for c in range(nch):
                    P.actv(sqc[:, nsq % 2, :], ps[:, bank0 + c, :], AF.Square)
                    P.mm(ps[:, sbank, :], self.ones[:, :], sqc[:, nsq % 2, :], c == 0, c == nch - 1)
                    nsq += 1
                P.actv(rt, ps[:, sbank, :], AF.Ln, bias=EPS, scale=1.0 / dim)
                P.actv(rt, rt, AF.Exp, scale=-0.5)
                for c in range(nch):
                    P.stt("dve", dst[:, c, bl], ps[:, bank0 + c, :], lg[:, gofs + c:gofs + c + 1], rt, ALU.mult, ALU.mult)
            for k in range(8):
                P.mm(ps[64:96, 7, :], wkr[:, 0, k, :], hT[:, k, bl], k == 0, k == 7)
            P.tt("dve", t1[64:96, :], ps[64:96, 7, :], COS[64:96, bl], ALU.mult)
            for k in range(8):
                P.mm(ps[64:96, 7, :], wkr[:, 1, k, :], hT[:, k, bl], k == 0, k == 7)
            P.tt("dve", t2[64:96, :], ps[64:96, 7, :], SIN[64:96, bl], ALU.mult)
            P.tt("pool", krT[64:96, bl], t1[64:96, :], t2[64:96, :], ALU.add)
        wv_view = wkv.rearrange("p c (h two d) -> p c h two d", h=8, two=2)
        for t in range(NT):
            tl = slice(t * 128, (t + 1) * 128)
            for c in range(2):
                P.mm(ps[:, 4 + t % 2, :].rearrange("p (h d) -> p h d", h=8), kvn[:, c, tl], wv_view[:, c, :, 1, :], c == 0, c == 1)
            pv = ps[:, 4 + t % 2, :].rearrange("p (j two d) -> p j two d", j=4, two=2)
            P.copy("act", Va[:, t, :, 0:64], pv[:, :, 0, :])
            P.copy("dve", Va[:, t, :, 128:192], pv[:, :, 1, :])
        scale = 96.0 ** -0.5
        P.phase = "mla_heads"
        items = [(h, b) for h in range(8) for b in range(NB)]
        ez = t2

        def head_prep(h):
            i = h % 2
            if h == 0:
                self.wload(wz[:, i], self.w_in, l, OFF["zb"] + h * 64, OFF["zb"] + (h + 1) * 64)
            for b in range(NB):
                bl = slice(b * 512, (b + 1) * 512)
                for c in range(2):
                    P.mm(ps[0:64, 6, :], wkv[:, c, h * 128:h * 128 + 64], kvn[:, c, bl], c == 0, c == 1)
                P.copy("dve", kT[0:64, i, bl], ps[0:64, 6, :])
            P.copy("pool", kT[64:96, i, :], krT[64:96, :])

        def prep_parts(it):
            h, b = items[it]
            i = h % 2
            par = h % 2
            orow = slice(64 * par, 64 * par + 64)
            bl = slice(b * 512, (b + 1) * 512)
            qb = qT[:, it % 2, :]

            def p0():
                if b == 0:
                    head_prep(h)
                if b == NB - 2 and h + 1 < 8:
                    h2 = h + 1
                    self.wload(wz[:, h2 % 2], self.w_in, l, OFF["zb"] + h2 * 64, OFF["zb"] + (h2 + 1) * 64)

            def p1():
                for c in range(3):
                    P.mm(ps[0:96, 6, :], wq[:, c, h * 96:(h + 1) * 96], qln[:, c, bl], c == 0, c == 2)
                for c in range(3):
                    P.mm(ps[64:96, 7, :], wqp[:, c, h * 32:(h + 1) * 32], qln[:, c, bl], c == 0, c == 2)
                P.copy("dve", qb[0:64, :], ps[0:64, 6, :])
                P.tt("dve", t1[64:96, :], ps[64:96, 6, :], COS[64:96, bl], ALU.mult)
                P.tt("dve", t2[64:96, :], ps[64:96, 7, :], SIN[64:96, bl], ALU.mult)
                P.tt("pool", qb[64:96, :], t1[64:96, :], t2[64:96, :], ALU.add)

            def p3():
                for k in range(8):
                    P.mm(ps[orow, 7, :], wz[:, i, k, :], hT[:, k, bl], k == 0, k == 7)
                P.actv(ez[orow, :], ps[orow, 7, :], AF.Exp, scale=-1.0)
                P.actv(ez[orow, :], ez[orow, :], AF.Ln, bias=1.0)
                P.actv(ez[orow, :], ez[orow, :], AF.Exp, scale=-1.0)
                P.tt("dve", sz[orow, it % 2, :], ps[orow, 7, :], ez[orow, :], ALU.mult)

            return [p0, p1, p3]

        def units(it, hooks):
            h, b = items[it]
            i = h % 2
            par = h % 2
            pr = h // 2
            orow = slice(64 * par, 64 * par + 64)
            srow = slice(64 * (1 - par), 64 * (1 - par) + 64)
            vsel = slice(0, 128) if par == 0 else slice(64, 192)
            bl = slice(b * 512, (b + 1) * 512)
            qb = qT[:, it % 2, :]
            ob = 2 + it % 2
            sbanks = [[0, 1], [4, 5]]

            def S(kc):
                P.mm(ps[:, sbanks[(kc // 2) % 2][kc % 2], :], kT[0:96, i, kc * 128:(kc + 1) * 128], qb[0:96, :])
            S(0)
            S(1)
            for k2 in range(8):
                if k2 + 1 < 8:
                    S(2 * k2 + 2)
                    S(2 * k2 + 3)
                for kc in (2 * k2, 2 * k2 + 1):
                    P.actv(pT[:, kc % 4, :], ps[:, sbanks[(kc // 2) % 2][kc % 2], :], AF.Exp, scale=scale)
                for kc in (2 * k2, 2 * k2 + 1):
                    P.mm(ps[:, ob, :], Va[:, kc, pr, vsel], pT[:, kc % 4, :], kc == 0, kc == 15)
                if k2 in hooks:
                    hooks[k2]()
            P.recip(rs[orow, :], ps[srow, ob, :])
            P.tt("dve", rs[orow, :], ps[orow, ob, :], rs[orow, :], ALU.mult)
            P.tt("pool", onT[orow, pr, bl], rs[orow, :], sz[orow, it % 2, :], ALU.mult)

        for f in prep_parts(0):
            f()
        for it in range(len(items)):
            hooks = {}
            if it + 1 < len(items):
                pp = prep_parts(it + 1)
                hooks = {0: pp[0], 2: pp[1], 5: pp[2]}
            units(it, hooks)

    def gla_load(self, l):
        P = self.P
        self.acur = 48 * KB
        wvc = self.alloc([8, 512], BF16)
        wzc = self.view(48 * KB, [8, 512], BF16)
        wqc = self.alloc([8, 256], BF16)
        wkc = self.alloc([8, 256], BF16)
        assert self.acur <= 64 * KB
        self.acur = 143 * KB
        wgu = self.alloc([2, 256], BF16)
        bg = self.alloc([2, 256], BF16)
        wglr = self.alloc([2, 8, 32], BF16)
        gg = self.alloc([4], F32)
        self.wload(wqc, self.w_in, l, OFF["qc"], OFF["qc"] + 256, engs=BULK)
        self.wload(wkc, self.w_in, l, OFF["kc"], OFF["kc"] + 256, engs=BULK)
        self.wload(wvc, self.w_in, l, OFF["vc"], OFF["vc"] + 512, engs=BULK)
        self.wload(wglr[:, 0], self.w_in, l, OFF["glr"], OFF["glr"] + 32, engs=BULK)
        self.wload(wglr[:, 1, :, 0:16], self.w_in, l, OFF["glr"] + 16, OFF["glr"] + 32, engs=BULK)
        self.wload(wglr[:, 1, :, 16:32], self.w_in, l, OFF["glr"], OFF["glr"] + 16, engs=BULK)
        P.memset("pool", wgu[0:32], 0.0)
        st = self.stage[0:16, self.nst % 4, :]
        self.nst += 1
        P.dma(st.rearrange("p (d c) -> p d c", d=2), self.w_gate_up[l].rearrange("d r c -> r d c"))
        P.copy("pool", wgu[0:16], st.rearrange("p (d c) -> p d c", d=2))
        st = self.stage[0:1, self.nst % 4, :]
        self.nst += 1
        P.dma(st.rearrange("p (d c) -> p d c", d=2), self.b_gate[l:l + 1])
        P.copy("pool", bg[0:1], st.rearrange("p (d c) -> p d c", d=2))
        P.dma(gg, self.gla_g[l])
        return (wvc, wzc, wqc, wkc, wglr, wgu, bg, gg)

    def phase_gla(self, l):
        P = self.P
        self.P.phase = "gla_prep"
        ps = self.ps
        hT = self.hT
        onT = self.onT
        self.acur = 64 * KB
        qTc = self.alloc([2, L], BF16)
        kTc = self.alloc([2, L], BF16)
        Vc = self.alloc([16, 512], BF16)
        ob = self.alloc([4, L], BF16)
        glrT = self.alloc([2, L], BF16)
        oblk = self.alloc([4, 512], F32)
        S = self.alloc([2, 128], F32)
        Sbf = self.alloc([2, 128], BF16)
        gsp = self.alloc([256], F32)
        gh = self.alloc([256], BF16)
        gl = self.alloc([256], BF16)
        eg = self.alloc([2, 128], F32)
        eng = self.alloc([2, 128], F32)
        ek = self.alloc([2, 128], F32)
        qg = self.alloc([2, 2, 128], BF16)
        kg = self.alloc([2, 128], BF16)
        kend = self.alloc([2, 2, 128], BF16)
        attm = self.alloc([2, 4, 128], BF16)
        kendT = self.alloc([2, 128], BF16)
        glast = self.alloc([2], F32)
        cd = self.alloc([2, 2], F32)
        assert self.acur <= 143 * KB
        self.acur = 56 * KB
        sq = self.alloc([512], BF16)
        rt = self.alloc([512], F32)
        sz = self.alloc([512], F32)
        tmp = self.alloc([512], F32)
        assert self.acur <= 64 * KB
        w = self.pre.pop("gla", None) or self.gla_load(l)
        wvc, wzc, wqc, wkc, wglr, wgu, bg, gg = w
        n = 0
        for j in range(2):
            for (wsrc, dst) in ((wqc, qTc), (wkc, kTc)):
                for b in range(NB):
                    bl = slice(b * 512, (b + 1) * 512)
                    bank = 4 + n % 4
                    for k in range(8):
                        P.mm(ps[:, bank, :], wsrc[:, k, j * 128:(j + 1) * 128], hT[:, k, bl], k == 0, k == 7)
                    P.copy("act" if n % 2 else "dve", dst[:, j, bl], ps[:, bank, :])
                    n += 1
        for t in range(NT):
            tl = slice(t * 128, (t + 1) * 128)
            bank = 4 + n % 4
            for k in range(8):
                P.mm(ps[:, bank, :], hT[:, k, tl], wvc[:, k, :], k == 0, k == 7)
            P.copy("act" if n % 2 else "dve", Vc[:, t, :], ps[:, bank, :])
            n += 1
        for d in range(2):
            for b in range(NB):
                bl = slice(b * 512, (b + 1) * 512)
                bank = 4 + n % 4
                for k in range(8):
                    P.mm(ps[0:32, bank, :], wglr[:, d, k, :], hT[:, k, bl], k == 0, k == 7)
                P.copy("act" if n % 2 else "dve", glrT[0:32, d, bl], ps[0:32, bank, :])
                n += 1

        self.wload(wzc, self.w_in, l, OFF["zc"], OFF["zc"] + 512)

        def bank3(b, a):
            return ps[:, b, 0:a * 128].rearrange("p (a c) -> p a c", a=a)


        P.phase = "gla_scan"
        psC = bank3(1, 2)
        psA2 = [bank3(2, 2), bank3(3, 2)]
        psO2 = [bank3(4, 2), bank3(5, 2)]
        psT = ps[:, 6, :].bitcast(BF16)[:, 0:256].rearrange("p (a c) -> p a c", a=2)
        psU = bank3(7, 2)
        obv = ob.rearrange("p (j two) t -> p j two t", two=2)
        oblv = oblk.rearrange("p (j two) t -> p j two t", two=2)

        def front(d, ci, t):
            Tm = self.TmFb if d == 0 else self.TmBb
            tri = self.triF if d == 0 else self.triB
            last = 127 if d == 0 else 0
            x = ci % 2
            tl = slice(t * 128, (t + 1) * 128)
            P.mm(ps[:, 0, 0:256], glrT[0:32, d, tl], wgu[0:32, d, :], True, False)
            P.mm(ps[:, 0, 0:256], self.ones[0:1, 0:128], bg[0:1, d, :], False, True)
            P.actv(gsp, ps[:, 0, 0:256], AF.Exp, scale=-1.0)
            P.actv(gsp, gsp, AF.Ln, bias=1.0)
            P.copy("dve", gh, gsp)
            P.tt("dve", gl, gsp, gh, ALU.subtract)
            for j in range(2):
                P.mm(psC[:, j, :], gh[:, j * 128:(j + 1) * 128], Tm[:, :], True, False)
                P.mm(psC[:, j, :], gl[:, j * 128:(j + 1) * 128], Tm[:, :], False, True)
            P.copy("dve", glast, psC[:, :, last])
            P.actv(eg.rearrange("p a c -> p (a c)"), ps[:, 1, 0:256], AF.Exp)
            P.actv(eng.rearrange("p a c -> p (a c)"), ps[:, 1, 0:256], AF.Exp, scale=-1.0)
            for j in range(2):
                P.actv(ek[:, j, :], psC[:, j, :], AF.Exp, scale=-1.0, bias=glast[:, j:j + 1])
            P.actv(cd[:, x, :], glast, AF.Exp)
            P.stt("dve", qg[:, x], qTc[:, :, tl], 0.125, eg, ALU.mult, ALU.mult)
            P.tt("dve", kg, kTc[:, :, tl], eng, ALU.mult)
            P.tt("dve", kend[:, x], kTc[:, :, tl], ek, ALU.mult)
            tri_b = bass.AP(tri, 0, [[128, 128], [0, 2], [1, 128]])
            attv = attm[:, x].rearrange("p (j two) c -> p j two c", two=2)
            for par in range(2):
                r = slice(64 * par, 64 * par + 64)
                for j in range(2):
                    P.mm(psA2[par][:, j, :], kg[r, j, :], qg[r, x, j, :], True, True)
                P.tt("dve", attv[:, :, par, :], psA2[par], tri_b, ALU.mult)

        def back(d, ci, t):
            x = ci % 2
            tl = slice(t * 128, (t + 1) * 128)
            for par in range(2):
                r = slice(64 * par, 64 * par + 64)
                for j in range(2):
                    h = 2 * j + par
                    P.mm(psO2[par][:, j, :], Vc[:, t, h * 128:(h + 1) * 128], attm[:, x, h, :], True, ci == 0)
                    if ci > 0:
                        P.mm(psO2[par][:, j, :], Sbf[r, j, :], qg[r, x, j, :], False, True)
                if d == 1:
                    P.copy("act", obv[:, :, par, tl], psO2[par])
                else:
                    c4 = t % 4
                    P.tt("dve", oblv[:, :, par, c4 * 128:(c4 + 1) * 128], psO2[par], obv[:, :, par, tl], ALU.add)
            if ci < NT - 1:
                for j in range(2):
                    P.tr(psT[:, j, :], kend[:, x, j, :], self.ident[:])
                P.copy("act", kendT, psT)
                for h in range(4):
                    j = h // 2
                    r = slice(64 * (h % 2), 64 * (h % 2) + 64)
                    P.mm(psU[r, j, :], kendT[:, j, 64 * (h % 2):64 * (h % 2) + 64], Vc[:, t, h * 128:(h + 1) * 128], True, True)
                for j in range(2):
                    if ci > 0:
                        P.stt("dve", S[:, j, :], S[:, j, :], cd[:, x, j:j + 1], psU[:, j, :], ALU.mult, ALU.add)
                    else:
                        P.copy("dve", S[:, j, :], psU[:, j, :])
                P.copy("pool", Sbf, S)
            if d == 0 and t % 4 == 3:
                b = t // 4
                bl = slice(b * 512, (b + 1) * 512)
                for h in range(4):
                    P.actv(sq, oblk[:, h, :], AF.Square)
                    P.mm(ps[:, 0, :], self.ones[:, :], sq, True, True)
                    P.actv(rt, ps[:, 0, :], AF.Ln, bias=EPS, scale=1.0 / 128)
                    P.actv(rt, rt, AF.Exp, scale=-0.5)
                    for k in range(8):
                        P.mm(ps[:, 1, :], wzc[:, k, h * 128:(h + 1) * 128], hT[:, k, bl], k == 0, k == 7)
                    P.actv(sz, ps[:, 1, :], AF.Silu)
                    P.stt("dve", tmp, oblk[:, h, :], gg[:, h:h + 1], rt, ALU.mult, ALU.mult)
                    P.tt("pool", onT[:, h, bl], tmp, sz, ALU.mult)

        for d in (1, 0):
            order = list(range(NT)) if d == 0 else list(range(NT - 1, -1, -1))
            front(d, 0, order[0])
            for ci, t in enumerate(order):
                if ci + 1 < NT:
                    front(d, ci + 1, order[ci + 1])
                back(d, ci, t)

    def ssd_load(self, l):
        P = self.P
        self.acur = 64 * KB
        wz = self.alloc([8, D], BF16)
        wdt = self.alloc([8, 32], BF16)
        anb = self.alloc([32], F32)
        dtb = self.alloc([32], F32)
        dsk = self.alloc([16], F32)
        gnb = self.alloc([D], F32)
        cp = self.alloc([12, 6], F32)
        self.ssd_end = self.acur
        P.dma(cp, self.conv_p[l])
        P.dma(anb, bass.AP(self.a_log.tensor, l * 32, [[0, 128], [1, 32]]))
        P.dma(dtb, bass.AP(self.dt_bias.tensor, l * 32, [[0, 128], [1, 32]]))
        P.dma(dsk, bass.AP(self.d_skip.tensor, l * 16, [[0, 128], [1, 16]]))
        P.dma(gnb, bass.AP(self.ssd_norm_g.tensor, l * D, [[0, 128], [1, D]]))
        self.wload(wdt, self.w_in, l, OFF["dt"], OFF["dt"] + 32, engs=BULK)
        self.wload(wz, self.w_in, l, OFF["za"], OFF["za"] + D, engs=BULK)
        return (wz, wdt, cp, anb, dtb, dsk, gnb)

    def phase_ssd(self, l):
        P = self.P
        self.P.phase = "ssd_conv"
        ps = self.ps
        hT = self.hT
        onT = self.onT
        xs_tok = self.view(0, [16, D], BF16)
        w = self.pre.pop("ssd", None) or self.ssd_load(l)
        wz, wdt, cp, anb, dtb, dsk, gnb = w
        self.acur = self.ssd_end
        BT = self.alloc([2, L], BF16)
        CT = self.alloc([2, L], BF16)
        Btok = self.alloc([16, 256], BF16)
        dt_all = self.alloc([16, 32], F32)
        dtaf = self.alloc([16, 32], F32)
        base = self.acur
        xb = self.alloc([L + 4], F32)
        acc = self.alloc([L], F32)
        cv = self.alloc([L], BF16)
        wx = self.alloc([3, 8, 128], BF16)
        P.actv(anb, anb, AF.Exp)
        P.ts("dve", anb, anb, -1.0, None, ALU.mult)
        P.memset("pool", xb[:, 0:2], 0.0)
        P.memset("pool", xb[:, L + 2:L + 4], 0.0)
        for c in range(2):
            self.wload(wx[:, c], self.w_in, l, OFF["xbc"] + c * 128, OFF["xbc"] + (c + 1) * 128)
        for c in range(12):
            i = c % 3
            if c + 2 < 12:
                self.wload(wx[:, (c + 2) % 3], self.w_in, l, OFF["xbc"] + (c + 2) * 128, OFF["xbc"] + (c + 3) * 128)
            for b in range(NB):
                bank = 4 + (c * 4 + b) % 4
                for k in range(8):
                    P.mm(ps[:, bank, :], wx[:, i, k, :], hT[:, k, b * 512:(b + 1) * 512], k == 0, k == 7)
                P.copy("act", xb[:, 2 + b * 512:2 + (b + 1) * 512], ps[:, bank, :])
            eng = "dve"
            P.ts(eng, acc, xb[:, 0:L], cp[:, c, 0:1], cp[:, c, 5:6], ALU.mult, ALU.add)
            for j in range(1, 5):
                P.stt(eng, acc, xb[:, j:j + L], cp[:, c, j:j + 1], acc, ALU.mult, ALU.add)
            if c < 8:
                P.actv(cv, acc, AF.Silu)
                dst_tok = lambda t, c=c: xs_tok[:, t, c * 128:(c + 1) * 128]
                src = cv
            elif c < 10:
                P.actv(BT[:, c - 8, :], acc, AF.Silu)
                dst_tok = lambda t, c=c: Btok[:, t, (c - 8) * 128:(c - 7) * 128]
                src = BT[:, c - 8, :]
            else:
                P.actv(CT[:, c - 10, :], acc, AF.Silu)
                src = None
            if src is not None:
                for half in range(2):
                    pb = ps[:, 2 + half, :].bitcast(BF16)
                    for tt_ in range(8):
                        t = half * 8 + tt_
                        P.tr(pb[:, tt_ * 128:(tt_ + 1) * 128], src[:, t * 128:(t + 1) * 128], self.ident[:])
                    if c < 8:
                        P.copy("act" if half else "dve", xs_tok[:, half * 8:(half + 1) * 8, c * 128:(c + 1) * 128],
                               pb.rearrange("p (t c) -> p t c", t=8))
                    else:
                        P.copy("act" if half else "dve", Btok[:, half * 8:(half + 1) * 8, (c - 8) * 128:(c - 7) * 128],
                               pb.rearrange("p (t c) -> p t c", t=8))
        P.phase = "ssd_scan"
        for t in range(NT):
            for k in range(8):
                P.mm(ps[:, 7, t * 32:(t + 1) * 32], hT[:, k, t * 128:(t + 1) * 128], wdt[:, k, :], k == 0, k == 7)
        dtb_b = bass.AP(dtb.tensor, dtb.offset, [list(dtb.ap[0]), [0, 16], [1, 32]])
        anb_b = bass.AP(anb.tensor, anb.offset, [list(anb.ap[0]), [0, 16], [1, 32]])
        P.tt("dve", dt_all, ps[:, 7, :].rearrange("p (t c) -> p t c", t=16), dtb_b, ALU.add)
        P.actv(dt_all, dt_all, AF.Exp)
        P.actv(dt_all, dt_all, AF.Ln, bias=1.0)
        self.acur = base
        self.allow_over = True
        Af = self.alloc([1, 8, 128], F32)
        E = self.alloc([1, 8, 128], BF16)
        W = self.alloc([1, 8, 128], BF16)
        Gm = self.alloc([2, 128], BF16)
        xd = self.alloc([D], BF16)
        xdd = self.alloc([D], BF16)
        S = self.alloc([D], F32)
        Sbf = self.alloc([D], BF16)
        ytmp = self.alloc([D], F32)
        y2 = self.alloc([2, D], F32)
        ybt = self.alloc([D], F32)
        eac = self.alloc([16], F32)
        cdb = self.alloc([16], F32)
        sz = self.alloc([D], F32)
        dta = sz[:, 0:512].rearrange("p (t c) -> p t c", t=16)
        junk = ytmp
        hb = xdd
        ss = self.alloc([4], F32)
        P.tt("dve", dta, dt_all, anb_b, ALU.mult)
        P.copy("dve", dtaf, dta)

        def hv(ap2):
            return ap2.rearrange("p (h q) -> p h q", h=16)

        def bc_h(ap_col16, n):
            return bass.AP(ap_col16.tensor, ap_col16.offset, [list(ap_col16.ap[0]), [ap_col16.ap[1][0], 16], [0, n]])

        def bc_h2(ap16, g):
            return bass.AP(ap16.tensor, ap16.offset + g * 8, [list(ap16.ap[0]), [ap16.ap[1][0], 8], [0, 128]])

        def fin_stages(ci, t):
            y = y2[:, ci % 2, :]
            tl = slice(t * 128, (t + 1) * 128)
            pz = ps[:, 0:2, :].rearrange("p b c -> p (b c)")

            def f1a():
                P.tt("dve", hv(ytmp), hv(xs_tok[:, t, :]), bc_h(dsk, 64), ALU.mult)
                P.tt("dve", y, y, ytmp, ALU.add)

            def f1b():
                for hf in range(2):
                    for k in range(8):
                        P.mm(ps[:, hf, :], hT[:, k, tl], wz[:, k, hf * 512:(hf + 1) * 512], k == 0, k == 7)

            def f2():
                P.actv(sz, pz, AF.Silu)

            def f3():
                P.tt("dve", y, y, sz, ALU.mult)

            def f4():
                P.actv(junk, y, AF.Square, accum_out=ss[:, 0:1])
                P.actv(ss[:, 1:2], ss[:, 0:1], AF.Ln, bias=EPS, scale=1.0 / D)
                P.actv(ss[:, 2:3], ss[:, 1:2], AF.Exp, scale=-0.5)

            def f5():
                P.stt("dve", hb, y, ss[:, 2:3], gnb, ALU.mult, ALU.mult)
                pb = ps[:, 7, :].bitcast(BF16)
                for k in range(8):
                    P.tr(pb[:, k * 128:(k + 1) * 128], hb[:, k * 128:(k + 1) * 128], self.ident[:])
                P.copy("act", onT[:, :, tl], pb.rearrange("p (k c) -> p k c", k=8))

            return [f1a, f1b, f2, f3, f4, f5]

        def main_stages(d, ci, t):
            Lm = self.LF if d == 0 else self.LB
            Tm = self.triF if d == 0 else self.triB
            tri = self.triF if d == 0 else self.triB
            last = 127 if d == 0 else 0
            y = y2[:, ci % 2, :]
            tl = slice(t * 128, (t + 1) * 128)
            dh = dtaf[:, t, d * 16:(d + 1) * 16]

            def m1():
                if d == 0:
                    P.dma(ybt, self.yb_d[tl, :])
                for g in range(2):
                    P.mm(ps[:, 4, g * 128:(g + 1) * 128], BT[:, g, tl], CT[:, g, tl], True, True)
                P.mm(ps[:, 4, 256:272], Tm[:, :], dh, True, True)
                P.mm(ps[:, 4, 272:288], self.onesf[:, :], dh, True, True)
                tri_b = bass.AP(tri, 0, [[128, 128], [0, 2], [1, 128]])
                P.tt("dve", Gm, ps[:, 4, 0:256].rearrange("p (g c) -> p g c", g=2), tri_b, ALU.mult)
                P.actv(eac, ps[:, 4, 256:272], AF.Exp)
                P.actv(cdb, ps[:, 4, 272:288], AF.Exp)
                P.tt("dve", hv(xd), hv(xs_tok[:, t, :]), bc_h(dt_all[:, t, d * 16:(d + 1) * 16], 64), ALU.mult)

            def mg(g):
                def f():
                    Lb = bass.AP(Lm, 0, [[128, 128], [0, 8], [1, 128]])
                    P.tt("pool", Af[:, 0], Lb, bc_h2(dh, g), ALU.mult)
                    for e in range(8):
                        out = ps[:, 2 * g + e // 4, (e % 4) * 128:(e % 4 + 1) * 128]
                        P.mm(out, Af[:, 0, e, :], Tm[:, :], True, True)
                    P.actv(E[:, 0], ps[:, 2 * g:2 * g + 2, :].rearrange("p b (e c) -> p (b e) c", e=4), AF.Exp)
                    Gb = bass.AP(Gm.tensor, Gm.offset + g * 128, [list(Gm.ap[0]), [0, 8], [1, 128]])
                    P.tt("dve", W[:, 0], E[:, 0], Gb, ALU.mult)
                    Elast = bass.AP(E.tensor, E.offset + last, [list(E.ap[0]), [128, 8], [0, 64]])
                    P.tt("dve", hv(xdd)[:, g * 8:(g + 1) * 8, :], hv(xd)[:, g * 8:(g + 1) * 8, :], Elast, ALU.mult)
                    for e in range(8):
                        h = g * 8 + e
                        P.mm(ps[:, 5 + g, e * 64:(e + 1) * 64], W[:, 0, e, :], xd[:, h * 64:(h + 1) * 64], True, True)
                return f

            def m4():
                if ci > 0:
                    for g in range(2):
                        P.mm(ps[:, g, :], CT[:, g, tl], Sbf[:, g * 512:(g + 1) * 512], True, True)
                    P.tt("dve", hv(ytmp), ps[:, 0:2, :].rearrange("p b (e q) -> p (b e) q", e=8), bc_h(eac, 64), ALU.mult)
                    P.tt("dve", y, ytmp, ps[:, 5:7, :].rearrange("p b c -> p (b c)"), ALU.add)
                else:
                    P.copy("act", y, ps[:, 5:7, :].rearrange("p b c -> p (b c)"))

            def m5():
                if ci < NT - 1:
                    for g in range(2):
                        P.mm(ps[:, 2 + g, :], Btok[:, t, g * 128:(g + 1) * 128], xdd[:, g * 512:(g + 1) * 512], True, True)
                    if ci > 0:
                        P.tt("pool", hv(S), hv(S), bc_h(cdb, 64), ALU.mult)
                        P.tt("dve", S, S, ps[:, 2:4, :].rearrange("p b c -> p (b c)"), ALU.add)
                    else:
                        P.copy("act", S, ps[:, 2:4, :].rearrange("p b c -> p (b c)"))
                    P.copy("act", Sbf, S)

            def m6():
                if d == 1:
                    P.dma(self.yb_d[tl, :], y)
                else:
                    P.tt("dve", y, y, ybt, ALU.add)

            return [m1, mg(0), mg(1), m4, m5, m6]

        for d in (1, 0):
            order = range(NT) if d == 0 else range(NT - 1, -1, -1)
            pending = None
            for ci, t in enumerate(order):
                ms = main_stages(d, ci, t)
                fs = fin_stages(*pending) if pending is not None else []
                for i in range(6):
                    ms[i]()
                    if i < len(fs):
                        fs[i]()
                if d == 0:
                    pending = (ci, t)
            if pending is not None:
                for f in fin_stages(*pending):
                    f()
        self.allow_over = False

    def merge(self, l, bi, nk, wbr_src, first, pre=None):
        P = self.P
        self.P.phase = "merge"
        ps = self.ps
        hT = self.hT
        onT = self.onT
        self.acur = 64 * KB
        wbr = self.alloc([nk, D], BF16)
        wg = self.alloc([3, 8, 128], BF16)
        sg = self.alloc([2, 512], F32)
        tm = self.alloc([2, 512], F32)
        self.wload(wbr, wbr_src, l, 0, D, engs=BULK)
        self.wload(wg[:, 0], self.w_in, l, bi * D, bi * D + 128)
        self.wload(wg[:, 1], self.w_in, l, bi * D + 128, bi * D + 256)
        if pre is not None:
            save = self.acur
            pre()
            self.acur = save
        n = 0
        for m in range(8):
            if m + 2 < 8:
                self.wload(wg[:, (m + 2) % 3], self.w_in, l, bi * D + (m + 2) * 128, bi * D + (m + 3) * 128)
            for b in range(NB):
                bl = slice(b * 512, (b + 1) * 512)
                yb = 4 + n % 2
                gb = 6 + n % 2
                for k in range(nk):
                    P.mm(ps[:, yb, :], wbr[:, k, m * 128:(m + 1) * 128], onT[:, k, bl], k == 0, k == nk - 1)
                for k in range(8):
                    P.mm(ps[:, gb, :], wg[:, m % 3, k, :], hT[:, k, bl], k == 0, k == 7)
                P.actv(sg[:, n % 2, :], ps[:, gb, :], AF.Sigmoid)
                if first:
                    P.tt("dve", self.mixT[:, m, bl], ps[:, yb, :], sg[:, n % 2, :], ALU.mult)
                else:
                    P.tt("dve", tm[:, n % 2, :], ps[:, yb, :], sg[:, n % 2, :], ALU.mult)
                    P.tt("pool", self.mixT[:, m, bl], self.mixT[:, m, bl], tm[:, n % 2, :], ALU.add)
                n += 1

    def outproj_load(self, l, last):
        self.acur = 96 * KB
        wo = self.alloc([8, D], BF16)
        gB = self.alloc([D], F32)
        self.wload(wo, self.w_out, l, 0, D, engs=BULK)
        grow = 2 if last else l + 1
        self.P.dma(gB, bass.AP(self.norm_g.tensor, grow * D, [[0, 128], [1, D]]))
        return (wo, gB)

    def outproj(self, s, l, last):
        P = self.P
        self.P.phase = "outproj"
        ps = self.ps
        w = self.pre.pop("outproj", None) or self.outproj_load(l, last)
        wo, gB = w
        self.acur = 116 * KB
        xt = self.alloc([2, D], F32)
        xn = self.alloc([2, D], F32)
        junk = self.alloc([D], BF16)
        ss = self.alloc([48], F32)
        hb = self.alloc([2, D], BF16)
        yo = self.alloc([2, D], F32)
        if not last:
            self.pre["ssd"] = self.ssd_load(self.layers[self.layers.index(l) + 1])
        xsrc = self.x if l == 0 else self.xres
        def finish(t):
            i = t % 2
            tl = slice(t * 128, (t + 1) * 128)
            if not last:
                P.dma(self.xres[s, tl, :], xn[:, i, :])
                self.norm_to_hT(xn[:, i, :], t, gB, (junk, ss, hb))
            else:
                P.actv(junk[:, :], xn[:, i, :], AF.Square, accum_out=ss[:, t:t + 1])
                P.actv(ss[:, 16 + t:17 + t], ss[:, t:t + 1], AF.Ln, bias=EPS, scale=1.0 / D)
                P.actv(ss[:, 32 + t:33 + t], ss[:, 16 + t:17 + t], AF.Exp, scale=-0.5)
                P.stt("dve", yo[:, i, :], xn[:, i, :], ss[:, 32 + t:33 + t], gB, ALU.mult, ALU.mult)
                P.dma(self.out[s, tl, :], yo[:, i, :])

        P.dma(xt[:, 0, :], xsrc[s, 0:128, :])
        for t in range(NT):
            i = t % 2
            tl = slice(t * 128, (t + 1) * 128)
            if t + 1 < NT:
                P.dma(xt[:, (t + 1) % 2, :], xsrc[s, (t + 1) * 128:(t + 2) * 128, :])
            for hf in range(2):
                bank = 4 + hf
                for k in range(8):
                    P.mm(ps[:, bank, :], self.mixT[:, k, tl], wo[:, k, hf * 512:(hf + 1) * 512], k == 0, k == 7)
                P.tt("dve", xn[:, i, hf * 512:(hf + 1) * 512], ps[:, bank, :], xt[:, i, hf * 512:(hf + 1) * 512], ALU.add)
            if t >= 1:
                finish(t - 1)
        finish(NT - 1)

    def build(self):
        self.setup()
        for s in range(self.nseq):
            nl = len(self.layers)
            for li, l in enumerate(self.layers):
                if li == 0:
                    self.phase_a(s, l)
                last = (li == nl - 1)
                order = [b for b in "abcd" if b in self.branches]
                phase = {"a": self.phase_ssd, "b": self.phase_mla, "c": self.phase_gla, "d": self.phase_gqa}
                loader = {"b": ("mla", self.mla_load), "c": ("gla", self.gla_load), "d": ("gqa", self.gqa_load)}
                mrg = {"a": (0, 8, self.w_br_a), "b": (1, 4, self.w_br_b), "c": (2, 4, self.w_br_c), "d": (3, 4, self.w_br_d)}
                for bi_, br in enumerate(order):
                    phase[br](l)
                    if bi_ + 1 < len(order):
                        key, fn = loader[order[bi_ + 1]]
                        pre = (lambda key=key, fn=fn: self.pre.__setitem__(key, fn(l)))
                    else:
                        pre = (lambda: self.pre.__setitem__("outproj", self.outproj_load(l, last)))
                    i_, nk_, wsrc_ = mrg[br]
                    self.merge(l, i_, nk_, wsrc_, bi_ == 0, pre=pre)
                self.outproj(s, l, last=(li == nl - 1))
        cnt = self.P.emit(self.es)
        return cnt


def _prep_inputs(inp):
    f = lambda a: np.ascontiguousarray(np.asarray(a, dtype=np.float32))
    w_in = f(inp["w_in"])
    p32, _ = _rope_perm_sign(32)
    p64, _ = _rope_perm_sign(64)
    kr = w_in[:, :, OFF["krope"]:OFF["krope"] + 32][:, :, p32]
    qd = w_in[:, :, OFF["qd"]:OFF["qd"] + 512].reshape(2, D, 8, 64)[:, :, :, p64].reshape(2, D, 512)
    kd = w_in[:, :, OFF["kd"]:OFF["kd"] + 128].reshape(2, D, 2, 64)[:, :, :, p64].reshape(2, D, 128)
    w_perm = np.ascontiguousarray(np.concatenate([kr, qd, kd], axis=2))
    qg = f(inp["q_norm_g"])
    kg = f(inp["k_norm_g"])
    qk_g = np.ascontiguousarray(np.stack([qg, qg[:, p64], kg, kg[:, p64]], axis=2))
    cos64, sin64 = _rope_tables(64)
    cos32, sin32 = _rope_tables(32)
    rope_cos = np.ones((128, L), np.float32)
    rope_sin = np.zeros((128, L), np.float32)
    rope_cos[0:64] = cos64
    rope_sin[0:64] = sin64
    rope_cos[64:96] = cos32
    rope_sin[64:96] = sin32
    wqb = f(inp["w_q_b"])
    w_qb_perm = np.ascontiguousarray(wqb.reshape(2, 384, 8, 96)[:, :, :, 64:96][:, :, :, p32].reshape(2, 384, 256))
    qlg = f(inp["q_lat_norm_g"]).reshape(2, 3, 128).transpose(0, 2, 1)
    kvg = f(inp["kv_lat_norm_g"]).reshape(2, 2, 128).transpose(0, 2, 1)
    lat_g = np.ascontiguousarray(np.concatenate([qlg, kvg], axis=2))
    gla_g = np.ascontiguousarray(f(inp["gla_norm_g"]).reshape(2, 4, 128).transpose(0, 2, 1))
    cw = f(inp["conv_w"]).reshape(2, 5, 12, 128).transpose(0, 3, 2, 1)
    cb = f(inp["conv_b"]).reshape(2, 12, 128).transpose(0, 2, 1)[..., None]
    conv_p = np.ascontiguousarray(np.concatenate([cw, cb], axis=3))
    shared = dict(
        conv_p=conv_p, a_log=f(inp["a_log"]).reshape(2, 32), dt_bias=f(inp["dt_bias"]).reshape(2, 32),
        d_skip=f(inp["d_skip"]), ssd_norm_g=f(inp["ssd_norm_g"]),
        w_gate_up=f(inp["w_gate_up"]), b_gate=f(inp["b_gate"]), gla_g=gla_g,
        w_in=w_in, w_perm=w_perm, w_q_b=wqb, w_qb_perm=w_qb_perm, w_kv_b=f(inp["w_kv_b"]), lat_g=lat_g,
        norm_g=np.ascontiguousarray(np.concatenate([f(inp["norm_g"]), f(inp["final_g"])[None, :]], axis=0)),
        qk_g=qk_g,
        w_br_a=f(inp["w_br_a"]), w_br_b=f(inp["w_br_b"]), w_br_c=f(inp["w_br_c"]), w_br_d=f(inp["w_br_d"]),
        w_out=f(inp["w_out"]), rope_cos=rope_cos, rope_sin=rope_sin,
    )
    return shared


_CACHE = {}


def kernel(**inputs):
    x = np.ascontiguousarray(np.asarray(inputs["x"], dtype=np.float32))
    n_cores = 8
    nseq = x.shape[0] // n_cores
    shared = _prep_inputs(inputs)
    mk = MK(nseq=nseq, layers=(0, 1), branches="abcd")
    mk.build()
    in_maps = []
    for c in range(n_cores):
        m = dict(shared)
        m["x"] = np.ascontiguousarray(x[c * nseq:(c + 1) * nseq])
        in_maps.append(m)
    res = run_bass_kernel_spmd(mk.nc, in_maps, core_ids=list(range(n_cores)))
    out = np.concatenate([np.asarray(r["out"]) for r in res.results], axis=0)
    return out.astype(np.float32)
```

```python
import sys
import numpy as np
from contextlib import ExitStack
import concourse.bass as bass
import concourse.mybir as mybir
from concourse.bass_utils import run_bass_kernel_spmd

F32 = mybir.dt.float32
BF16 = mybir.dt.bfloat16
ALU = mybir.AluOpType
AF = mybir.ActivationFunctionType
AX = mybir.AxisListType

_ESZ = {F32: 4, BF16: 2}


def _esz(dt):
    return _ESZ.get(dt, 4)


def region(ap):
    a = ap.ap
    off = int(ap.offset)
    es = _esz(ap.dtype)
    name = ap.tensor.name
    sp = str(ap.space)
    if sp == "DRAM":
        ext = sum((c - 1) * abs(s) for s, c in a) + 1
        return (name, 0, 1, off * es, (off + ext) * es)
    pstep, pcnt = a[0]
    if pstep == 0:
        pstep = 1 << 40
    p0 = off // pstep
    f0 = off % pstep
    ext = sum((c - 1) * abs(s) for s, c in a[1:]) + 1
    if sp == "PSUM":
        b0 = (f0 * es) // 2048 * 2048
        b1 = ((f0 + ext) * es + 2047) // 2048 * 2048
        return (name, p0 // 32 * 32, (p0 + pcnt + 31) // 32 * 32, b0, b1)
    return (name, p0, p0 + pcnt, f0 * es, (f0 + ext) * es)


def _ovl(r, s):
    return r[1] < s[2] and s[1] < r[2] and r[3] < s[4] and s[3] < r[4]


def _covers(r, s):
    return r[1] <= s[1] and r[2] >= s[2] and r[3] <= s[3] and r[4] >= s[4]


class Op:
    __slots__ = ("eng", "fn", "deps", "signal", "dma", "eidx", "rank", "waits", "line", "phase")

    def __init__(self, eng, fn):
        self.eng = eng
        self.fn = fn
        self.deps = set()
        self.signal = False
        self.dma = None
        self.waits = []


ENGS = ("pe", "act", "dve", "pool", "sp")
_WRAPPERS = ("mm", "tr", "actv", "tt", "ts", "stt", "copy", "memset", "recip", "dma")
NDMASEM = 8
EPOCH = 20000


class Prog:
    def __init__(self, nc):
        self.nc = nc
        self.ops = []
        self.acc = {}
        self.ndma = 0
        self.phase = ""

    def add(self, eng, fn, reads, writes, dma=False):
        op = Op(eng, fn)
        op.phase = self.phase
        try:
            fr = sys._getframe(1)
            if fr.f_code.co_filename == __file__ and fr.f_code.co_name in _WRAPPERS:
                fr = fr.f_back
            op.line = fr.f_lineno
        except Exception:
            op.line = 0
        rec_eng = "dma" if dma else eng
        idx = len(self.ops)
        ops = self.ops
        for ap in reads:
            r = region(ap)
            lst = self.acc.setdefault(r[0], [])
            done = False
            is_psum = (r[0] == "ps")
            for rec in lst:
                if rec[2]:
                    if _ovl(rec[0], r):
                        op.deps.add(rec[1])
                elif (not done) and (not dma) and rec[3] == rec_eng and rec[0] == r:
                    rec[1] = idx
                    done = True
                elif is_psum and rec[3] != rec_eng and _ovl(rec[0], r):
                    op.deps.add(rec[1])
            if not done:
                lst.append([r, idx, False, rec_eng])
        for ap in writes:
            r = region(ap)
            lst = self.acc.setdefault(r[0], [])
            keep = []
            for rec in lst:
                if _ovl(rec[0], r):
                    if rec[1] != idx:
                        op.deps.add(rec[1])
                    if _covers(r, rec[0]) and rec[1] != idx:
                        continue
                keep.append(rec)
            keep.append([r, idx, True, rec_eng])
            self.acc[r[0]] = keep
        if eng == "pe":
            op.deps = {d for d in op.deps if ops[d].eng != "pe"}
        if dma:
            op.dma = self.ndma
            self.ndma += 1
        ops.append(op)
        return op

    def mm(self, out, lhsT, rhs, start=True, stop=True):
        self.add("pe", lambda e: e.matmul(out, lhsT, rhs, start=start, stop=stop),
                 [lhsT, rhs], [out])

    def tr(self, out, in_, ident):
        self.add("pe", lambda e: e.transpose(out, in_, ident), [in_, ident], [out])

    def actv(self, out, in_, func, bias=None, scale=None, accum_out=None):
        kw = {}
        rd = [in_]
        wr = [out]
        if bias is not None:
            kw["bias"] = bias
            if not isinstance(bias, (int, float)):
                rd.append(bias)
        if scale is not None:
            kw["scale"] = scale
            if not isinstance(scale, (int, float)):
                rd.append(scale)
        if accum_out is not None:
            kw["accum_out"] = accum_out
            wr.append(accum_out)
        self.add("act", lambda e: e.activation(out, in_, func, **kw), rd, wr)

    def _veng(self, eng):
        return eng

    def tt(self, eng, out, in0, in1, op):
        self.add(eng, lambda e: e.tensor_tensor(out, in0, in1, op), [in0, in1], [out])

    def ts(self, eng, out, in0, s1, s2, op0, op1=None, accum_out=None):
        rd = [in0]
        if not isinstance(s1, (int, float)):
            rd.append(s1)
        if s2 is not None and not isinstance(s2, (int, float)):
            rd.append(s2)
        wr = [out]
        kw = {}
        if accum_out is not None:
            kw["accum_out"] = accum_out
            wr.append(accum_out)
        if op1 is None:
            self.add(eng, lambda e: e.tensor_scalar(out, in0, s1, None, op0, **kw), rd, wr)
        else:
            self.add(eng, lambda e: e.tensor_scalar(out, in0, s1, s2, op0, op1, **kw), rd, wr)

    def stt(self, eng, out, in0, scalar, in1, op0, op1):
        rd = [in0, in1]
        if not isinstance(scalar, (int, float)):
            rd.append(scalar)
        self.add(eng, lambda e: e.scalar_tensor_tensor(out, in0, scalar, in1, op0, op1), rd, [out])

    def copy(self, eng, out, in_):
        if eng == "act":
            self.add("act", lambda e: e.activation(out, in_, AF.Copy), [in_], [out])
        else:
            self.add(eng, lambda e: e.tensor_copy(out, in_), [in_], [out])

    def memset(self, eng, out, val):
        self.add(eng, lambda e: e.memset(out, val), [], [out])

    def recip(self, out, in_):
        self.add("dve", lambda e: e.reciprocal(out, in_), [in_], [out])

    def dma(self, out, in_, eng="sp", **kw):
        self.add(eng, lambda e: e.dma_start(out, in_, **kw), [in_], [out], dma=True)

    def emit(self, es, final_wait_all=True):
        nc = self.nc
        ops = self.ops
        cnt = {e: 0 for e in ENGS}
        for op in ops:
            op.eidx = cnt[op.eng]
            cnt[op.eng] += 1
        for i, op in enumerate(ops):
            for d in op.deps:
                dop = ops[d]
                if dop.dma is None:
                    if dop.eng == op.eng and op.eng != "sp":
                        pass
                    dop.signal = True
        last = {}
        for i, op in enumerate(ops):
            if op.dma is None:
                last[op.eng] = i
        for e, i in last.items():
            ops[i].signal = True
        rk = {e: 0 for e in ENGS}
        for op in ops:
            if op.dma is None and op.signal:
                rk[op.eng] += 1
                op.rank = rk[op.eng]
        nep = {e: (rk[e] + EPOCH - 1) // EPOCH + 1 for e in ENGS}
        sems = {}
        for e in ENGS:
            if e == "sp":
                continue
            sems[e] = [es.enter_context(nc.semaphore(f"s_{e}_{k}")) for k in range(nep[e])]
        dsem = [es.enter_context(nc.semaphore(f"s_dma_{k}")) for k in range(NDMASEM)]

        import os as _os
        simmode = bool(_os.environ.get("SIMMODE"))
        dinfo = {}
        m = 0
        for op in ops:
            if op.dma is None:
                continue
            if simmode and op.eng == "pool":
                sem = es.enter_context(nc.semaphore(f"s_u_{op.dma}"))
                dinfo[op.dma] = (("udma", op.dma), sem, 16)
            else:
                k = m % NDMASEM
                dinfo[op.dma] = (("dma", k), dsem[k], 16 * (m // NDMASEM + 1))
                m += 1

        def target(dop):
            if dop.dma is not None:
                return dinfo[dop.dma]
            r = dop.rank - 1
            return ((dop.eng, r // EPOCH), sems[dop.eng][r // EPOCH], r % EPOCH + 1)

        waited = {e: {} for e in ENGS}
        for i, op in enumerate(ops):
            w = waited[op.eng]
            need = {}
            for d in op.deps:
                key, sem, val = target(ops[d])
                if w.get(key, 0) >= val:
                    continue
                if key not in need or need[key][1] < val:
                    need[key] = (sem, val)
            if op.dma is not None:
                key, sem, val = dinfo[op.dma]
                val -= 16
                if val > 0 and w.get(key, 0) < val and (key not in need or need[key][1] < val):
                    need[key] = (sem, val)
            for key, (sem, val) in need.items():
                w[key] = val
                op.waits.append((sem, val))
        final_waits = []
        for e, i in last.items():
            key, sem, val = target(ops[i])
            final_waits.append((sem, val))
        fin = {}
        for key, sem, val in dinfo.values():
            if key not in fin or fin[key][1] < val:
                fin[key] = (sem, val)
        final_waits.extend(fin.values())

        byeng = {e: [op for op in ops if op.eng == e] for e in ENGS}

        annotate = bool(_os.environ.get("SIMMODE"))

        def run(e, name):
            for op in byeng[name]:
                for sem, val in op.waits:
                    e.wait_ge(sem, val)
                ins = op.fn(e)
                if annotate:
                    ins.annotate(f"L{op.line}")
                if op.dma is not None:
                    ins.then_inc(dinfo[op.dma][1], 16)
                elif op.signal:
                    r = op.rank - 1
                    ins.then_inc(sems[name][r // EPOCH], 1)
            if name == "sp":
                for sem, val in final_waits:
                    e.wait_ge(sem, val)

        with nc.Block() as block:
            @block.tensor
            def _(e):
                run(e, "pe")

            @block.scalar
            def _(e):
                run(e, "act")

            @block.vector
            def _(e):
                run(e, "dve")

            @block.gpsimd
            def _(e):
                run(e, "pool")

            @block.sync
            def _(e):
                run(e, "sp")
        return cnt


L = 2048
D = 1024
NT = 16
NB = 4
EPS = 1e-6
OFF = dict(g=0, za=4096, xbc=5120, dt=6656, zb=6688, qlat=7200, kvlat=7584, krope=7840,
           zc=7872, qc=8384, kc=8640, vc=8896, glr=9408, zd=9440, qd=9952, kd=10464, vd=10592)
N_IN = 10720
KB = 1024
BULK = ("dve", "dve", "pool")


def _rope_perm_sign(d):
    m = d // 2
    hm = m // 2
    perm = np.zeros(d, np.int64)
    sign = np.zeros(d, np.float32)
    for j in range(d):
        jj = j % m
        if jj < hm:
            perm[j] = j + hm
            sign[j] = -1.0
        else:
            perm[j] = j - hm
            sign[j] = 1.0
    return perm, sign


def _rope_tables(d):
    rows = L // 64
    row = np.repeat(np.arange(rows), 64).astype(np.float32)
    col = np.tile(np.arange(64), rows).astype(np.float32)
    m = d // 2
    inv = (np.float32(10000.0) ** (-np.arange(0, m, 2, dtype=np.float32) / np.float32(m))).astype(np.float32)
    ang_r = row[:, None] * inv
    ang_c = col[:, None] * inv
    ang = np.concatenate([ang_r, ang_r, ang_c, ang_c], axis=-1).astype(np.float32)
    _, sign = _rope_perm_sign(d)
    cos = np.cos(ang).astype(np.float32).T
    sin = (np.sin(ang).astype(np.float32) * sign[None, :]).T
    return np.ascontiguousarray(cos), np.ascontiguousarray(sin)


class MK:
    def __init__(self, nseq=2, layers=(0, 1), branches="abcd", debug=False):
        self.nseq = nseq
        self.layers = layers
        self.branches = branches
        self.debug = debug
        nc = self.nc = bass.Bass("TRN2", target_bir_lowering=False)
        es = self.es = ExitStack()
        self.P = Prog(nc)
        self.dbg_outs = {}
        di = lambda n, s: nc.dram_tensor(n, s, F32, kind="ExternalInput").ap()
        self.x = di("x", [nseq, L, D])
        self.out = nc.dram_tensor("out", [nseq, L, D], F32, kind="ExternalOutput").ap()
        self.xres = nc.dram_tensor("xres", [nseq, L, D], F32).ap()
        self.w_in = di("w_in", [2, D, N_IN])
        self.w_perm = di("w_perm", [2, D, 672])
        self.norm_g = di("norm_g", [3, D])
        self.qk_g = di("qk_g", [2, 64, 4])
        self.w_q_b = di("w_q_b", [2, 384, 768])
        self.w_qb_perm = di("w_qb_perm", [2, 384, 256])
        self.w_kv_b = di("w_kv_b", [2, 256, 1024])
        self.lat_g = di("lat_g", [2, 128, 5])
        self.w_gate_up = di("w_gate_up", [2, 2, 16, 256])
        self.b_gate = di("b_gate", [2, 2, 256])
        self.gla_g = di("gla_g", [2, 128, 4])
        self.conv_p = di("conv_p", [2, 128, 12, 6])
        self.a_log = di("a_log", [2, 32])
        self.dt_bias = di("dt_bias", [2, 32])
        self.d_skip = di("d_skip", [2, 16])
        self.ssd_norm_g = di("ssd_norm_g", [2, D])
        self.yb_d = nc.dram_tensor("yb_d", [L, D], F32).ap()
        self.w_br_a = di("w_br_a", [2, 1024, D])
        self.w_br_b = di("w_br_b", [2, 512, D])
        self.w_br_c = di("w_br_c", [2, 512, D])
        self.w_br_d = di("w_br_d", [2, 512, D])
        self.w_out = di("w_out", [2, D, D])
        self.rope_cos = di("rope_cos", [128, L])
        self.rope_sin = di("rope_sin", [128, L])
        sb = lambda n, s, d: es.enter_context(nc.sbuf_tensor(n, s, d))
        self.hT = sb("hT", [128, 8, L], BF16)
        self.COS = sb("COS", [128, L], F32)
        self.SIN = sb("SIN", [128, L], F32)
        self.ident = sb("ident", [128, 128], BF16)
        self.identf = sb("identf", [128, 128], F32)
        self.ones = sb("ones", [128, 128], BF16)
        self.onesf = sb("onesf", [128, 128], F32)
        self.triF = sb("triF", [128, 128], F32)
        self.triB = sb("triB", [128, 128], F32)
        self.LF = sb("LF", [128, 128], F32)
        self.LB = sb("LB", [128, 128], F32)
        self.TmFb = sb("TmFb", [128, 128], BF16)
        self.TmBb = sb("TmBb", [128, 128], BF16)
        self.ARW = 155 * 256
        self.AR = sb("AR", [128, self.ARW], F32)
        self.ps = es.enter_context(nc.psum_tensor("ps", [128, 8, 512], F32))
        self.mixT = self.view(0, [8, L], BF16)
        self.onT = self.view(32 * KB, [8, L], BF16)
        self.acur = 64 * KB
        self.stage = self.view(147 * KB, [4, 512], F32)
        self.nst = 0
        self.pre = {}
        self.alim = 147 * KB

    def view(self, off, shape, dt, p0=0, parts=128):
        es_ = _esz(dt)
        n = int(np.prod(shape))
        assert off % 4 == 0
        w0 = off // 4
        w1 = (off + n * es_ + 3) // 4
        assert w1 <= self.ARW, (off, shape)
        ap = self.AR[p0:p0 + parts, w0:w1]
        if dt != F32:
            ap = ap.bitcast(dt)
        if len(shape) == 2:
            ap = ap.rearrange("p (a b) -> p a b", a=shape[0])
        elif len(shape) == 3:
            ap = ap.rearrange("p (a b c) -> p a b c", a=shape[0], b=shape[1])
        return ap

    def alloc(self, shape, dt, p0=0, parts=128):
        n = int(np.prod(shape)) * _esz(dt)
        n = (n + 63) // 64 * 64
        v = self.view(self.acur, shape, dt, p0, parts)
        self.acur += n
        assert self.acur <= self.alim or getattr(self, "allow_over", False), (self.acur, shape)
        return v

    def dbg(self, name, ap_sb, shape):
        if not self.debug:
            return
        d = self.nc.dram_tensor("dbg_" + name, shape, ap_sb.dtype, kind="ExternalOutput").ap()
        self.dbg_outs[name] = d
        self.P.dma(d, ap_sb)

    def wcols(self, src, l, c0, c1):
        return src[l].rearrange("(k p) c -> p k c", p=128)[:, :, c0:c1]

    def wload(self, dst, src, l, c0, c1, engs=("pool",)):
        P = self.P
        nk = dst.shape[1]
        C = c1 - c0
        for cc in range(0, C, 512):
            for k in range(nk):
                w = min(512, C - cc)
                st = self.stage[:, self.nst % 4, 0:w]
                self.nst += 1
                P.dma(st, src[l, k * 128:(k + 1) * 128, c0 + cc:c0 + cc + w])
                P.copy(engs[self.nst % len(engs)], dst[:, k, cc:cc + w], st)

    def setup(self):
        P = self.P
        P.dma(self.COS[:], self.rope_cos)
        P.dma(self.SIN[:], self.rope_sin)
        P.memset("pool", self.onesf[:], 1.0)
        P.memset("pool", self.ones[:], 1.0)
        P.memset("pool", self.identf[:], 0.0)
        idf = self.identf
        onf = self.onesf
        P.add("pool", lambda e: e.affine_select(idf[:], onf[:], [[-1, 128]], ALU.is_equal, 0.0,
                                                base=0, channel_multiplier=1), [onf[:]], [idf[:]])
        P.copy("dve", self.ident[:], self.identf[:])
        tf, tb = self.triF, self.triB
        P.add("pool", lambda e: e.affine_select(tf[:], onf[:], [[1, 128]], ALU.is_ge, 0.0,
                                                base=0, channel_multiplier=-1), [onf[:]], [tf[:]])
        P.add("pool", lambda e: e.affine_select(tb[:], onf[:], [[-1, 128]], ALU.is_ge, 0.0,
                                                base=0, channel_multiplier=1), [onf[:]], [tb[:]])
        P.ts("dve", self.TmFb[:], tf[:], -1.0 / 16, None, ALU.mult)
        P.tt("dve", self.LF[:], tb[:], self.identf[:], ALU.subtract)
        P.tt("dve", self.LB[:], tf[:], self.identf[:], ALU.subtract)
        P.ts("dve", self.TmBb[:], tb[:], -1.0 / 16, None, ALU.mult)

    def norm_to_hT(self, xt, t, gB, scr):
        P = self.P
        junk, ss, hb = scr
        i = t % 2
        P.actv(junk[:, :], xt, AF.Square, accum_out=ss[:, t:t + 1])
        P.actv(ss[:, 16 + t:17 + t], ss[:, t:t + 1], AF.Ln, bias=EPS, scale=1.0 / D)
        P.actv(ss[:, 32 + t:33 + t], ss[:, 16 + t:17 + t], AF.Exp, scale=-0.5)
        P.stt("dve", hb[:, i, :], xt, ss[:, 32 + t:33 + t], gB, ALU.mult, ALU.mult)
        pb = self.ps[:, 7, :].bitcast(BF16)
        for k in range(8):
            P.tr(pb[:, k * 128:(k + 1) * 128], hb[:, i, k * 128:(k + 1) * 128], self.ident[:])
        P.copy("act", self.hT[:, :, t * 128:(t + 1) * 128],
               pb.rearrange("p (k c) -> p k c", k=8))

    def phase_a(self, s, l):
        P = self.P
        self.P.phase = "A"
        self.pre["ssd"] = self.ssd_load(l)
        self.acur = 100 * KB
        xt = self.alloc([2, D], F32)
        gB = self.alloc([D], F32)
        junk = self.alloc([D], BF16)
        ss = self.alloc([48], F32)
        hb = self.alloc([2, D], BF16)
        P.dma(gB, bass.AP(self.norm_g.tensor, l * D, [[0, 128], [1, D]]))
        for t in range(NT):
            P.dma(xt[:, t % 2, :], self.x[s, t * 128:(t + 1) * 128, :])
            self.norm_to_hT(xt[:, t % 2, :], t, gB, (junk, ss, hb))

    def attn_units(self, kT_fn, qT, V_fn, ob, pT, scale):
        P = self.P
        ps = self.ps
        P.mm(ps[:, 0, :], kT_fn(0), qT)
        for kc in range(16):
            if kc + 1 < 16:
                P.mm(ps[:, (kc + 1) % 2, :], kT_fn(kc + 1), qT)
            P.actv(pT[:, kc % 2, :], ps[:, kc % 2, :], AF.Exp, scale=scale)
            P.mm(ps[:, ob, :], V_fn(kc), pT[:, kc % 2, :], start=(kc == 0), stop=(kc == 15))

    def qk_norm_rope(self, psA, psB, gtile, gc, out, b, tmp):
        P = self.P
        sq, rt, t1, t2 = tmp
        bl = slice(b * 512, (b + 1) * 512)
        P.actv(sq[0:64, :], psA, AF.Square)
        P.mm(self.ps[0:64, 6, :], self.ones[0:64, 0:64], sq[0:64, :])
        P.actv(rt[0:64, :], self.ps[0:64, 6, :], AF.Ln, bias=EPS, scale=1.0 / 64)
        P.actv(rt[0:64, :], rt[0:64, :], AF.Exp, scale=-0.5)
        P.stt("dve", t1[0:64, :], psA, gtile[0:64, gc:gc + 1], self.COS[0:64, bl], ALU.mult, ALU.mult)
        P.stt("dve", t2[0:64, :], psB, gtile[0:64, gc + 1:gc + 2], self.SIN[0:64, bl], ALU.mult, ALU.mult)
        P.tt("pool", t1[0:64, :], t1[0:64, :], t2[0:64, :], ALU.add)
        P.tt("pool", out, t1[0:64, :], rt[0:64, :], ALU.mult)

    def gqa_load(self, l):
        P = self.P
        self.acur = 108 * KB
        COSG = self.alloc([L], F32)
        SING = self.alloc([L], F32)
        wk2 = self.alloc([2, 8, 128], BF16)
        wkp2 = self.alloc([2, 8, 128], BF16)
        wv = self.alloc([8, 128], BF16)
        wq = self.alloc([2, 8, 128], BF16)
        wqp = self.alloc([2, 8, 128], BF16)
        wz = self.alloc([2, 8, 128], BF16)
        gt = self.alloc([4], F32)
        for g in range(2):
            for hlf in range(2):
                self.wload(wk2[:, g, :, 64 * hlf:64 * hlf + 64], self.w_in, l, OFF["kd"] + g * 64, OFF["kd"] + (g + 1) * 64, engs=BULK)
                self.wload(wkp2[:, g, :, 64 * hlf:64 * hlf + 64], self.w_perm, l, 544 + g * 64, 544 + (g + 1) * 64, engs=BULK)
        for hlf in range(2):
            rows = slice(64 * hlf, 64 * hlf + 64)
            P.dma(COSG[rows, :], self.rope_cos[0:64, :])
            P.dma(SING[rows, :], self.rope_sin[0:64, :])
            P.dma(gt[rows, :], self.qk_g[l])
        self.wload(wv, self.w_in, l, OFF["vd"], OFF["vd"] + 128, engs=BULK)
        self.wload(wq[:, 0], self.w_in, l, OFF["qd"], OFF["qd"] + 128, engs=BULK)
        self.wload(wqp[:, 0], self.w_perm, l, 32, 32 + 128, engs=BULK)
        self.wload(wz[:, 0], self.w_in, l, OFF["zd"], OFF["zd"] + 128, engs=BULK)
        return (COSG, SING, wk2, wkp2, wv, wq, wqp, wz, gt)

    def phase_gqa(self, l):
        P = self.P
        self.P.phase = "gqa_prep"
        ps = self.ps
        hT = self.hT
        onT = self.onT
        self.acur = 64 * KB
        BD = self.alloc([128], BF16)
        kT2 = self.alloc([2, L], BF16)
        Va = self.alloc([2, 16, 192], BF16)
        qT2 = self.alloc([2, 512], BF16)
        pT = self.alloc([2, 2, 512], BF16)
        sq = self.alloc([512], BF16)
        rt = self.alloc([512], F32)
        t1 = self.alloc([512], F32)
        t2 = self.alloc([512], F32)
        sz = self.alloc([2, 512], F32)
        ez = self.alloc([512], F32)
        rs = t1
        nt = t2
        osb = self.alloc([2, 512], F32)
        assert self.acur <= 108 * KB
        w = self.pre.pop("gqa", None) or self.gqa_load(l)
        COSG, SING, wk2, wkp2, wv, wq, wqp, wz, gt = w
        P.memset("pool", BD, 0.0)
        P.memset("pool", BD[0:64, 0:64], 1.0)
        P.memset("pool", BD[64:128, 64:128], 1.0)
        P.memset("pool", Va[:, :, :, 64:128], 1.0)

        def proj(wa, wb, bl):
            for k in range(8):
                P.mm(ps[:, 6, :], wa(k), hT[:, k, bl], k == 0, k == 7)
            for k in range(8):
                P.mm(ps[:, 7, :], wb(k), hT[:, k, bl], k == 0, k == 7)
            P.actv(sq, ps[:, 6, :], AF.Square)

        def rope1(gc, bl):
            P.stt("dve", t1, ps[:, 6, :], gt[:, gc:gc + 1], COSG[:, bl], ALU.mult, ALU.mult)
            P.stt("dve", t2, ps[:, 7, :], gt[:, gc + 1:gc + 2], SING[:, bl], ALU.mult, ALU.mult)

        def rope2(out):
            P.mm(ps[:, 7, :], BD, sq, True, True)
            P.actv(rt, ps[:, 7, :], AF.Ln, bias=EPS, scale=1.0 / 64)
            P.actv(rt, rt, AF.Exp, scale=-0.5)
            P.tt("pool", t1, t1, t2, ALU.add)
            P.tt("pool", out, t1, rt, ALU.mult)

        for g in range(2):
            for b in range(NB):
                bl = slice(b * 512, (b + 1) * 512)
                proj(lambda k: wk2[:, g, k, :], lambda k: wkp2[:, g, k, :], bl)
                rope1(2, bl)
                rope2(kT2[:, g, bl])
        for t in range(NT):
            for k in range(8):
                P.mm(ps[:, 4 + t % 2, 0:128], hT[:, k, t * 128:(t + 1) * 128], wv[:, k, :], k == 0, k == 7)
            for g in range(2):
                P.copy("act", Va[:, g, t, 0:64], ps[:, 4 + t % 2, g * 64:(g + 1) * 64])
                P.copy("dve", Va[:, g, t, 128:192], ps[:, 4 + t % 2, g * 64:(g + 1) * 64])
        P.phase = "gqa_heads"
        items = [(pr, b) for pr in range(4) for b in range(NB)]

        def prep_parts(it):
            pr, b = items[it]
            i = pr % 2
            bl = slice(b * 512, (b + 1) * 512)
            hooks = {}

            def add(kc, f):
                prev = hooks.get(kc)

                def both(prev=prev, f=f):
                    if prev is not None:
                        prev()
                    f()
                hooks[kc] = both

            def loads():
                if b == NB - 1 and pr + 1 < 4:
                    p2 = pr + 1
                    i2 = p2 % 2
                    self.wload(wq[:, i2], self.w_in, l, OFF["qd"] + p2 * 128, OFF["qd"] + (p2 + 1) * 128)
                    self.wload(wqp[:, i2], self.w_perm, l, 32 + p2 * 128, 32 + (p2 + 1) * 128)
                    self.wload(wz[:, i2], self.w_in, l, OFF["zd"] + p2 * 128, OFF["zd"] + (p2 + 1) * 128)
            add(0, loads)

            def pk(k):
                P.mm(ps[:, 6, :], wq[:, i, k, :], hT[:, k, bl], k == 0, k == 7)
                P.mm(ps[:, 7, :], wqp[:, i, k, :], hT[:, k, bl], k == 0, k == 7)
            for k in range(8):
                add(k, lambda k=k: pk(k))

            def sq_rope1():
                P.actv(sq, ps[:, 6, :], AF.Square)
                rope1(0, bl)
            add(8, sq_rope1)
            add(9, lambda: rope2(qT2[:, it % 2, :]))
            for k in range(8):
                add(10 + min(k, 5), lambda k=k: P.mm(ps[:, 6, :], wz[:, i, k, :], hT[:, k, bl], k == 0, k == 7))

            def silu():
                P.actv(ez, ps[:, 6, :], AF.Exp, scale=-1.0)
                P.actv(ez, ez, AF.Ln, bias=1.0)
                P.actv(ez, ez, AF.Exp, scale=-1.0)
                P.tt("dve", sz[:, it % 2, :], ps[:, 6, :], ez, ALU.mult)
            add(15, silu)
            return hooks

        def units(it, hooks):
            pr, b = items[it]
            g = pr // 2
            bl = slice(b * 512, (b + 1) * 512)
            qb = qT2[:, it % 2, :]
            sbank = [[0, 1], [4, 5]]
            rows = [slice(0, 64), slice(64, 128)]
            vsel = [slice(0, 128), slice(64, 192)]
            for par in range(2):
                P.mm(ps[:, sbank[par][0], :], kT2[rows[par], g, 0:128], qb[rows[par], :])
            for kc in range(16):
                if kc + 1 < 16:
                    for par in range(2):
                        P.mm(ps[:, sbank[par][(kc + 1) % 2], :], kT2[rows[par], g, (kc + 1) * 128:(kc + 2) * 128], qb[rows[par], :])
                for par in range(2):
                    P.actv(pT[:, par, kc % 2, :], ps[:, sbank[par][kc % 2], :], AF.Exp, scale=0.125)
                for par in range(2):
                    P.mm(ps[:, 2 + par, :], Va[:, g, kc, vsel[par]], pT[:, par, kc % 2, :], kc == 0, kc == 15)
                if kc in hooks:
                    hooks[kc]()
            for par in range(2):
                P.copy("dve", osb[:, par, :], ps[:, 2 + par, :])
            for par in range(2):
                orow = rows[par]
                srow = rows[1 - par]
                P.recip(rs[orow, :], osb[srow, par, :])
                P.tt("dve", nt[orow, :], osb[orow, par, :], rs[orow, :], ALU.mult)
                P.tt("pool", onT[orow, pr, bl], nt[orow, :], sz[orow, it % 2, :], ALU.mult)

        h0 = prep_parts(0)
        for kc in sorted(h0):
            h0[kc]()
        for it in range(len(items)):
            hooks = prep_parts(it + 1) if it + 1 < len(items) else {}
            units(it, hooks)

    def mla_load(self, l):
        self.acur = 124 * KB
        wq = self.alloc([3, 768], BF16)
        wqp = self.alloc([3, 256], BF16)
        wkv = self.alloc([2, 1024], BF16)
        wql = self.alloc([8, 384], BF16)
        wkvl = self.alloc([8, 256], BF16)
        wkr = self.alloc([2, 8, 32], BF16)
        lg = self.alloc([5], F32)
        self.wload(wql, self.w_in, l, OFF["qlat"], OFF["qlat"] + 384, engs=BULK)
        self.wload(wkvl, self.w_in, l, OFF["kvlat"], OFF["kvlat"] + 256, engs=BULK)
        self.wload(wkr[:, 0], self.w_in, l, OFF["krope"], OFF["krope"] + 32, engs=BULK)
        self.wload(wkr[:, 1], self.w_perm, l, 0, 32, engs=BULK)
        self.wload(wq, self.w_q_b, l, 0, 768, engs=BULK)
        self.wload(wqp, self.w_qb_perm, l, 0, 256, engs=BULK)
        self.wload(wkv, self.w_kv_b, l, 0, 1024, engs=BULK)
        self.P.dma(lg, self.lat_g[l])
        return (wq, wqp, wkv, wql, wkvl, wkr, lg)

    def phase_mla(self, l):
        P = self.P
        self.P.phase = "mla_prep"
        ps = self.ps
        hT = self.hT
        onT = self.onT
        COS, SIN = self.COS, self.SIN
        self.acur = 64 * KB
        qln = self.alloc([3, L], BF16)
        kvn = self.alloc([2, L], BF16)
        krT = self.alloc([L], BF16)
        Va = self.alloc([16, 4, 192], BF16)
        qT = self.alloc([2, 512], BF16)
        pT = self.alloc([4, 512], BF16)
        sz = self.alloc([2, 512], F32)
        wz = self.alloc([2, 8, 64], BF16)
        assert self.acur <= 124 * KB
        self.acur = 48 * KB
        kT = self.alloc([2, L], BF16)
        t1 = self.alloc([512], F32)
        t2 = self.alloc([512], F32)
        sqc = self.alloc([2, 512], BF16)
        rt = self.alloc([512], F32)
        rs = t1
        assert self.acur <= 64 * KB
        w = self.pre.pop("mla", None) or self.mla_load(l)
        wq, wqp, wkv, wql, wkvl, wkr, lg = w
        P.memset("pool", Va[:, :, :, 64:128], 1.0)
        nsq = 0
        for b in range(NB):
            bl = slice(b * 512, (b + 1) * 512)
            for (wsrc, nch, bank0, sbank, dst, gofs, dim) in ((wql, 3, 0, 3, qln, 0, 384), (wkvl, 2, 4, 6, kvn, 3, 256)):
                for c in range(nch):
                    for k in range(8):
                        P.mm(ps[:, bank0 + c, :], wsrc[:, k, c * 128:(c + 1) * 128], hT[:, k, bl], k == 0, k == 7)
                for c in range(nch):
                    P.actv(sqc[:, nsq % 2, :], ps[:, bank0 + c, :], AF.Square)
                    P.mm(ps[:, sbank, :], self.ones[:, :], sqc[:, nsq % 2, :], c == 0, c == nch - 1)
                    nsq += 1
                P.actv(rt, ps[:, sbank, :], AF.Ln, bias=EPS, scale=1.0 / dim)
                P.actv(rt, rt, AF.Exp, scale=-0.5)
                for c in range(nch):
                    P.stt("dve", dst[:, c, bl], ps[:, bank0 + c, :], lg[:, gofs + c:gofs + c + 1], rt, ALU.mult, ALU.mult)
            for k in range(8):
                P.mm(ps[64:96, 7, :], wkr[:, 0, k, :], hT[:, k, bl], k == 0, k == 7)
            P.tt("dve", t1[64:96, :], ps[64:96, 7, :], COS[64:96, bl], ALU.mult)
            for k in range(8):
                P.mm(ps[64:96, 7, :], wkr[:, 1, k, :], hT[:, k, bl], k == 0, k == 7)
            P.tt("dve", t2[64:96, :], ps[64:96, 7, :], SIN[64:96, bl], ALU.mult)
            P.tt("pool", krT[64:96, bl], t1[64:96, :], t2[64:96, :], ALU.add)
        wv_view = wkv.rearrange("p c (h two d) -> p c h two d", h=8, two=2)
        for t in range(NT):
            tl = slice(t * 128, (t + 1) * 128)
            for c in range(2):
                P.mm(ps[:, 4 + t % 2, :].rearrange("p (h d) -> p h d", h=8), kvn[:, c, tl], wv_view[:, c, :, 1, :], c == 0, c == 1)
            pv = ps[:, 4 + t % 2, :].rearrange("p (j two d) -> p j two d", j=4, two=2)
            P.copy("act", Va[:, t, :, 0:64], pv[:, :, 0, :])
            P.copy("dve", Va[:, t, :, 128:192], pv[:, :, 1, :])
        scale = 96.0 ** -0.5
        P.phase = "mla_heads"
        items = [(h, b) for h in range(8) for b in range(NB)]
        ez = t2

        def head_prep(h):
            i = h % 2
            if h == 0:
                self.wload(wz[:, i], self.w_in, l, OFF["zb"] + h * 64, OFF["zb"] + (h + 1) * 64)
            for b in range(NB):
                bl = slice(b * 512, (b + 1) * 512)
                for c in range(2):
                    P.mm(ps[0:64, 6, :], wkv[:, c, h * 128:h * 128 + 64], kvn[:, c, bl], c == 0, c == 1)
                P.copy("dve", kT[0:64, i, bl], ps[0:64, 6, :])
            P.copy("pool", kT[64:96, i, :], krT[64:96, :])

        def prep_parts(it):
            h, b = items[it]
            i = h % 2
            par = h % 2
            orow = slice(64 * par, 64 * par + 64)
            bl = slice(b * 512, (b + 1) * 512)
            qb = qT[:, it % 2, :]

            def p0():
                if b == 0:
                    head_prep(h)
                if b == NB - 2 and h + 1 < 8:
                    h2 = h + 1
                    self.wload(wz[:, h2 % 2], self.w_in, l, OFF["zb"] + h2 * 64, OFF["zb"] + (h2 + 1) * 64)

            def p1():
                for c in range(3):
                    P.mm(ps[0:96, 6, :], wq[:, c, h * 96:(h + 1) * 96], qln[:, c, bl], c == 0, c == 2)
                for c in range(3):
                    P.mm(ps[64:96, 7, :], wqp[:, c, h * 32:(h + 1) * 32], qln[:, c, bl], c == 0, c == 2)
                P.copy("dve", qb[0:64, :], ps[0:64, 6, :])
                P.tt("dve", t1[64:96, :], ps[64:96, 6, :], COS[64:96, bl], ALU.mult)
                P.tt("dve", t2[64:96, :], ps[64:96, 7, :], SIN[64:96, bl], ALU.mult)
                P.tt("pool", qb[64:96, :], t1[64:96, :], t2[64:96, :], ALU.add)

            def p3():
                for k in range(8):
                    P.mm(ps[orow, 7, :], wz[:, i, k, :], hT[:, k, bl], k == 0, k == 7)
                P.actv(ez[orow, :], ps[orow, 7, :], AF.Exp, scale=-1.0)
                P.actv(ez[orow, :], ez[orow, :], AF.Ln, bias=1.0)
                P.actv(ez[orow, :], ez[orow, :], AF.Exp, scale=-1.0)
                P.tt("dve", sz[orow, it % 2, :], ps[orow, 7, :], ez[orow, :], ALU.mult)

            return [p0, p1, p3]

        def units(it, hooks):
            h, b = items[it]
            i = h % 2
            par = h % 2
            pr = h // 2
            orow = slice(64 * par, 64 * par + 64)
            srow = slice(64 * (1 - par), 64 * (1 - par) + 64)
            vsel = slice(0, 128) if par == 0 else slice(64, 192)
            bl = slice(b * 512, (b + 1) * 512)
            qb = qT[:, it % 2, :]
            ob = 2 + it % 2
            sbanks = [[0, 1], [4, 5]]

            def S(kc):
                P.mm(ps[:, sbanks[(kc // 2) % 2][kc % 2], :], kT[0:96, i, kc * 128:(kc + 1) * 128], qb[0:96, :])
            S(0)
            S(1)
            for k2 in range(8):
                if k2 + 1 < 8:
                    S(2 * k2 + 2)
                    S(2 * k2 + 3)
                for kc in (2 * k2, 2 * k2 + 1):
                    P.actv(pT[:, kc % 4, :], ps[:, sbanks[(kc // 2) % 2][kc % 2], :], AF.Exp, scale=scale)
                for kc in (2 * k2, 2 * k2 + 1):
                    P.mm(ps[:, ob, :], Va[:, kc, pr, vsel], pT[:, kc % 4, :], kc == 0, kc == 15)
                if k2 in hooks:
                    hooks[k2]()
            P.recip(rs[orow, :], ps[srow, ob, :])
            P.tt("dve", rs[orow, :], ps[orow, ob, :], rs[orow, :], ALU.mult)
            P.tt("pool", onT[orow, pr, bl], rs[orow, :], sz[orow, it % 2, :], ALU.mult)

        for f in prep_parts(0):
            f()
        for it in range(len(items)):
            hooks = {}
            if it + 1 < len(items):
                pp = prep_parts(it + 1)
                hooks = {0: pp[0], 2: pp[1], 5: pp[2]}
            units(it, hooks)

    def gla_load(self, l):
        P = self.P
        self.acur = 48 * KB
        wvc = self.alloc([8, 512], BF16)
        wzc = self.view(48 * KB, [8, 512], BF16)
        wqc = self.alloc([8, 256], BF16)
        wkc = self.alloc([8, 256], BF16)
        assert self.acur <= 64 * KB
        self.acur = 143 * KB
        wgu = self.alloc([2, 256], BF16)
        bg = self.alloc([2, 256], BF16)
        wglr = self.alloc([2, 8, 32], BF16)
        gg = self.alloc([4], F32)
        self.wload(wqc, self.w_in, l, OFF["qc"], OFF["qc"] + 256, engs=BULK)
        self.wload(wkc, self.w_in, l, OFF["kc"], OFF["kc"] + 256, engs=BULK)
        self.wload(wvc, self.w_in, l, OFF["vc"], OFF["vc"] + 512, engs=BULK)
        self.wload(wglr[:, 0], self.w_in, l, OFF["glr"], OFF["glr"] + 32, engs=BULK)
        self.wload(wglr[:, 1, :, 0:16], self.w_in, l, OFF["glr"] + 16, OFF["glr"] + 32, engs=BULK)
        self.wload(wglr[:, 1, :, 16:32], self.w_in, l, OFF["glr"], OFF["glr"] + 16, engs=BULK)
        P.memset("pool", wgu[0:32], 0.0)
        st = self.stage[0:16, self.nst % 4, :]
        self.nst += 1
        P.dma(st.rearrange("p (d c) -> p d c", d=2), self.w_gate_up[l].rearrange("d r c -> r d c"))
        P.copy("pool", wgu[0:16], st.rearrange("p (d c) -> p d c", d=2))
        st = self.stage[0:1, self.nst % 4, :]
        self.nst += 1
        P.dma(st.rearrange("p (d c) -> p d c", d=2), self.b_gate[l:l + 1])
        P.copy("pool", bg[0:1], st.rearrange("p (d c) -> p d c", d=2))
        P.dma(gg, self.gla_g[l])
        return (wvc, wzc, wqc, wkc, wglr, wgu, bg, gg)

    def phase_gla(self, l):
        P = self.P
        self.P.phase = "gla_prep"
        ps = self.ps
        hT = self.hT
        onT = self.onT
        self.acur = 64 * KB
        qTc = self.alloc([2, L], BF16)
        kTc = self.alloc([2, L], BF16)
        Vc = self.alloc([16, 512], BF16)
        ob = self.alloc([4, L], BF16)
        glrT = self.alloc([2, L], BF16)
        oblk = self.alloc([4, 512], F32)
        S = self.alloc([2, 128], F32)
        Sbf = self.alloc([2, 128], BF16)
        gsp = self.alloc([256], F32)
        gh = self.alloc([256], BF16)
        gl = self.alloc([256], BF16)
        eg = self.alloc([2, 128], F32)
        eng = self.alloc([2, 128], F32)
        ek = self.alloc([2, 128], F32)
        qg = self.alloc([2, 2, 128], BF16)
        kg = self.alloc([2, 128], BF16)
        kend = self.alloc([2, 2, 128], BF16)
        attm = self.alloc([2, 4, 128], BF16)
        kendT = self.alloc([2, 128], BF16)
        glast = self.alloc([2], F32)
        cd = self.alloc([2, 2], F32)
        assert self.acur <= 143 * KB
        self.acur = 56 * KB
        sq = self.alloc([512], BF16)
        rt = self.alloc([512], F32)
        sz = self.alloc([512], F32)
        tmp = self.alloc([512], F32)
        assert self.acur <= 64 * KB
        w = self.pre.pop("gla", None) or self.gla_load(l)
        wvc, wzc, wqc, wkc, wglr, wgu, bg, gg = w
        n = 0
        for j in range(2):
            for (wsrc, dst) in ((wqc, qTc), (wkc, kTc)):
                for b in range(NB):
                    bl = slice(b * 512, (b + 1) * 512)
                    bank = 4 + n % 4
                    for k in range(8):
                        P.mm(ps[:, bank, :], wsrc[:, k, j * 128:(j + 1) * 128], hT[:, k, bl], k == 0, k == 7)
                    P.copy("act" if n % 2 else "dve", dst[:, j, bl], ps[:, bank, :])
                    n += 1
        for t in range(NT):
            tl = slice(t * 128, (t + 1) * 128)
            bank = 4 + n % 4
            for k in range(8):
                P.mm(ps[:, bank, :], hT[:, k, tl], wvc[:, k, :], k == 0, k == 7)
            P.copy("act" if n % 2 else "dve", Vc[:, t, :], ps[:, bank, :])
            n += 1
        for d in range(2):
            for b in range(NB):
                bl = slice(b * 512, (b + 1) * 512)
                bank = 4 + n % 4
                for k in range(8):
                    P.mm(ps[0:32, bank, :], wglr[:, d, k, :], hT[:, k, bl], k == 0, k == 7)
                P.copy("act" if n % 2 else "dve", glrT[0:32, d, bl], ps[0:32, bank, :])
                n += 1

        self.wload(wzc, self.w_in, l, OFF["zc"], OFF["zc"] + 512)

        def bank3(b, a):
            return ps[:, b, 0:a * 128].rearrange("p (a c) -> p a c", a=a)


        P.phase = "gla_scan"
        psC = bank3(1, 2)
        psA2 = [bank3(2, 2), bank3(3, 2)]
        psO2 = [bank3(4, 2), bank3(5, 2)]
        psT = ps[:, 6, :].bitcast(BF16)[:, 0:256].rearrange("p (a c) -> p a c", a=2)
        psU = bank3(7, 2)
        obv = ob.rearrange("p (j two) t -> p j two t", two=2)
        oblv = oblk.rearrange("p (j two) t -> p j two t", two=2)

        def front(d, ci, t):
            Tm = self.TmFb if d == 0 else self.TmBb
            tri = self.triF if d == 0 else self.triB
            last = 127 if d == 0 else 0
            x = ci % 2
            tl = slice(t * 128, (t + 1) * 128)
            P.mm(ps[:, 0, 0:256], glrT[0:32, d, tl], wgu[0:32, d, :], True, False)
            P.mm(ps[:, 0, 0:256], self.ones[0:1, 0:128], bg[0:1, d, :], False, True)
            P.actv(gsp, ps[:, 0, 0:256], AF.Exp, scale=-1.0)
            P.actv(gsp, gsp, AF.Ln, bias=1.0)
            P.copy("dve", gh, gsp)
            P.tt("dve", gl, gsp, gh, ALU.subtract)
            for j in range(2):
                P.mm(psC[:, j, :], gh[:, j * 128:(j + 1) * 128], Tm[:, :], True, False)
                P.mm(psC[:, j, :], gl[:, j * 128:(j + 1) * 128], Tm[:, :], False, True)
            P.copy("dve", glast, psC[:, :, last])
            P.actv(eg.rearrange("p a c -> p (a c)"), ps[:, 1, 0:256], AF.Exp)
            P.actv(eng.rearrange("p a c -> p (a c)"), ps[:, 1, 0:256], AF.Exp, scale=-1.0)
            for j in range(2):
                P.actv(ek[:, j, :], psC[:, j, :], AF.Exp, scale=-1.0, bias=glast[:, j:j + 1])
            P.actv(cd[:, x, :], glast, AF.Exp)
            P.stt("dve", qg[:, x], qTc[:, :, tl], 0.125, eg, ALU.mult, ALU.mult)
            P.tt("dve", kg, kTc[:, :, tl], eng, ALU.mult)
            P.tt("dve", kend[:, x], kTc[:, :, tl], ek, ALU.mult)
            tri_b = bass.AP(tri, 0, [[128, 128], [0, 2], [1, 128]])
            attv = attm[:, x].rearrange("p (j two) c -> p j two c", two=2)
            for par in range(2):
                r = slice(64 * par, 64 * par + 64)
                for j in range(2):
                    P.mm(psA2[par][:, j, :], kg[r, j, :], qg[r, x, j, :], True, True)
                P.tt("dve", attv[:, :, par, :], psA2[par], tri_b, ALU.mult)

        def back(d, ci, t):
            x = ci % 2
            tl = slice(t * 128, (t + 1) * 128)
            for par in range(2):
                r = slice(64 * par, 64 * par + 64)
                for j in range(2):
                    h = 2 * j + par
                    P.mm(psO2[par][:, j, :], Vc[:, t, h * 128:(h + 1) * 128], attm[:, x, h, :], True, ci == 0)
                    if ci > 0:
                        P.mm(psO2[par][:, j, :], Sbf[r, j, :], qg[r, x, j, :], False, True)
                if d == 1:
                    P.copy("act", obv[:, :, par, tl], psO2[par])
                else:
                    c4 = t % 4
                    P.tt("dve", oblv[:, :, par, c4 * 128:(c4 + 1) * 128], psO2[par], obv[:, :, par, tl], ALU.add)
            if ci < NT - 1:
                for j in range(2):
                    P.tr(psT[:, j, :], kend[:, x, j, :], self.ident[:])
                P.copy("act", kendT, psT)
                for h in range(4):
                    j = h // 2
                    r = slice(64 * (h % 2), 64 * (h % 2) + 64)
                    P.mm(psU[r, j, :], kendT[:, j, 64 * (h % 2):64 * (h % 2) + 64], Vc[:, t, h * 128:(h + 1) * 128], True, True)
                for j in range(2):
                    if ci > 0:
                        P.stt("dve", S[:, j, :], S[:, j, :], cd[:, x, j:j + 1], psU[:, j, :], ALU.mult, ALU.add)
                    else:
                        P.copy("dve", S[:, j, :], psU[:, j, :])
                P.copy("pool", Sbf, S)
            if d == 0 and t % 4 == 3:
                b = t // 4
                bl = slice(b * 512, (b + 1) * 512)
                for h in range(4):
                    P.actv(sq, oblk[:, h, :], AF.Square)
                    P.mm(ps[:, 0, :], self.ones[:, :], sq, True, True)
                    P.actv(rt, ps[:, 0, :], AF.Ln, bias=EPS, scale=1.0 / 128)
                    P.actv(rt, rt, AF.Exp, scale=-0.5)
                    for k in range(8):
                        P.mm(ps[:, 1, :], wzc[:, k, h * 128:(h + 1) * 128], hT[:, k, bl], k == 0, k == 7)
                    P.actv(sz, ps[:, 1, :], AF.Silu)
                    P.stt("dve", tmp, oblk[:, h, :], gg[:, h:h + 1], rt, ALU.mult, ALU.mult)
                    P.tt("pool", onT[:, h, bl], tmp, sz, ALU.mult)

        for d in (1, 0):
            order = list(range(NT)) if d == 0 else list(range(NT - 1, -1, -1))
            front(d, 0, order[0])
            for ci, t in enumerate(order):
                if ci + 1 < NT:
                    front(d, ci + 1, order[ci + 1])
                back(d, ci, t)

    def ssd_load(self, l):
        P = self.P
        self.acur = 64 * KB
        wz = self.alloc([8, D], BF16)
        wdt = self.alloc([8, 32], BF16)
        anb = self.alloc([32], F32)
        dtb = self.alloc([32], F32)
        dsk = self.alloc([16], F32)
        gnb = self.alloc([D], F32)
        cp = self.alloc([12, 6], F32)
        self.ssd_end = self.acur
        P.dma(cp, self.conv_p[l])
        P.dma(anb, bass.AP(self.a_log.tensor, l * 32, [[0, 128], [1, 32]]))
        P.dma(dtb, bass.AP(self.dt_bias.tensor, l * 32, [[0, 128], [1, 32]]))
        P.dma(dsk, bass.AP(self.d_skip.tensor, l * 16, [[0, 128], [1, 16]]))
        P.dma(gnb, bass.AP(self.ssd_norm_g.tensor, l * D, [[0, 128], [1, D]]))
        self.wload(wdt, self.w_in, l, OFF["dt"], OFF["dt"] + 32, engs=BULK)
        self.wload(wz, self.w_in, l, OFF["za"], OFF["za"] + D, engs=BULK)
        return (wz, wdt, cp, anb, dtb, dsk, gnb)

    def phase_ssd(self, l):
        P = self.P
        self.P.phase = "ssd_conv"
        ps = self.ps
        hT = self.hT
        onT = self.onT
        xs_tok = self.view(0, [16, D], BF16)
        w = self.pre.pop("ssd", None) or self.ssd_load(l)
        wz, wdt, cp, anb, dtb, dsk, gnb = w
        self.acur = self.ssd_end
        BT = self.alloc([2, L], BF16)
        CT = self.alloc([2, L], BF16)
        Btok = self.alloc([16, 256], BF16)
        dt_all = self.alloc([16, 32], F32)
        dtaf = self.alloc([16, 32], F32)
        base = self.acur
        xb = self.alloc([L + 4], F32)
        acc = self.alloc([L], F32)
        cv = self.alloc([L], BF16)
        wx = self.alloc([3, 8, 128], BF16)
        P.actv(anb, anb, AF.Exp)
        P.ts("dve", anb, anb, -1.0, None, ALU.mult)
        P.memset("pool", xb[:, 0:2], 0.0)
        P.memset("pool", xb[:, L + 2:L + 4], 0.0)
        for c in range(2):
            self.wload(wx[:, c], self.w_in, l, OFF["xbc"] + c * 128, OFF["xbc"] + (c + 1) * 128)
        for c in range(12):
            i = c % 3
            if c + 2 < 12:
                self.wload(wx[:, (c + 2) % 3], self.w_in, l, OFF["xbc"] + (c + 2) * 128, OFF["xbc"] + (c + 3) * 128)
            for b in range(NB):
                bank = 4 + (c * 4 + b) % 4
                for k in range(8):
                    P.mm(ps[:, bank, :], wx[:, i, k, :], hT[:, k, b * 512:(b + 1) * 512], k == 0, k == 7)
                P.copy("act", xb[:, 2 + b * 512:2 + (b + 1) * 512], ps[:, bank, :])
            eng = "dve"
            P.ts(eng, acc, xb[:, 0:L], cp[:, c, 0:1], cp[:, c, 5:6], ALU.mult, ALU.add)
            for j in range(1, 5):
                P.stt(eng, acc, xb[:, j:j + L], cp[:, c, j:j + 1], acc, ALU.mult, ALU.add)
            if c < 8:
                P.actv(cv, acc, AF.Silu)
                dst_tok = lambda t, c=c: xs_tok[:, t, c * 128:(c + 1) * 128]
                src = cv
            elif c < 10:
                P.actv(BT[:, c - 8, :], acc, AF.Silu)
                dst_tok = lambda t, c=c: Btok[:, t, (c - 8) * 128:(c - 7) * 128]
                src = BT[:, c - 8, :]
            else:
                P.actv(CT[:, c - 10, :], acc, AF.Silu)
                src = None
            if src is not None:
                for half in range(2):
                    pb = ps[:, 2 + half, :].bitcast(BF16)
                    for tt_ in range(8):
                        t = half * 8 + tt_
                        P.tr(pb[:, tt_ * 128:(tt_ + 1) * 128], src[:, t * 128:(t + 1) * 128], self.ident[:])
                    if c < 8:
                        P.copy("act" if half else "dve", xs_tok[:, half * 8:(half + 1) * 8, c * 128:(c + 1) * 128],
                               pb.rearrange("p (t c) -> p t c", t=8))
                    else:
                        P.copy("act" if half else "dve", Btok[:, half * 8:(half + 1) * 8, (c - 8) * 128:(c - 7) * 128],
                               pb.rearrange("p (t c) -> p t c", t=8))
        P.phase = "ssd_scan"
        for t in range(NT):
            for k in range(8):
                P.mm(ps[:, 7, t * 32:(t + 1) * 32], hT[:, k, t * 128:(t + 1) * 128], wdt[:, k, :], k == 0, k == 7)
        dtb_b = bass.AP(dtb.tensor, dtb.offset, [list(dtb.ap[0]), [0, 16], [1, 32]])
        anb_b = bass.AP(anb.tensor, anb.offset, [list(anb.ap[0]), [0, 16], [1, 32]])
        P.tt("dve", dt_all, ps[:, 7, :].rearrange("p (t c) -> p t c", t=16), dtb_b, ALU.add)
        P.actv(dt_all, dt_all, AF.Exp)
        P.actv(dt_all, dt_all, AF.Ln, bias=1.0)
        self.acur = base
        self.allow_over = True
        Af = self.alloc([1, 8, 128], F32)
        E = self.alloc([1, 8, 128], BF16)
        W = self.alloc([1, 8, 128], BF16)
        Gm = self.alloc([2, 128], BF16)
        xd = self.alloc([D], BF16)
        xdd = self.alloc([D], BF16)
        S = self.alloc([D], F32)
        Sbf = self.alloc([D], BF16)
        ytmp = self.alloc([D], F32)
        y2 = self.alloc([2, D], F32)
        ybt = self.alloc([D], F32)
        eac = self.alloc([16], F32)
        cdb = self.alloc([16], F32)
        sz = self.alloc([D], F32)
        dta = sz[:, 0:512].rearrange("p (t c) -> p t c", t=16)
        junk = ytmp
        hb = xdd
        ss = self.alloc([4], F32)
        P.tt("dve", dta, dt_all, anb_b, ALU.mult)
        P.copy("dve", dtaf, dta)

        def hv(ap2):
            return ap2.rearrange("p (h q) -> p h q", h=16)

        def bc_h(ap_col16, n):
            return bass.AP(ap_col16.tensor, ap_col16.offset, [list(ap_col16.ap[0]), [ap_col16.ap[1][0], 16], [0, n]])

        def bc_h2(ap16, g):
            return bass.AP(ap16.tensor, ap16.offset + g * 8, [list(ap16.ap[0]), [ap16.ap[1][0], 8], [0, 128]])

        def fin_stages(ci, t):
            y = y2[:, ci % 2, :]
            tl = slice(t * 128, (t + 1) * 128)
            pz = ps[:, 0:2, :].rearrange("p b c -> p (b c)")

            def f1a():
                P.tt("dve", hv(ytmp), hv(xs_tok[:, t, :]), bc_h(dsk, 64), ALU.mult)
                P.tt("dve", y, y, ytmp, ALU.add)

            def f1b():
                for hf in range(2):
                    for k in range(8):
                        P.mm(ps[:, hf, :], hT[:, k, tl], wz[:, k, hf * 512:(hf + 1) * 512], k == 0, k == 7)

            def f2():
                P.actv(sz, pz, AF.Silu)

            def f3():
                P.tt("dve", y, y, sz, ALU.mult)

            def f4():
                P.actv(junk, y, AF.Square, accum_out=ss[:, 0:1])
                P.actv(ss[:, 1:2], ss[:, 0:1], AF.Ln, bias=EPS, scale=1.0 / D)
                P.actv(ss[:, 2:3], ss[:, 1:2], AF.Exp, scale=-0.5)

            def f5():
                P.stt("dve", hb, y, ss[:, 2:3], gnb, ALU.mult, ALU.mult)
                pb = ps[:, 7, :].bitcast(BF16)
                for k in range(8):
                    P.tr(pb[:, k * 128:(k + 1) * 128], hb[:, k * 128:(k + 1) * 128], self.ident[:])
                P.copy("act", onT[:, :, tl], pb.rearrange("p (k c) -> p k c", k=8))

            return [f1a, f1b, f2, f3, f4, f5]

        def main_stages(d, ci, t):
            Lm = self.LF if d == 0 else self.LB
            Tm = self.triF if d == 0 else self.triB
            tri = self.triF if d == 0 else self.triB
            last = 127 if d == 0 else 0
            y = y2[:, ci % 2, :]
            tl = slice(t * 128, (t + 1) * 128)
            dh = dtaf[:, t, d * 16:(d + 1) * 16]

            def m1():
                if d == 0:
                    P.dma(ybt, self.yb_d[tl, :])
                for g in range(2):
                    P.mm(ps[:, 4, g * 128:(g + 1) * 128], BT[:, g, tl], CT[:, g, tl], True, True)
                P.mm(ps[:, 4, 256:272], Tm[:, :], dh, True, True)
                P.mm(ps[:, 4, 272:288], self.onesf[:, :], dh, True, True)
                tri_b = bass.AP(tri, 0, [[128, 128], [0, 2], [1, 128]])
                P.tt("dve", Gm, ps[:, 4, 0:256].rearrange("p (g c) -> p g c", g=2), tri_b, ALU.mult)
                P.actv(eac, ps[:, 4, 256:272], AF.Exp)
                P.actv(cdb, ps[:, 4, 272:288], AF.Exp)
                P.tt("dve", hv(xd), hv(xs_tok[:, t, :]), bc_h(dt_all[:, t, d * 16:(d + 1) * 16], 64), ALU.mult)

            def mg(g):
                def f():
                    Lb = bass.AP(Lm, 0, [[128, 128], [0, 8], [1, 128]])
                    P.tt("pool", Af[:, 0], Lb, bc_h2(dh, g), ALU.mult)
                    for e in range(8):
                        out = ps[:, 2 * g + e // 4, (e % 4) * 128:(e % 4 + 1) * 128]
                        P.mm(out, Af[:, 0, e, :], Tm[:, :], True, True)
                    P.actv(E[:, 0], ps[:, 2 * g:2 * g + 2, :].rearrange("p b (e c) -> p (b e) c", e=4), AF.Exp)
                    Gb = bass.AP(Gm.tensor, Gm.offset + g * 128, [list(Gm.ap[0]), [0, 8], [1, 128]])
                    P.tt("dve", W[:, 0], E[:, 0], Gb, ALU.mult)
                    Elast = bass.AP(E.tensor, E.offset + last, [list(E.ap[0]), [128, 8], [0, 64]])
                    P.tt("dve", hv(xdd)[:, g * 8:(g + 1) * 8, :], hv(xd)[:, g * 8:(g + 1) * 8, :], Elast, ALU.mult)
                    for e in range(8):
                        h = g * 8 + e
                        P.mm(ps[:, 5 + g, e * 64:(e + 1) * 64], W[:, 0, e, :], xd[:, h * 64:(h + 1) * 64], True, True)
                return f

            def m4():
                if ci > 0:
                    for g in range(2):
                        P.mm(ps[:, g, :], CT[:, g, tl], Sbf[:, g * 512:(g + 1) * 512], True, True)
                    P.tt("dve", hv(ytmp), ps[:, 0:2, :].rearrange("p b (e q) -> p (b e) q", e=8), bc_h(eac, 64), ALU.mult)
                    P.tt("dve", y, ytmp, ps[:, 5:7, :].rearrange("p b c -> p (b c)"), ALU.add)
                else:
                    P.copy("act", y, ps[:, 5:7, :].rearrange("p b c -> p (b c)"))

            def m5():
                if ci < NT - 1:
                    for g in range(2):
                        P.mm(ps[:, 2 + g, :], Btok[:, t, g * 128:(g + 1) * 128], xdd[:, g * 512:(g + 1) * 512], True, True)
                    if ci > 0:
                        P.tt("pool", hv(S), hv(S), bc_h(cdb, 64), ALU.mult)
                        P.tt("dve", S, S, ps[:, 2:4, :].rearrange("p b c -> p (b c)"), ALU.add)
                    else:
                        P.copy("act", S, ps[:, 2:4, :].rearrange("p b c -> p (b c)"))
                    P.copy("act", Sbf, S)

            def m6():
                if d == 1:
                    P.dma(self.yb_d[tl, :], y)
                else:
                    P.tt("dve", y, y, ybt, ALU.add)

            return [m1, mg(0), mg(1), m4, m5, m6]

        for d in (1, 0):
            order = range(NT) if d == 0 else range(NT - 1, -1, -1)
            pending = None
            for ci, t in enumerate(order):
                ms = main_stages(d, ci, t)
                fs = fin_stages(*pending) if pending is not None else []
                for i in range(6):
                    ms[i]()
                    if i < len(fs):
                        fs[i]()
                if d == 0:
                    pending = (ci, t)
            if pending is not None:
                for f in fin_stages(*pending):
                    f()
        self.allow_over = False

    def merge(self, l, bi, nk, wbr_src, first, pre=None):
        P = self.P
        self.P.phase = "merge"
        ps = self.ps
        hT = self.hT
        onT = self.onT
        self.acur = 64 * KB
        wbr = self.alloc([nk, D], BF16)
        wg = self.alloc([4, 8, 128], BF16)
        sg = self.alloc([2, 512], F32)
        tm = self.alloc([2, 512], F32)
        self.wload(wbr, wbr_src, l, 0, D, engs=BULK)
        self.wload(wg[:, 0], self.w_in, l, bi * D, bi * D + 128)
        self.wload(wg[:, 1], self.w_in, l, bi * D + 128, bi * D + 256)
        self.wload(wg[:, 2], self.w_in, l, bi * D + 256, bi * D + 384)
        if pre is not None:
            save = self.acur
            pre()
            self.acur = save
        n = 0
        for m in range(8):
            if m + 3 < 8:
                self.wload(wg[:, (m + 3) % 4], self.w_in, l, bi * D + (m + 3) * 128, bi * D + (m + 4) * 128)
            for b in range(NB):
                bl = slice(b * 512, (b + 1) * 512)
                yb = 4 + n % 2
                gb = 6 + n % 2
                for k in range(nk):
                    P.mm(ps[:, yb, :], wbr[:, k, m * 128:(m + 1) * 128], onT[:, k, bl], k == 0, k == nk - 1)
                for k in range(8):
                    P.mm(ps[:, gb, :], wg[:, m % 4, k, :], hT[:, k, bl], k == 0, k == 7)
                P.actv(sg[:, n % 2, :], ps[:, gb, :], AF.Sigmoid)
                if first:
                    P.tt("dve", self.mixT[:, m, bl], ps[:, yb, :], sg[:, n % 2, :], ALU.mult)
                else:
                    P.tt("dve", tm[:, n % 2, :], ps[:, yb, :], sg[:, n % 2, :], ALU.mult)
                    P.tt("pool", self.mixT[:, m, bl], self.mixT[:, m, bl], tm[:, n % 2, :], ALU.add)
                n += 1

    def outproj_load(self, l, last):
        self.acur = 96 * KB
        wo = self.alloc([8, D], BF16)
        gB = self.alloc([D], F32)
        self.wload(wo, self.w_out, l, 0, D, engs=BULK)
        grow = 2 if last else l + 1
        self.P.dma(gB, bass.AP(self.norm_g.tensor, grow * D, [[0, 128], [1, D]]))
        return (wo, gB)

    def outproj(self, s, l, last):
        P = self.P
        self.P.phase = "outproj"
        ps = self.ps
        w = self.pre.pop("outproj", None) or self.outproj_load(l, last)
        wo, gB = w
        self.acur = 116 * KB
        xt = self.alloc([2, D], F32)
        xn = self.alloc([2, D], F32)
        junk = self.alloc([D], BF16)
        ss = self.alloc([48], F32)
        hb = self.alloc([2, D], BF16)
        yo = self.alloc([2, D], F32)
        if not last:
            self.pre["ssd"] = self.ssd_load(self.layers[self.layers.index(l) + 1])
        xsrc = self.x if l == 0 else self.xres
        def finish(t):
            i = t % 2
            tl = slice(t * 128, (t + 1) * 128)
            if not last:
                P.dma(self.xres[s, tl, :], xn[:, i, :])
                self.norm_to_hT(xn[:, i, :], t, gB, (junk, ss, hb))
            else:
                P.actv(junk[:, :], xn[:, i, :], AF.Square, accum_out=ss[:, t:t + 1])
                P.actv(ss[:, 16 + t:17 + t], ss[:, t:t + 1], AF.Ln, bias=EPS, scale=1.0 / D)
                P.actv(ss[:, 32 + t:33 + t], ss[:, 16 + t:17 + t], AF.Exp, scale=-0.5)
                P.stt("dve", yo[:, i, :], xn[:, i, :], ss[:, 32 + t:33 + t], gB, ALU.mult, ALU.mult)
                P.dma(self.out[s, tl, :], yo[:, i, :])

        P.dma(xt[:, 0, :], xsrc[s, 0:128, :])
        for t in range(NT):
            i = t % 2
            tl = slice(t * 128, (t + 1) * 128)
            if t + 1 < NT:
                P.dma(xt[:, (t + 1) % 2, :], xsrc[s, (t + 1) * 128:(t + 2) * 128, :])
            for hf in range(2):
                bank = 4 + hf
                for k in range(8):
                    P.mm(ps[:, bank, :], self.mixT[:, k, tl], wo[:, k, hf * 512:(hf + 1) * 512], k == 0, k == 7)
                P.tt("dve", xn[:, i, hf * 512:(hf + 1) * 512], ps[:, bank, :], xt[:, i, hf * 512:(hf + 1) * 512], ALU.add)
            if t >= 1:
                finish(t - 1)
        finish(NT - 1)

    def build(self):
        self.setup()
        for s in range(self.nseq):
            nl = len(self.layers)
            for li, l in enumerate(self.layers):
                if li == 0:
                    self.phase_a(s, l)
                last = (li == nl - 1)
                order = [b for b in "abcd" if b in self.branches]
                phase = {"a": self.phase_ssd, "b": self.phase_mla, "c": self.phase_gla, "d": self.phase_gqa}
                loader = {"b": ("mla", self.mla_load), "c": ("gla", self.gla_load), "d": ("gqa", self.gqa_load)}
                mrg = {"a": (0, 8, self.w_br_a), "b": (1, 4, self.w_br_b), "c": (2, 4, self.w_br_c), "d": (3, 4, self.w_br_d)}
                for bi_, br in enumerate(order):
                    phase[br](l)
                    if bi_ + 1 < len(order):
                        key, fn = loader[order[bi_ + 1]]
                        pre = (lambda key=key, fn=fn: self.pre.__setitem__(key, fn(l)))
                    else:
                        pre = (lambda: self.pre.__setitem__("outproj", self.outproj_load(l, last)))
                    i_, nk_, wsrc_ = mrg[br]
                    self.merge(l, i_, nk_, wsrc_, bi_ == 0, pre=pre)
                self.outproj(s, l, last=(li == nl - 1))
        cnt = self.P.emit(self.es)
        return cnt


def _prep_inputs(inp):
    f = lambda a: np.ascontiguousarray(np.asarray(a, dtype=np.float32))
    w_in = f(inp["w_in"])
    p32, _ = _rope_perm_sign(32)
    p64, _ = _rope_perm_sign(64)
    kr = w_in[:, :, OFF["krope"]:OFF["krope"] + 32][:, :, p32]
    qd = w_in[:, :, OFF["qd"]:OFF["qd"] + 512].reshape(2, D, 8, 64)[:, :, :, p64].reshape(2, D, 512)
    kd = w_in[:, :, OFF["kd"]:OFF["kd"] + 128].reshape(2, D, 2, 64)[:, :, :, p64].reshape(2, D, 128)
    w_perm = np.ascontiguousarray(np.concatenate([kr, qd, kd], axis=2))
    qg = f(inp["q_norm_g"])
    kg = f(inp["k_norm_g"])
    qk_g = np.ascontiguousarray(np.stack([qg, qg[:, p64], kg, kg[:, p64]], axis=2))
    cos64, sin64 = _rope_tables(64)
    cos32, sin32 = _rope_tables(32)
    rope_cos = np.ones((128, L), np.float32)
    rope_sin = np.zeros((128, L), np.float32)
    rope_cos[0:64] = cos64
    rope_sin[0:64] = sin64
    rope_cos[64:96] = cos32
    rope_sin[64:96] = sin32
    wqb = f(inp["w_q_b"])
    w_qb_perm = np.ascontiguousarray(wqb.reshape(2, 384, 8, 96)[:, :, :, 64:96][:, :, :, p32].reshape(2, 384, 256))
    qlg = f(inp["q_lat_norm_g"]).reshape(2, 3, 128).transpose(0, 2, 1)
    kvg = f(inp["kv_lat_norm_g"]).reshape(2, 2, 128).transpose(0, 2, 1)
    lat_g = np.ascontiguousarray(np.concatenate([qlg, kvg], axis=2))
    gla_g = np.ascontiguousarray(f(inp["gla_norm_g"]).reshape(2, 4, 128).transpose(0, 2, 1))
    cw = f(inp["conv_w"]).reshape(2, 5, 12, 128).transpose(0, 3, 2, 1)
    cb = f(inp["conv_b"]).reshape(2, 12, 128).transpose(0, 2, 1)[..., None]
    conv_p = np.ascontiguousarray(np.concatenate([cw, cb], axis=3))
    shared = dict(
        conv_p=conv_p, a_log=f(inp["a_log"]).reshape(2, 32), dt_bias=f(inp["dt_bias"]).reshape(2, 32),
        d_skip=f(inp["d_skip"]), ssd_norm_g=f(inp["ssd_norm_g"]),
        w_gate_up=f(inp["w_gate_up"]), b_gate=f(inp["b_gate"]), gla_g=gla_g,
        w_in=w_in, w_perm=w_perm, w_q_b=wqb, w_qb_perm=w_qb_perm, w_kv_b=f(inp["w_kv_b"]), lat_g=lat_g,
        norm_g=np.ascontiguousarray(np.concatenate([f(inp["norm_g"]), f(inp["final_g"])[None, :]], axis=0)),
        qk_g=qk_g,
        w_br_a=f(inp["w_br_a"]), w_br_b=f(inp["w_br_b"]), w_br_c=f(inp["w_br_c"]), w_br_d=f(inp["w_br_d"]),
        w_out=f(inp["w_out"]), rope_cos=rope_cos, rope_sin=rope_sin,
    )
    return shared


_CACHE = {}


def kernel(**inputs):
    x = np.ascontiguousarray(np.asarray(inputs["x"], dtype=np.float32))
    n_cores = 8
    nseq = x.shape[0] // n_cores
    shared = _prep_inputs(inputs)
    mk = MK(nseq=nseq, layers=(0, 1), branches="abcd")
    mk.build()
    in_maps = []
    for c in range(n_cores):
        m = dict(shared)
        m["x"] = np.ascontiguousarray(x[c * nseq:(c + 1) * nseq])
        in_maps.append(m)
    res = run_bass_kernel_spmd(mk.nc, in_maps, core_ids=list(range(n_cores)))
    out = np.concatenate([np.asarray(r["out"]) for r in res.results], axis=0)
    return out.astype(np.float32)
```
